# Optimizing a Trainium2 kernel written in Bass

```python
import math
import jax, jax.numpy as jnp
from jax import lax
import numpy as np

D_MODEL = 1024
BATCH = 8
SEQ = 2048
DEPTH = 2
DEC_BATCH = 8
DEC_SEQ = 32
PAST_LEN = 2048

CHUNK = 64
Q_BLOCK = 128
EPS = 1e-6
DA_HEADS = 8
DA_HEAD_DIM = 64
DA_QK_DIM = 2 * DA_HEAD_DIM
DA_V_DIM = 2 * DA_HEAD_DIM
DA_WIDTH = DA_HEADS * DA_V_DIM
ROPE_THETA = 500000.0
ROT_DIM = DA_HEAD_DIM // 4
RW_HEAD_DIM = 64
RW_HEADS = D_MODEL // RW_HEAD_DIM
RW_WIDTH = RW_HEADS * RW_HEAD_DIM
DECAY_LORA = 64
AAA_LORA = 64
GATE_LORA = 128
GN_EPS = 64e-5
D_FF = 2816
CONV_W = 3
P_DA = 3 * DA_WIDTH
RW_OFF_R = 0
RW_OFF_K = RW_WIDTH
RW_OFF_V = 2 * RW_WIDTH
RW_OFF_WD = 3 * RW_WIDTH
RW_OFF_AD = RW_OFF_WD + DECAY_LORA
RW_OFF_GD = RW_OFF_AD + AAA_LORA
P_RW = RW_OFF_GD + GATE_LORA
P_GATE = 2 * D_MODEL
P_TOTAL = P_DA + P_RW + P_GATE

kernel_name = 'diff_rwkv7_convffn_stream_step'


def rmsnorm(x, g):
    xf = x.astype(jnp.float32)
    y = xf * lax.rsqrt(jnp.mean(xf * xf, axis=-1, keepdims=True) + EPS)
    return (y * g.astype(jnp.float32)).astype(x.dtype)


def partial_rope(x, pos):
    inv = ROPE_THETA ** (-jnp.arange(0, ROT_DIM, 2, dtype=jnp.float32) / ROT_DIM)
    ang = pos.astype(jnp.float32)[:, None] * inv[None, :]
    cos = jnp.cos(ang)[None, :, None, None, :]
    sin = jnp.sin(ang)[None, :, None, None, :]
    xr = x[..., :ROT_DIM].astype(jnp.float32)
    x1, x2 = xr[..., :ROT_DIM // 2], xr[..., ROT_DIM // 2:]
    rot = jnp.concatenate([x1 * cos - x2 * sin, x2 * cos + x1 * sin], axis=-1).astype(x.dtype)
    return jnp.concatenate([rot, x[..., ROT_DIM:]], axis=-1)


def diff_attend(q, k, v, lam, mask):
    s = jnp.einsum('bqhcd,bkhcd->bchqk', q.astype(jnp.float32), k.astype(jnp.float32)) * (DA_HEAD_DIM ** -0.5)
    if mask is not None:
        s = jnp.where(mask, s, -1e30)
    p = jax.nn.softmax(s, axis=-1)
    pd = p[:, 0] - lam * p[:, 1]
    return jnp.einsum('bhqk,bkhd->bqhd', pd, v.astype(jnp.float32)).astype(v.dtype)


def diff_attn_prompt(q, k, v, lam):
    B, T = q.shape[0], q.shape[1]
    nb = T // Q_BLOCK
    qb = jnp.moveaxis(q.reshape(B, nb, Q_BLOCK, DA_HEADS, 2, DA_HEAD_DIM), 1, 0)
    kchunk = jnp.arange(T, dtype=jnp.int32) // CHUNK

    def blk(args):
        qi, i = args
        qchunk = (i * Q_BLOCK + jnp.arange(Q_BLOCK, dtype=jnp.int32)) // CHUNK
        mask = kchunk[None, :] <= qchunk[:, None]
        return diff_attend(qi, k, v, lam, mask)

    out = lax.map(blk, (qb, jnp.arange(nb, dtype=jnp.int32)))
    return jnp.moveaxis(out, 0, 1).reshape(B, T, DA_HEADS, DA_V_DIM)


def heads(t):
    return t.reshape(t.shape[0], t.shape[1], RW_HEADS, RW_HEAD_DIM).astype(jnp.float32)


def wkv7_scan(S0, r, w, k, v, kk, a):
    def step(S, inp):
        r_t, w_t, k_t, v_t, kk_t, a_t = inp
        sa = jnp.einsum('bhvk,bhk->bhv', S, -kk_t)
        S = (S * w_t[:, :, None, :] + sa[..., None] * (kk_t * a_t)[:, :, None, :]
             + v_t[..., None] * k_t[:, :, None, :])
        y = jnp.einsum('bhvk,bhk->bhv', S, r_t)
        return S, y

    xs = tuple(jnp.moveaxis(t, 1, 0) for t in (r, w, k, v, kk, a))
    S, ys = lax.scan(step, S0.astype(jnp.float32), xs)
    return S, jnp.moveaxis(ys, 0, 1)


def layer(x, pos, lidx, past_k, past_v, wkv0, shift0, conv0,
          norm_mix, w_in, da_lambda, da_subln, w_o_da,
          rw_mu, rw_w0, rw_w2, rw_a0, rw_a2, rw_g2, rw_k_k, rw_k_a, rw_r_k,
          rw_ln_w, rw_ln_b, w_o_rw, w_out, norm_ffn, w_up, ffn_conv, ffn_conv_b, w_down):
    B, T, _ = x.shape
    h = rmsnorm(x, norm_mix)
    proj = h @ w_in
    u_da = proj[..., :P_DA]
    u_rw = proj[..., P_DA:P_DA + P_RW]
    u_gate = proj[..., P_DA + P_RW:]

    q = partial_rope(u_da[..., :DA_WIDTH].reshape(B, T, DA_HEADS, 2, DA_HEAD_DIM), pos)
    k = partial_rope(u_da[..., DA_WIDTH:2 * DA_WIDTH].reshape(B, T, DA_HEADS, 2, DA_HEAD_DIM), pos)
    v = u_da[..., 2 * DA_WIDTH:].reshape(B, T, DA_HEADS, DA_V_DIM)
    lam_init = 0.8 - 0.6 * math.exp(-0.3 * lidx)
    lp = da_lambda.astype(jnp.float32)
    lam = jnp.exp(jnp.sum(lp[0] * lp[1])) - jnp.exp(jnp.sum(lp[2] * lp[3])) + lam_init
    if past_k is None:
        o = diff_attn_prompt(q, k, v, lam)
    else:
        k_all = jnp.concatenate([past_k.reshape(B, -1, DA_HEADS, 2, DA_HEAD_DIM).astype(k.dtype), k], axis=1)
        v_all = jnp.concatenate([past_v.astype(v.dtype), v], axis=1)
        o = diff_attend(q, k_all, v_all, lam, None)
    o = rmsnorm(o, da_subln) * (1.0 - lam_init)
    o_da = o.reshape(B, T, DA_WIDTH) @ w_o_da
    new_k = k.reshape(B, T, DA_HEADS, DA_QK_DIM)

    u_prev = jnp.concatenate([shift0.astype(u_rw.dtype), u_rw[:, :-1]], axis=1)
    us = (u_rw + (u_prev - u_rw) * rw_mu).astype(jnp.float32)
    new_shift = u_rw[:, -1:]
    r = heads(us[..., RW_OFF_R:RW_OFF_K])
    kr = us[..., RW_OFF_K:RW_OFF_V]
    vr = heads(us[..., RW_OFF_V:RW_OFF_WD])
    wd = us[..., RW_OFF_WD:RW_OFF_AD]
    ad = us[..., RW_OFF_AD:RW_OFF_GD]
    gd = us[..., RW_OFF_GD:]
    w_log = -jax.nn.softplus(-(rw_w0 + jnp.tanh(wd) @ rw_w2)) - 0.5
    decay = heads(jnp.exp(-jnp.exp(w_log)))
    a = jax.nn.sigmoid(rw_a0 + ad @ rw_a2)
    g = jax.nn.sigmoid(gd) @ rw_g2
    kk = heads(kr * rw_k_k)
    kk = kk * lax.rsqrt(jnp.maximum(jnp.sum(kk * kk, axis=-1, keepdims=True), 1e-24))
    kmod = heads(kr * (1.0 + (a - 1.0) * rw_k_a))
    S, y = wkv7_scan(wkv0, r, decay, kmod, vr, kk, heads(a))
    mu = jnp.mean(y, axis=-1, keepdims=True)
    var = jnp.mean(jnp.square(y - mu), axis=-1, keepdims=True)
    yn = ((y - mu) * lax.rsqrt(var + GN_EPS)).reshape(B, T, RW_WIDTH) * rw_ln_w + rw_ln_b
    bonus = (jnp.sum(r * kmod * rw_r_k, axis=-1, keepdims=True) * vr).reshape(B, T, RW_WIDTH)
    o_rw = ((yn + bonus) * g).astype(x.dtype) @ w_o_rw

    g_da = jax.nn.sigmoid(u_gate[..., :D_MODEL])
    g_rw = jax.nn.sigmoid(u_gate[..., D_MODEL:])
    x = x + (g_da * o_da + g_rw * o_rw) @ w_out

    h2 = rmsnorm(x, norm_ffn)
    up = h2 @ w_up
    hp = jnp.concatenate([conv0.astype(up.dtype), up], axis=1)
    c = ffn_conv_b + hp[:, 0:T] * ffn_conv[0]
    for j in range(1, CONV_W):
        c = c + hp[:, j:j + T] * ffn_conv[j]
    x = x + (jax.nn.silu(c[..., :D_FF]) * c[..., D_FF:]) @ w_down
    new_conv = hp[:, -(CONV_W - 1):]
    return x, new_k, v, S.astype(wkv0.dtype), new_shift, new_conv


def setup_inputs(seed: int = 0) -> dict:
    key = jax.random.key(seed)
    ks = jax.random.split(key, 32)
    f32 = jnp.float32

    def nrm(i, shape, scale):
        return jax.random.normal(ks[i], shape, f32) * scale

    return {
        'x_prompt': nrm(0, (BATCH, SEQ, D_MODEL), 1.0),
        'x_sample': nrm(1, (DEC_BATCH, DEC_SEQ, D_MODEL), 1.0),
        'cache_k': nrm(2, (DEPTH, DEC_BATCH, PAST_LEN, DA_HEADS, DA_QK_DIM), 1.0),
        'cache_v': nrm(3, (DEPTH, DEC_BATCH, PAST_LEN, DA_HEADS, DA_V_DIM), 1.0),
        'state_wkv': nrm(4, (DEPTH, DEC_BATCH, RW_HEADS, RW_HEAD_DIM, RW_HEAD_DIM), 0.3),
        'state_shift': nrm(5, (DEPTH, DEC_BATCH, 1, P_RW), 1.0),
        'state_ffn_conv': nrm(6, (DEPTH, DEC_BATCH, CONV_W - 1, 2 * D_FF), 1.0),
        'norm_mix': 1.0 + nrm(7, (DEPTH, D_MODEL), 0.02),
        'w_in': nrm(8, (DEPTH, D_MODEL, P_TOTAL), D_MODEL ** -0.5),
        'da_lambda': nrm(9, (DEPTH, 4, DA_HEAD_DIM), 0.1),
        'da_subln': 1.0 + nrm(10, (DEPTH, DA_V_DIM), 0.02),
        'w_o_da': nrm(11, (DEPTH, DA_WIDTH, D_MODEL), DA_WIDTH ** -0.5),
        'rw_mu': 0.5 + nrm(12, (DEPTH, P_RW), 0.1),
        'rw_w0': -1.0 + nrm(13, (DEPTH, RW_WIDTH), 0.5),
        'rw_w2': nrm(14, (DEPTH, DECAY_LORA, RW_WIDTH), 0.1),
        'rw_a0': nrm(15, (DEPTH, RW_WIDTH), 0.1),
        'rw_a2': nrm(16, (DEPTH, AAA_LORA, RW_WIDTH), 0.1),
        'rw_g2': nrm(17, (DEPTH, GATE_LORA, RW_WIDTH), GATE_LORA ** -0.5),
        'rw_k_k': 0.85 + nrm(18, (DEPTH, RW_WIDTH), 0.05),
        'rw_k_a': 1.0 + nrm(19, (DEPTH, RW_WIDTH), 0.05),
        'rw_r_k': nrm(20, (DEPTH, RW_HEADS, RW_HEAD_DIM), 0.1),
        'rw_ln_w': 1.0 + nrm(21, (DEPTH, RW_WIDTH), 0.02),
        'rw_ln_b': nrm(22, (DEPTH, RW_WIDTH), 0.01),
        'w_o_rw': nrm(23, (DEPTH, RW_WIDTH, D_MODEL), RW_WIDTH ** -0.5),
        'w_out': nrm(24, (DEPTH, D_MODEL, D_MODEL), D_MODEL ** -0.5),
        'norm_ffn': 1.0 + nrm(25, (DEPTH, D_MODEL), 0.02),
        'w_up': nrm(26, (DEPTH, D_MODEL, 2 * D_FF), D_MODEL ** -0.5),
        'ffn_conv': nrm(27, (DEPTH, CONV_W, 2 * D_FF), 0.5),
        'ffn_conv_b': nrm(28, (DEPTH, 2 * D_FF), 0.01),
        'w_down': nrm(29, (DEPTH, D_FF, D_MODEL), D_FF ** -0.5),
        'norm_final': 1.0 + nrm(30, (D_MODEL,), 0.02),
    }


def reference(x_prompt, x_sample, cache_k, cache_v, state_wkv, state_shift, state_ffn_conv,
              norm_mix, w_in, da_lambda, da_subln, w_o_da,
              rw_mu, rw_w0, rw_w2, rw_a0, rw_a2, rw_g2, rw_k_k, rw_k_a, rw_r_k,
              rw_ln_w, rw_ln_b, w_o_rw, w_out, norm_ffn, w_up, ffn_conv, ffn_conv_b, w_down,
              norm_final):
    Bp, Tp, _ = x_prompt.shape
    Bs, Ts, _ = x_sample.shape
    past = cache_k.shape[2]
    pos_p = jnp.arange(Tp, dtype=jnp.int32)
    pos_s = past + jnp.arange(Ts, dtype=jnp.int32)
    wkv_zero = jnp.zeros((Bp, RW_HEADS, RW_HEAD_DIM, RW_HEAD_DIM), x_prompt.dtype)
    shift_zero = jnp.zeros((Bp, 1, P_RW), x_prompt.dtype)
    conv_zero = jnp.zeros((Bp, CONV_W - 1, 2 * D_FF), x_prompt.dtype)

    xp, xs = x_prompt, x_sample
    outs_p, outs_s = [], []
    for l in range(DEPTH):
        wl = (norm_mix[l], w_in[l], da_lambda[l], da_subln[l], w_o_da[l],
              rw_mu[l], rw_w0[l], rw_w2[l], rw_a0[l], rw_a2[l], rw_g2[l], rw_k_k[l], rw_k_a[l], rw_r_k[l],
              rw_ln_w[l], rw_ln_b[l], w_o_rw[l], w_out[l], norm_ffn[l], w_up[l], ffn_conv[l], ffn_conv_b[l], w_down[l])
        xp, kp, vp, sp, shp, cp = layer(xp, pos_p, l, None, None, wkv_zero, shift_zero, conv_zero, *wl)
        xs, kq, vq, sq, shq, cq = layer(xs, pos_s, l, cache_k[l], cache_v[l], state_wkv[l],
                                        state_shift[l], state_ffn_conv[l], *wl)
        outs_p.append((kp, vp, sp, shp, cp))
        outs_s.append((kq, vq, sq, shq, cq))

    k_prompt, v_prompt, wkv_prompt, shift_prompt, conv_prompt = [jnp.stack(t) for t in zip(*outs_p)]
    k_sample, v_sample, wkv_sample, shift_sample, conv_sample = [jnp.stack(t) for t in zip(*outs_s)]
    y_prompt = rmsnorm(xp, norm_final)
    y_sample = rmsnorm(xs, norm_final)
    return (y_prompt, y_sample, k_prompt, v_prompt, wkv_prompt, shift_prompt, conv_prompt,
            k_sample, v_sample, wkv_sample, shift_sample, conv_sample)
```

```python
import numpy as np
from contextlib import ExitStack

import concourse.bass as bass
import concourse.mybir as mybir

F32 = mybir.dt.float32
BF16 = mybir.dt.bfloat16
F32R = mybir.dt.float32r
AF = mybir.ActivationFunctionType
ALU = mybir.AluOpType
AX = mybir.AxisListType

ENGS = ("pe", "act", "dve", "pool", "sp")
SEM_EPOCH = 20000
DMA_ROT = 8


class Buf:
    __slots__ = ("w", "r", "name")

    def __init__(self, name=""):
        self.w = None
        self.r = []
        self.name = name


class T:
    __slots__ = ("ap", "bufs")

    def __init__(self, ap, bufs):
        self.ap = ap
        self.bufs = tuple(bufs)

    def __getitem__(self, key):
        return T(self.ap[key], self.bufs)

    def v(self, ap):
        return T(ap, self.bufs)

    def bitcast(self, dt):
        return T(self.ap.bitcast(dt), self.bufs)


class Rec:
    __slots__ = ("eng", "idx", "fn", "deps", "dma", "signaled", "ev", "selfwait")

    def __init__(self, eng, idx, fn, dma):
        self.eng = eng
        self.idx = idx
        self.fn = fn
        self.deps = []
        self.dma = dma
        self.signaled = False
        self.ev = None
        self.selfwait = None


class Sched:
    def __init__(self, nc):
        self.nc = nc
        self.q = {e: [] for e in ENGS}
        self.stack = ExitStack()
        self.nbuf = 0
        self.same_engine_sync = True
        self._bar_pos = {}
        self._pending = {e: [] for e in ENGS}

    def sbuf(self, name, shape, dtype, nbufs=1):
        h = self.stack.enter_context(self.nc.sbuf_tensor(name, list(shape), dtype))
        return T(h[:], [Buf(name)])

    def psum(self, name, shape, dtype=F32):
        h = self.stack.enter_context(self.nc.psum_tensor(name, list(shape), dtype))
        return T(h[:], [Buf(name)])

    def dram(self, name, shape, dtype, kind):
        h = self.nc.dram_tensor(name, list(shape), dtype, kind=kind)
        return T(h.ap(), [Buf(name)])

    def newbuf(self, name=""):
        return Buf(name)

    def add(self, eng, fn, reads=(), writes=(), dma=False):
        q = self.q[eng]
        rec = Rec(eng, len(q), fn, dma)
        deps = {}
        for t in reads:
            for b in t.bufs:
                if b.w is not None:
                    deps[id(b.w)] = b.w
        for t in writes:
            for b in t.bufs:
                if b.w is not None:
                    deps[id(b.w)] = b.w
                for r in b.r:
                    deps[id(r)] = r
        if self._pending[eng]:
            for d in self._pending[eng]:
                deps[id(d)] = d
            self._pending[eng] = []
            barrier_deps = True
        else:
            barrier_deps = False
        for d in deps.values():
            if d is rec:
                continue
            if d.eng == eng and not d.dma and not dma:
                if eng == "pe" or eng == "sp" or not self.same_engine_sync:
                    continue
            rec.deps.append(d)
        for t in reads:
            for b in t.bufs:
                b.r.append(rec)
        for t in writes:
            for b in t.bufs:
                b.w = rec
                b.r = []
        q.append(rec)
        return rec

    def emit(self):
        nc = self.nc
        for e in ENGS:
            for rec in self.q[e]:
                for d in rec.deps:
                    d.signaled = True
        sems = {}
        for e in ENGS:
            cnt = 0
            for rec in self.q[e]:
                if rec.dma:
                    continue
                if rec.signaled:
                    ep = cnt // SEM_EPOCH
                    key = (e, ep)
                    if key not in sems:
                        sems[key] = self.stack.enter_context(nc.semaphore(f"s_{e}_{ep}"))
                    rec.ev = (sems[key], cnt % SEM_EPOCH + 1, key)
                    cnt += 1
        self.final_waits = []
        for e in ENGS:
            j = 0
            last = {}
            for rec in self.q[e]:
                if not rec.dma:
                    continue
                slot = j % DMA_ROT
                key = ("dma", e, slot)
                if key not in sems:
                    sems[key] = self.stack.enter_context(nc.semaphore(f"d_{e}_{slot}"))
                val = 16 * (j // DMA_ROT + 1)
                rec.ev = (sems[key], val, key)
                if j >= DMA_ROT:
                    rec.selfwait = (sems[key], val - 16, key)
                last[key] = (sems[key], val, key)
                j += 1
            self.final_waits.extend(last.values())

        block = self.stack.enter_context(nc.Block())
        sched = self

        def run(engname, eobj):
            waited = {}
            for rec in sched.q[engname]:
                waits = {}
                if rec.selfwait is not None:
                    s, v, key = rec.selfwait
                    waits[key] = (s, v)
                for d in rec.deps:
                    s, v, key = d.ev
                    if key in waits:
                        if waits[key][1] < v:
                            waits[key] = (s, v)
                    else:
                        waits[key] = (s, v)
                for key, (s, v) in waits.items():
                    if waited.get(key, 0) >= v:
                        continue
                    eobj.wait_ge(s, v)
                    waited[key] = v
                ins = rec.fn(eobj)
                if rec.dma:
                    ins.then_inc(rec.ev[0], 16)
                elif rec.signaled:
                    ins.then_inc(rec.ev[0], 1)
            if engname == "sp":
                for s, v, key in sched.final_waits:
                    eobj.wait_ge(s, v)

        @block.tensor
        def _(e):
            run("pe", e)

        @block.scalar
        def _(e):
            run("act", e)

        @block.vector
        def _(e):
            run("dve", e)

        @block.gpsimd
        def _(e):
            run("pool", e)

        @block.sync
        def _(e):
            run("sp", e)

    def close(self):
        self.stack.close()

    def dma(self, out, in_, eng="sp", **kw):
        return self.add(eng, lambda e: e.dma_start(out=out.ap, in_=in_.ap, **kw),
                        reads=[in_], writes=[out], dma=True)

    def mm(self, out, lhsT, rhs, start=True, stop=True, extra_reads=(), **kw):
        return self.add("pe", lambda e: e.matmul(out.ap, lhsT.ap, rhs.ap, start=start, stop=stop, **kw),
                        reads=[lhsT, rhs, *extra_reads], writes=[out])

    def tr(self, out, in_, ident, **kw):
        return self.add("pe", lambda e: e.transpose(out.ap, in_.ap, ident.ap, **kw),
                        reads=[in_, ident], writes=[out])

    def actf(self, out, in_, func, bias=None, scale=None, accum=None, eng="act"):
        kw = {}
        reads = [in_]
        writes = [out]
        if bias is not None:
            if isinstance(bias, T):
                kw["bias"] = bias.ap
                reads.append(bias)
            else:
                kw["bias"] = bias
        if scale is not None:
            if isinstance(scale, T):
                kw["scale"] = scale.ap
                reads.append(scale)
            else:
                kw["scale"] = scale
        if accum is not None:
            kw["accum_out"] = accum.ap
            writes.append(accum)
        return self.add("act", lambda e: e.activation(out.ap, in_.ap, func, **kw), reads=reads, writes=writes)

    def tt(self, out, a, b, op, eng="dve"):
        return self.add(eng, lambda e: e.tensor_tensor(out.ap, a.ap, b.ap, op), reads=[a, b], writes=[out])

    def ts(self, out, a, s1, op0, s2=None, op1=None, accum=None, eng="dve"):
        reads = [a]
        writes = [out]
        s1v = s1.ap if isinstance(s1, T) else s1
        s2v = s2.ap if isinstance(s2, T) else s2
        if isinstance(s1, T):
            reads.append(s1)
        if isinstance(s2, T):
            reads.append(s2)
        kw = {}
        if op1 is not None:
            kw["op1"] = op1
        if accum is not None:
            kw["accum_out"] = accum.ap
            writes.append(accum)
        return self.add(eng, lambda e: e.tensor_scalar(out.ap, a.ap, s1v, s2v, op0, **kw), reads=reads, writes=writes)

    def stt(self, out, a, s, b, op0, op1, eng="dve"):
        reads = [a, b]
        sv = s.ap if isinstance(s, T) else s
        if isinstance(s, T):
            reads.append(s)
        return self.add(eng, lambda e: e.scalar_tensor_tensor(out.ap, a.ap, sv, b.ap, op0, op1), reads=reads, writes=[out])

    def copy(self, out, in_, eng="dve"):
        if eng == "act":
            return self.add("act", lambda e: e.copy(out.ap, in_.ap), reads=[in_], writes=[out])
        return self.add(eng, lambda e: e.tensor_copy(out.ap, in_.ap), reads=[in_], writes=[out])

    def memset(self, out, val, eng="pool"):
        return self.add(eng, lambda e: e.memset(out.ap, val), reads=[], writes=[out])

    def recip(self, out, in_):
        return self.add("dve", lambda e: e.reciprocal(out.ap, in_.ap), reads=[in_], writes=[out])

    def scan(self, out, d0, d1, init, op0, op1):
        reads = [d0, d1]
        iv = init.ap if isinstance(init, T) else init
        if isinstance(init, T):
            reads.append(init)
        return self.add("dve", lambda e: e.tensor_tensor_scan(out.ap, d0.ap, d1.ap, iv, op0, op1), reads=reads, writes=[out])


def _barrier(self):
    deps = []
    for e in ENGS:
        q = self.q[e]
        last_c = None
        for rec in reversed(q):
            if not rec.dma:
                last_c = rec
                break
        if last_c is not None:
            deps.append(last_c)
        for rec in q[self._bar_pos.get(e, 0):]:
            if rec.dma:
                deps.append(rec)
        self._bar_pos[e] = len(q)
    for e in ENGS:
        self._pending[e] = list(deps)


Sched.barrier = _barrier


from concourse.bass_utils import run_bass_kernel_spmd

D = 1024
TP = 2048
TSV = 32
NTOK = 2176
NTB = 17
P_DA = 3072
P_RW = 3328
PTOT = 8448
DFF = 2816
EPS = 1e-6
GN_EPS = 64e-5
ROPE_THETA = 500000.0
TQ = [(0, 512), (512, 512), (1024, 512), (1536, 512), (2048, 128)]
PF_NMIX, PF_NFFN, PF_MU, PF_W0, PF_A0, PF_KK, PF_KA, PF_RK, PF_LNW, PF_LNB = 0, 8, 16, 42, 50, 58, 66, 74, 82, 90
PF_CONV, PF_CONVB, PF_SUBLN, NPF = 98, 230, 274, 275
C_ID, C_BD, C_MU, C_ML, C_MUI, C_RST, C_COS, C_SIN, NCONST = 0, 128, 256, 384, 512, 576, 832, 1104, 1376
DEC_C = -0.6065306597126334


def make_consts():
    c = np.zeros((128, NCONST), np.float32)
    c[:, C_ID:C_ID + 128] = np.eye(128, dtype=np.float32)
    p = np.arange(128)
    h = p // 64
    s = p % 64
    bd = (h[:, None] == h[None, :]).astype(np.float32)
    c[:, C_BD:C_BD + 128] = bd
    c[:, C_MU:C_MU + 128] = bd * (s[:, None] < s[None, :])
    c[:, C_ML:C_ML + 128] = bd * (s[:, None] > s[None, :])
    c[:, C_MUI:C_MUI + 64] = (s[:, None] <= np.arange(64)[None, :])
    rst = np.ones(256, np.float32)
    rst[::64] = 0
    c[:, C_RST:C_RST + 256] = rst[None, :]
    inv = (np.float32(ROPE_THETA) ** (-np.arange(0, 16, 2, dtype=np.float32) / np.float32(16))).astype(np.float32)
    for tb in range(NTB):
        pos = (tb * 128 + p) if tb < 16 else (TP + p)
        ang = pos.astype(np.float32)[:, None] * inv[None, :]
        co = np.cos(ang).astype(np.float32)
        si = np.sin(ang).astype(np.float32)
        c[:, C_COS + tb * 16:C_COS + tb * 16 + 8] = co
        c[:, C_COS + tb * 16 + 8:C_COS + tb * 16 + 16] = co
        c[:, C_SIN + tb * 16:C_SIN + tb * 16 + 8] = si
        c[:, C_SIN + tb * 16 + 8:C_SIN + tb * 16 + 16] = si
    return c


def r32(t):
    return T(t.ap.bitcast(F32R), t.bufs)


class Arena:
    def __init__(self, t, width):
        self.t = t
        self.W = width
        self.off = 0

    def seek(self, off):
        self.off = off

    def alloc(self, free_shape, dtype):
        n = 1
        for d in free_shape:
            n *= d
        words = n if dtype != BF16 else (n + 1) // 2
        words = (words + 7) // 8 * 8
        assert self.off + words <= self.W, ("arena overflow", self.off, words, self.W)
        ap = self.t.ap[:, self.off:self.off + words]
        if dtype == BF16:
            ap = ap.bitcast(BF16)
        ap = ap[:, 0:n]
        if len(free_shape) > 1:
            names = [f"d{i}" for i in range(len(free_shape))]
            pat = "p (" + " ".join(names) + ") -> p " + " ".join(names)
            kw = {nm: sz for nm, sz in zip(names[:-1], free_shape[:-1])}
            ap = ap.rearrange(pat, **kw)
        self.off += words
        return T(ap, [Buf()])


class Ring:
    def __init__(self, items):
        self.items = items
        self.i = 0

    def next(self):
        t = self.items[self.i % len(self.items)]
        self.i += 1
        return t


class StopBuild(Exception):
    pass


def build(dbg=None, stop_after=None):
    def chk(tag, l=0):
        if stop_after == (tag, l):
            raise StopBuild()

    nc = bass.Bass("TRN2", target_bir_lowering=False)
    S = Sched(nc)

    def din(name, shape):
        return S.dram(name, shape, F32, "ExternalInput")

    def dout(name, shape):
        return S.dram(name, shape, F32, "ExternalOutput")

    xp = din("xp", [TP, D])
    xs = din("xs", [TSV, D])
    ck = din("ck", [2, TP, D])
    cv = din("cv", [2, TP, D])
    swkv = din("swkv", [2, 8, 128, 64])
    sfm = din("sfm", [2, 128, 26])
    cfm = din("cfm", [2, 128, 88])
    pfm = din("pfm", [2, 128, NPF])
    consts = din("consts", [128, NCONST])
    w_in = din("w_in", [2, D, PTOT])
    da_lambda = din("da_lambda", [512])
    w_o_da = din("w_o_da", [2, D, D])
    rw_w2 = din("rw_w2", [2, 64, D])
    rw_a2 = din("rw_a2", [2, 64, D])
    rw_g2 = din("rw_g2", [2, 128, D])
    w_o_rw = din("w_o_rw", [2, D, D])
    w_out = din("w_out", [2, D, D])
    w_up = din("w_up", [2, D, 2 * DFF])
    w_down = din("w_down", [2, DFF, D])
    nfin = din("nfin", [D])

    yp = dout("yp", [TP, D])
    ys = dout("ys", [TSV, D])
    kp = dout("kp", [2, TP, D])
    vp = dout("vp", [2, TP, D])
    wkvp = dout("wkvp", [2, 8, 128, 64])
    shp = dout("shp", [2, 128, 26])
    cvp = dout("cvp", [2, 128, 88])
    ks = dout("ks", [2, TSV, D])
    vs = dout("vs", [2, TSV, D])
    wkvs = dout("wkvs", [2, 8, 128, 64])
    shs = dout("shs", [2, 128, 26])
    cvs = dout("cvs", [2, 128, 88])
    X1 = S.dram("X1", [NTOK, D], F32, "Internal")
    X2 = S.dram("X2", [NTOK, D], F32, "Internal")
    dbg_out = {}
    if dbg:
        for name, shape in dbg.items():
            dbg_out[name] = dout("dbg_" + name, shape)

    cst = S.sbuf("cst", [128, NCONST], F32)
    S.dma(cst, consts)
    ident_bf = S.sbuf("ident_bf", [128, 128], BF16)
    S.copy(ident_bf, cst[:, C_ID:C_ID + 128])
    ones_bf = S.sbuf("ones_bf", [128, 128], BF16)
    S.memset(ones_bf, 1.0)
    ident_f = cst[:, C_ID:C_ID + 128]
    bdmask = cst[:, C_BD:C_BD + 128]
    bd_bf = S.sbuf("bd_bf", [128, 128], BF16)
    S.copy(bd_bf, cst[:, C_BD:C_BD + 128])
    fr = S.sbuf("fr", [128, 21, 256], F32)
    pf = S.sbuf("pf", [128, 2, NPF], F32)
    for l in range(2):
        S.dma(pf[:, l, :], pfm[l])
    gfin = S.sbuf("gfin", [128, D], F32)
    S.dma(gfin, nfin.v(nfin.ap.partition_broadcast(128)))
    lamb = S.sbuf("lamb", [128, 512], F32)
    S.dma(lamb, da_lambda.v(da_lambda.ap.partition_broadcast(128)))
    lamv = S.sbuf("lamv", [128, 16], F32)
    epsb = S.sbuf("epsb", [128, 2], F32)
    S.memset(epsb[:, 0:1], EPS)
    S.memset(epsb[:, 1:2], GN_EPS)
    ljunk = S.sbuf("ljunk", [128, 64], F32)
    shp_t = S.sbuf("shp_t", [128, 26], F32)
    shs_t = S.sbuf("shs_t", [128, 26], F32)
    carry_p = S.sbuf("carry_p", [128, 44, 2], F32)
    carry_s = S.sbuf("carry_s", [128, 44, 2], F32)

    ps = [S.psum(f"ps{i}", [128, 512], F32) for i in range(8)]
    psr = Ring(ps)
    halves = []
    for i in range(8):
        halves.append(T(ps[i].ap[:, 0:256], ps[i].bufs))
        halves.append(T(ps[i].ap[:, 256:512], ps[i].bufs))

    AW = ((nc.sbuf_bytes_remaining - 256) // 4) // 8 * 8
    ar = S.sbuf("arena", [128, AW], F32)
    A = Arena(ar, AW)
    hT = A.alloc([8, NTOK], BF16)
    OZ = A.alloc([8, NTOK], BF16)
    MT_OFF = A.off
    MT = A.alloc([8, NTOK], BF16)
    LOC = A.off
    OZ_OFF = MT_OFF - (MT_OFF - 0) // 2 if False else None
    S.memset(hT, 0.0)
    S.memset(OZ, 0.0, eng="dve")
    S.memset(MT, 0.0)

    for l in range(2):
        lam_init = 0.8 - 0.6 * float(np.exp(-0.3 * l))
        b = l * 256
        pr = ljunk
        S.tt(pr, lamb[:, b:b + 64], lamb[:, b + 64:b + 128], ALU.mult)
        S.ts(pr, pr, 1.0, ALU.mult, None, ALU.add, accum=lamv[:, l * 8 + 4:l * 8 + 5])
        S.tt(pr, lamb[:, b + 128:b + 192], lamb[:, b + 192:b + 256], ALU.mult)
        S.ts(pr, pr, 1.0, ALU.mult, None, ALU.add, accum=lamv[:, l * 8 + 5:l * 8 + 6])
        S.actf(lamv[:, l * 8 + 2:l * 8 + 4], lamv[:, l * 8 + 4:l * 8 + 6], AF.Exp)
        S.tt(lamv[:, l * 8:l * 8 + 1], lamv[:, l * 8 + 3:l * 8 + 4], lamv[:, l * 8 + 2:l * 8 + 3], ALU.subtract)
        S.ts(lamv[:, l * 8:l * 8 + 1], lamv[:, l * 8:l * 8 + 1], -lam_init, ALU.add)
        S.ts(lamv[:, l * 8 + 1:l * 8 + 2], pf[:, l, PF_SUBLN:PF_SUBLN + 1], 1.0 - lam_init, ALU.mult)

    def wslab(dst, src_ap):
        return S.dma(dst, T(src_ap.rearrange("(kt p) c -> p kt c", p=128), ()), eng="pool")

    def make_norm_scratch():
        d = {}
        d["junk"] = A.alloc([D], BF16)
        d["xn"] = Ring([A.alloc([D], BF16) for _ in range(2)])
        d["st"] = Ring([A.alloc([4], F32) for _ in range(3)])
        return d

    def norm_stats(x_t, nsc):
        st = nsc["st"].next()
        S.actf(nsc["junk"], x_t, AF.Square, accum=st[:, 0:1])
        S.actf(st[:, 1:2], st[:, 0:1], AF.Sqrt, scale=1.0 / D, bias=epsb[:, 0:1])
        S.recip(st[:, 2:3], st[:, 1:2])
        return st[:, 2:3]

    def norm_to_hT(x_t, l, pfoff, tb, nsc):
        rstd = norm_stats(x_t, nsc)
        xn = nsc["xn"].next()
        S.ts(xn, x_t, rstd, ALU.mult)
        pb = psr.next()
        pst = T(pb.ap.bitcast(BF16).rearrange("p (k t) -> p k t", k=8), pb.bufs)
        for kt in range(8):
            S.tr(pst[:, kt, :], xn[:, kt * 128:(kt + 1) * 128], ident_bf)
        g = pf[:, l, pfoff:pfoff + 8]
        S.tt(hT[:, :, tb * 128:(tb + 1) * 128], pst, g.v(g.ap.unsqueeze(2).broadcast_to([128, 8, 128])), ALU.mult)

    def x_rows(tb):
        return (xp[tb * 128:(tb + 1) * 128, :], 128) if tb < 16 else (xs, TSV)

    def phase_norm0():
        A.seek(LOC)
        nsc = make_norm_scratch()
        xr = Ring([A.alloc([D], F32) for _ in range(3)])
        for tb in range(NTB):
            xt = xr.next()
            src, nr = x_rows(tb)
            if nr < 128:
                S.memset(xt, 0.0)
            S.dma(xt[0:nr, :], src)
            norm_to_hT(xt, 0, PF_NMIX, tb, nsc)

    def phase_rwkv(l):
        A.seek(MT_OFF)
        w_l = w_in[l]
        mu = lambda c: pf[:, l, PF_MU + c:PF_MU + c + 1]
        W2p = A.alloc([D], BF16)
        A2p = A.alloc([D], BF16)
        G2 = A.alloc([D], BF16)
        S.memset(W2p[64:128, :], 0.0)
        S.memset(A2p[0:64, :], 0.0)
        S.dma(W2p[0:64, :], rw_w2[l], eng="pool")
        S.dma(A2p[64:128, :], rw_a2[l], eng="pool")
        S.dma(G2, rw_g2[l], eng="pool")
        tanh_w = A.alloc([NTOK], BF16)
        raw_w = A.alloc([NTOK], BF16)
        sig_g = A.alloc([NTOK], BF16)
        sfm_t = A.alloc([26], F32)
        S.dma(sfm_t, sfm[l])
        NB = 256
        GRP_OFF = A.off
        Wl = A.alloc([8, 256], BF16)
        wslab(Wl, w_l.ap[:, P_DA + 3072:P_DA + 3328])
        u_ring = Ring([A.alloc([513], F32) for _ in range(3)])
        tmp = Ring([A.alloc([512], F32) for _ in range(6)])

        def shifted(psb, n, carry_src, mucol, u_out_last=None, last_idx=None):
            U = u_ring.next()
            if carry_src is None:
                S.memset(U[:, 0:1], 0.0, eng="dve")
            else:
                S.copy(U[:, 0:1], carry_src, eng="dve")
            S.copy(U[:, 1:n + 1], psb[:, 0:n], eng="act")
            d = tmp.next()
            S.tt(d[:, 0:n], U[:, 0:n], U[:, 1:n + 1], ALU.subtract)
            return d, U

        prev = {0: None, 1: None}
        for qi, (c0, n) in enumerate(TQ):
            for which in range(2):
                pb = psr.next()
                for kt in range(8):
                    S.mm(pb[:, 0:n], Wl[:, kt, which * 128:(which + 1) * 128], hT[:, kt, c0:c0 + n],
                         start=(kt == 0), stop=(kt == 7))
                mc = 24 + which
                if qi == 0:
                    carry = None
                elif qi == 4:
                    carry = sfm_t[:, mc:mc + 1]
                else:
                    carry = prev[which]
                d, U = shifted(pb, n, carry, mc)
                us = tmp.next()
                S.stt(us[:, 0:n], d[:, 0:n], mu(mc), U[:, 1:n + 1], ALU.mult, ALU.add)
                prev[which] = U[:, n:n + 1]
                if qi == 3:
                    S.copy(shp_t[:, mc:mc + 1], U[:, n:n + 1], eng="pool")
                if qi == 4:
                    S.copy(shs_t[:, mc:mc + 1], U[:, TSV:TSV + 1], eng="pool")
                if which == 0:
                    S.actf(tanh_w[:, c0:c0 + n], us[:, 0:n], AF.Tanh)
                    S.copy(raw_w[:, c0:c0 + n], us[:, 0:n], eng="pool")
                else:
                    S.actf(sig_g[:, c0:c0 + n], us[:, 0:n], AF.Sigmoid)

        chk("rwkv_lora", l)
        S.barrier()
        A.seek(GRP_OFF)
        Wg = A.alloc([8, 768], BF16)
        AT = A.alloc([2, NB], BF16)
        BT = A.alloc([2, NB], BF16)
        KT_ = A.alloc([2, NB], BF16)
        RT = A.alloc([2, NB], BF16)
        VT = A.alloc([2, NB], BF16)
        WT = A.alloc([2, NB], F32)
        BON = A.alloc([2, NB], F32)
        YT = A.alloc([2, NB], F32)
        S32 = A.alloc([2, 128], F32)
        Sb = A.alloc([2, 128], BF16)
        swt = A.alloc([2, 64], F32)
        carr = A.alloc([8], F32)
        t256 = Ring([A.alloc([NB], F32) for _ in range(11)])
        b256 = Ring([A.alloc([NB], BF16) for _ in range(3)])
        usrk = Ring([A.alloc([NB], F32) for _ in range(2)])
        u257 = Ring([A.alloc([NB + 1], F32) for _ in range(2)])
        NCH = 4
        wide = lambda: A.alloc([2, 128], BF16)
        BDa = [wide() for _ in range(NCH)]
        BDb = [wide() for _ in range(NCH)]
        BDk = [wide() for _ in range(NCH)]
        BDx = Ring([wide() for _ in range(3)])
        AKm = [wide() for _ in range(NCH)]
        Vm = [wide() for _ in range(NCH)]
        Bhm = [wide() for _ in range(NCH)]
        Khm = [wide() for _ in range(NCH)]
        Rm = [wide() for _ in range(NCH)]
        Um = Ring([wide() for _ in range(1)])
        fi = [0]

        def frt():
            t = T(fr.ap[:, fi[0], :].rearrange("p (g c) -> p g c", g=2), [Buf()])
            fi[0] += 1
            return t

        Nm = [[frt() for _ in range(NCH)] for _ in range(2)]
        NTm = [[frt() for _ in range(NCH)] for _ in range(2)]
        Pm = [frt() for _ in range(NCH)]
        Hm = Ring([frt() for _ in range(1)])
        class _HB:
            def __init__(self):
                self.k = 0
                self.pend = None

            def next(self):
                if self.pend is not None:
                    t = self.pend
                    self.pend = None
                    return t
                i = self.k % 8
                self.k += 1
                self.pend = halves[2 * i + 1]
                return halves[2 * i]

            def newbank(self):
                self.pend = None

        hbr = _HB()

        def w4(t):
            return T(t.ap.rearrange("p g (h t) -> p g h t", h=2), t.bufs)

        def v3(t):
            return T(t.ap.rearrange("p (g c) -> p g c", g=2), t.bufs)

        def v3b(t):
            return T(t.ap.bitcast(BF16)[:, 0:256].rearrange("p (g c) -> p g c", g=2), t.bufs)

        def bcast_mask(coff):
            m = cst[:, coff:coff + 128]
            return T(m.ap.unsqueeze(1).broadcast_to([128, 2, 128]), m.bufs)

        mU_b = bcast_mask(C_MU)
        mL_b = bcast_mask(C_ML)
        id_b = bcast_mask(C_ID)
        bd4 = T(bdmask.ap.rearrange("p (h t) -> p h t", h=2).unsqueeze(1).broadcast_to([128, 2, 2, 64]), bdmask.bufs)
        mui = cst[:, C_MUI:C_MUI + 64]
        mui4 = T(mui.ap.unsqueeze(1).unsqueeze(1).broadcast_to([128, 2, 2, 64]), mui.bufs)
        rstm = cst[:, C_RST:C_RST + NB]

        for grp in range(4):
            for x in range(3):
                wslab(Wg[:, :, x * 256:(x + 1) * 256],
                      w_l.ap[:, P_DA + x * 1024 + grp * 256:P_DA + x * 1024 + (grp + 1) * 256])
            for seq in range(2):
                blocks = [(i * NB, NB) for i in range(8)] if seq == 0 else [(TP, 64)]
                nvalid_last = NB if seq == 0 else TSV
                if seq == 0:
                    S.memset(S32, 0.0, eng="dve")
                    S.memset(Sb, 0.0, eng="dve")
                else:
                    for g in range(2):
                        S.dma(swt[:, g, :], swkv[l, 2 * grp + g])
                    S.tt(w4(S32), swt.v(swt.ap.unsqueeze(2).broadcast_to([128, 2, 2, 64])), bd4, ALU.mult)
                    S.copy(Sb, S32)
                carry = {}
                for bi, (c0, n) in enumerate(blocks):
                    nch = n // 64
                    for g in range(2):
                        hp = 2 * grp + g
                        us = {}
                        for x in range(3):
                            pb = psr.next()
                            off = x * 256 + g * 128
                            for kt in range(8):
                                S.mm(pb[:, 0:n], Wg[:, kt, off:off + 128], hT[:, kt, c0:c0 + n],
                                     start=(kt == 0), stop=(kt == 7))
                            mc = x * 8 + hp
                            U = u257.next()
                            if bi == 0:
                                if seq == 0:
                                    S.memset(U[:, 0:1], 0.0, eng="dve")
                                else:
                                    S.copy(U[:, 0:1], sfm_t[:, mc:mc + 1], eng="dve")
                            else:
                                S.copy(U[:, 0:1], carry[(g, x)], eng="dve")
                            S.copy(U[:, 1:n + 1], pb[:, 0:n], eng="act")
                            d = t256.next()
                            S.tt(d[:, 0:n], U[:, 0:n], U[:, 1:n + 1], ALU.subtract)
                            ut = usrk.next() if x < 2 else t256.next()
                            us[x] = ut[:, 0:n]
                            S.stt(ut[:, 0:n], d[:, 0:n], mu(mc), U[:, 1:n + 1], ALU.mult, ALU.add)
                            S.copy(carr[:, g * 3 + x:g * 3 + x + 1], U[:, n:n + 1], eng="pool")
                            carry[(g, x)] = carr[:, g * 3 + x:g * 3 + x + 1]
                            if bi == len(blocks) - 1:
                                sht = shp_t if seq == 0 else shs_t
                                S.copy(sht[:, mc:mc + 1], U[:, nvalid_last:nvalid_last + 1], eng="pool")
                        us_r, us_k, us_v = us[0], us[1], us[2]
                        S.copy(VT[:, g, 0:n], us_v, eng="pool")
                        pcol = lambda o: pf[:, l, o + hp:o + hp + 1]
                        pb = psr.next()
                        S.mm(pb[:, 0:n], W2p[:, hp * 128:(hp + 1) * 128], tanh_w[:, c0:c0 + n])
                        sg = t256.next()
                        S.actf(sg[:, 0:n], pb[:, 0:n], AF.Sigmoid, bias=pcol(PF_W0))
                        lw = t256.next()
                        S.ts(lw[:, 0:n], sg[:, 0:n], DEC_C, ALU.mult, eng="pool")
                        pb = psr.next()
                        S.mm(pb[:, 0:n], A2p[:, hp * 128:(hp + 1) * 128], raw_w[:, c0:c0 + n])
                        a_t = t256.next()
                        S.actf(a_t[:, 0:n], pb[:, 0:n], AF.Sigmoid, bias=pcol(PF_A0))
                        kk = t256.next()
                        S.ts(kk[:, 0:n], us_k, pcol(PF_KK), ALU.mult)
                        sq = b256.next()
                        S.actf(sq[:, 0:n], kk[:, 0:n], AF.Square)
                        pb = psr.next()
                        S.mm(pb[:, 0:n], bd_bf, sq[:, 0:n])
                        rn = t256.next()
                        S.actf(rn[:, 0:n], pb[:, 0:n], AF.Sqrt)
                        S.ts(rn[:, 0:n], rn[:, 0:n], 1e-12, ALU.max)
                        S.recip(rn[:, 0:n], rn[:, 0:n])
                        kkn = kk
                        S.tt(kkn[:, 0:n], kk[:, 0:n], rn[:, 0:n], ALU.mult)
                        t1 = t256.next()
                        S.ts(t1[:, 0:n], a_t[:, 0:n], -1.0, ALU.add, pcol(PF_KA), ALU.mult)
                        kmod = t256.next()
                        S.stt(kmod[:, 0:n], t1[:, 0:n], 1.0, us_k, ALU.add, ALU.mult)
                        bb = t1
                        S.tt(bb[:, 0:n], kkn[:, 0:n], a_t[:, 0:n], ALU.mult, eng="pool")
                        cum = t256.next()
                        S.scan(cum[:, 0:n], rstm[:, 0:n], lw[:, 0:n], 0.0, ALU.mult, ALU.add)
                        cumex = sg
                        S.tt(cumex[:, 0:n], cum[:, 0:n], lw[:, 0:n], ALU.subtract, eng="pool")
                        S.actf(WT[:, g, 0:n], cum[:, 0:n], AF.Exp)
                        winv = lw
                        S.actf(winv[:, 0:n], cum[:, 0:n], AF.Exp, scale=-1.0)
                        wex = cum
                        S.actf(wex[:, 0:n], cumex[:, 0:n], AF.Exp)
                        S.stt(AT[:, g, 0:n], kkn[:, 0:n], -1.0, wex[:, 0:n], ALU.mult, ALU.mult)
                        S.tt(BT[:, g, 0:n], bb[:, 0:n], winv[:, 0:n], ALU.mult)
                        S.tt(KT_[:, g, 0:n], kmod[:, 0:n], winv[:, 0:n], ALU.mult)
                        S.tt(RT[:, g, 0:n], us_r, WT[:, g, 0:n], ALU.mult)
                        rk = b256.next()
                        S.stt(rk[:, 0:n], us_r, pcol(PF_RK), kmod[:, 0:n], ALU.mult, ALU.mult)
                        pb = psr.next()
                        S.mm(pb[:, 0:n], bd_bf, rk[:, 0:n])
                        S.tt(BON[:, g, 0:n], pb[:, 0:n], us_v, ALU.mult)
                        if seq == 1:
                            for arr in (AT, BT, KT_, RT, VT):
                                S.memset(arr[:, g, TSV:64], 0.0)
                    chk("rwkv_A", l)
                    def chunk_src(arr, ci):
                        a = arr[:, :, ci * 64:(ci + 1) * 64]
                        return T(a.ap.unsqueeze(2).broadcast_to([128, 2, 2, 64]), a.bufs)

                    wcols = []
                    for ci in range(nch):
                        lastc = ci * 64 + (63 if seq == 0 else TSV - 1)
                        wc = WT[:, :, lastc:lastc + 1]
                        wcols.append(wc)
                        wcb = T(wc.ap.broadcast_to([128, 2, 128]), wc.bufs)
                        S.tt(w4(BDa[ci]), chunk_src(AT, ci), bd4, ALU.mult)
                        S.tt(w4(BDb[ci]), chunk_src(BT, ci), bd4, ALU.mult, eng="pool")
                        S.tt(w4(BDk[ci]), chunk_src(KT_, ci), bd4, ALU.mult)
                        bdv = BDx.next()
                        S.tt(w4(bdv), chunk_src(VT, ci), bd4, ALU.mult, eng="pool")
                        bdbh = BDx.next()
                        S.tt(bdbh, BDb[ci], wcb, ALU.mult)
                        bdkh = BDx.next()
                        S.tt(bdkh, BDk[ci], wcb, ALU.mult, eng="pool")
                        chk("B1a", l)
                        hbr.newbank()
                        pN, pNT, pAK, pR, pV, pBh, pKh = [hbr.next() for _ in range(7)]
                        hbr.newbank()
                        for g in range(2):
                            S.mm(v3(pN)[:, g, :], BDb[ci][:, g, :], BDa[ci][:, g, :])
                        for g in range(2):
                            S.mm(v3(pNT)[:, g, :], BDa[ci][:, g, :], BDb[ci][:, g, :])
                        for g in range(2):
                            S.mm(v3(pAK)[:, g, :], BDk[ci][:, g, :], BDa[ci][:, g, :])
                        chk("B1b", l)
                        pR4 = T(pR.ap.rearrange("p (g x t) -> p g x t", g=2, x=2), pR.bufs)
                        for g in range(2):
                            S.mm(pR4[:, g, 0, :], BDb[ci][:, g, :], RT[:, g, ci * 64:(ci + 1) * 64])
                            S.mm(pR4[:, g, 1, :], BDk[ci][:, g, :], RT[:, g, ci * 64:(ci + 1) * 64])
                        chk("B1c", l)
                        for g in range(2):
                            S.tr(v3b(pV)[:, g, :], bdv[:, g, :], ident_bf)
                            S.tr(v3b(pBh)[:, g, :], bdbh[:, g, :], ident_bf)
                            S.tr(v3b(pKh)[:, g, :], bdkh[:, g, :], ident_bf)
                        chk("B1d", l)
                        S.tt(r32(Nm[0][ci]), v3(pN), mU_b, ALU.mult)
                        S.tt(r32(NTm[0][ci]), v3(pNT), mL_b, ALU.mult)
                        S.tt(AKm[ci], v3(pAK), mU_b, ALU.mult)
                        S.tt(w4(Rm[ci]), pR4, mui4, ALU.mult)
                        chk("B1e", l)
                        S.copy(Vm[ci], v3b(pV), eng="act")
                        S.copy(Bhm[ci], v3b(pBh), eng="act")
                        S.copy(Khm[ci], v3b(pKh), eng="act")
                        chk("B1f", l)
                        S.tt(r32(Pm[ci]), Nm[0][ci], id_b, ALU.add)
                        chk("B1g", l)
                        if ci == 1:
                            chk("B1h", l)
                    chk("rwkv_B1", l)
                    cur = 0
                    for rd in range(1, 6):
                        nxt = 1 - cur
                        pairs = []
                        for ci in range(nch):
                            hbr.newbank()
                            pN2 = hbr.next() if rd < 5 else None
                            pNT2 = hbr.next()
                            for g in range(2):
                                if rd < 5:
                                    S.mm(v3(pN2)[:, g, :], r32(NTm[cur][ci][:, g, :]), r32(Nm[cur][ci][:, g, :]))
                                S.mm(v3(pNT2)[:, g, :], r32(Nm[cur][ci][:, g, :]), r32(NTm[cur][ci][:, g, :]))
                            pairs.append((pN2, pNT2))
                        for ci in range(nch):
                            pN2, pNT2 = pairs[ci]
                            S.copy(r32(NTm[nxt][ci]), v3(pNT2), eng="act")
                            if rd < 5:
                                S.copy(r32(Nm[nxt][ci]), v3(pN2), eng="act")
                        pps = []
                        for ci in range(nch):
                            hbr.newbank()
                            pP = hbr.next()
                            for g in range(2):
                                S.mm(v3(pP)[:, g, :], r32(NTm[nxt][ci][:, g, :]), r32(Pm[ci][:, g, :]))
                            pps.append(pP)
                        for ci in range(nch):
                            S.tt(r32(Pm[ci]), v3(pps[ci]), Pm[ci], ALU.add)
                        cur = nxt
                    chk("rwkv_inv", l)
                    for ci in range(nch):
                        hbr.newbank()
                        pH = hbr.next()
                        for g in range(2):
                            S.mm(v3(pH)[:, g, :], BDa[ci][:, g, :], Sb[:, g, :], start=True, stop=False)
                            S.mm(v3(pH)[:, g, :], AKm[ci][:, g, :], Vm[ci][:, g, :], start=False, stop=True)
                        H = Hm.next()
                        S.copy(r32(H), v3(pH), eng="act")
                        hbr.newbank()
                        pU = hbr.next()
                        for g in range(2):
                            S.mm(v3(pU)[:, g, :], r32(Pm[ci][:, g, :]), r32(H[:, g, :]))
                        U = Um.next()
                        S.copy(U, v3(pU), eng="dve")
                        hbr.newbank()
                        pY = hbr.next()
                        pY3 = T(pY.ap[:, 0:128].rearrange("p (g t) -> p g t", g=2), pY.bufs)
                        R4 = w4(Rm[ci])
                        for g in range(2):
                            S.mm(pY3[:, g, :], Sb[:, g, :], RT[:, g, ci * 64:(ci + 1) * 64], start=True, stop=False)
                            S.mm(pY3[:, g, :], U[:, g, :], R4[:, g, 0, :], start=False, stop=False)
                            S.mm(pY3[:, g, :], Vm[ci][:, g, :], R4[:, g, 1, :], start=False, stop=True)
                        S.copy(YT[:, :, ci * 64:(ci + 1) * 64], pY3, eng="act")
                        hbr.newbank()
                        pS = hbr.next()
                        for g in range(2):
                            S.mm(v3(pS)[:, g, :], Bhm[ci][:, g, :], U[:, g, :], start=True, stop=False)
                            S.mm(v3(pS)[:, g, :], Khm[ci][:, g, :], Vm[ci][:, g, :], start=False, stop=True)
                        for g in range(2):
                            S.stt(S32[:, g, :], S32[:, g, :], wcols[ci][:, g, :], v3(pS)[:, g, :], ALU.mult, ALU.add)
                        S.copy(Sb, S32, eng="pool")
                    chk("rwkv_chain", l)
                    for g in range(2):
                        hp = 2 * grp + g
                        pcol = lambda o: pf[:, l, o + hp:o + hp + 1]
                        yb = b256.next()
                        S.copy(yb[:, 0:n], YT[:, g, 0:n], eng="pool")
                        pb = psr.next()
                        S.mm(pb[:, 0:n], bd_bf, yb[:, 0:n])
                        yc = t256.next()
                        S.stt(yc[:, 0:n], pb[:, 0:n], -1.0 / 64, YT[:, g, 0:n], ALU.mult, ALU.add)
                        sq = b256.next()
                        S.actf(sq[:, 0:n], yc[:, 0:n], AF.Square)
                        pb = psr.next()
                        S.mm(pb[:, 0:n], bd_bf, sq[:, 0:n])
                        sd = t256.next()
                        S.actf(sd[:, 0:n], pb[:, 0:n], AF.Sqrt, scale=1.0 / 64, bias=epsb[:, 1:2])
                        S.recip(sd[:, 0:n], sd[:, 0:n])
                        S.tt(yc[:, 0:n], yc[:, 0:n], sd[:, 0:n], ALU.mult)
                        S.ts(yc[:, 0:n], yc[:, 0:n], pcol(PF_LNW), ALU.mult, pcol(PF_LNB), ALU.add)
                        S.tt(yc[:, 0:n], yc[:, 0:n], BON[:, g, 0:n], ALU.add, eng="pool")
                        pb = psr.next()
                        S.mm(pb[:, 0:n], G2[:, hp * 128:(hp + 1) * 128], sig_g[:, c0:c0 + n])
                        S.tt(OZ[:, hp, c0:c0 + n], yc[:, 0:n], pb[:, 0:n], ALU.mult)
                    chk("rwkv_C", l)
                chk("rwkv_seq", l)
                dst = wkvp if seq == 0 else wkvs
                for g in range(2):
                    for h in range(2):
                        S.dma(dst[l, 2 * grp + g, h * 64:(h + 1) * 64, :], S32[h * 64:(h + 1) * 64, g, h * 64:(h + 1) * 64])
        S.dma(shp[l], shp_t)
        S.dma(shs[l], shs_t)

    def phase_gproj(l, w_o, gate_col0, accumulate):
        A.seek(LOC)
        slabs = Ring([A.alloc([8, 256], BF16) for _ in range(2)])
        sgr = Ring([A.alloc([512], F32) for _ in range(3)])
        tr_ = Ring([A.alloc([512], F32) for _ in range(2)])
        for nt in range(8):
            sl = slabs.next()
            wslab(sl[:, :, 0:128], w_o[l].ap[:, nt * 128:(nt + 1) * 128])
            wslab(sl[:, :, 128:256], w_in[l].ap[:, gate_col0 + nt * 128:gate_col0 + (nt + 1) * 128])
            for (c0, n) in TQ:
                pa = psr.next()
                for kt in range(8):
                    S.mm(pa[:, 0:n], sl[:, kt, 0:128], OZ[:, kt, c0:c0 + n], start=(kt == 0), stop=(kt == 7))
                pg = psr.next()
                for kt in range(8):
                    S.mm(pg[:, 0:n], sl[:, kt, 128:256], hT[:, kt, c0:c0 + n], start=(kt == 0), stop=(kt == 7))
                sg = sgr.next()
                S.actf(sg[:, 0:n], pg[:, 0:n], AF.Sigmoid)
                if not accumulate:
                    S.tt(MT[:, nt, c0:c0 + n], pa[:, 0:n], sg[:, 0:n], ALU.mult)
                else:
                    t = tr_.next()
                    S.tt(t[:, 0:n], pa[:, 0:n], sg[:, 0:n], ALU.mult)
                    S.tt(MT[:, nt, c0:c0 + n], t[:, 0:n], MT[:, nt, c0:c0 + n], ALU.add, eng="pool")

    def phase_da(l):
        A.seek(LOC)
        Wd = A.alloc([8, 384], BF16)
        qkv_tok = A.alloc([NTB, 384], BF16)
        q_tok = qkv_tok[:, :, 0:128]
        k_tok = qkv_tok[:, :, 128:256]
        v_tok = qkv_tok[:, :, 256:384]
        kc_tok = A.alloc([16, 128], BF16)
        vc_tok = A.alloc([16, 128], BF16)
        KTh = A.alloc([NTOK + TP], BF16)
        Q1p = A.alloc([512], BF16)
        Q2p = A.alloc([512], BF16)
        Pr = Ring([A.alloc([512], BF16) for _ in range(3)])
        f512 = Ring([A.alloc([512], F32) for _ in range(4)])
        qkvr = Ring([A.alloc([384], F32) for _ in range(2)])
        rtmp = Ring([A.alloc([4, 16], F32) for _ in range(4)])
        S.memset(Q1p, 0.0)
        S.memset(Q2p, 0.0)
        nlam = lamv[:, l * 8:l * 8 + 1]
        subs = lamv[:, l * 8 + 1:l * 8 + 2]
        psO, psL = ps[0], ps[1]
        pr6 = Ring(ps[2:8])

        def rope_inplace(x, tb):
            x4 = T(x.ap.rearrange("p (m d) -> p m d", m=4), x.bufs)
            cc = cst[:, C_COS + tb * 16:C_COS + tb * 16 + 16]
            ss = cst[:, C_SIN + tb * 16:C_SIN + tb * 16 + 16]
            ccb = T(cc.ap.unsqueeze(1).broadcast_to([128, 4, 16]), cc.bufs)
            ssb = T(ss.ap.unsqueeze(1).broadcast_to([128, 4, 16]), ss.bufs)
            tc_ = rtmp.next()
            ts_ = rtmp.next()
            S.tt(tc_, x4[:, :, 0:16], ccb, ALU.mult)
            S.tt(ts_, x4[:, :, 0:16], ssb, ALU.mult)
            S.tt(x4[:, :, 0:8], tc_[:, :, 0:8], ts_[:, :, 8:16], ALU.subtract)
            S.tt(x4[:, :, 8:16], tc_[:, :, 8:16], ts_[:, :, 0:8], ALU.add)

        def attend(c0, nq, tbs, keyblocks):
            pb = pr6.next()
            pst = T(pb.ap.bitcast(BF16), pb.bufs)
            for i, tb in enumerate(tbs):
                S.tr(pst[:, i * 128:(i + 1) * 128], q_tok[:, tb, :], ident_bf)
            S.copy(Q1p[0:64, 0:nq], pst[0:64, 0:nq], eng="act")
            S.copy(Q2p[64:128, 0:nq], pst[64:128, 0:nq], eng="act")
            o1 = f512.next()
            t = None
            for mp, Qp in enumerate((Q1p, Q2p)):
                nk = len(keyblocks)
                for j, (kap, vap, c_lo, zspec) in enumerate(keyblocks):
                    pS = pr6.next()
                    S.mm(pS[:, c_lo:nq], kap, Qp[:, c_lo:nq])
                    P = Pr.next()
                    S.actf(P[:, c_lo:nq], pS[:, c_lo:nq], AF.Exp, scale=0.125)
                    if zspec == "diag":
                        S.memset(P[64:128, c_lo:c_lo + 64], 0.0)
                    elif zspec == "rows":
                        S.memset(P[32:64, 0:nq], 0.0)
                        S.memset(P[64:128, 0:nq], 0.0)
                    S.mm(psO[:, c_lo:nq], vap, P[:, c_lo:nq], start=(j == 0), stop=(j == nk - 1))
                    S.mm(psL[:, c_lo:nq], ones_bf, P[:, c_lo:nq], start=(j == 0), stop=(j == nk - 1))
                rd = f512.next()
                S.recip(rd[:, 0:nq], psL[:, 0:nq])
                if mp == 0:
                    S.tt(o1[:, 0:nq], psO[:, 0:nq], rd[:, 0:nq], ALU.mult)
                else:
                    t = f512.next()
                    S.tt(t[:, 0:nq], psO[:, 0:nq], rd[:, 0:nq], ALU.mult)
                    S.stt(o1[:, 0:nq], t[:, 0:nq], nlam, o1[:, 0:nq], ALU.mult, ALU.add)
            sq = Pr.next()
            S.actf(sq[:, 0:nq], o1[:, 0:nq], AF.Square)
            pb = pr6.next()
            S.mm(pb[:, 0:nq], ones_bf, sq[:, 0:nq])
            sd = t
            S.actf(sd[:, 0:nq], pb[:, 0:nq], AF.Sqrt, scale=1.0 / 128, bias=epsb[:, 0:1])
            S.recip(sd[:, 0:nq], sd[:, 0:nq])
            S.tt(o1[:, 0:nq], o1[:, 0:nq], sd[:, 0:nq], ALU.mult, eng="pool")
            return o1

        for hd in range(8):
            for x in range(3):
                wslab(Wd[:, :, x * 128:(x + 1) * 128], w_in[l].ap[:, x * 1024 + hd * 128:x * 1024 + (hd + 1) * 128])
            S.dma(kc_tok, T(ck[l].ap[:, hd * 128:(hd + 1) * 128].rearrange("(j p) c -> p j c", p=128), ()), eng="pool")
            S.dma(vc_tok, T(cv[l].ap[:, hd * 128:(hd + 1) * 128].rearrange("(j p) c -> p j c", p=128), ()), eng="pool")
            chk("da_load", l)
            for tb in range(NTB):
                if tb == 1:
                    chk("da_tb0", l)
                if tb == 16:
                    chk("da_tb15", l)
                pb = pr6.next()
                for kt in range(8):
                    S.mm(pb[:, 0:384], hT[:, kt, tb * 128:(tb + 1) * 128], Wd[:, kt, :],
                         start=(kt == 0), stop=(kt == 7))
                qkv = qkvr.next()
                S.copy(qkv, pb[:, 0:384], eng="act")
                rope_inplace(qkv[:, 0:256], tb)
                S.copy(qkv_tok[:, tb, :], qkv, eng="pool")
                if tb < 16:
                    S.dma(kp[l, tb * 128:(tb + 1) * 128, hd * 128:(hd + 1) * 128], qkv[:, 128:256])
                    S.dma(vp[l, tb * 128:(tb + 1) * 128, hd * 128:(hd + 1) * 128], qkv[:, 256:384])
                else:
                    S.dma(ks[l, :, hd * 128:(hd + 1) * 128], qkv[0:TSV, 128:256])
                    S.dma(vs[l, :, hd * 128:(hd + 1) * 128], qkv[0:TSV, 256:384])
            chk("da_proj", l)
            srcs = [(k_tok, tb) for tb in range(NTB)] + [(kc_tok, j) for j in range(16)]
            base = 0
            ei = 0
            while base < len(srcs):
                grp_ = srcs[base:base + 8]
                pb = pr6.next()
                pst = T(pb.ap.bitcast(BF16), pb.bufs)
                for i, (src, idx) in enumerate(grp_):
                    S.tr(pst[:, i * 128:(i + 1) * 128], src[:, idx, :], ident_bf)
                S.copy(KTh[:, base * 128:(base + len(grp_)) * 128], pst[:, 0:len(grp_) * 128],
                       eng=("act" if ei % 2 == 0 else "dve"))
                base += len(grp_)
                ei += 1
            chk("da_kt", l)
            for i in range(4):
                if i == 1:
                    chk("da_att0", l)
                kb = []
                for j in range(4 * i + 4):
                    m = j - 4 * i
                    c_lo = 128 * m if m > 0 else 0
                    kb.append((KTh[:, j * 128:(j + 1) * 128], v_tok[:, j, :], c_lo, "diag" if m >= 0 else None))
                o = attend(i * 512, 512, [4 * i + m for m in range(4)], kb)
                S.ts(OZ[:, hd, i * 512:(i + 1) * 512], o[:, 0:512], subs, ALU.mult)
            chk("da_attp", l)
            kb = [(KTh[:, NTOK + j * 128:NTOK + (j + 1) * 128], vc_tok[:, j, :], 0, None) for j in range(16)]
            kb.append((KTh[:, TP:TP + 128], v_tok[:, 16, :], 0, "rows"))
            o = attend(TP, TSV, [16], kb)
            S.ts(OZ[:, hd, TP:TP + TSV], o[:, 0:TSV], subs, ALU.mult)
            chk("da_head0", l)

    def phase_out(l):
        A.seek(LOC)
        nsc = make_norm_scratch()
        wo = A.alloc([8, D], BF16)
        wslab(wo[:, :, 0:512], w_out[l].ap[:, 0:512])
        wslab(wo[:, :, 512:1024], w_out[l].ap[:, 512:1024])
        xr = Ring([A.alloc([D], F32) for _ in range(2)])
        x1r = Ring([A.alloc([D], F32) for _ in range(2)])
        for tb in range(NTB):
            xt = xr.next()
            if l == 0:
                src, nr = x_rows(tb)
                if nr < 128:
                    S.memset(xt, 0.0)
                S.dma(xt[0:nr, :], src)
            else:
                S.dma(xt, X2[tb * 128:(tb + 1) * 128, :])
            x1 = x1r.next()
            for half in range(2):
                pb = psr.next()
                for kt in range(8):
                    S.mm(pb, MT[:, kt, tb * 128:(tb + 1) * 128], wo[:, kt, half * 512:(half + 1) * 512],
                         start=(kt == 0), stop=(kt == 7))
                S.tt(x1[:, half * 512:(half + 1) * 512], pb, xt[:, half * 512:(half + 1) * 512], ALU.add)
            S.dma(X1[tb * 128:(tb + 1) * 128, :], x1)
            norm_to_hT(x1, l, PF_NFFN, tb, nsc)

    def phase_ffn(l):
        A.seek(0)
        A.alloc([8, NTOK], BF16)
        wdn = A.alloc([22, D], BF16)
        GT = A.alloc([22, 512], BF16)
        assert A.off <= LOC
        A.seek(LOC)
        nsc = make_norm_scratch()
        for c in range(2):
            for hh in range(2):
                S.dma(wdn[:, 11 * hh:11 * (hh + 1), c * 512:(c + 1) * 512],
                      T(w_down[l].ap[11 * hh * 128:11 * (hh + 1) * 128, c * 512:(c + 1) * 512].rearrange("(kt p) c -> p kt c", p=128), ()), eng="pool")
        slabs = Ring([A.alloc([8, 512], BF16) for _ in range(2)])
        hpr = Ring([A.alloc([514], F32) for _ in range(4)])
        cr = Ring([A.alloc([512], F32) for _ in range(4)])
        xr = Ring([A.alloc([D], F32) for _ in range(2)])
        x2r = Ring([A.alloc([D], F32) for _ in range(2)])
        yr = Ring([A.alloc([D], F32) for _ in range(2)])
        S.memset(carry_p, 0.0)
        S.dma(T(carry_s.ap.rearrange("p a b -> p (a b)"), carry_s.bufs), cfm[l])
        for qi, (c0, n) in enumerate(TQ):
            carry = carry_p if qi < 4 else carry_s
            nval = n if qi < 4 else TSV
            for f2 in range(11):
                sl = slabs.next()
                wslab(sl[:, :, 0:256], w_up[l].ap[:, f2 * 256:(f2 + 1) * 256])
                wslab(sl[:, :, 256:512], w_up[l].ap[:, DFF + f2 * 256:DFF + (f2 + 1) * 256])
                for fi in range(2):
                    ft = 2 * f2 + fi
                    cs = []
                    for which in range(2):
                        fidx = which * 22 + ft
                        pb = psr.next()
                        off = which * 256 + fi * 128
                        for kt in range(8):
                            S.mm(pb[:, 0:n], sl[:, kt, off:off + 128], hT[:, kt, c0:c0 + n], start=(kt == 0), stop=(kt == 7))
                        hp_ = hpr.next()
                        S.copy(hp_[:, 0:2], carry[:, fidx, :], eng="pool")
                        S.copy(hp_[:, 2:n + 2], pb[:, 0:n], eng="act")
                        S.copy(carry[:, fidx, :], hp_[:, nval:nval + 2], eng="pool")
                        cw = lambda j: pf[:, l, PF_CONV + j * 44 + fidx:PF_CONV + j * 44 + fidx + 1]
                        cb = pf[:, l, PF_CONVB + fidx:PF_CONVB + fidx + 1]
                        c_ = cr.next()
                        eng = "dve" if which == 0 else "pool"
                        S.ts(c_[:, 0:n], hp_[:, 0:n], cw(0), ALU.mult, cb, ALU.add, eng=eng)
                        S.stt(c_[:, 0:n], hp_[:, 1:n + 1], cw(1), c_[:, 0:n], ALU.mult, ALU.add)
                        S.stt(c_[:, 0:n], hp_[:, 2:n + 2], cw(2), c_[:, 0:n], ALU.mult, ALU.add)
                        cs.append(c_)
                    S.actf(cs[0][:, 0:n], cs[0][:, 0:n], AF.Silu)
                    S.tt(GT[:, ft, 0:n], cs[0][:, 0:n], cs[1][:, 0:n], ALU.mult, eng="pool")
            for tbl in range(n // 128):
                tb = c0 // 128 + tbl
                xt = xr.next()
                S.dma(xt, X1[tb * 128:(tb + 1) * 128, :])
                x2 = x2r.next()
                for half in range(2):
                    pb = psr.next()
                    for ft in range(22):
                        S.mm(pb, GT[:, ft, tbl * 128:(tbl + 1) * 128], wdn[:, ft, half * 512:(half + 1) * 512],
                             start=(ft == 0), stop=(ft == 21))
                    S.tt(x2[:, half * 512:(half + 1) * 512], pb, xt[:, half * 512:(half + 1) * 512], ALU.add)
                if l == 1:
                    rstd = norm_stats(x2, nsc)
                    y = yr.next()
                    S.stt(y, x2, rstd, gfin, ALU.mult, ALU.mult)
                    if tb < 16:
                        S.dma(yp[tb * 128:(tb + 1) * 128, :], y)
                    else:
                        S.dma(ys, y[0:TSV, :])
                else:
                    S.dma(X2[tb * 128:(tb + 1) * 128, :], x2)
                    norm_to_hT(x2, 1, PF_NMIX, tb, nsc)
        S.dma(cvp[l], T(carry_p.ap.rearrange("p a b -> p (a b)"), carry_p.bufs))
        S.dma(cvs[l], T(carry_s.ap.rearrange("p a b -> p (a b)"), carry_s.bufs))

    def dump(name, src):
        if name in dbg_out:
            S.barrier()
            S.dma(dbg_out[name], src)
            S.barrier()

    try:
        _program(S, locals())
    except StopBuild:
        pass
    S.emit()
    S.close()
    return nc


def _program(S, L):
    phase_norm0, phase_rwkv, phase_gproj, phase_da, phase_out, phase_ffn = (
        L["phase_norm0"], L["phase_rwkv"], L["phase_gproj"], L["phase_da"], L["phase_out"], L["phase_ffn"])
    dump, stop_after, hT, OZ, MT = L["dump"], L["stop_after"], L["hT"], L["OZ"], L["MT"]
    w_o_rw, w_o_da = L["w_o_rw"], L["w_o_da"]
    S.barrier()
    phase_norm0()
    S.barrier()
    dump("hT0", hT)
    done = False
    for l in range(2):
        if stop_after == ("norm", l):
            break
        phase_rwkv(l)
        S.barrier()
        dump(f"Z{l}", OZ)
        if stop_after == ("rwkv", l):
            break
        phase_gproj(l, w_o_rw, P_DA + P_RW + D, False)
        S.barrier()
        if stop_after == ("gproj1", l):
            break
        phase_da(l)
        S.barrier()
        dump(f"O{l}", OZ)
        if stop_after == ("da", l):
            break
        phase_gproj(l, w_o_da, P_DA + P_RW, True)
        S.barrier()
        dump(f"M{l}", MT)
        phase_out(l)
        S.barrier()
        dump(f"h2T{l}", hT)
        if stop_after == ("out", l):
            break
        phase_ffn(l)
        S.barrier()


_NC_CACHE = {}


def _prep_inputs(inp):
    f = lambda a: np.ascontiguousarray(np.asarray(a, dtype=np.float32))
    g = {k: f(v) for k, v in inp.items()}
    consts = make_consts()

    def fm(v, nt):
        return v.reshape(nt, 128).T

    pfm = np.zeros((2, 128, NPF), np.float32)
    for l in range(2):
        pfm[l, :, PF_NMIX:PF_NMIX + 8] = fm(g["norm_mix"][l], 8)
        pfm[l, :, PF_NFFN:PF_NFFN + 8] = fm(g["norm_ffn"][l], 8)
        pfm[l, :, PF_MU:PF_MU + 26] = fm(g["rw_mu"][l], 26)
        pfm[l, :, PF_W0:PF_W0 + 8] = fm(g["rw_w0"][l], 8)
        pfm[l, :, PF_A0:PF_A0 + 8] = fm(g["rw_a0"][l], 8)
        pfm[l, :, PF_KK:PF_KK + 8] = fm(g["rw_k_k"][l], 8)
        pfm[l, :, PF_KA:PF_KA + 8] = fm(g["rw_k_a"][l], 8)
        pfm[l, :, PF_RK:PF_RK + 8] = fm(g["rw_r_k"][l].reshape(-1), 8)
        pfm[l, :, PF_LNW:PF_LNW + 8] = fm(g["rw_ln_w"][l], 8)
        pfm[l, :, PF_LNB:PF_LNB + 8] = fm(g["rw_ln_b"][l], 8)
        for j in range(3):
            pfm[l, :, PF_CONV + j * 44:PF_CONV + (j + 1) * 44] = fm(g["ffn_conv"][l, j], 44)
        pfm[l, :, PF_CONVB:PF_CONVB + 44] = fm(g["ffn_conv_b"][l], 44)
        pfm[l, :, PF_SUBLN] = g["da_subln"][l]
    shared = {
        "pfm": pfm, "consts": consts, "w_in": g["w_in"], "da_lambda": g["da_lambda"].reshape(512),
        "w_o_da": g["w_o_da"], "rw_w2": g["rw_w2"], "rw_a2": g["rw_a2"], "rw_g2": g["rw_g2"],
        "w_o_rw": g["w_o_rw"], "w_out": g["w_out"], "w_up": g["w_up"], "w_down": g["w_down"],
        "nfin": g["norm_final"],
    }
    maps = []
    for b in range(8):
        m = dict(shared)
        m["xp"] = g["x_prompt"][b]
        m["xs"] = g["x_sample"][b]
        m["ck"] = np.ascontiguousarray(g["cache_k"][:, b].reshape(2, TP, D))
        m["cv"] = np.ascontiguousarray(g["cache_v"][:, b].reshape(2, TP, D))
        sw = g["state_wkv"][:, b].reshape(2, 8, 2, 64, 64).transpose(0, 1, 2, 4, 3).reshape(2, 8, 128, 64)
        m["swkv"] = np.ascontiguousarray(sw)
        m["sfm"] = np.ascontiguousarray(g["state_shift"][:, b, 0].reshape(2, 26, 128).transpose(0, 2, 1))
        cf = g["state_ffn_conv"][:, b].reshape(2, 2, 44, 128).transpose(0, 3, 2, 1).reshape(2, 128, 88)
        m["cfm"] = np.ascontiguousarray(cf)
        maps.append(m)
    return maps


def _assemble(results):
    def st(name):
        return np.stack([np.asarray(r[name]) for r in results], axis=0)

    y_prompt = st("yp")
    y_sample = st("ys")
    k_prompt = st("kp").transpose(1, 0, 2, 3).reshape(2, 8, TP, 8, 128)
    v_prompt = st("vp").transpose(1, 0, 2, 3).reshape(2, 8, TP, 8, 128)
    k_sample = st("ks").transpose(1, 0, 2, 3).reshape(2, 8, TSV, 8, 128)
    v_sample = st("vs").transpose(1, 0, 2, 3).reshape(2, 8, TSV, 8, 128)

    def wkv(name):
        a = st(name).reshape(8, 2, 8, 2, 64, 64).transpose(1, 0, 2, 3, 5, 4).reshape(2, 8, 16, 64, 64)
        return np.ascontiguousarray(a)

    def shift(name):
        a = st(name).transpose(1, 0, 3, 2).reshape(2, 8, 1, P_RW)
        return np.ascontiguousarray(a)

    def conv(name):
        a = st(name).reshape(8, 2, 128, 44, 2).transpose(1, 0, 4, 3, 2).reshape(2, 8, 2, 2 * DFF)
        return np.ascontiguousarray(a)

    return (np.ascontiguousarray(y_prompt), np.ascontiguousarray(y_sample),
            np.ascontiguousarray(k_prompt), np.ascontiguousarray(v_prompt),
            wkv("wkvp"), shift("shp"), conv("cvp"),
            np.ascontiguousarray(k_sample), np.ascontiguousarray(v_sample),
            wkv("wkvs"), shift("shs"), conv("cvs"))


def kernel(**inputs):
    maps = _prep_inputs(inputs)
    nc = build()
    res = run_bass_kernel_spmd(nc, maps, core_ids=list(range(8)))
    return _assemble(res.results)
```

```python
import numpy as np
from contextlib import ExitStack

import concourse.bass as bass
import concourse.mybir as mybir

F32 = mybir.dt.float32
BF16 = mybir.dt.bfloat16
F32R = mybir.dt.float32r
AF = mybir.ActivationFunctionType
ALU = mybir.AluOpType
AX = mybir.AxisListType

ENGS = ("pe", "act", "dve", "pool", "sp")
SEM_EPOCH = 20000
DMA_ROT = 8


class Buf:
    __slots__ = ("w", "r", "name")

    def __init__(self, name=""):
        self.w = None
        self.r = []
        self.name = name


class T:
    __slots__ = ("ap", "bufs")

    def __init__(self, ap, bufs):
        self.ap = ap
        self.bufs = tuple(bufs)

    def __getitem__(self, key):
        return T(self.ap[key], self.bufs)

    def v(self, ap):
        return T(ap, self.bufs)

    def bitcast(self, dt):
        return T(self.ap.bitcast(dt), self.bufs)


class Rec:
    __slots__ = ("eng", "idx", "fn", "deps", "dma", "signaled", "ev", "selfwait")

    def __init__(self, eng, idx, fn, dma):
        self.eng = eng
        self.idx = idx
        self.fn = fn
        self.deps = []
        self.dma = dma
        self.signaled = False
        self.ev = None
        self.selfwait = None


class Sched:
    def __init__(self, nc):
        self.nc = nc
        self.q = {e: [] for e in ENGS}
        self.stack = ExitStack()
        self.nbuf = 0
        self.same_engine_sync = True
        self._bar_pos = {}
        self._pending = {e: [] for e in ENGS}

    def sbuf(self, name, shape, dtype, nbufs=1):
        h = self.stack.enter_context(self.nc.sbuf_tensor(name, list(shape), dtype))
        return T(h[:], [Buf(name)])

    def psum(self, name, shape, dtype=F32):
        h = self.stack.enter_context(self.nc.psum_tensor(name, list(shape), dtype))
        return T(h[:], [Buf(name)])

    def dram(self, name, shape, dtype, kind):
        h = self.nc.dram_tensor(name, list(shape), dtype, kind=kind)
        return T(h.ap(), [Buf(name)])

    def newbuf(self, name=""):
        return Buf(name)

    def add(self, eng, fn, reads=(), writes=(), dma=False):
        q = self.q[eng]
        rec = Rec(eng, len(q), fn, dma)
        deps = {}
        for t in reads:
            for b in t.bufs:
                if b.w is not None:
                    deps[id(b.w)] = b.w
        for t in writes:
            for b in t.bufs:
                if b.w is not None:
                    deps[id(b.w)] = b.w
                for r in b.r:
                    deps[id(r)] = r
        if self._pending[eng]:
            for d in self._pending[eng]:
                deps[id(d)] = d
            self._pending[eng] = []
            barrier_deps = True
        else:
            barrier_deps = False
        for d in deps.values():
            if d is rec:
                continue
            if d.eng == eng and not d.dma and not dma:
                if eng == "pe" or eng == "sp" or not self.same_engine_sync:
                    continue
            rec.deps.append(d)
        for t in reads:
            for b in t.bufs:
                b.r.append(rec)
        for t in writes:
            for b in t.bufs:
                b.w = rec
                b.r = []
        q.append(rec)
        return rec

    def emit(self):
        nc = self.nc
        for e in ENGS:
            for rec in self.q[e]:
                for d in rec.deps:
                    d.signaled = True
        sems = {}
        for e in ENGS:
            cnt = 0
            for rec in self.q[e]:
                if rec.dma:
                    continue
                if rec.signaled:
                    ep = cnt // SEM_EPOCH
                    key = (e, ep)
                    if key not in sems:
                        sems[key] = self.stack.enter_context(nc.semaphore(f"s_{e}_{ep}"))
                    rec.ev = (sems[key], cnt % SEM_EPOCH + 1, key)
                    cnt += 1
        self.final_waits = []
        for e in ENGS:
            j = 0
            last = {}
            for rec in self.q[e]:
                if not rec.dma:
                    continue
                slot = j % DMA_ROT
                key = ("dma", e, slot)
                if key not in sems:
                    sems[key] = self.stack.enter_context(nc.semaphore(f"d_{e}_{slot}"))
                val = 16 * (j // DMA_ROT + 1)
                rec.ev = (sems[key], val, key)
                if j >= DMA_ROT:
                    rec.selfwait = (sems[key], val - 16, key)
                last[key] = (sems[key], val, key)
                j += 1
            self.final_waits.extend(last.values())

        block = self.stack.enter_context(nc.Block())
        sched = self

        def run(engname, eobj):
            waited = {}
            for rec in sched.q[engname]:
                waits = {}
                if rec.selfwait is not None:
                    s, v, key = rec.selfwait
                    waits[key] = (s, v)
                for d in rec.deps:
                    s, v, key = d.ev
                    if key in waits:
                        if waits[key][1] < v:
                            waits[key] = (s, v)
                    else:
                        waits[key] = (s, v)
                for key, (s, v) in waits.items():
                    if waited.get(key, 0) >= v:
                        continue
                    eobj.wait_ge(s, v)
                    waited[key] = v
                ins = rec.fn(eobj)
                if rec.dma:
                    ins.then_inc(rec.ev[0], 16)
                elif rec.signaled:
                    ins.then_inc(rec.ev[0], 1)
            if engname == "sp":
                for s, v, key in sched.final_waits:
                    eobj.wait_ge(s, v)

        @block.tensor
        def _(e):
            run("pe", e)

        @block.scalar
        def _(e):
            run("act", e)

        @block.vector
        def _(e):
            run("dve", e)

        @block.gpsimd
        def _(e):
            run("pool", e)

        @block.sync
        def _(e):
            run("sp", e)

    def close(self):
        self.stack.close()

    def dma(self, out, in_, eng="sp", **kw):
        return self.add(eng, lambda e: e.dma_start(out=out.ap, in_=in_.ap, **kw),
                        reads=[in_], writes=[out], dma=True)

    def mm(self, out, lhsT, rhs, start=True, stop=True, extra_reads=(), **kw):
        return self.add("pe", lambda e: e.matmul(out.ap, lhsT.ap, rhs.ap, start=start, stop=stop, **kw),
                        reads=[lhsT, rhs, *extra_reads], writes=[out])

    def tr(self, out, in_, ident, **kw):
        return self.add("pe", lambda e: e.transpose(out.ap, in_.ap, ident.ap, **kw),
                        reads=[in_, ident], writes=[out])

    def actf(self, out, in_, func, bias=None, scale=None, accum=None, eng="act"):
        kw = {}
        reads = [in_]
        writes = [out]
        if bias is not None:
            if isinstance(bias, T):
                kw["bias"] = bias.ap
                reads.append(bias)
            else:
                kw["bias"] = bias
        if scale is not None:
            if isinstance(scale, T):
                kw["scale"] = scale.ap
                reads.append(scale)
            else:
                kw["scale"] = scale
        if accum is not None:
            kw["accum_out"] = accum.ap
            writes.append(accum)
        return self.add("act", lambda e: e.activation(out.ap, in_.ap, func, **kw), reads=reads, writes=writes)

    def tt(self, out, a, b, op, eng="dve"):
        return self.add(eng, lambda e: e.tensor_tensor(out.ap, a.ap, b.ap, op), reads=[a, b], writes=[out])

    def ts(self, out, a, s1, op0, s2=None, op1=None, accum=None, eng="dve"):
        reads = [a]
        writes = [out]
        s1v = s1.ap if isinstance(s1, T) else s1
        s2v = s2.ap if isinstance(s2, T) else s2
        if isinstance(s1, T):
            reads.append(s1)
        if isinstance(s2, T):
            reads.append(s2)
        kw = {}
        if op1 is not None:
            kw["op1"] = op1
        if accum is not None:
            kw["accum_out"] = accum.ap
            writes.append(accum)
        return self.add(eng, lambda e: e.tensor_scalar(out.ap, a.ap, s1v, s2v, op0, **kw), reads=reads, writes=writes)

    def stt(self, out, a, s, b, op0, op1, eng="dve"):
        reads = [a, b]
        sv = s.ap if isinstance(s, T) else s
        if isinstance(s, T):
            reads.append(s)
        return self.add(eng, lambda e: e.scalar_tensor_tensor(out.ap, a.ap, sv, b.ap, op0, op1), reads=reads, writes=[out])

    def copy(self, out, in_, eng="dve"):
        if eng == "act":
            return self.add("act", lambda e: e.copy(out.ap, in_.ap), reads=[in_], writes=[out])
        return self.add(eng, lambda e: e.tensor_copy(out.ap, in_.ap), reads=[in_], writes=[out])

    def memset(self, out, val, eng="pool"):
        return self.add(eng, lambda e: e.memset(out.ap, val), reads=[], writes=[out])

    def recip(self, out, in_):
        return self.add("dve", lambda e: e.reciprocal(out.ap, in_.ap), reads=[in_], writes=[out])

    def scan(self, out, d0, d1, init, op0, op1):
        reads = [d0, d1]
        iv = init.ap if isinstance(init, T) else init
        if isinstance(init, T):
            reads.append(init)
        return self.add("dve", lambda e: e.tensor_tensor_scan(out.ap, d0.ap, d1.ap, iv, op0, op1), reads=reads, writes=[out])


def _barrier(self):
    deps = []
    for e in ENGS:
        q = self.q[e]
        last_c = None
        for rec in reversed(q):
            if not rec.dma:
                last_c = rec
                break
        if last_c is not None:
            deps.append(last_c)
        for rec in q[self._bar_pos.get(e, 0):]:
            if rec.dma:
                deps.append(rec)
        self._bar_pos[e] = len(q)
    for e in ENGS:
        self._pending[e] = list(deps)


Sched.barrier = _barrier


from concourse.bass_utils import run_bass_kernel_spmd

D = 1024
TP = 2048
TSV = 32
NTOK = 2176
NTB = 17
P_DA = 3072
P_RW = 3328
PTOT = 8448
DFF = 2816
EPS = 1e-6
GN_EPS = 64e-5
ROPE_THETA = 500000.0
TQ = [(0, 512), (512, 512), (1024, 512), (1536, 512), (2048, 128)]
PF_NMIX, PF_NFFN, PF_MU, PF_W0, PF_A0, PF_KK, PF_KA, PF_RK, PF_LNW, PF_LNB = 0, 8, 16, 42, 50, 58, 66, 74, 82, 90
PF_CONV, PF_CONVB, PF_SUBLN, NPF = 98, 230, 274, 275
C_ID, C_BD, C_MU, C_ML, C_MUI, C_RST, C_COS, C_SIN, NCONST = 0, 128, 256, 384, 512, 576, 832, 1104, 1376
DEC_C = -0.6065306597126334


def make_consts():
    c = np.zeros((128, NCONST), np.float32)
    c[:, C_ID:C_ID + 128] = np.eye(128, dtype=np.float32)
    p = np.arange(128)
    h = p // 64
    s = p % 64
    bd = (h[:, None] == h[None, :]).astype(np.float32)
    c[:, C_BD:C_BD + 128] = bd
    c[:, C_MU:C_MU + 128] = bd * (s[:, None] < s[None, :])
    c[:, C_ML:C_ML + 128] = bd * (s[:, None] > s[None, :])
    c[:, C_MUI:C_MUI + 64] = (s[:, None] <= np.arange(64)[None, :])
    rst = np.ones(256, np.float32)
    rst[::64] = 0
    c[:, C_RST:C_RST + 256] = rst[None, :]
    inv = (np.float32(ROPE_THETA) ** (-np.arange(0, 16, 2, dtype=np.float32) / np.float32(16))).astype(np.float32)
    for tb in range(NTB):
        pos = (tb * 128 + p) if tb < 16 else (TP + p)
        ang = pos.astype(np.float32)[:, None] * inv[None, :]
        co = np.cos(ang).astype(np.float32)
        si = np.sin(ang).astype(np.float32)
        c[:, C_COS + tb * 16:C_COS + tb * 16 + 8] = co
        c[:, C_COS + tb * 16 + 8:C_COS + tb * 16 + 16] = co
        c[:, C_SIN + tb * 16:C_SIN + tb * 16 + 8] = si
        c[:, C_SIN + tb * 16 + 8:C_SIN + tb * 16 + 16] = si
    return c


CHAIN_BF16 = True


def r32(t):
    if CHAIN_BF16:
        return t
    return T(t.ap.bitcast(F32R), t.bufs)


class Arena:
    def __init__(self, t, width):
        self.t = t
        self.W = width
        self.off = 0

    def seek(self, off):
        self.off = off

    def alloc(self, free_shape, dtype):
        n = 1
        for d in free_shape:
            n *= d
        words = n if dtype != BF16 else (n + 1) // 2
        words = (words + 7) // 8 * 8
        assert self.off + words <= self.W, ("arena overflow", self.off, words, self.W)
        ap = self.t.ap[:, self.off:self.off + words]
        if dtype == BF16:
            ap = ap.bitcast(BF16)
        ap = ap[:, 0:n]
        if len(free_shape) > 1:
            names = [f"d{i}" for i in range(len(free_shape))]
            pat = "p (" + " ".join(names) + ") -> p " + " ".join(names)
            kw = {nm: sz for nm, sz in zip(names[:-1], free_shape[:-1])}
            ap = ap.rearrange(pat, **kw)
        self.off += words
        return T(ap, [Buf()])


class Ring:
    def __init__(self, items):
        self.items = items
        self.i = 0

    def next(self):
        t = self.items[self.i % len(self.items)]
        self.i += 1
        return t


class StopBuild(Exception):
    pass


def build(dbg=None, stop_after=None):
    def chk(tag, l=0):
        if stop_after == (tag, l):
            raise StopBuild()

    nc = bass.Bass("TRN2", target_bir_lowering=False)
    S = Sched(nc)

    def din(name, shape):
        return S.dram(name, shape, F32, "ExternalInput")

    def dout(name, shape):
        return S.dram(name, shape, F32, "ExternalOutput")

    xp = din("xp", [TP, D])
    xs = din("xs", [TSV, D])
    ck = din("ck", [2, TP, D])
    cv = din("cv", [2, TP, D])
    swkv = din("swkv", [2, 8, 128, 64])
    sfm = din("sfm", [2, 128, 26])
    cfm = din("cfm", [2, 128, 88])
    pfm = din("pfm", [2, 128, NPF])
    consts = din("consts", [128, NCONST])
    w_in = din("w_in", [2, D, PTOT])
    da_lambda = din("da_lambda", [512])
    w_o_da = din("w_o_da", [2, D, D])
    rw_w2 = din("rw_w2", [2, 64, D])
    rw_a2 = din("rw_a2", [2, 64, D])
    rw_g2 = din("rw_g2", [2, 128, D])
    w_o_rw = din("w_o_rw", [2, D, D])
    w_out = din("w_out", [2, D, D])
    w_up = din("w_up", [2, D, 2 * DFF])
    w_down = din("w_down", [2, DFF, D])
    nfin = din("nfin", [D])

    yp = dout("yp", [TP, D])
    ys = dout("ys", [TSV, D])
    kp = dout("kp", [2, TP, D])
    vp = dout("vp", [2, TP, D])
    wkvp = dout("wkvp", [2, 8, 128, 64])
    shp = dout("shp", [2, 128, 26])
    cvp = dout("cvp", [2, 128, 88])
    ks = dout("ks", [2, TSV, D])
    vs = dout("vs", [2, TSV, D])
    wkvs = dout("wkvs", [2, 8, 128, 64])
    shs = dout("shs", [2, 128, 26])
    cvs = dout("cvs", [2, 128, 88])
    X1 = S.dram("X1", [NTOK, D], F32, "Internal")
    X2 = S.dram("X2", [NTOK, D], F32, "Internal")
    dbg_out = {}
    if dbg:
        for name, shape in dbg.items():
            dbg_out[name] = dout("dbg_" + name, shape)

    cst = S.sbuf("cst", [128, NCONST], F32)
    S.dma(cst, consts)
    ident_bf = S.sbuf("ident_bf", [128, 128], BF16)
    S.copy(ident_bf, cst[:, C_ID:C_ID + 128])
    ones_bf = S.sbuf("ones_bf", [128, 128], BF16)
    S.memset(ones_bf, 1.0)
    ident_f = cst[:, C_ID:C_ID + 128]
    bdmask = cst[:, C_BD:C_BD + 128]
    bd_bf = S.sbuf("bd_bf", [128, 128], BF16)
    S.copy(bd_bf, cst[:, C_BD:C_BD + 128])
    fr = None if CHAIN_BF16 else S.sbuf("fr", [128, 21, 256], F32)
    pf = S.sbuf("pf", [128, 2, NPF], F32)
    for l in range(2):
        S.dma(pf[:, l, :], pfm[l])
    gfin = S.sbuf("gfin", [128, D], F32)
    S.dma(gfin, nfin.v(nfin.ap.partition_broadcast(128)))
    lamb = S.sbuf("lamb", [128, 512], F32)
    S.dma(lamb, da_lambda.v(da_lambda.ap.partition_broadcast(128)))
    lamv = S.sbuf("lamv", [128, 16], F32)
    epsb = S.sbuf("epsb", [128, 2], F32)
    S.memset(epsb[:, 0:1], EPS)
    S.memset(epsb[:, 1:2], GN_EPS)
    ljunk = S.sbuf("ljunk", [128, 64], F32)
    shp_t = S.sbuf("shp_t", [128, 26], F32)
    shs_t = S.sbuf("shs_t", [128, 26], F32)
    carry_p = S.sbuf("carry_p", [128, 44, 2], F32)
    carry_s = S.sbuf("carry_s", [128, 44, 2], F32)

    ps = [S.psum(f"ps{i}", [128, 512], F32) for i in range(8)]
    psr = Ring(ps)
    halves = []
    for i in range(8):
        halves.append(T(ps[i].ap[:, 0:256], ps[i].bufs))
        halves.append(T(ps[i].ap[:, 256:512], ps[i].bufs))

    AW = ((nc.sbuf_bytes_remaining - 256) // 4) // 8 * 8
    ar = S.sbuf("arena", [128, AW], F32)
    A = Arena(ar, AW)
    hT = A.alloc([8, NTOK], BF16)
    OZ = A.alloc([8, NTOK], BF16)
    MT_OFF = A.off
    MT = A.alloc([8, NTOK], BF16)
    LOC = A.off
    OZ_OFF = MT_OFF - (MT_OFF - 0) // 2 if False else None
    S.memset(hT, 0.0)
    S.memset(OZ, 0.0, eng="dve")
    S.memset(MT, 0.0)

    for l in range(2):
        lam_init = 0.8 - 0.6 * float(np.exp(-0.3 * l))
        b = l * 256
        pr = ljunk
        S.tt(pr, lamb[:, b:b + 64], lamb[:, b + 64:b + 128], ALU.mult)
        S.ts(pr, pr, 1.0, ALU.mult, None, ALU.add, accum=lamv[:, l * 8 + 4:l * 8 + 5])
        S.tt(pr, lamb[:, b + 128:b + 192], lamb[:, b + 192:b + 256], ALU.mult)
        S.ts(pr, pr, 1.0, ALU.mult, None, ALU.add, accum=lamv[:, l * 8 + 5:l * 8 + 6])
        S.actf(lamv[:, l * 8 + 2:l * 8 + 4], lamv[:, l * 8 + 4:l * 8 + 6], AF.Exp)
        S.tt(lamv[:, l * 8:l * 8 + 1], lamv[:, l * 8 + 3:l * 8 + 4], lamv[:, l * 8 + 2:l * 8 + 3], ALU.subtract)
        S.ts(lamv[:, l * 8:l * 8 + 1], lamv[:, l * 8:l * 8 + 1], -lam_init, ALU.add)
        S.ts(lamv[:, l * 8 + 1:l * 8 + 2], pf[:, l, PF_SUBLN:PF_SUBLN + 1], 1.0 - lam_init, ALU.mult)

    def wslab(dst, src_ap):
        return S.dma(dst, T(src_ap.rearrange("(kt p) c -> p kt c", p=128), ()), eng="pool")

    def make_norm_scratch():
        d = {}
        d["junk"] = A.alloc([D], BF16)
        d["xn"] = Ring([A.alloc([D], BF16) for _ in range(2)])
        d["st"] = Ring([A.alloc([4], F32) for _ in range(3)])
        return d

    def norm_stats(x_t, nsc):
        st = nsc["st"].next()
        S.actf(nsc["junk"], x_t, AF.Square, accum=st[:, 0:1])
        S.actf(st[:, 1:2], st[:, 0:1], AF.Sqrt, scale=1.0 / D, bias=epsb[:, 0:1])
        S.recip(st[:, 2:3], st[:, 1:2])
        return st[:, 2:3]

    def norm_to_hT(x_t, l, pfoff, tb, nsc):
        rstd = norm_stats(x_t, nsc)
        xn = nsc["xn"].next()
        S.ts(xn, x_t, rstd, ALU.mult)
        pb = psr.next()
        pst = T(pb.ap.bitcast(BF16).rearrange("p (k t) -> p k t", k=8), pb.bufs)
        for kt in range(8):
            S.tr(pst[:, kt, :], xn[:, kt * 128:(kt + 1) * 128], ident_bf)
        g = pf[:, l, pfoff:pfoff + 8]
        S.tt(hT[:, :, tb * 128:(tb + 1) * 128], pst, g.v(g.ap.unsqueeze(2).broadcast_to([128, 8, 128])), ALU.mult)

    def x_rows(tb):
        return (xp[tb * 128:(tb + 1) * 128, :], 128) if tb < 16 else (xs, TSV)

    def phase_norm0():
        A.seek(LOC)
        nsc = make_norm_scratch()
        xr = Ring([A.alloc([D], F32) for _ in range(3)])
        for tb in range(NTB):
            xt = xr.next()
            src, nr = x_rows(tb)
            if nr < 128:
                S.memset(xt, 0.0)
            S.dma(xt[0:nr, :], src)
            norm_to_hT(xt, 0, PF_NMIX, tb, nsc)

    def phase_rwkv(l):
        A.seek(MT_OFF)
        w_l = w_in[l]
        mu = lambda c: pf[:, l, PF_MU + c:PF_MU + c + 1]
        W2p = A.alloc([D], BF16)
        A2p = A.alloc([D], BF16)
        G2 = A.alloc([D], BF16)
        S.memset(W2p[64:128, :], 0.0)
        S.memset(A2p[0:64, :], 0.0)
        S.dma(W2p[0:64, :], rw_w2[l], eng="pool")
        S.dma(A2p[64:128, :], rw_a2[l], eng="pool")
        S.dma(G2, rw_g2[l], eng="pool")
        tanh_w = A.alloc([NTOK], BF16)
        raw_w = A.alloc([NTOK], BF16)
        sig_g = A.alloc([NTOK], BF16)
        sfm_t = A.alloc([26], F32)
        S.dma(sfm_t, sfm[l])
        NB = 256
        GRP_OFF = A.off
        Wl = A.alloc([8, 256], BF16)
        wslab(Wl, w_l.ap[:, P_DA + 3072:P_DA + 3328])
        u_ring = Ring([A.alloc([513], F32) for _ in range(3)])
        tmp = Ring([A.alloc([512], F32) for _ in range(6)])

        def shifted(psb, n, carry_src, mucol, u_out_last=None, last_idx=None):
            U = u_ring.next()
            if carry_src is None:
                S.memset(U[:, 0:1], 0.0, eng="dve")
            else:
                S.copy(U[:, 0:1], carry_src, eng="dve")
            S.copy(U[:, 1:n + 1], psb[:, 0:n], eng="act")
            d = tmp.next()
            S.tt(d[:, 0:n], U[:, 0:n], U[:, 1:n + 1], ALU.subtract)
            return d, U

        prev = {0: None, 1: None}
        for qi, (c0, n) in enumerate(TQ):
            for which in range(2):
                pb = psr.next()
                for kt in range(8):
                    S.mm(pb[:, 0:n], Wl[:, kt, which * 128:(which + 1) * 128], hT[:, kt, c0:c0 + n],
                         start=(kt == 0), stop=(kt == 7))
                mc = 24 + which
                if qi == 0:
                    carry = None
                elif qi == 4:
                    carry = sfm_t[:, mc:mc + 1]
                else:
                    carry = prev[which]
                d, U = shifted(pb, n, carry, mc)
                us = tmp.next()
                S.stt(us[:, 0:n], d[:, 0:n], mu(mc), U[:, 1:n + 1], ALU.mult, ALU.add)
                prev[which] = U[:, n:n + 1]
                if qi == 3:
                    S.copy(shp_t[:, mc:mc + 1], U[:, n:n + 1], eng="pool")
                if qi == 4:
                    S.copy(shs_t[:, mc:mc + 1], U[:, TSV:TSV + 1], eng="pool")
                if which == 0:
                    S.actf(tanh_w[:, c0:c0 + n], us[:, 0:n], AF.Tanh)
                    S.copy(raw_w[:, c0:c0 + n], us[:, 0:n], eng="pool")
                else:
                    S.actf(sig_g[:, c0:c0 + n], us[:, 0:n], AF.Sigmoid)

        chk("rwkv_lora", l)
        S.barrier()
        A.seek(GRP_OFF)
        Wg = A.alloc([8, 768], BF16)
        AT = A.alloc([2, NB], BF16)
        BT = A.alloc([2, NB], BF16)
        KT_ = A.alloc([2, NB], BF16)
        RT = A.alloc([2, NB], BF16)
        VT = A.alloc([2, NB], BF16)
        WT = A.alloc([2, NB], F32)
        BON = A.alloc([2, NB], F32)
        YT = A.alloc([2, NB], F32)
        S32 = A.alloc([2, 128], F32)
        Sb = A.alloc([2, 128], BF16)
        swt = A.alloc([2, 64], F32)
        carr = A.alloc([8], F32)
        t256 = Ring([A.alloc([NB], F32) for _ in range(11)])
        b256 = Ring([A.alloc([NB], BF16) for _ in range(3)])
        usrk = Ring([A.alloc([NB], F32) for _ in range(2)])
        u257 = Ring([A.alloc([NB + 1], F32) for _ in range(2)])
        NCH = 4
        wide = lambda: A.alloc([2, 128], BF16)
        BDa = [wide() for _ in range(NCH)]
        BDb = [wide() for _ in range(NCH)]
        BDk = [wide() for _ in range(NCH)]
        BDx = Ring([wide() for _ in range(3)])
        AKm = [wide() for _ in range(NCH)]
        Vm = [wide() for _ in range(NCH)]
        Bhm = [wide() for _ in range(NCH)]
        Khm = [wide() for _ in range(NCH)]
        Rm = [wide() for _ in range(NCH)]
        Um = Ring([wide() for _ in range(1)])
        fi = [0]

        def frt():
            if CHAIN_BF16:
                return wide()
            t = T(fr.ap[:, fi[0], :].rearrange("p (g c) -> p g c", g=2), [Buf()])
            fi[0] += 1
            return t

        Nm = [[frt() for _ in range(NCH)] for _ in range(2)]
        NTm = [[frt() for _ in range(NCH)] for _ in range(2)]
        Pm = [frt() for _ in range(NCH)]
        Hm = Ring([frt() for _ in range(1)])
        class _HB:
            def __init__(self):
                self.k = 0
                self.pend = None

            def next(self):
                if self.pend is not None:
                    t = self.pend
                    self.pend = None
                    return t
                i = self.k % 8
                self.k += 1
                self.pend = halves[2 * i + 1]
                return halves[2 * i]

            def newbank(self):
                self.pend = None

        hbr = _HB()

        def w4(t):
            return T(t.ap.rearrange("p g (h t) -> p g h t", h=2), t.bufs)

        def v3(t):
            return T(t.ap.rearrange("p (g c) -> p g c", g=2), t.bufs)

        def v3b(t):
            return T(t.ap.bitcast(BF16)[:, 0:256].rearrange("p (g c) -> p g c", g=2), t.bufs)

        def bcast_mask(coff):
            m = cst[:, coff:coff + 128]
            return T(m.ap.unsqueeze(1).broadcast_to([128, 2, 128]), m.bufs)

        mU_b = bcast_mask(C_MU)
        mL_b = bcast_mask(C_ML)
        id_b = bcast_mask(C_ID)
        bd4 = T(bdmask.ap.rearrange("p (h t) -> p h t", h=2).unsqueeze(1).broadcast_to([128, 2, 2, 64]), bdmask.bufs)
        mui = cst[:, C_MUI:C_MUI + 64]
        mui4 = T(mui.ap.unsqueeze(1).unsqueeze(1).broadcast_to([128, 2, 2, 64]), mui.bufs)
        rstm = cst[:, C_RST:C_RST + NB]

        for grp in range(4):
            for x in range(3):
                wslab(Wg[:, :, x * 256:(x + 1) * 256],
                      w_l.ap[:, P_DA + x * 1024 + grp * 256:P_DA + x * 1024 + (grp + 1) * 256])
            for seq in range(2):
                blocks = [(i * NB, NB) for i in range(8)] if seq == 0 else [(TP, 64)]
                nvalid_last = NB if seq == 0 else TSV
                if seq == 0:
                    S.memset(S32, 0.0, eng="dve")
                    S.memset(Sb, 0.0, eng="dve")
                else:
                    for g in range(2):
                        S.dma(swt[:, g, :], swkv[l, 2 * grp + g])
                    S.tt(w4(S32), swt.v(swt.ap.unsqueeze(2).broadcast_to([128, 2, 2, 64])), bd4, ALU.mult)
                    S.copy(Sb, S32)
                carry = {}
                for bi, (c0, n) in enumerate(blocks):
                    nch = n // 64
                    for g in range(2):
                        hp = 2 * grp + g
                        us = {}
                        for x in range(3):
                            pb = psr.next()
                            off = x * 256 + g * 128
                            for kt in range(8):
                                S.mm(pb[:, 0:n], Wg[:, kt, off:off + 128], hT[:, kt, c0:c0 + n],
                                     start=(kt == 0), stop=(kt == 7))
                            mc = x * 8 + hp
                            U = u257.next()
                            if bi == 0:
                                if seq == 0:
                                    S.memset(U[:, 0:1], 0.0, eng="dve")
                                else:
                                    S.copy(U[:, 0:1], sfm_t[:, mc:mc + 1], eng="dve")
                            else:
                                S.copy(U[:, 0:1], carry[(g, x)], eng="dve")
                            S.copy(U[:, 1:n + 1], pb[:, 0:n], eng="act")
                            d = t256.next()
                            S.tt(d[:, 0:n], U[:, 0:n], U[:, 1:n + 1], ALU.subtract)
                            ut = usrk.next() if x < 2 else t256.next()
                            us[x] = ut[:, 0:n]
                            S.stt(ut[:, 0:n], d[:, 0:n], mu(mc), U[:, 1:n + 1], ALU.mult, ALU.add)
                            S.copy(carr[:, g * 3 + x:g * 3 + x + 1], U[:, n:n + 1], eng="pool")
                            carry[(g, x)] = carr[:, g * 3 + x:g * 3 + x + 1]
                            if bi == len(blocks) - 1:
                                sht = shp_t if seq == 0 else shs_t
                                S.copy(sht[:, mc:mc + 1], U[:, nvalid_last:nvalid_last + 1], eng="pool")
                        us_r, us_k, us_v = us[0], us[1], us[2]
                        S.copy(VT[:, g, 0:n], us_v, eng="pool")
                        pcol = lambda o: pf[:, l, o + hp:o + hp + 1]
                        pb = psr.next()
                        S.mm(pb[:, 0:n], W2p[:, hp * 128:(hp + 1) * 128], tanh_w[:, c0:c0 + n])
                        sg = t256.next()
                        S.actf(sg[:, 0:n], pb[:, 0:n], AF.Sigmoid, bias=pcol(PF_W0))
                        lw = t256.next()
                        S.ts(lw[:, 0:n], sg[:, 0:n], DEC_C, ALU.mult, eng="pool")
                        pb = psr.next()
                        S.mm(pb[:, 0:n], A2p[:, hp * 128:(hp + 1) * 128], raw_w[:, c0:c0 + n])
                        a_t = t256.next()
                        S.actf(a_t[:, 0:n], pb[:, 0:n], AF.Sigmoid, bias=pcol(PF_A0))
                        kk = t256.next()
                        S.ts(kk[:, 0:n], us_k, pcol(PF_KK), ALU.mult)
                        sq = b256.next()
                        S.actf(sq[:, 0:n], kk[:, 0:n], AF.Square)
                        pb = psr.next()
                        S.mm(pb[:, 0:n], bd_bf, sq[:, 0:n])
                        rn = t256.next()
                        S.actf(rn[:, 0:n], pb[:, 0:n], AF.Sqrt)
                        S.ts(rn[:, 0:n], rn[:, 0:n], 1e-12, ALU.max)
                        S.recip(rn[:, 0:n], rn[:, 0:n])
                        kkn = kk
                        S.tt(kkn[:, 0:n], kk[:, 0:n], rn[:, 0:n], ALU.mult)
                        t1 = t256.next()
                        S.ts(t1[:, 0:n], a_t[:, 0:n], -1.0, ALU.add, pcol(PF_KA), ALU.mult)
                        kmod = t256.next()
                        S.stt(kmod[:, 0:n], t1[:, 0:n], 1.0, us_k, ALU.add, ALU.mult)
                        bb = t1
                        S.tt(bb[:, 0:n], kkn[:, 0:n], a_t[:, 0:n], ALU.mult, eng="pool")
                        cum = t256.next()
                        S.scan(cum[:, 0:n], rstm[:, 0:n], lw[:, 0:n], 0.0, ALU.mult, ALU.add)
                        cumex = sg
                        S.tt(cumex[:, 0:n], cum[:, 0:n], lw[:, 0:n], ALU.subtract, eng="pool")
                        S.actf(WT[:, g, 0:n], cum[:, 0:n], AF.Exp)
                        winv = lw
                        S.actf(winv[:, 0:n], cum[:, 0:n], AF.Exp, scale=-1.0)
                        wex = cum
                        S.actf(wex[:, 0:n], cumex[:, 0:n], AF.Exp)
                        S.stt(AT[:, g, 0:n], kkn[:, 0:n], -1.0, wex[:, 0:n], ALU.mult, ALU.mult)
                        S.tt(BT[:, g, 0:n], bb[:, 0:n], winv[:, 0:n], ALU.mult)
                        S.tt(KT_[:, g, 0:n], kmod[:, 0:n], winv[:, 0:n], ALU.mult)
                        S.tt(RT[:, g, 0:n], us_r, WT[:, g, 0:n], ALU.mult)
                        rk = b256.next()
                        S.stt(rk[:, 0:n], us_r, pcol(PF_RK), kmod[:, 0:n], ALU.mult, ALU.mult)
                        pb = psr.next()
                        S.mm(pb[:, 0:n], bd_bf, rk[:, 0:n])
                        S.tt(BON[:, g, 0:n], pb[:, 0:n], us_v, ALU.mult)
                        if seq == 1:
                            for arr in (AT, BT, KT_, RT, VT):
                                S.memset(arr[:, g, TSV:64], 0.0)
                    chk("rwkv_A", l)
                    def chunk_src(arr, ci):
                        a = arr[:, :, ci * 64:(ci + 1) * 64]
                        return T(a.ap.unsqueeze(2).broadcast_to([128, 2, 2, 64]), a.bufs)

                    wcols = []
                    for ci in range(nch):
                        lastc = ci * 64 + (63 if seq == 0 else TSV - 1)
                        wc = WT[:, :, lastc:lastc + 1]
                        wcols.append(wc)
                        wcb = T(wc.ap.broadcast_to([128, 2, 128]), wc.bufs)
                        S.tt(w4(BDa[ci]), chunk_src(AT, ci), bd4, ALU.mult)
                        S.tt(w4(BDb[ci]), chunk_src(BT, ci), bd4, ALU.mult, eng="pool")
                        S.tt(w4(BDk[ci]), chunk_src(KT_, ci), bd4, ALU.mult)
                        bdv = BDx.next()
                        S.tt(w4(bdv), chunk_src(VT, ci), bd4, ALU.mult, eng="pool")
                        bdbh = BDx.next()
                        S.tt(bdbh, BDb[ci], wcb, ALU.mult)
                        bdkh = BDx.next()
                        S.tt(bdkh, BDk[ci], wcb, ALU.mult, eng="pool")
                        chk("B1a", l)
                        hbr.newbank()
                        pN, pNT, pAK, pR, pV, pBh, pKh = [hbr.next() for _ in range(7)]
                        hbr.newbank()
                        for g in range(2):
                            S.mm(v3(pN)[:, g, :], BDb[ci][:, g, :], BDa[ci][:, g, :])
                        for g in range(2):
                            S.mm(v3(pNT)[:, g, :], BDa[ci][:, g, :], BDb[ci][:, g, :])
                        for g in range(2):
                            S.mm(v3(pAK)[:, g, :], BDk[ci][:, g, :], BDa[ci][:, g, :])
                        chk("B1b", l)
                        pR4 = T(pR.ap.rearrange("p (g x t) -> p g x t", g=2, x=2), pR.bufs)
                        for g in range(2):
                            S.mm(pR4[:, g, 0, :], BDb[ci][:, g, :], RT[:, g, ci * 64:(ci + 1) * 64])
                            S.mm(pR4[:, g, 1, :], BDk[ci][:, g, :], RT[:, g, ci * 64:(ci + 1) * 64])
                        chk("B1c", l)
                        for g in range(2):
                            S.tr(v3b(pV)[:, g, :], bdv[:, g, :], ident_bf)
                            S.tr(v3b(pBh)[:, g, :], bdbh[:, g, :], ident_bf)
                            S.tr(v3b(pKh)[:, g, :], bdkh[:, g, :], ident_bf)
                        chk("B1d", l)
                        S.tt(r32(Nm[0][ci]), v3(pN), mU_b, ALU.mult)
                        S.tt(r32(NTm[0][ci]), v3(pNT), mL_b, ALU.mult)
                        S.tt(AKm[ci], v3(pAK), mU_b, ALU.mult)
                        S.tt(w4(Rm[ci]), pR4, mui4, ALU.mult)
                        chk("B1e", l)
                        S.copy(Vm[ci], v3b(pV), eng="act")
                        S.copy(Bhm[ci], v3b(pBh), eng="act")
                        S.copy(Khm[ci], v3b(pKh), eng="act")
                        chk("B1f", l)
                        S.tt(r32(Pm[ci]), Nm[0][ci], id_b, ALU.add)
                        chk("B1g", l)
                        if ci == 1:
                            chk("B1h", l)
                    chk("rwkv_B1", l)
                    cur = 0
                    for rd in range(1, 6):
                        nxt = 1 - cur
                        pairs = []
                        for ci in range(nch):
                            hbr.newbank()
                            pN2 = hbr.next() if rd < 5 else None
                            pNT2 = hbr.next()
                            for g in range(2):
                                if rd < 5:
                                    S.mm(v3(pN2)[:, g, :], r32(NTm[cur][ci][:, g, :]), r32(Nm[cur][ci][:, g, :]))
                                S.mm(v3(pNT2)[:, g, :], r32(Nm[cur][ci][:, g, :]), r32(NTm[cur][ci][:, g, :]))
                            pairs.append((pN2, pNT2))
                        for ci in range(nch):
                            pN2, pNT2 = pairs[ci]
                            S.copy(r32(NTm[nxt][ci]), v3(pNT2), eng="act")
                            if rd < 5:
                                S.copy(r32(Nm[nxt][ci]), v3(pN2), eng="act")
                        pps = []
                        for ci in range(nch):
                            hbr.newbank()
                            pP = hbr.next()
                            for g in range(2):
                                S.mm(v3(pP)[:, g, :], r32(NTm[nxt][ci][:, g, :]), r32(Pm[ci][:, g, :]))
                            pps.append(pP)
                        for ci in range(nch):
                            S.tt(r32(Pm[ci]), v3(pps[ci]), Pm[ci], ALU.add)
                        cur = nxt
                    chk("rwkv_inv", l)
                    for ci in range(nch):
                        hbr.newbank()
                        pH = hbr.next()
                        for g in range(2):
                            S.mm(v3(pH)[:, g, :], BDa[ci][:, g, :], Sb[:, g, :], start=True, stop=False)
                            S.mm(v3(pH)[:, g, :], AKm[ci][:, g, :], Vm[ci][:, g, :], start=False, stop=True)
                        H = Hm.next()
                        S.copy(r32(H), v3(pH), eng="act")
                        hbr.newbank()
                        pU = hbr.next()
                        for g in range(2):
                            S.mm(v3(pU)[:, g, :], r32(Pm[ci][:, g, :]), r32(H[:, g, :]))
                        U = Um.next()
                        S.copy(U, v3(pU), eng="dve")
                        hbr.newbank()
                        pY = hbr.next()
                        pY3 = T(pY.ap[:, 0:128].rearrange("p (g t) -> p g t", g=2), pY.bufs)
                        R4 = w4(Rm[ci])
                        for g in range(2):
                            S.mm(pY3[:, g, :], Sb[:, g, :], RT[:, g, ci * 64:(ci + 1) * 64], start=True, stop=False)
                            S.mm(pY3[:, g, :], U[:, g, :], R4[:, g, 0, :], start=False, stop=False)
                            S.mm(pY3[:, g, :], Vm[ci][:, g, :], R4[:, g, 1, :], start=False, stop=True)
                        S.copy(YT[:, :, ci * 64:(ci + 1) * 64], pY3, eng="act")
                        hbr.newbank()
                        pS = hbr.next()
                        for g in range(2):
                            S.mm(v3(pS)[:, g, :], Bhm[ci][:, g, :], U[:, g, :], start=True, stop=False)
                            S.mm(v3(pS)[:, g, :], Khm[ci][:, g, :], Vm[ci][:, g, :], start=False, stop=True)
                        for g in range(2):
                            S.stt(S32[:, g, :], S32[:, g, :], wcols[ci][:, g, :], v3(pS)[:, g, :], ALU.mult, ALU.add)
                        S.copy(Sb, S32, eng="pool")
                    chk("rwkv_chain", l)
                    for g in range(2):
                        hp = 2 * grp + g
                        pcol = lambda o: pf[:, l, o + hp:o + hp + 1]
                        yb = b256.next()
                        S.copy(yb[:, 0:n], YT[:, g, 0:n], eng="pool")
                        pb = psr.next()
                        S.mm(pb[:, 0:n], bd_bf, yb[:, 0:n])
                        yc = t256.next()
                        S.stt(yc[:, 0:n], pb[:, 0:n], -1.0 / 64, YT[:, g, 0:n], ALU.mult, ALU.add)
                        sq = b256.next()
                        S.actf(sq[:, 0:n], yc[:, 0:n], AF.Square)
                        pb = psr.next()
                        S.mm(pb[:, 0:n], bd_bf, sq[:, 0:n])
                        sd = t256.next()
                        S.actf(sd[:, 0:n], pb[:, 0:n], AF.Sqrt, scale=1.0 / 64, bias=epsb[:, 1:2])
                        S.recip(sd[:, 0:n], sd[:, 0:n])
                        S.tt(yc[:, 0:n], yc[:, 0:n], sd[:, 0:n], ALU.mult)
                        S.ts(yc[:, 0:n], yc[:, 0:n], pcol(PF_LNW), ALU.mult, pcol(PF_LNB), ALU.add)
                        S.tt(yc[:, 0:n], yc[:, 0:n], BON[:, g, 0:n], ALU.add, eng="pool")
                        pb = psr.next()
                        S.mm(pb[:, 0:n], G2[:, hp * 128:(hp + 1) * 128], sig_g[:, c0:c0 + n])
                        S.tt(OZ[:, hp, c0:c0 + n], yc[:, 0:n], pb[:, 0:n], ALU.mult)
                    chk("rwkv_C", l)
                chk("rwkv_seq", l)
                dst = wkvp if seq == 0 else wkvs
                for g in range(2):
                    for h in range(2):
                        S.dma(dst[l, 2 * grp + g, h * 64:(h + 1) * 64, :], S32[h * 64:(h + 1) * 64, g, h * 64:(h + 1) * 64])
        S.dma(shp[l], shp_t)
        S.dma(shs[l], shs_t)

    def phase_gproj(l, w_o, gate_col0, accumulate):
        A.seek(LOC)
        slabs = Ring([A.alloc([8, 256], BF16) for _ in range(2)])
        sgr = Ring([A.alloc([512], F32) for _ in range(3)])
        tr_ = Ring([A.alloc([512], F32) for _ in range(2)])
        for nt in range(8):
            sl = slabs.next()
            wslab(sl[:, :, 0:128], w_o[l].ap[:, nt * 128:(nt + 1) * 128])
            wslab(sl[:, :, 128:256], w_in[l].ap[:, gate_col0 + nt * 128:gate_col0 + (nt + 1) * 128])
            for (c0, n) in TQ:
                pa = psr.next()
                for kt in range(8):
                    S.mm(pa[:, 0:n], sl[:, kt, 0:128], OZ[:, kt, c0:c0 + n], start=(kt == 0), stop=(kt == 7))
                pg = psr.next()
                for kt in range(8):
                    S.mm(pg[:, 0:n], sl[:, kt, 128:256], hT[:, kt, c0:c0 + n], start=(kt == 0), stop=(kt == 7))
                sg = sgr.next()
                S.actf(sg[:, 0:n], pg[:, 0:n], AF.Sigmoid)
                if not accumulate:
                    S.tt(MT[:, nt, c0:c0 + n], pa[:, 0:n], sg[:, 0:n], ALU.mult)
                else:
                    t = tr_.next()
                    S.tt(t[:, 0:n], pa[:, 0:n], sg[:, 0:n], ALU.mult)
                    S.tt(MT[:, nt, c0:c0 + n], t[:, 0:n], MT[:, nt, c0:c0 + n], ALU.add, eng="pool")

    def phase_da(l):
        A.seek(LOC)
        Wd = A.alloc([8, 384], BF16)
        qkv_tok = A.alloc([NTB, 384], BF16)
        q_tok = qkv_tok[:, :, 0:128]
        k_tok = qkv_tok[:, :, 128:256]
        v_tok = qkv_tok[:, :, 256:384]
        kc_tok = A.alloc([16, 128], BF16)
        vc_tok = A.alloc([16, 128], BF16)
        KTh = A.alloc([NTOK + TP], BF16)
        Q1p = A.alloc([512], BF16)
        Q2p = A.alloc([512], BF16)
        Pr = Ring([A.alloc([512], BF16) for _ in range(3)])
        f512 = Ring([A.alloc([512], F32) for _ in range(4)])
        qkvr = Ring([A.alloc([384], F32) for _ in range(2)])
        rtmp = Ring([A.alloc([4, 16], F32) for _ in range(4)])
        S.memset(Q1p, 0.0)
        S.memset(Q2p, 0.0)
        nlam = lamv[:, l * 8:l * 8 + 1]
        subs = lamv[:, l * 8 + 1:l * 8 + 2]
        psO, psL = ps[0], ps[1]
        pr6 = Ring(ps[2:8])

        def rope_inplace(x, tb):
            x4 = T(x.ap.rearrange("p (m d) -> p m d", m=4), x.bufs)
            cc = cst[:, C_COS + tb * 16:C_COS + tb * 16 + 16]
            ss = cst[:, C_SIN + tb * 16:C_SIN + tb * 16 + 16]
            ccb = T(cc.ap.unsqueeze(1).broadcast_to([128, 4, 16]), cc.bufs)
            ssb = T(ss.ap.unsqueeze(1).broadcast_to([128, 4, 16]), ss.bufs)
            tc_ = rtmp.next()
            ts_ = rtmp.next()
            S.tt(tc_, x4[:, :, 0:16], ccb, ALU.mult)
            S.tt(ts_, x4[:, :, 0:16], ssb, ALU.mult)
            S.tt(x4[:, :, 0:8], tc_[:, :, 0:8], ts_[:, :, 8:16], ALU.subtract)
            S.tt(x4[:, :, 8:16], tc_[:, :, 8:16], ts_[:, :, 0:8], ALU.add)

        def attend(c0, nq, tbs, keyblocks):
            pb = pr6.next()
            pst = T(pb.ap.bitcast(BF16), pb.bufs)
            for i, tb in enumerate(tbs):
                S.tr(pst[:, i * 128:(i + 1) * 128], q_tok[:, tb, :], ident_bf)
            S.copy(Q1p[0:64, 0:nq], pst[0:64, 0:nq], eng="act")
            S.copy(Q2p[64:128, 0:nq], pst[64:128, 0:nq], eng="act")
            o1 = f512.next()
            t = None
            for mp, Qp in enumerate((Q1p, Q2p)):
                nk = len(keyblocks)
                for j, (kap, vap, c_lo, zspec) in enumerate(keyblocks):
                    pS = pr6.next()
                    S.mm(pS[:, c_lo:nq], kap, Qp[:, c_lo:nq])
                    P = Pr.next()
                    S.actf(P[:, c_lo:nq], pS[:, c_lo:nq], AF.Exp, scale=0.125)
                    if zspec == "diag":
                        S.memset(P[64:128, c_lo:c_lo + 64], 0.0)
                    elif zspec == "rows":
                        S.memset(P[32:64, 0:nq], 0.0)
                        S.memset(P[64:128, 0:nq], 0.0)
                    S.mm(psO[:, c_lo:nq], vap, P[:, c_lo:nq], start=(j == 0), stop=(j == nk - 1))
                    S.mm(psL[:, c_lo:nq], ones_bf, P[:, c_lo:nq], start=(j == 0), stop=(j == nk - 1))
                rd = f512.next()
                S.recip(rd[:, 0:nq], psL[:, 0:nq])
                if mp == 0:
                    S.tt(o1[:, 0:nq], psO[:, 0:nq], rd[:, 0:nq], ALU.mult)
                else:
                    t = f512.next()
                    S.tt(t[:, 0:nq], psO[:, 0:nq], rd[:, 0:nq], ALU.mult)
                    S.stt(o1[:, 0:nq], t[:, 0:nq], nlam, o1[:, 0:nq], ALU.mult, ALU.add)
            sq = Pr.next()
            S.actf(sq[:, 0:nq], o1[:, 0:nq], AF.Square)
            pb = pr6.next()
            S.mm(pb[:, 0:nq], ones_bf, sq[:, 0:nq])
            sd = t
            S.actf(sd[:, 0:nq], pb[:, 0:nq], AF.Sqrt, scale=1.0 / 128, bias=epsb[:, 0:1])
            S.recip(sd[:, 0:nq], sd[:, 0:nq])
            S.tt(o1[:, 0:nq], o1[:, 0:nq], sd[:, 0:nq], ALU.mult, eng="pool")
            return o1

        for hd in range(8):
            for x in range(3):
                wslab(Wd[:, :, x * 128:(x + 1) * 128], w_in[l].ap[:, x * 1024 + hd * 128:x * 1024 + (hd + 1) * 128])
            S.dma(kc_tok, T(ck[l].ap[:, hd * 128:(hd + 1) * 128].rearrange("(j p) c -> p j c", p=128), ()), eng="pool")
            S.dma(vc_tok, T(cv[l].ap[:, hd * 128:(hd + 1) * 128].rearrange("(j p) c -> p j c", p=128), ()), eng="pool")
            chk("da_load", l)
            for tb in range(NTB):
                if tb == 1:
                    chk("da_tb0", l)
                if tb == 16:
                    chk("da_tb15", l)
                pb = pr6.next()
                for kt in range(8):
                    S.mm(pb[:, 0:384], hT[:, kt, tb * 128:(tb + 1) * 128], Wd[:, kt, :],
                         start=(kt == 0), stop=(kt == 7))
                qkv = qkvr.next()
                S.copy(qkv, pb[:, 0:384], eng="act")
                rope_inplace(qkv[:, 0:256], tb)
                S.copy(qkv_tok[:, tb, :], qkv, eng="pool")
                if tb < 16:
                    S.dma(kp[l, tb * 128:(tb + 1) * 128, hd * 128:(hd + 1) * 128], qkv[:, 128:256])
                    S.dma(vp[l, tb * 128:(tb + 1) * 128, hd * 128:(hd + 1) * 128], qkv[:, 256:384])
                else:
                    S.dma(ks[l, :, hd * 128:(hd + 1) * 128], qkv[0:TSV, 128:256])
                    S.dma(vs[l, :, hd * 128:(hd + 1) * 128], qkv[0:TSV, 256:384])
            chk("da_proj", l)
            srcs = [(k_tok, tb) for tb in range(NTB)] + [(kc_tok, j) for j in range(16)]
            base = 0
            ei = 0
            while base < len(srcs):
                grp_ = srcs[base:base + 8]
                pb = pr6.next()
                pst = T(pb.ap.bitcast(BF16), pb.bufs)
                for i, (src, idx) in enumerate(grp_):
                    S.tr(pst[:, i * 128:(i + 1) * 128], src[:, idx, :], ident_bf)
                S.copy(KTh[:, base * 128:(base + len(grp_)) * 128], pst[:, 0:len(grp_) * 128],
                       eng=("act" if ei % 2 == 0 else "dve"))
                base += len(grp_)
                ei += 1
            chk("da_kt", l)
            for i in range(4):
                if i == 1:
                    chk("da_att0", l)
                kb = []
                for j in range(4 * i + 4):
                    m = j - 4 * i
                    c_lo = 128 * m if m > 0 else 0
                    kb.append((KTh[:, j * 128:(j + 1) * 128], v_tok[:, j, :], c_lo, "diag" if m >= 0 else None))
                o = attend(i * 512, 512, [4 * i + m for m in range(4)], kb)
                S.ts(OZ[:, hd, i * 512:(i + 1) * 512], o[:, 0:512], subs, ALU.mult)
            chk("da_attp", l)
            kb = [(KTh[:, NTOK + j * 128:NTOK + (j + 1) * 128], vc_tok[:, j, :], 0, None) for j in range(16)]
            kb.append((KTh[:, TP:TP + 128], v_tok[:, 16, :], 0, "rows"))
            o = attend(TP, TSV, [16], kb)
            S.ts(OZ[:, hd, TP:TP + TSV], o[:, 0:TSV], subs, ALU.mult)
            chk("da_head0", l)

    def phase_out(l):
        A.seek(LOC)
        nsc = make_norm_scratch()
        wo = A.alloc([8, D], BF16)
        wslab(wo[:, :, 0:512], w_out[l].ap[:, 0:512])
        wslab(wo[:, :, 512:1024], w_out[l].ap[:, 512:1024])
        xr = Ring([A.alloc([D], F32) for _ in range(2)])
        x1r = Ring([A.alloc([D], F32) for _ in range(2)])
        for tb in range(NTB):
            xt = xr.next()
            if l == 0:
                src, nr = x_rows(tb)
                if nr < 128:
                    S.memset(xt, 0.0)
                S.dma(xt[0:nr, :], src)
            else:
                S.dma(xt, X2[tb * 128:(tb + 1) * 128, :])
            x1 = x1r.next()
            for half in range(2):
                pb = psr.next()
                for kt in range(8):
                    S.mm(pb, MT[:, kt, tb * 128:(tb + 1) * 128], wo[:, kt, half * 512:(half + 1) * 512],
                         start=(kt == 0), stop=(kt == 7))
                S.tt(x1[:, half * 512:(half + 1) * 512], pb, xt[:, half * 512:(half + 1) * 512], ALU.add)
            S.dma(X1[tb * 128:(tb + 1) * 128, :], x1)
            norm_to_hT(x1, l, PF_NFFN, tb, nsc)

    def phase_ffn(l):
        A.seek(0)
        A.alloc([8, NTOK], BF16)
        wdn = A.alloc([22, D], BF16)
        GT = A.alloc([22, 512], BF16)
        assert A.off <= LOC
        A.seek(LOC)
        nsc = make_norm_scratch()
        for c in range(2):
            for hh in range(2):
                S.dma(wdn[:, 11 * hh:11 * (hh + 1), c * 512:(c + 1) * 512],
                      T(w_down[l].ap[11 * hh * 128:11 * (hh + 1) * 128, c * 512:(c + 1) * 512].rearrange("(kt p) c -> p kt c", p=128), ()), eng="pool")
        slabs = Ring([A.alloc([8, 512], BF16) for _ in range(2)])
        hpr = Ring([A.alloc([514], F32) for _ in range(4)])
        cr = Ring([A.alloc([512], F32) for _ in range(4)])
        xr = Ring([A.alloc([D], F32) for _ in range(2)])
        x2r = Ring([A.alloc([D], F32) for _ in range(2)])
        yr = Ring([A.alloc([D], F32) for _ in range(2)])
        S.memset(carry_p, 0.0)
        S.dma(T(carry_s.ap.rearrange("p a b -> p (a b)"), carry_s.bufs), cfm[l])
        for qi, (c0, n) in enumerate(TQ):
            carry = carry_p if qi < 4 else carry_s
            nval = n if qi < 4 else TSV
            for f2 in range(11):
                sl = slabs.next()
                wslab(sl[:, :, 0:256], w_up[l].ap[:, f2 * 256:(f2 + 1) * 256])
                wslab(sl[:, :, 256:512], w_up[l].ap[:, DFF + f2 * 256:DFF + (f2 + 1) * 256])
                for fi in range(2):
                    ft = 2 * f2 + fi
                    cs = []
                    for which in range(2):
                        fidx = which * 22 + ft
                        pb = psr.next()
                        off = which * 256 + fi * 128
                        for kt in range(8):
                            S.mm(pb[:, 0:n], sl[:, kt, off:off + 128], hT[:, kt, c0:c0 + n], start=(kt == 0), stop=(kt == 7))
                        hp_ = hpr.next()
                        S.copy(hp_[:, 0:2], carry[:, fidx, :], eng="pool")
                        S.copy(hp_[:, 2:n + 2], pb[:, 0:n], eng="act")
                        S.copy(carry[:, fidx, :], hp_[:, nval:nval + 2], eng="pool")
                        cw = lambda j: pf[:, l, PF_CONV + j * 44 + fidx:PF_CONV + j * 44 + fidx + 1]
                        cb = pf[:, l, PF_CONVB + fidx:PF_CONVB + fidx + 1]
                        c_ = cr.next()
                        eng = "dve" if which == 0 else "pool"
                        S.ts(c_[:, 0:n], hp_[:, 0:n], cw(0), ALU.mult, cb, ALU.add, eng=eng)
                        S.stt(c_[:, 0:n], hp_[:, 1:n + 1], cw(1), c_[:, 0:n], ALU.mult, ALU.add)
                        S.stt(c_[:, 0:n], hp_[:, 2:n + 2], cw(2), c_[:, 0:n], ALU.mult, ALU.add)
                        cs.append(c_)
                    S.actf(cs[0][:, 0:n], cs[0][:, 0:n], AF.Silu)
                    S.tt(GT[:, ft, 0:n], cs[0][:, 0:n], cs[1][:, 0:n], ALU.mult, eng="pool")
            for tbl in range(n // 128):
                tb = c0 // 128 + tbl
                xt = xr.next()
                S.dma(xt, X1[tb * 128:(tb + 1) * 128, :])
                x2 = x2r.next()
                for half in range(2):
                    pb = psr.next()
                    for ft in range(22):
                        S.mm(pb, GT[:, ft, tbl * 128:(tbl + 1) * 128], wdn[:, ft, half * 512:(half + 1) * 512],
                             start=(ft == 0), stop=(ft == 21))
                    S.tt(x2[:, half * 512:(half + 1) * 512], pb, xt[:, half * 512:(half + 1) * 512], ALU.add)
                if l == 1:
                    rstd = norm_stats(x2, nsc)
                    y = yr.next()
                    S.stt(y, x2, rstd, gfin, ALU.mult, ALU.mult)
                    if tb < 16:
                        S.dma(yp[tb * 128:(tb + 1) * 128, :], y)
                    else:
                        S.dma(ys, y[0:TSV, :])
                else:
                    S.dma(X2[tb * 128:(tb + 1) * 128, :], x2)
                    norm_to_hT(x2, 1, PF_NMIX, tb, nsc)
        S.dma(cvp[l], T(carry_p.ap.rearrange("p a b -> p (a b)"), carry_p.bufs))
        S.dma(cvs[l], T(carry_s.ap.rearrange("p a b -> p (a b)"), carry_s.bufs))

    def dump(name, src):
        if name in dbg_out:
            S.barrier()
            S.dma(dbg_out[name], src)
            S.barrier()

    try:
        _program(S, locals())
    except StopBuild:
        pass
    S.emit()
    S.close()
    return nc


def _program(S, L):
    phase_norm0, phase_rwkv, phase_gproj, phase_da, phase_out, phase_ffn = (
        L["phase_norm0"], L["phase_rwkv"], L["phase_gproj"], L["phase_da"], L["phase_out"], L["phase_ffn"])
    dump, stop_after, hT, OZ, MT = L["dump"], L["stop_after"], L["hT"], L["OZ"], L["MT"]
    w_o_rw, w_o_da = L["w_o_rw"], L["w_o_da"]
    S.barrier()
    phase_norm0()
    S.barrier()
    dump("hT0", hT)
    done = False
    for l in range(2):
        if stop_after == ("norm", l):
            break
        phase_rwkv(l)
        S.barrier()
        dump(f"Z{l}", OZ)
        if stop_after == ("rwkv", l):
            break
        phase_gproj(l, w_o_rw, P_DA + P_RW + D, False)
        S.barrier()
        if stop_after == ("gproj1", l):
            break
        phase_da(l)
        S.barrier()
        dump(f"O{l}", OZ)
        if stop_after == ("da", l):
            break
        phase_gproj(l, w_o_da, P_DA + P_RW, True)
        S.barrier()
        dump(f"M{l}", MT)
        phase_out(l)
        S.barrier()
        dump(f"h2T{l}", hT)
        if stop_after == ("out", l):
            break
        phase_ffn(l)
        S.barrier()


_NC_CACHE = {}


def _prep_inputs(inp):
    f = lambda a: np.ascontiguousarray(np.asarray(a, dtype=np.float32))
    g = {k: f(v) for k, v in inp.items()}
    consts = make_consts()

    def fm(v, nt):
        return v.reshape(nt, 128).T

    pfm = np.zeros((2, 128, NPF), np.float32)
    for l in range(2):
        pfm[l, :, PF_NMIX:PF_NMIX + 8] = fm(g["norm_mix"][l], 8)
        pfm[l, :, PF_NFFN:PF_NFFN + 8] = fm(g["norm_ffn"][l], 8)
        pfm[l, :, PF_MU:PF_MU + 26] = fm(g["rw_mu"][l], 26)
        pfm[l, :, PF_W0:PF_W0 + 8] = fm(g["rw_w0"][l], 8)
        pfm[l, :, PF_A0:PF_A0 + 8] = fm(g["rw_a0"][l], 8)
        pfm[l, :, PF_KK:PF_KK + 8] = fm(g["rw_k_k"][l], 8)
        pfm[l, :, PF_KA:PF_KA + 8] = fm(g["rw_k_a"][l], 8)
        pfm[l, :, PF_RK:PF_RK + 8] = fm(g["rw_r_k"][l].reshape(-1), 8)
        pfm[l, :, PF_LNW:PF_LNW + 8] = fm(g["rw_ln_w"][l], 8)
        pfm[l, :, PF_LNB:PF_LNB + 8] = fm(g["rw_ln_b"][l], 8)
        for j in range(3):
            pfm[l, :, PF_CONV + j * 44:PF_CONV + (j + 1) * 44] = fm(g["ffn_conv"][l, j], 44)
        pfm[l, :, PF_CONVB:PF_CONVB + 44] = fm(g["ffn_conv_b"][l], 44)
        pfm[l, :, PF_SUBLN] = g["da_subln"][l]
    shared = {
        "pfm": pfm, "consts": consts, "w_in": g["w_in"], "da_lambda": g["da_lambda"].reshape(512),
        "w_o_da": g["w_o_da"], "rw_w2": g["rw_w2"], "rw_a2": g["rw_a2"], "rw_g2": g["rw_g2"],
        "w_o_rw": g["w_o_rw"], "w_out": g["w_out"], "w_up": g["w_up"], "w_down": g["w_down"],
        "nfin": g["norm_final"],
    }
    maps = []
    for b in range(8):
        m = dict(shared)
        m["xp"] = g["x_prompt"][b]
        m["xs"] = g["x_sample"][b]
        m["ck"] = np.ascontiguousarray(g["cache_k"][:, b].reshape(2, TP, D))
        m["cv"] = np.ascontiguousarray(g["cache_v"][:, b].reshape(2, TP, D))
        sw = g["state_wkv"][:, b].reshape(2, 8, 2, 64, 64).transpose(0, 1, 2, 4, 3).reshape(2, 8, 128, 64)
        m["swkv"] = np.ascontiguousarray(sw)
        m["sfm"] = np.ascontiguousarray(g["state_shift"][:, b, 0].reshape(2, 26, 128).transpose(0, 2, 1))
        cf = g["state_ffn_conv"][:, b].reshape(2, 2, 44, 128).transpose(0, 3, 2, 1).reshape(2, 128, 88)
        m["cfm"] = np.ascontiguousarray(cf)
        maps.append(m)
    return maps


def _assemble(results):
    def st(name):
        return np.stack([np.asarray(r[name]) for r in results], axis=0)

    y_prompt = st("yp")
    y_sample = st("ys")
    k_prompt = st("kp").transpose(1, 0, 2, 3).reshape(2, 8, TP, 8, 128)
    v_prompt = st("vp").transpose(1, 0, 2, 3).reshape(2, 8, TP, 8, 128)
    k_sample = st("ks").transpose(1, 0, 2, 3).reshape(2, 8, TSV, 8, 128)
    v_sample = st("vs").transpose(1, 0, 2, 3).reshape(2, 8, TSV, 8, 128)

    def wkv(name):
        a = st(name).reshape(8, 2, 8, 2, 64, 64).transpose(1, 0, 2, 3, 5, 4).reshape(2, 8, 16, 64, 64)
        return np.ascontiguousarray(a)

    def shift(name):
        a = st(name).transpose(1, 0, 3, 2).reshape(2, 8, 1, P_RW)
        return np.ascontiguousarray(a)

    def conv(name):
        a = st(name).reshape(8, 2, 128, 44, 2).transpose(1, 0, 4, 3, 2).reshape(2, 8, 2, 2 * DFF)
        return np.ascontiguousarray(a)

    return (np.ascontiguousarray(y_prompt), np.ascontiguousarray(y_sample),
            np.ascontiguousarray(k_prompt), np.ascontiguousarray(v_prompt),
            wkv("wkvp"), shift("shp"), conv("cvp"),
            np.ascontiguousarray(k_sample), np.ascontiguousarray(v_sample),
            wkv("wkvs"), shift("shs"), conv("cvs"))


def kernel(**inputs):
    maps = _prep_inputs(inputs)
    nc = build()
    res = run_bass_kernel_spmd(nc, maps, core_ids=list(range(8)))
    return _assemble(res.results)
```

```python
import numpy as np
from contextlib import ExitStack

import concourse.bass as bass
import concourse.mybir as mybir

F32 = mybir.dt.float32
BF16 = mybir.dt.bfloat16
F32R = mybir.dt.float32r
AF = mybir.ActivationFunctionType
ALU = mybir.AluOpType
AX = mybir.AxisListType

ENGS = ("pe", "act", "dve", "pool", "sp")
SEM_EPOCH = 20000
DMA_ROT = 8


class Buf:
    __slots__ = ("w", "r", "name")

    def __init__(self, name=""):
        self.w = None
        self.r = []
        self.name = name


class T:
    __slots__ = ("ap", "bufs")

    def __init__(self, ap, bufs):
        self.ap = ap
        self.bufs = tuple(bufs)

    def __getitem__(self, key):
        return T(self.ap[key], self.bufs)

    def v(self, ap):
        return T(ap, self.bufs)

    def bitcast(self, dt):
        return T(self.ap.bitcast(dt), self.bufs)


class Rec:
    __slots__ = ("eng", "idx", "fn", "deps", "dma", "signaled", "ev", "selfwait")

    def __init__(self, eng, idx, fn, dma):
        self.eng = eng
        self.idx = idx
        self.fn = fn
        self.deps = []
        self.dma = dma
        self.signaled = False
        self.ev = None
        self.selfwait = None


class Sched:
    def __init__(self, nc):
        self.nc = nc
        self.q = {e: [] for e in ENGS}
        self.stack = ExitStack()
        self.nbuf = 0
        self.same_engine_sync = True
        self._bar_pos = {}
        self._pending = {e: [] for e in ENGS}

    def sbuf(self, name, shape, dtype, nbufs=1):
        h = self.stack.enter_context(self.nc.sbuf_tensor(name, list(shape), dtype))
        return T(h[:], [Buf(name)])

    def psum(self, name, shape, dtype=F32):
        h = self.stack.enter_context(self.nc.psum_tensor(name, list(shape), dtype))
        return T(h[:], [Buf(name)])

    def dram(self, name, shape, dtype, kind):
        h = self.nc.dram_tensor(name, list(shape), dtype, kind=kind)
        return T(h.ap(), [Buf(name)])

    def newbuf(self, name=""):
        return Buf(name)

    def add(self, eng, fn, reads=(), writes=(), dma=False):
        q = self.q[eng]
        rec = Rec(eng, len(q), fn, dma)
        deps = {}
        for t in reads:
            for b in t.bufs:
                if b.w is not None:
                    deps[id(b.w)] = b.w
        for t in writes:
            for b in t.bufs:
                if b.w is not None:
                    deps[id(b.w)] = b.w
                for r in b.r:
                    deps[id(r)] = r
        if self._pending[eng]:
            for d in self._pending[eng]:
                deps[id(d)] = d
            self._pending[eng] = []
            barrier_deps = True
        else:
            barrier_deps = False
        for d in deps.values():
            if d is rec:
                continue
            if d.eng == eng and not d.dma and not dma:
                if eng == "pe" or eng == "sp" or not self.same_engine_sync:
                    continue
            rec.deps.append(d)
        for t in reads:
            for b in t.bufs:
                b.r.append(rec)
        for t in writes:
            for b in t.bufs:
                b.w = rec
                b.r = []
        q.append(rec)
        return rec

    def emit(self):
        nc = self.nc
        for e in ENGS:
            for rec in self.q[e]:
                for d in rec.deps:
                    d.signaled = True
        sems = {}
        for e in ENGS:
            cnt = 0
            for rec in self.q[e]:
                if rec.dma:
                    continue
                if rec.signaled:
                    ep = cnt // SEM_EPOCH
                    key = (e, ep)
                    if key not in sems:
                        sems[key] = self.stack.enter_context(nc.semaphore(f"s_{e}_{ep}"))
                    rec.ev = (sems[key], cnt % SEM_EPOCH + 1, key)
                    cnt += 1
        self.final_waits = []
        for e in ENGS:
            j = 0
            last = {}
            for rec in self.q[e]:
                if not rec.dma:
                    continue
                slot = j % DMA_ROT
                key = ("dma", e, slot)
                if key not in sems:
                    sems[key] = self.stack.enter_context(nc.semaphore(f"d_{e}_{slot}"))
                val = 16 * (j // DMA_ROT + 1)
                rec.ev = (sems[key], val, key)
                if j >= DMA_ROT:
                    rec.selfwait = (sems[key], val - 16, key)
                last[key] = (sems[key], val, key)
                j += 1
            self.final_waits.extend(last.values())

        block = self.stack.enter_context(nc.Block())
        sched = self

        def run(engname, eobj):
            waited = {}
            for rec in sched.q[engname]:
                waits = {}
                if rec.selfwait is not None:
                    s, v, key = rec.selfwait
                    waits[key] = (s, v)
                for d in rec.deps:
                    s, v, key = d.ev
                    if key in waits:
                        if waits[key][1] < v:
                            waits[key] = (s, v)
                    else:
                        waits[key] = (s, v)
                for key, (s, v) in waits.items():
                    if waited.get(key, 0) >= v:
                        continue
                    eobj.wait_ge(s, v)
                    waited[key] = v
                ins = rec.fn(eobj)
                if rec.dma:
                    ins.then_inc(rec.ev[0], 16)
                elif rec.signaled:
                    ins.then_inc(rec.ev[0], 1)
            if engname == "sp":
                for s, v, key in sched.final_waits:
                    eobj.wait_ge(s, v)

        @block.tensor
        def _(e):
            run("pe", e)

        @block.scalar
        def _(e):
            run("act", e)

        @block.vector
        def _(e):
            run("dve", e)

        @block.gpsimd
        def _(e):
            run("pool", e)

        @block.sync
        def _(e):
            run("sp", e)

    def close(self):
        self.stack.close()

    def dma(self, out, in_, eng="sp", **kw):
        return self.add(eng, lambda e: e.dma_start(out=out.ap, in_=in_.ap, **kw),
                        reads=[in_], writes=[out], dma=True)

    def mm(self, out, lhsT, rhs, start=True, stop=True, extra_reads=(), **kw):
        return self.add("pe", lambda e: e.matmul(out.ap, lhsT.ap, rhs.ap, start=start, stop=stop, **kw),
                        reads=[lhsT, rhs, *extra_reads], writes=[out])

    def tr(self, out, in_, ident, **kw):
        return self.add("pe", lambda e: e.transpose(out.ap, in_.ap, ident.ap, **kw),
                        reads=[in_, ident], writes=[out])

    def actf(self, out, in_, func, bias=None, scale=None, accum=None, eng="act"):
        kw = {}
        reads = [in_]
        writes = [out]
        if bias is not None:
            if isinstance(bias, T):
                kw["bias"] = bias.ap
                reads.append(bias)
            else:
                kw["bias"] = bias
        if scale is not None:
            if isinstance(scale, T):
                kw["scale"] = scale.ap
                reads.append(scale)
            else:
                kw["scale"] = scale
        if accum is not None:
            kw["accum_out"] = accum.ap
            writes.append(accum)
        return self.add("act", lambda e: e.activation(out.ap, in_.ap, func, **kw), reads=reads, writes=writes)

    def tt(self, out, a, b, op, eng="dve"):
        return self.add(eng, lambda e: e.tensor_tensor(out.ap, a.ap, b.ap, op), reads=[a, b], writes=[out])

    def ts(self, out, a, s1, op0, s2=None, op1=None, accum=None, eng="dve"):
        reads = [a]
        writes = [out]
        s1v = s1.ap if isinstance(s1, T) else s1
        s2v = s2.ap if isinstance(s2, T) else s2
        if isinstance(s1, T):
            reads.append(s1)
        if isinstance(s2, T):
            reads.append(s2)
        kw = {}
        if op1 is not None:
            kw["op1"] = op1
        if accum is not None:
            kw["accum_out"] = accum.ap
            writes.append(accum)
        return self.add(eng, lambda e: e.tensor_scalar(out.ap, a.ap, s1v, s2v, op0, **kw), reads=reads, writes=writes)

    def stt(self, out, a, s, b, op0, op1, eng="dve"):
        reads = [a, b]
        sv = s.ap if isinstance(s, T) else s
        if isinstance(s, T):
            reads.append(s)
        return self.add(eng, lambda e: e.scalar_tensor_tensor(out.ap, a.ap, sv, b.ap, op0, op1), reads=reads, writes=[out])

    def copy(self, out, in_, eng="dve"):
        if eng == "act":
            return self.add("act", lambda e: e.copy(out.ap, in_.ap), reads=[in_], writes=[out])
        return self.add(eng, lambda e: e.tensor_copy(out.ap, in_.ap), reads=[in_], writes=[out])

    def memset(self, out, val, eng="pool"):
        return self.add(eng, lambda e: e.memset(out.ap, val), reads=[], writes=[out])

    def recip(self, out, in_):
        return self.add("dve", lambda e: e.reciprocal(out.ap, in_.ap), reads=[in_], writes=[out])

    def scan(self, out, d0, d1, init, op0, op1):
        reads = [d0, d1]
        iv = init.ap if isinstance(init, T) else init
        if isinstance(init, T):
            reads.append(init)
        return self.add("dve", lambda e: e.tensor_tensor_scan(out.ap, d0.ap, d1.ap, iv, op0, op1), reads=reads, writes=[out])


def _barrier(self):
    deps = []
    for e in ENGS:
        q = self.q[e]
        last_c = None
        for rec in reversed(q):
            if not rec.dma:
                last_c = rec
                break
        if last_c is not None:
            deps.append(last_c)
        for rec in q[self._bar_pos.get(e, 0):]:
            if rec.dma:
                deps.append(rec)
        self._bar_pos[e] = len(q)
    for e in ENGS:
        self._pending[e] = list(deps)


Sched.barrier = _barrier


from concourse.bass_utils import run_bass_kernel_spmd

D = 1024
TP = 2048
TSV = 32
NTOK = 2176
NTB = 17
P_DA = 3072
P_RW = 3328
PTOT = 8448
DFF = 2816
EPS = 1e-6
GN_EPS = 64e-5
ROPE_THETA = 500000.0
TQ = [(0, 512), (512, 512), (1024, 512), (1536, 512), (2048, 128)]
PF_NMIX, PF_NFFN, PF_MU, PF_W0, PF_A0, PF_KK, PF_KA, PF_RK, PF_LNW, PF_LNB = 0, 8, 16, 42, 50, 58, 66, 74, 82, 90
PF_CONV, PF_CONVB, PF_SUBLN, NPF = 98, 230, 274, 275
C_ID, C_BD, C_MU, C_ML, C_MUI, C_RST, C_COS, C_SIN, NCONST = 0, 128, 256, 384, 512, 576, 832, 1104, 1376
DEC_C = -0.6065306597126334


def make_consts():
    c = np.zeros((128, NCONST), np.float32)
    c[:, C_ID:C_ID + 128] = np.eye(128, dtype=np.float32)
    p = np.arange(128)
    h = p // 64
    s = p % 64
    bd = (h[:, None] == h[None, :]).astype(np.float32)
    c[:, C_BD:C_BD + 128] = bd
    c[:, C_MU:C_MU + 128] = bd * (s[:, None] < s[None, :])
    c[:, C_ML:C_ML + 128] = bd * (s[:, None] > s[None, :])
    c[:, C_MUI:C_MUI + 64] = (s[:, None] <= np.arange(64)[None, :])
    rst = np.ones(256, np.float32)
    rst[::64] = 0
    c[:, C_RST:C_RST + 256] = rst[None, :]
    inv = (np.float32(ROPE_THETA) ** (-np.arange(0, 16, 2, dtype=np.float32) / np.float32(16))).astype(np.float32)
    for tb in range(NTB):
        pos = (tb * 128 + p) if tb < 16 else (TP + p)
        ang = pos.astype(np.float32)[:, None] * inv[None, :]
        co = np.cos(ang).astype(np.float32)
        si = np.sin(ang).astype(np.float32)
        c[:, C_COS + tb * 16:C_COS + tb * 16 + 8] = co
        c[:, C_COS + tb * 16 + 8:C_COS + tb * 16 + 16] = co
        c[:, C_SIN + tb * 16:C_SIN + tb * 16 + 8] = si
        c[:, C_SIN + tb * 16 + 8:C_SIN + tb * 16 + 16] = si
    return c


CHAIN_BF16 = True


def r32(t):
    if CHAIN_BF16:
        return t
    return T(t.ap.bitcast(F32R), t.bufs)


class Arena:
    def __init__(self, t, width):
        self.t = t
        self.W = width
        self.off = 0

    def seek(self, off):
        self.off = off

    def alloc(self, free_shape, dtype):
        n = 1
        for d in free_shape:
            n *= d
        words = n if dtype != BF16 else (n + 1) // 2
        words = (words + 7) // 8 * 8
        assert self.off + words <= self.W, ("arena overflow", self.off, words, self.W)
        ap = self.t.ap[:, self.off:self.off + words]
        if dtype == BF16:
            ap = ap.bitcast(BF16)
        ap = ap[:, 0:n]
        if len(free_shape) > 1:
            names = [f"d{i}" for i in range(len(free_shape))]
            pat = "p (" + " ".join(names) + ") -> p " + " ".join(names)
            kw = {nm: sz for nm, sz in zip(names[:-1], free_shape[:-1])}
            ap = ap.rearrange(pat, **kw)
        self.off += words
        return T(ap, [Buf()])


class Ring:
    def __init__(self, items):
        self.items = items
        self.i = 0

    def next(self):
        t = self.items[self.i % len(self.items)]
        self.i += 1
        return t


class StopBuild(Exception):
    pass


def build(dbg=None, stop_after=None):
    def chk(tag, l=0):
        if stop_after == (tag, l):
            raise StopBuild()

    nc = bass.Bass("TRN2", target_bir_lowering=False)
    S = Sched(nc)

    def din(name, shape):
        return S.dram(name, shape, F32, "ExternalInput")

    def dout(name, shape):
        return S.dram(name, shape, F32, "ExternalOutput")

    xp = din("xp", [TP, D])
    xs = din("xs", [TSV, D])
    ck = din("ck", [2, TP, D])
    cv = din("cv", [2, TP, D])
    swkv = din("swkv", [2, 8, 128, 64])
    sfm = din("sfm", [2, 128, 26])
    cfm = din("cfm", [2, 128, 88])
    pfm = din("pfm", [2, 128, NPF])
    consts = din("consts", [128, NCONST])
    w_in = din("w_in", [2, D, PTOT])
    da_lambda = din("da_lambda", [512])
    w_o_da = din("w_o_da", [2, D, D])
    rw_w2 = din("rw_w2", [2, 64, D])
    rw_a2 = din("rw_a2", [2, 64, D])
    rw_g2 = din("rw_g2", [2, 128, D])
    w_o_rw = din("w_o_rw", [2, D, D])
    w_out = din("w_out", [2, D, D])
    w_up = din("w_up", [2, D, 2 * DFF])
    w_down = din("w_down", [2, DFF, D])
    nfin = din("nfin", [D])

    yp = dout("yp", [TP, D])
    ys = dout("ys", [TSV, D])
    kp = dout("kp", [2, TP, D])
    vp = dout("vp", [2, TP, D])
    wkvp = dout("wkvp", [2, 8, 128, 64])
    shp = dout("shp", [2, 128, 26])
    cvp = dout("cvp", [2, 128, 88])
    ks = dout("ks", [2, TSV, D])
    vs = dout("vs", [2, TSV, D])
    wkvs = dout("wkvs", [2, 8, 128, 64])
    shs = dout("shs", [2, 128, 26])
    cvs = dout("cvs", [2, 128, 88])
    X1 = S.dram("X1", [NTOK, D], F32, "Internal")
    X2 = S.dram("X2", [NTOK, D], F32, "Internal")
    dbg_out = {}
    if dbg:
        for name, shape in dbg.items():
            dbg_out[name] = dout("dbg_" + name, shape)

    cst = S.sbuf("cst", [128, NCONST], F32)
    S.dma(cst, consts)
    ident_bf = S.sbuf("ident_bf", [128, 128], BF16)
    S.copy(ident_bf, cst[:, C_ID:C_ID + 128])
    ones_bf = S.sbuf("ones_bf", [128, 128], BF16)
    S.memset(ones_bf, 1.0)
    ones_f = S.sbuf("ones_f", [128, 128], F32)
    S.memset(ones_f, 1.0)
    ident_f = cst[:, C_ID:C_ID + 128]
    bdmask = cst[:, C_BD:C_BD + 128]
    bd_bf = S.sbuf("bd_bf", [128, 128], BF16)
    S.copy(bd_bf, cst[:, C_BD:C_BD + 128])
    fr = None if CHAIN_BF16 else S.sbuf("fr", [128, 21, 256], F32)
    pf = S.sbuf("pf", [128, 2, NPF], F32)
    for l in range(2):
        S.dma(pf[:, l, :], pfm[l])
    gfin = S.sbuf("gfin", [128, D], F32)
    S.dma(gfin, nfin.v(nfin.ap.partition_broadcast(128)))
    lamb = S.sbuf("lamb", [128, 512], F32)
    S.dma(lamb, da_lambda.v(da_lambda.ap.partition_broadcast(128)))
    lamv = S.sbuf("lamv", [128, 16], F32)
    epsb = S.sbuf("epsb", [128, 2], F32)
    S.memset(epsb[:, 0:1], EPS)
    S.memset(epsb[:, 1:2], GN_EPS)
    ljunk = S.sbuf("ljunk", [128, 64], F32)
    shp_t = S.sbuf("shp_t", [128, 26], F32)
    shs_t = S.sbuf("shs_t", [128, 26], F32)
    carry_p = S.sbuf("carry_p", [128, 44, 2], F32)
    carry_s = S.sbuf("carry_s", [128, 44, 2], F32)

    ps = [S.psum(f"ps{i}", [128, 512], F32) for i in range(8)]
    psr = Ring(ps)
    halves = []
    for i in range(8):
        halves.append(T(ps[i].ap[:, 0:256], ps[i].bufs))
        halves.append(T(ps[i].ap[:, 256:512], ps[i].bufs))

    AW = ((nc.sbuf_bytes_remaining - 256) // 4) // 8 * 8
    ar = S.sbuf("arena", [128, AW], F32)
    A = Arena(ar, AW)
    hT = A.alloc([8, NTOK], BF16)
    OZ = A.alloc([8, NTOK], BF16)
    MT_OFF = A.off
    MT = A.alloc([8, NTOK], BF16)
    LOC = A.off
    OZ_OFF = MT_OFF - (MT_OFF - 0) // 2 if False else None
    S.memset(hT, 0.0)
    S.memset(OZ, 0.0, eng="dve")
    S.memset(MT, 0.0)

    for l in range(2):
        lam_init = 0.8 - 0.6 * float(np.exp(-0.3 * l))
        b = l * 256
        pr = ljunk
        S.tt(pr, lamb[:, b:b + 64], lamb[:, b + 64:b + 128], ALU.mult)
        S.ts(pr, pr, 1.0, ALU.mult, None, ALU.add, accum=lamv[:, l * 8 + 4:l * 8 + 5])
        S.tt(pr, lamb[:, b + 128:b + 192], lamb[:, b + 192:b + 256], ALU.mult)
        S.ts(pr, pr, 1.0, ALU.mult, None, ALU.add, accum=lamv[:, l * 8 + 5:l * 8 + 6])
        S.actf(lamv[:, l * 8 + 2:l * 8 + 4], lamv[:, l * 8 + 4:l * 8 + 6], AF.Exp)
        S.tt(lamv[:, l * 8:l * 8 + 1], lamv[:, l * 8 + 3:l * 8 + 4], lamv[:, l * 8 + 2:l * 8 + 3], ALU.subtract)
        S.ts(lamv[:, l * 8:l * 8 + 1], lamv[:, l * 8:l * 8 + 1], -lam_init, ALU.add)
        S.ts(lamv[:, l * 8 + 1:l * 8 + 2], pf[:, l, PF_SUBLN:PF_SUBLN + 1], 1.0 - lam_init, ALU.mult)

    def wslab(dst, src_ap):
        return S.dma(dst, T(src_ap.rearrange("(kt p) c -> p kt c", p=128), ()), eng="pool")

    def make_norm_scratch():
        d = {}
        d["junk"] = A.alloc([D], BF16)
        d["xn"] = Ring([A.alloc([D], BF16) for _ in range(2)])
        d["st"] = Ring([A.alloc([4], F32) for _ in range(3)])
        return d

    def norm_stats(x_t, nsc):
        st = nsc["st"].next()
        S.actf(nsc["junk"], x_t, AF.Square, accum=st[:, 0:1])
        S.actf(st[:, 1:2], st[:, 0:1], AF.Sqrt, scale=1.0 / D, bias=epsb[:, 0:1])
        S.recip(st[:, 2:3], st[:, 1:2])
        return st[:, 2:3]

    def norm_to_hT(x_t, l, pfoff, tb, nsc):
        rstd = norm_stats(x_t, nsc)
        xn = nsc["xn"].next()
        S.ts(xn, x_t, rstd, ALU.mult)
        pb = psr.next()
        pst = T(pb.ap.bitcast(BF16).rearrange("p (k t) -> p k t", k=8), pb.bufs)
        for kt in range(8):
            S.tr(pst[:, kt, :], xn[:, kt * 128:(kt + 1) * 128], ident_bf)
        g = pf[:, l, pfoff:pfoff + 8]
        S.tt(hT[:, :, tb * 128:(tb + 1) * 128], pst, g.v(g.ap.unsqueeze(2).broadcast_to([128, 8, 128])), ALU.mult)

    def x_rows(tb):
        return (xp[tb * 128:(tb + 1) * 128, :], 128) if tb < 16 else (xs, TSV)

    def phase_norm0():
        A.seek(LOC)
        nsc = make_norm_scratch()
        xr = Ring([A.alloc([D], F32) for _ in range(3)])
        for tb in range(NTB):
            xt = xr.next()
            src, nr = x_rows(tb)
            if nr < 128:
                S.memset(xt, 0.0)
            S.dma(xt[0:nr, :], src)
            norm_to_hT(xt, 0, PF_NMIX, tb, nsc)

    def phase_rwkv(l):
        A.seek(MT_OFF)
        w_l = w_in[l]
        mu = lambda c: pf[:, l, PF_MU + c:PF_MU + c + 1]
        W2p = A.alloc([D], BF16)
        A2p = A.alloc([D], BF16)
        G2 = A.alloc([D], BF16)
        S.memset(W2p[64:128, :], 0.0)
        S.memset(A2p[0:64, :], 0.0)
        S.dma(W2p[0:64, :], rw_w2[l], eng="pool")
        S.dma(A2p[64:128, :], rw_a2[l], eng="pool")
        S.dma(G2, rw_g2[l], eng="pool")
        tanh_w = A.alloc([NTOK], BF16)
        raw_w = A.alloc([NTOK], BF16)
        sig_g = A.alloc([NTOK], BF16)
        sfm_t = A.alloc([26], F32)
        S.dma(sfm_t, sfm[l])
        NB = 256
        GRP_OFF = A.off
        Wl = A.alloc([8, 256], BF16)
        wslab(Wl, w_l.ap[:, P_DA + 3072:P_DA + 3328])
        u_ring = Ring([A.alloc([513], F32) for _ in range(3)])
        tmp = Ring([A.alloc([512], F32) for _ in range(6)])

        def shifted(psb, n, carry_src, mucol, u_out_last=None, last_idx=None):
            U = u_ring.next()
            if carry_src is None:
                S.memset(U[:, 0:1], 0.0, eng="dve")
            else:
                S.copy(U[:, 0:1], carry_src, eng="dve")
            S.copy(U[:, 1:n + 1], psb[:, 0:n], eng="act")
            d = tmp.next()
            S.tt(d[:, 0:n], U[:, 0:n], U[:, 1:n + 1], ALU.subtract)
            return d, U

        prev = {0: None, 1: None}
        for qi, (c0, n) in enumerate(TQ):
            for which in range(2):
                pb = psr.next()
                for kt in range(8):
                    S.mm(pb[:, 0:n], Wl[:, kt, which * 128:(which + 1) * 128], hT[:, kt, c0:c0 + n],
                         start=(kt == 0), stop=(kt == 7))
                mc = 24 + which
                if qi == 0:
                    carry = None
                elif qi == 4:
                    carry = sfm_t[:, mc:mc + 1]
                else:
                    carry = prev[which]
                d, U = shifted(pb, n, carry, mc)
                us = tmp.next()
                S.stt(us[:, 0:n], d[:, 0:n], mu(mc), U[:, 1:n + 1], ALU.mult, ALU.add)
                prev[which] = U[:, n:n + 1]
                if qi == 3:
                    S.copy(shp_t[:, mc:mc + 1], U[:, n:n + 1], eng="pool")
                if qi == 4:
                    S.copy(shs_t[:, mc:mc + 1], U[:, TSV:TSV + 1], eng="pool")
                if which == 0:
                    S.actf(tanh_w[:, c0:c0 + n], us[:, 0:n], AF.Tanh)
                    S.copy(raw_w[:, c0:c0 + n], us[:, 0:n], eng="pool")
                else:
                    S.actf(sig_g[:, c0:c0 + n], us[:, 0:n], AF.Sigmoid)

        chk("rwkv_lora", l)
        S.barrier()
        A.seek(GRP_OFF)
        Wg = A.alloc([8, 768], BF16)
        AT = A.alloc([2, NB], BF16)
        BT = A.alloc([2, NB], BF16)
        KT_ = A.alloc([2, NB], BF16)
        RT = A.alloc([2, NB], BF16)
        VT = A.alloc([2, NB], BF16)
        WT = A.alloc([2, NB], F32)
        BON = A.alloc([2, NB], F32)
        YT = A.alloc([2, NB], F32)
        S32 = A.alloc([2, 128], F32)
        Sb = A.alloc([2, 128], BF16)
        swt = A.alloc([2, 64], F32)
        carr = A.alloc([8], F32)
        t256 = Ring([A.alloc([NB], F32) for _ in range(11)])
        b256 = Ring([A.alloc([NB], BF16) for _ in range(3)])
        usrk = Ring([A.alloc([NB], F32) for _ in range(2)])
        u257 = Ring([A.alloc([NB + 1], F32) for _ in range(2)])
        NCH = 4
        wide = lambda: A.alloc([2, 128], BF16)
        BDa = [wide() for _ in range(NCH)]
        BDb = [wide() for _ in range(NCH)]
        BDk = [wide() for _ in range(NCH)]
        BDx = Ring([wide() for _ in range(3)])
        AKm = [wide() for _ in range(NCH)]
        Vm = [wide() for _ in range(NCH)]
        Bhm = [wide() for _ in range(NCH)]
        Khm = [wide() for _ in range(NCH)]
        Rm = [wide() for _ in range(NCH)]
        Um = Ring([wide() for _ in range(1)])
        fi = [0]

        def frt():
            if CHAIN_BF16:
                return wide()
            t = T(fr.ap[:, fi[0], :].rearrange("p (g c) -> p g c", g=2), [Buf()])
            fi[0] += 1
            return t

        Nm = [[frt() for _ in range(NCH)] for _ in range(2)]
        NTm = [[frt() for _ in range(NCH)] for _ in range(2)]
        Pm = [frt() for _ in range(NCH)]
        Hm = Ring([frt() for _ in range(1)])
        class _HB:
            def __init__(self):
                self.k = 0
                self.pend = None

            def next(self):
                if self.pend is not None:
                    t = self.pend
                    self.pend = None
                    return t
                i = self.k % 8
                self.k += 1
                self.pend = halves[2 * i + 1]
                return halves[2 * i]

            def newbank(self):
                self.pend = None

        hbr = _HB()

        def w4(t):
            return T(t.ap.rearrange("p g (h t) -> p g h t", h=2), t.bufs)

        def v3(t):
            return T(t.ap.rearrange("p (g c) -> p g c", g=2), t.bufs)

        def v3b(t):
            return T(t.ap.bitcast(BF16)[:, 0:256].rearrange("p (g c) -> p g c", g=2), t.bufs)

        def bcast_mask(coff):
            m = cst[:, coff:coff + 128]
            return T(m.ap.unsqueeze(1).broadcast_to([128, 2, 128]), m.bufs)

        mU_b = bcast_mask(C_MU)
        mL_b = bcast_mask(C_ML)
        id_b = bcast_mask(C_ID)
        bd4 = T(bdmask.ap.rearrange("p (h t) -> p h t", h=2).unsqueeze(1).broadcast_to([128, 2, 2, 64]), bdmask.bufs)
        mui = cst[:, C_MUI:C_MUI + 64]
        mui4 = T(mui.ap.unsqueeze(1).unsqueeze(1).broadcast_to([128, 2, 2, 64]), mui.bufs)
        rstm = cst[:, C_RST:C_RST + NB]

        for grp in range(4):
            for x in range(3):
                wslab(Wg[:, :, x * 256:(x + 1) * 256],
                      w_l.ap[:, P_DA + x * 1024 + grp * 256:P_DA + x * 1024 + (grp + 1) * 256])
            for seq in range(2):
                blocks = [(i * NB, NB) for i in range(8)] if seq == 0 else [(TP, 64)]
                nvalid_last = NB if seq == 0 else TSV
                if seq == 0:
                    S.memset(S32, 0.0, eng="dve")
                    S.memset(Sb, 0.0, eng="dve")
                else:
                    for g in range(2):
                        S.dma(swt[:, g, :], swkv[l, 2 * grp + g])
                    S.tt(w4(S32), swt.v(swt.ap.unsqueeze(2).broadcast_to([128, 2, 2, 64])), bd4, ALU.mult)
                    S.copy(Sb, S32)
                carry = {}
                for bi, (c0, n) in enumerate(blocks):
                    nch = n // 64
                    for g in range(2):
                        hp = 2 * grp + g
                        us = {}
                        for x in range(3):
                            pb = psr.next()
                            off = x * 256 + g * 128
                            for kt in range(8):
                                S.mm(pb[:, 0:n], Wg[:, kt, off:off + 128], hT[:, kt, c0:c0 + n],
                                     start=(kt == 0), stop=(kt == 7))
                            mc = x * 8 + hp
                            U = u257.next()
                            if bi == 0:
                                if seq == 0:
                                    S.memset(U[:, 0:1], 0.0, eng="dve")
                                else:
                                    S.copy(U[:, 0:1], sfm_t[:, mc:mc + 1], eng="dve")
                            else:
                                S.copy(U[:, 0:1], carry[(g, x)], eng="dve")
                            S.copy(U[:, 1:n + 1], pb[:, 0:n], eng="act")
                            d = t256.next()
                            S.tt(d[:, 0:n], U[:, 0:n], U[:, 1:n + 1], ALU.subtract)
                            ut = usrk.next() if x < 2 else t256.next()
                            us[x] = ut[:, 0:n]
                            S.stt(ut[:, 0:n], d[:, 0:n], mu(mc), U[:, 1:n + 1], ALU.mult, ALU.add)
                            S.copy(carr[:, g * 3 + x:g * 3 + x + 1], U[:, n:n + 1], eng="pool")
                            carry[(g, x)] = carr[:, g * 3 + x:g * 3 + x + 1]
                            if bi == len(blocks) - 1:
                                sht = shp_t if seq == 0 else shs_t
                                S.copy(sht[:, mc:mc + 1], U[:, nvalid_last:nvalid_last + 1], eng="pool")
                        us_r, us_k, us_v = us[0], us[1], us[2]
                        S.copy(VT[:, g, 0:n], us_v, eng="pool")
                        pcol = lambda o: pf[:, l, o + hp:o + hp + 1]
                        pb = psr.next()
                        S.mm(pb[:, 0:n], W2p[:, hp * 128:(hp + 1) * 128], tanh_w[:, c0:c0 + n])
                        sg = t256.next()
                        S.actf(sg[:, 0:n], pb[:, 0:n], AF.Sigmoid, bias=pcol(PF_W0))
                        lw = t256.next()
                        S.ts(lw[:, 0:n], sg[:, 0:n], DEC_C, ALU.mult, eng="pool")
                        pb = psr.next()
                        S.mm(pb[:, 0:n], A2p[:, hp * 128:(hp + 1) * 128], raw_w[:, c0:c0 + n])
                        a_t = t256.next()
                        S.actf(a_t[:, 0:n], pb[:, 0:n], AF.Sigmoid, bias=pcol(PF_A0))
                        kk = t256.next()
                        S.ts(kk[:, 0:n], us_k, pcol(PF_KK), ALU.mult)
                        sq = b256.next()
                        S.actf(sq[:, 0:n], kk[:, 0:n], AF.Square)
                        pb = psr.next()
                        S.mm(pb[:, 0:n], bd_bf, sq[:, 0:n])
                        rn = t256.next()
                        S.actf(rn[:, 0:n], pb[:, 0:n], AF.Sqrt)
                        S.ts(rn[:, 0:n], rn[:, 0:n], 1e-12, ALU.max)
                        S.recip(rn[:, 0:n], rn[:, 0:n])
                        kkn = kk
                        S.tt(kkn[:, 0:n], kk[:, 0:n], rn[:, 0:n], ALU.mult)
                        t1 = t256.next()
                        S.ts(t1[:, 0:n], a_t[:, 0:n], -1.0, ALU.add, pcol(PF_KA), ALU.mult)
                        kmod = t256.next()
                        S.stt(kmod[:, 0:n], t1[:, 0:n], 1.0, us_k, ALU.add, ALU.mult)
                        bb = t1
                        S.tt(bb[:, 0:n], kkn[:, 0:n], a_t[:, 0:n], ALU.mult, eng="pool")
                        cum = t256.next()
                        S.scan(cum[:, 0:n], rstm[:, 0:n], lw[:, 0:n], 0.0, ALU.mult, ALU.add)
                        cumex = sg
                        S.tt(cumex[:, 0:n], cum[:, 0:n], lw[:, 0:n], ALU.subtract, eng="pool")
                        S.actf(WT[:, g, 0:n], cum[:, 0:n], AF.Exp)
                        winv = lw
                        S.actf(winv[:, 0:n], cum[:, 0:n], AF.Exp, scale=-1.0)
                        wex = cum
                        S.actf(wex[:, 0:n], cumex[:, 0:n], AF.Exp)
                        S.stt(AT[:, g, 0:n], kkn[:, 0:n], -1.0, wex[:, 0:n], ALU.mult, ALU.mult)
                        S.tt(BT[:, g, 0:n], bb[:, 0:n], winv[:, 0:n], ALU.mult)
                        S.tt(KT_[:, g, 0:n], kmod[:, 0:n], winv[:, 0:n], ALU.mult)
                        S.tt(RT[:, g, 0:n], us_r, WT[:, g, 0:n], ALU.mult)
                        rk = b256.next()
                        S.stt(rk[:, 0:n], us_r, pcol(PF_RK), kmod[:, 0:n], ALU.mult, ALU.mult)
                        pb = psr.next()
                        S.mm(pb[:, 0:n], bd_bf, rk[:, 0:n])
                        S.tt(BON[:, g, 0:n], pb[:, 0:n], us_v, ALU.mult)
                        if seq == 1:
                            for arr in (AT, BT, KT_, RT, VT):
                                S.memset(arr[:, g, TSV:64], 0.0)
                    chk("rwkv_A", l)
                    def chunk_src(arr, ci):
                        a = arr[:, :, ci * 64:(ci + 1) * 64]
                        return T(a.ap.unsqueeze(2).broadcast_to([128, 2, 2, 64]), a.bufs)

                    wcols = []
                    for ci in range(nch):
                        lastc = ci * 64 + (63 if seq == 0 else TSV - 1)
                        wc = WT[:, :, lastc:lastc + 1]
                        wcols.append(wc)
                        wcb = T(wc.ap.broadcast_to([128, 2, 128]), wc.bufs)
                        S.tt(w4(BDa[ci]), chunk_src(AT, ci), bd4, ALU.mult)
                        S.tt(w4(BDb[ci]), chunk_src(BT, ci), bd4, ALU.mult, eng="pool")
                        S.tt(w4(BDk[ci]), chunk_src(KT_, ci), bd4, ALU.mult)
                        bdv = BDx.next()
                        S.tt(w4(bdv), chunk_src(VT, ci), bd4, ALU.mult, eng="pool")
                        bdbh = BDx.next()
                        S.tt(bdbh, BDb[ci], wcb, ALU.mult)
                        bdkh = BDx.next()
                        S.tt(bdkh, BDk[ci], wcb, ALU.mult, eng="pool")
                        chk("B1a", l)
                        hbr.newbank()
                        pN, pNT, pAK, pR, pV, pBh, pKh = [hbr.next() for _ in range(7)]
                        hbr.newbank()
                        for g in range(2):
                            S.mm(v3(pN)[:, g, :], BDb[ci][:, g, :], BDa[ci][:, g, :])
                        for g in range(2):
                            S.mm(v3(pNT)[:, g, :], BDa[ci][:, g, :], BDb[ci][:, g, :])
                        for g in range(2):
                            S.mm(v3(pAK)[:, g, :], BDk[ci][:, g, :], BDa[ci][:, g, :])
                        chk("B1b", l)
                        pR4 = T(pR.ap.rearrange("p (g x t) -> p g x t", g=2, x=2), pR.bufs)
                        for g in range(2):
                            S.mm(pR4[:, g, 0, :], BDb[ci][:, g, :], RT[:, g, ci * 64:(ci + 1) * 64])
                            S.mm(pR4[:, g, 1, :], BDk[ci][:, g, :], RT[:, g, ci * 64:(ci + 1) * 64])
                        chk("B1c", l)
                        for g in range(2):
                            S.tr(v3b(pV)[:, g, :], bdv[:, g, :], ident_bf)
                            S.tr(v3b(pBh)[:, g, :], bdbh[:, g, :], ident_bf)
                            S.tr(v3b(pKh)[:, g, :], bdkh[:, g, :], ident_bf)
                        chk("B1d", l)
                        S.tt(r32(Nm[0][ci]), v3(pN), mU_b, ALU.mult)
                        S.tt(r32(NTm[0][ci]), v3(pNT), mL_b, ALU.mult)
                        S.tt(AKm[ci], v3(pAK), mU_b, ALU.mult)
                        S.tt(w4(Rm[ci]), pR4, mui4, ALU.mult)
                        chk("B1e", l)
                        S.copy(Vm[ci], v3b(pV), eng="act")
                        S.copy(Bhm[ci], v3b(pBh), eng="act")
                        S.copy(Khm[ci], v3b(pKh), eng="act")
                        chk("B1f", l)
                        S.tt(r32(Pm[ci]), Nm[0][ci], id_b, ALU.add)
                        chk("B1g", l)
                        if ci == 1:
                            chk("B1h", l)
                    chk("rwkv_B1", l)
                    cur = 0
                    for rd in range(1, 6):
                        nxt = 1 - cur
                        pairs = []
                        for ci in range(nch):
                            hbr.newbank()
                            pN2 = hbr.next() if rd < 5 else None
                            pNT2 = hbr.next()
                            for g in range(2):
                                if rd < 5:
                                    S.mm(v3(pN2)[:, g, :], r32(NTm[cur][ci][:, g, :]), r32(Nm[cur][ci][:, g, :]))
                                S.mm(v3(pNT2)[:, g, :], r32(Nm[cur][ci][:, g, :]), r32(NTm[cur][ci][:, g, :]))
                            pairs.append((pN2, pNT2))
                        for ci in range(nch):
                            pN2, pNT2 = pairs[ci]
                            S.copy(r32(NTm[nxt][ci]), v3(pNT2), eng="act")
                            if rd < 5:
                                S.copy(r32(Nm[nxt][ci]), v3(pN2), eng="act")
                        pps = []
                        for ci in range(nch):
                            hbr.newbank()
                            pP = hbr.next()
                            for g in range(2):
                                S.mm(v3(pP)[:, g, :], r32(NTm[nxt][ci][:, g, :]), r32(Pm[ci][:, g, :]))
                            pps.append(pP)
                        for ci in range(nch):
                            S.tt(r32(Pm[ci]), v3(pps[ci]), Pm[ci], ALU.add)
                        cur = nxt
                    chk("rwkv_inv", l)
                    for ci in range(nch):
                        hbr.newbank()
                        pH = hbr.next()
                        for g in range(2):
                            S.mm(v3(pH)[:, g, :], BDa[ci][:, g, :], Sb[:, g, :], start=True, stop=False)
                            S.mm(v3(pH)[:, g, :], AKm[ci][:, g, :], Vm[ci][:, g, :], start=False, stop=True)
                        H = Hm.next()
                        S.copy(r32(H), v3(pH), eng="act")
                        hbr.newbank()
                        pU = hbr.next()
                        for g in range(2):
                            S.mm(v3(pU)[:, g, :], r32(Pm[ci][:, g, :]), r32(H[:, g, :]))
                        U = Um.next()
                        S.copy(U, v3(pU), eng="dve")
                        hbr.newbank()
                        pY = hbr.next()
                        pY3 = T(pY.ap[:, 0:128].rearrange("p (g t) -> p g t", g=2), pY.bufs)
                        R4 = w4(Rm[ci])
                        for g in range(2):
                            S.mm(pY3[:, g, :], Sb[:, g, :], RT[:, g, ci * 64:(ci + 1) * 64], start=True, stop=False)
                            S.mm(pY3[:, g, :], U[:, g, :], R4[:, g, 0, :], start=False, stop=False)
                            S.mm(pY3[:, g, :], Vm[ci][:, g, :], R4[:, g, 1, :], start=False, stop=True)
                        S.copy(YT[:, :, ci * 64:(ci + 1) * 64], pY3, eng="act")
                        hbr.newbank()
                        pS = hbr.next()
                        for g in range(2):
                            S.mm(v3(pS)[:, g, :], Bhm[ci][:, g, :], U[:, g, :], start=True, stop=False)
                            S.mm(v3(pS)[:, g, :], Khm[ci][:, g, :], Vm[ci][:, g, :], start=False, stop=True)
                        for g in range(2):
                            S.stt(S32[:, g, :], S32[:, g, :], wcols[ci][:, g, :], v3(pS)[:, g, :], ALU.mult, ALU.add)
                        S.copy(Sb, S32, eng="pool")
                    chk("rwkv_chain", l)
                    for g in range(2):
                        hp = 2 * grp + g
                        pcol = lambda o: pf[:, l, o + hp:o + hp + 1]
                        yb = b256.next()
                        S.copy(yb[:, 0:n], YT[:, g, 0:n], eng="pool")
                        pb = psr.next()
                        S.mm(pb[:, 0:n], bd_bf, yb[:, 0:n])
                        yc = t256.next()
                        S.stt(yc[:, 0:n], pb[:, 0:n], -1.0 / 64, YT[:, g, 0:n], ALU.mult, ALU.add)
                        sq = b256.next()
                        S.actf(sq[:, 0:n], yc[:, 0:n], AF.Square)
                        pb = psr.next()
                        S.mm(pb[:, 0:n], bd_bf, sq[:, 0:n])
                        sd = t256.next()
                        S.actf(sd[:, 0:n], pb[:, 0:n], AF.Sqrt, scale=1.0 / 64, bias=epsb[:, 1:2])
                        S.recip(sd[:, 0:n], sd[:, 0:n])
                        S.tt(yc[:, 0:n], yc[:, 0:n], sd[:, 0:n], ALU.mult)
                        S.ts(yc[:, 0:n], yc[:, 0:n], pcol(PF_LNW), ALU.mult, pcol(PF_LNB), ALU.add)
                        S.tt(yc[:, 0:n], yc[:, 0:n], BON[:, g, 0:n], ALU.add, eng="pool")
                        pb = psr.next()
                        S.mm(pb[:, 0:n], G2[:, hp * 128:(hp + 1) * 128], sig_g[:, c0:c0 + n])
                        S.tt(OZ[:, hp, c0:c0 + n], yc[:, 0:n], pb[:, 0:n], ALU.mult)
                    chk("rwkv_C", l)
                chk("rwkv_seq", l)
                dst = wkvp if seq == 0 else wkvs
                for g in range(2):
                    for h in range(2):
                        S.dma(dst[l, 2 * grp + g, h * 64:(h + 1) * 64, :], S32[h * 64:(h + 1) * 64, g, h * 64:(h + 1) * 64])
        S.dma(shp[l], shp_t)
        S.dma(shs[l], shs_t)

    def phase_gproj(l, w_o, gate_col0, accumulate):
        A.seek(LOC)
        slabs = Ring([A.alloc([8, 256], BF16) for _ in range(2)])
        sgr = Ring([A.alloc([512], F32) for _ in range(3)])
        tr_ = Ring([A.alloc([512], F32) for _ in range(2)])
        for nt in range(8):
            sl = slabs.next()
            wslab(sl[:, :, 0:128], w_o[l].ap[:, nt * 128:(nt + 1) * 128])
            wslab(sl[:, :, 128:256], w_in[l].ap[:, gate_col0 + nt * 128:gate_col0 + (nt + 1) * 128])
            for (c0, n) in TQ:
                pa = psr.next()
                for kt in range(8):
                    S.mm(pa[:, 0:n], sl[:, kt, 0:128], OZ[:, kt, c0:c0 + n], start=(kt == 0), stop=(kt == 7))
                pg = psr.next()
                for kt in range(8):
                    S.mm(pg[:, 0:n], sl[:, kt, 128:256], hT[:, kt, c0:c0 + n], start=(kt == 0), stop=(kt == 7))
                sg = sgr.next()
                S.actf(sg[:, 0:n], pg[:, 0:n], AF.Sigmoid)
                if not accumulate:
                    S.tt(MT[:, nt, c0:c0 + n], pa[:, 0:n], sg[:, 0:n], ALU.mult)
                else:
                    t = tr_.next()
                    S.tt(t[:, 0:n], pa[:, 0:n], sg[:, 0:n], ALU.mult)
                    S.tt(MT[:, nt, c0:c0 + n], t[:, 0:n], MT[:, nt, c0:c0 + n], ALU.add, eng="pool")

    def phase_da(l):
        A.seek(LOC)
        Wd = A.alloc([8, 384], BF16)
        qkv_tok = A.alloc([NTB, 384], BF16)
        q_tok = qkv_tok[:, :, 0:128]
        k_tok = qkv_tok[:, :, 128:256]
        v_tok = qkv_tok[:, :, 256:384]
        kc_tok = A.alloc([16, 128], BF16)
        vc_tok = A.alloc([16, 128], BF16)
        KTh = A.alloc([NTOK + TP], BF16)
        Q1p = A.alloc([512], BF16)
        Q2p = A.alloc([512], BF16)
        Pr = Ring([A.alloc([512], BF16) for _ in range(4)])
        f512 = Ring([A.alloc([512], F32) for _ in range(4)])
        qkvr = Ring([A.alloc([384], F32) for _ in range(2)])
        rtmp = Ring([A.alloc([4, 16], F32) for _ in range(4)])
        Lacc = [A.alloc([512], F32) for _ in range(2)]
        S.memset(Q1p, 0.0)
        S.memset(Q2p, 0.0)
        nlam = lamv[:, l * 8:l * 8 + 1]
        subs = lamv[:, l * 8 + 1:l * 8 + 2]
        psOL = [(ps[0], ps[1]), (ps[2], ps[3])]
        pr6 = Ring(ps[4:8])

        def rope_inplace(x, tb):
            x4 = T(x.ap.rearrange("p (m d) -> p m d", m=4), x.bufs)
            cc = cst[:, C_COS + tb * 16:C_COS + tb * 16 + 16]
            ss = cst[:, C_SIN + tb * 16:C_SIN + tb * 16 + 16]
            ccb = T(cc.ap.unsqueeze(1).broadcast_to([128, 4, 16]), cc.bufs)
            ssb = T(ss.ap.unsqueeze(1).broadcast_to([128, 4, 16]), ss.bufs)
            tc_ = rtmp.next()
            ts_ = rtmp.next()
            S.tt(tc_, x4[:, :, 0:16], ccb, ALU.mult)
            S.tt(ts_, x4[:, :, 0:16], ssb, ALU.mult)
            S.tt(x4[:, :, 0:8], tc_[:, :, 0:8], ts_[:, :, 8:16], ALU.subtract)
            S.tt(x4[:, :, 8:16], tc_[:, :, 8:16], ts_[:, :, 0:8], ALU.add)

        def attend(c0, nq, tbs, keyblocks):
            pb = pr6.next()
            pst = T(pb.ap.bitcast(BF16), pb.bufs)
            for i, tb in enumerate(tbs):
                S.tr(pst[:, i * 128:(i + 1) * 128], q_tok[:, tb, :], ident_bf)
            S.copy(Q1p[0:64, 0:nq], pst[0:64, 0:nq], eng="act")
            S.copy(Q2p[64:128, 0:nq], pst[64:128, 0:nq], eng="act")
            o1 = f512.next()
            nk = len(keyblocks)
            steps = [(mp, j) for mp in range(2) for j in range(nk)]
            Ps = {}
            LOOK = 2
            tlast = [None]

            def issue_s(i):
                mp, j = steps[i]
                kap, vap, c_lo, zspec = keyblocks[j]
                Qp = Q1p if mp == 0 else Q2p
                pS = pr6.next()
                S.mm(pS[:, c_lo:nq], kap, Qp[:, c_lo:nq])
                P = Pr.next()
                S.actf(P[:, c_lo:nq], pS[:, c_lo:nq], AF.Exp, scale=0.125)
                if zspec == "diag":
                    S.memset(P[64:128, c_lo:c_lo + 64], 0.0)
                elif zspec == "rows":
                    S.memset(P[32:64, 0:nq], 0.0)
                    S.memset(P[64:128, 0:nq], 0.0)
                Ps[i] = P

            def issue_ol(i):
                mp, j = steps[i]
                kap, vap, c_lo, zspec = keyblocks[j]
                P = Ps.pop(i)
                pO, pL = psOL[mp]
                S.mm(pO[:, c_lo:nq], vap, P[:, c_lo:nq], start=(j == 0), stop=(j == nk - 1))
                eng = "pool" if j % 2 == 0 else "dve"
                La = Lacc[j % 2]
                if j < 2:
                    if c_lo > 0:
                        S.memset(La[:, 0:c_lo], 0.0, eng=eng)
                    S.copy(La[:, c_lo:nq], P[:, c_lo:nq], eng=eng)
                else:
                    S.tt(La[:, c_lo:nq], La[:, c_lo:nq], P[:, c_lo:nq], ALU.add, eng=eng)
                if j == nk - 1:
                    S.mm(pL[:, 0:nq], ones_f, Lacc[0][:, 0:nq], start=True, stop=False)
                    S.mm(pL[:, 0:nq], ones_f, Lacc[1][:, 0:nq], start=False, stop=True)
                    rd = f512.next()
                    S.recip(rd[:, 0:nq], pL[:, 0:nq])
                    if mp == 0:
                        S.tt(o1[:, 0:nq], pO[:, 0:nq], rd[:, 0:nq], ALU.mult)
                    else:
                        t = f512.next()
                        S.tt(t[:, 0:nq], pO[:, 0:nq], rd[:, 0:nq], ALU.mult)
                        S.stt(o1[:, 0:nq], t[:, 0:nq], nlam, o1[:, 0:nq], ALU.mult, ALU.add)
                        tlast[0] = t

            for i in range(len(steps) + LOOK):
                if i < len(steps):
                    issue_s(i)
                if i - LOOK >= 0:
                    issue_ol(i - LOOK)
            t = tlast[0]
            sq = Pr.next()
            S.actf(sq[:, 0:nq], o1[:, 0:nq], AF.Square)
            pb = pr6.next()
            S.mm(pb[:, 0:nq], ones_bf, sq[:, 0:nq])
            sd = t
            S.actf(sd[:, 0:nq], pb[:, 0:nq], AF.Sqrt, scale=1.0 / 128, bias=epsb[:, 0:1])
            S.recip(sd[:, 0:nq], sd[:, 0:nq])
            S.tt(o1[:, 0:nq], o1[:, 0:nq], sd[:, 0:nq], ALU.mult, eng="pool")
            return o1

        for hd in range(8):
            for x in range(3):
                wslab(Wd[:, :, x * 128:(x + 1) * 128], w_in[l].ap[:, x * 1024 + hd * 128:x * 1024 + (hd + 1) * 128])
            S.dma(kc_tok, T(ck[l].ap[:, hd * 128:(hd + 1) * 128].rearrange("(j p) c -> p j c", p=128), ()), eng="pool")
            S.dma(vc_tok, T(cv[l].ap[:, hd * 128:(hd + 1) * 128].rearrange("(j p) c -> p j c", p=128), ()), eng="pool")
            chk("da_load", l)
            for tb in range(NTB):
                if tb == 1:
                    chk("da_tb0", l)
                if tb == 16:
                    chk("da_tb15", l)
                pb = pr6.next()
                for kt in range(8):
                    S.mm(pb[:, 0:384], hT[:, kt, tb * 128:(tb + 1) * 128], Wd[:, kt, :],
                         start=(kt == 0), stop=(kt == 7))
                qkv = qkvr.next()
                S.copy(qkv, pb[:, 0:384], eng="act")
                rope_inplace(qkv[:, 0:256], tb)
                S.copy(qkv_tok[:, tb, :], qkv, eng="pool")
                if tb < 16:
                    S.dma(kp[l, tb * 128:(tb + 1) * 128, hd * 128:(hd + 1) * 128], qkv[:, 128:256])
                    S.dma(vp[l, tb * 128:(tb + 1) * 128, hd * 128:(hd + 1) * 128], qkv[:, 256:384])
                else:
                    S.dma(ks[l, :, hd * 128:(hd + 1) * 128], qkv[0:TSV, 128:256])
                    S.dma(vs[l, :, hd * 128:(hd + 1) * 128], qkv[0:TSV, 256:384])
            chk("da_proj", l)
            srcs = [(k_tok, tb) for tb in range(NTB)] + [(kc_tok, j) for j in range(16)]
            base = 0
            ei = 0
            while base < len(srcs):
                grp_ = srcs[base:base + 8]
                pb = pr6.next()
                pst = T(pb.ap.bitcast(BF16), pb.bufs)
                for i, (src, idx) in enumerate(grp_):
                    S.tr(pst[:, i * 128:(i + 1) * 128], src[:, idx, :], ident_bf)
                S.copy(KTh[:, base * 128:(base + len(grp_)) * 128], pst[:, 0:len(grp_) * 128],
                       eng=("act" if ei % 2 == 0 else "dve"))
                base += len(grp_)
                ei += 1
            chk("da_kt", l)
            for i in range(4):
                if i == 1:
                    chk("da_att0", l)
                kb = []
                for j in range(4 * i + 4):
                    m = j - 4 * i
                    c_lo = 128 * m if m > 0 else 0
                    kb.append((KTh[:, j * 128:(j + 1) * 128], v_tok[:, j, :], c_lo, "diag" if m >= 0 else None))
                o = attend(i * 512, 512, [4 * i + m for m in range(4)], kb)
                S.ts(OZ[:, hd, i * 512:(i + 1) * 512], o[:, 0:512], subs, ALU.mult)
            chk("da_attp", l)
            kb = [(KTh[:, NTOK + j * 128:NTOK + (j + 1) * 128], vc_tok[:, j, :], 0, None) for j in range(16)]
            kb.append((KTh[:, TP:TP + 128], v_tok[:, 16, :], 0, "rows"))
            o = attend(TP, TSV, [16], kb)
            S.ts(OZ[:, hd, TP:TP + TSV], o[:, 0:TSV], subs, ALU.mult)
            chk("da_head0", l)

    def phase_out(l):
        A.seek(LOC)
        nsc = make_norm_scratch()
        wo = A.alloc([8, D], BF16)
        wslab(wo[:, :, 0:512], w_out[l].ap[:, 0:512])
        wslab(wo[:, :, 512:1024], w_out[l].ap[:, 512:1024])
        xr = Ring([A.alloc([D], F32) for _ in range(2)])
        x1r = Ring([A.alloc([D], F32) for _ in range(2)])
        for tb in range(NTB):
            xt = xr.next()
            if l == 0:
                src, nr = x_rows(tb)
                if nr < 128:
                    S.memset(xt, 0.0)
                S.dma(xt[0:nr, :], src)
            else:
                S.dma(xt, X2[tb * 128:(tb + 1) * 128, :])
            x1 = x1r.next()
            for half in range(2):
                pb = psr.next()
                for kt in range(8):
                    S.mm(pb, MT[:, kt, tb * 128:(tb + 1) * 128], wo[:, kt, half * 512:(half + 1) * 512],
                         start=(kt == 0), stop=(kt == 7))
                S.tt(x1[:, half * 512:(half + 1) * 512], pb, xt[:, half * 512:(half + 1) * 512], ALU.add)
            S.dma(X1[tb * 128:(tb + 1) * 128, :], x1)
            norm_to_hT(x1, l, PF_NFFN, tb, nsc)

    def phase_ffn(l):
        A.seek(0)
        A.alloc([8, NTOK], BF16)
        wdn = A.alloc([22, D], BF16)
        GT = A.alloc([22, 512], BF16)
        assert A.off <= LOC
        A.seek(LOC)
        nsc = make_norm_scratch()
        for c in range(2):
            for hh in range(2):
                S.dma(wdn[:, 11 * hh:11 * (hh + 1), c * 512:(c + 1) * 512],
                      T(w_down[l].ap[11 * hh * 128:11 * (hh + 1) * 128, c * 512:(c + 1) * 512].rearrange("(kt p) c -> p kt c", p=128), ()), eng="pool")
        slabs = Ring([A.alloc([8, 512], BF16) for _ in range(2)])
        hpr = Ring([A.alloc([514], F32) for _ in range(4)])
        cr = Ring([A.alloc([512], F32) for _ in range(4)])
        xr = Ring([A.alloc([D], F32) for _ in range(2)])
        x2r = Ring([A.alloc([D], F32) for _ in range(2)])
        yr = Ring([A.alloc([D], F32) for _ in range(2)])
        S.memset(carry_p, 0.0)
        S.dma(T(carry_s.ap.rearrange("p a b -> p (a b)"), carry_s.bufs), cfm[l])
        for qi, (c0, n) in enumerate(TQ):
            carry = carry_p if qi < 4 else carry_s
            nval = n if qi < 4 else TSV
            for f2 in range(11):
                sl = slabs.next()
                wslab(sl[:, :, 0:256], w_up[l].ap[:, f2 * 256:(f2 + 1) * 256])
                wslab(sl[:, :, 256:512], w_up[l].ap[:, DFF + f2 * 256:DFF + (f2 + 1) * 256])
                for fi in range(2):
                    ft = 2 * f2 + fi
                    cs = []
                    for which in range(2):
                        fidx = which * 22 + ft
                        pb = psr.next()
                        off = which * 256 + fi * 128
                        for kt in range(8):
                            S.mm(pb[:, 0:n], sl[:, kt, off:off + 128], hT[:, kt, c0:c0 + n], start=(kt == 0), stop=(kt == 7))
                        hp_ = hpr.next()
                        S.copy(hp_[:, 0:2], carry[:, fidx, :], eng="pool")
                        S.copy(hp_[:, 2:n + 2], pb[:, 0:n], eng="act")
                        S.copy(carry[:, fidx, :], hp_[:, nval:nval + 2], eng="pool")
                        cw = lambda j: pf[:, l, PF_CONV + j * 44 + fidx:PF_CONV + j * 44 + fidx + 1]
                        cb = pf[:, l, PF_CONVB + fidx:PF_CONVB + fidx + 1]
                        c_ = cr.next()
                        eng = "dve" if which == 0 else "pool"
                        S.ts(c_[:, 0:n], hp_[:, 0:n], cw(0), ALU.mult, cb, ALU.add, eng=eng)
                        S.stt(c_[:, 0:n], hp_[:, 1:n + 1], cw(1), c_[:, 0:n], ALU.mult, ALU.add)
                        S.stt(c_[:, 0:n], hp_[:, 2:n + 2], cw(2), c_[:, 0:n], ALU.mult, ALU.add)
                        cs.append(c_)
                    S.actf(cs[0][:, 0:n], cs[0][:, 0:n], AF.Silu)
                    S.tt(GT[:, ft, 0:n], cs[0][:, 0:n], cs[1][:, 0:n], ALU.mult, eng="pool")
            for tbl in range(n // 128):
                tb = c0 // 128 + tbl
                xt = xr.next()
                S.dma(xt, X1[tb * 128:(tb + 1) * 128, :])
                x2 = x2r.next()
                for half in range(2):
                    pb = psr.next()
                    for ft in range(22):
                        S.mm(pb, GT[:, ft, tbl * 128:(tbl + 1) * 128], wdn[:, ft, half * 512:(half + 1) * 512],
                             start=(ft == 0), stop=(ft == 21))
                    S.tt(x2[:, half * 512:(half + 1) * 512], pb, xt[:, half * 512:(half + 1) * 512], ALU.add)
                if l == 1:
                    rstd = norm_stats(x2, nsc)
                    y = yr.next()
                    S.stt(y, x2, rstd, gfin, ALU.mult, ALU.mult)
                    if tb < 16:
                        S.dma(yp[tb * 128:(tb + 1) * 128, :], y)
                    else:
                        S.dma(ys, y[0:TSV, :])
                else:
                    S.dma(X2[tb * 128:(tb + 1) * 128, :], x2)
                    norm_to_hT(x2, 1, PF_NMIX, tb, nsc)
        S.dma(cvp[l], T(carry_p.ap.rearrange("p a b -> p (a b)"), carry_p.bufs))
        S.dma(cvs[l], T(carry_s.ap.rearrange("p a b -> p (a b)"), carry_s.bufs))

    def dump(name, src):
        if name in dbg_out:
            S.barrier()
            S.dma(dbg_out[name], src)
            S.barrier()

    try:
        _program(S, locals())
    except StopBuild:
        pass
    S.emit()
    S.close()
    return nc


def _program(S, L):
    phase_norm0, phase_rwkv, phase_gproj, phase_da, phase_out, phase_ffn = (
        L["phase_norm0"], L["phase_rwkv"], L["phase_gproj"], L["phase_da"], L["phase_out"], L["phase_ffn"])
    dump, stop_after, hT, OZ, MT = L["dump"], L["stop_after"], L["hT"], L["OZ"], L["MT"]
    w_o_rw, w_o_da = L["w_o_rw"], L["w_o_da"]
    S.barrier()
    phase_norm0()
    S.barrier()
    dump("hT0", hT)
    done = False
    for l in range(2):
        if stop_after == ("norm", l):
            break
        phase_rwkv(l)
        S.barrier()
        dump(f"Z{l}", OZ)
        if stop_after == ("rwkv", l):
            break
        phase_gproj(l, w_o_rw, P_DA + P_RW + D, False)
        S.barrier()
        if stop_after == ("gproj1", l):
            break
        phase_da(l)
        S.barrier()
        dump(f"O{l}", OZ)
        if stop_after == ("da", l):
            break
        phase_gproj(l, w_o_da, P_DA + P_RW, True)
        S.barrier()
        dump(f"M{l}", MT)
        phase_out(l)
        S.barrier()
        dump(f"h2T{l}", hT)
        if stop_after == ("out", l):
            break
        phase_ffn(l)
        S.barrier()


_NC_CACHE = {}


def _prep_inputs(inp):
    f = lambda a: np.ascontiguousarray(np.asarray(a, dtype=np.float32))
    g = {k: f(v) for k, v in inp.items()}
    consts = make_consts()

    def fm(v, nt):
        return v.reshape(nt, 128).T

    pfm = np.zeros((2, 128, NPF), np.float32)
    for l in range(2):
        pfm[l, :, PF_NMIX:PF_NMIX + 8] = fm(g["norm_mix"][l], 8)
        pfm[l, :, PF_NFFN:PF_NFFN + 8] = fm(g["norm_ffn"][l], 8)
        pfm[l, :, PF_MU:PF_MU + 26] = fm(g["rw_mu"][l], 26)
        pfm[l, :, PF_W0:PF_W0 + 8] = fm(g["rw_w0"][l], 8)
        pfm[l, :, PF_A0:PF_A0 + 8] = fm(g["rw_a0"][l], 8)
        pfm[l, :, PF_KK:PF_KK + 8] = fm(g["rw_k_k"][l], 8)
        pfm[l, :, PF_KA:PF_KA + 8] = fm(g["rw_k_a"][l], 8)
        pfm[l, :, PF_RK:PF_RK + 8] = fm(g["rw_r_k"][l].reshape(-1), 8)
        pfm[l, :, PF_LNW:PF_LNW + 8] = fm(g["rw_ln_w"][l], 8)
        pfm[l, :, PF_LNB:PF_LNB + 8] = fm(g["rw_ln_b"][l], 8)
        for j in range(3):
            pfm[l, :, PF_CONV + j * 44:PF_CONV + (j + 1) * 44] = fm(g["ffn_conv"][l, j], 44)
        pfm[l, :, PF_CONVB:PF_CONVB + 44] = fm(g["ffn_conv_b"][l], 44)
        pfm[l, :, PF_SUBLN] = g["da_subln"][l]
    shared = {
        "pfm": pfm, "consts": consts, "w_in": g["w_in"], "da_lambda": g["da_lambda"].reshape(512),
        "w_o_da": g["w_o_da"], "rw_w2": g["rw_w2"], "rw_a2": g["rw_a2"], "rw_g2": g["rw_g2"],
        "w_o_rw": g["w_o_rw"], "w_out": g["w_out"], "w_up": g["w_up"], "w_down": g["w_down"],
        "nfin": g["norm_final"],
    }
    maps = []
    for b in range(8):
        m = dict(shared)
        m["xp"] = g["x_prompt"][b]
        m["xs"] = g["x_sample"][b]
        m["ck"] = np.ascontiguousarray(g["cache_k"][:, b].reshape(2, TP, D))
        m["cv"] = np.ascontiguousarray(g["cache_v"][:, b].reshape(2, TP, D))
        sw = g["state_wkv"][:, b].reshape(2, 8, 2, 64, 64).transpose(0, 1, 2, 4, 3).reshape(2, 8, 128, 64)
        m["swkv"] = np.ascontiguousarray(sw)
        m["sfm"] = np.ascontiguousarray(g["state_shift"][:, b, 0].reshape(2, 26, 128).transpose(0, 2, 1))
        cf = g["state_ffn_conv"][:, b].reshape(2, 2, 44, 128).transpose(0, 3, 2, 1).reshape(2, 128, 88)
        m["cfm"] = np.ascontiguousarray(cf)
        maps.append(m)
    return maps


def _assemble(results):
    def st(name):
        return np.stack([np.asarray(r[name]) for r in results], axis=0)

    y_prompt = st("yp")
    y_sample = st("ys")
    k_prompt = st("kp").transpose(1, 0, 2, 3).reshape(2, 8, TP, 8, 128)
    v_prompt = st("vp").transpose(1, 0, 2, 3).reshape(2, 8, TP, 8, 128)
    k_sample = st("ks").transpose(1, 0, 2, 3).reshape(2, 8, TSV, 8, 128)
    v_sample = st("vs").transpose(1, 0, 2, 3).reshape(2, 8, TSV, 8, 128)

    def wkv(name):
        a = st(name).reshape(8, 2, 8, 2, 64, 64).transpose(1, 0, 2, 3, 5, 4).reshape(2, 8, 16, 64, 64)
        return np.ascontiguousarray(a)

    def shift(name):
        a = st(name).transpose(1, 0, 3, 2).reshape(2, 8, 1, P_RW)
        return np.ascontiguousarray(a)

    def conv(name):
        a = st(name).reshape(8, 2, 128, 44, 2).transpose(1, 0, 4, 3, 2).reshape(2, 8, 2, 2 * DFF)
        return np.ascontiguousarray(a)

    return (np.ascontiguousarray(y_prompt), np.ascontiguousarray(y_sample),
            np.ascontiguousarray(k_prompt), np.ascontiguousarray(v_prompt),
            wkv("wkvp"), shift("shp"), conv("cvp"),
            np.ascontiguousarray(k_sample), np.ascontiguousarray(v_sample),
            wkv("wkvs"), shift("shs"), conv("cvs"))


def kernel(**inputs):
    maps = _prep_inputs(inputs)
    nc = build()
    res = run_bass_kernel_spmd(nc, maps, core_ids=list(range(8)))
    return _assemble(res.results)
```

```python
import numpy as np
from contextlib import ExitStack

import concourse.bass as bass
import concourse.mybir as mybir

F32 = mybir.dt.float32
BF16 = mybir.dt.bfloat16
F32R = mybir.dt.float32r
AF = mybir.ActivationFunctionType
ALU = mybir.AluOpType
AX = mybir.AxisListType

ENGS = ("pe", "act", "dve", "pool", "sp")
SEM_EPOCH = 20000
DMA_ROT = 8


class Buf:
    __slots__ = ("w", "r", "name")

    def __init__(self, name=""):
        self.w = None
        self.r = []
        self.name = name


class T:
    __slots__ = ("ap", "bufs")

    def __init__(self, ap, bufs):
        self.ap = ap
        self.bufs = tuple(bufs)

    def __getitem__(self, key):
        return T(self.ap[key], self.bufs)

    def v(self, ap):
        return T(ap, self.bufs)

    def bitcast(self, dt):
        return T(self.ap.bitcast(dt), self.bufs)


class Rec:
    __slots__ = ("eng", "idx", "fn", "deps", "dma", "signaled", "ev", "selfwait")

    def __init__(self, eng, idx, fn, dma):
        self.eng = eng
        self.idx = idx
        self.fn = fn
        self.deps = []
        self.dma = dma
        self.signaled = False
        self.ev = None
        self.selfwait = None


class Sched:
    def __init__(self, nc):
        self.nc = nc
        self.q = {e: [] for e in ENGS}
        self.stack = ExitStack()
        self.nbuf = 0
        self.same_engine_sync = True
        self._bar_pos = {}
        self._pending = {e: [] for e in ENGS}

    def sbuf(self, name, shape, dtype, nbufs=1):
        h = self.stack.enter_context(self.nc.sbuf_tensor(name, list(shape), dtype))
        return T(h[:], [Buf(name)])

    def psum(self, name, shape, dtype=F32):
        h = self.stack.enter_context(self.nc.psum_tensor(name, list(shape), dtype))
        return T(h[:], [Buf(name)])

    def dram(self, name, shape, dtype, kind):
        h = self.nc.dram_tensor(name, list(shape), dtype, kind=kind)
        return T(h.ap(), [Buf(name)])

    def newbuf(self, name=""):
        return Buf(name)

    def add(self, eng, fn, reads=(), writes=(), dma=False):
        q = self.q[eng]
        rec = Rec(eng, len(q), fn, dma)
        deps = {}
        for t in reads:
            for b in t.bufs:
                if b.w is not None:
                    deps[id(b.w)] = b.w
        for t in writes:
            for b in t.bufs:
                if b.w is not None:
                    deps[id(b.w)] = b.w
                for r in b.r:
                    deps[id(r)] = r
        if self._pending[eng]:
            for d in self._pending[eng]:
                deps[id(d)] = d
            self._pending[eng] = []
            barrier_deps = True
        else:
            barrier_deps = False
        for d in deps.values():
            if d is rec:
                continue
            if d.eng == eng and not d.dma and not dma:
                if eng == "pe" or eng == "sp" or not self.same_engine_sync:
                    continue
            rec.deps.append(d)
        for t in reads:
            for b in t.bufs:
                b.r.append(rec)
        for t in writes:
            for b in t.bufs:
                b.w = rec
                b.r = []
        q.append(rec)
        return rec

    def emit(self):
        nc = self.nc
        for e in ENGS:
            for rec in self.q[e]:
                for d in rec.deps:
                    d.signaled = True
        sems = {}
        for e in ENGS:
            cnt = 0
            for rec in self.q[e]:
                if rec.dma:
                    continue
                if rec.signaled:
                    ep = cnt // SEM_EPOCH
                    key = (e, ep)
                    if key not in sems:
                        sems[key] = self.stack.enter_context(nc.semaphore(f"s_{e}_{ep}"))
                    rec.ev = (sems[key], cnt % SEM_EPOCH + 1, key)
                    cnt += 1
        self.final_waits = []
        for e in ENGS:
            j = 0
            last = {}
            for rec in self.q[e]:
                if not rec.dma:
                    continue
                slot = j % DMA_ROT
                key = ("dma", e, slot)
                if key not in sems:
                    sems[key] = self.stack.enter_context(nc.semaphore(f"d_{e}_{slot}"))
                val = 16 * (j // DMA_ROT + 1)
                rec.ev = (sems[key], val, key)
                if j >= DMA_ROT:
                    rec.selfwait = (sems[key], val - 16, key)
                last[key] = (sems[key], val, key)
                j += 1
            self.final_waits.extend(last.values())

        block = self.stack.enter_context(nc.Block())
        sched = self

        def run(engname, eobj):
            waited = {}
            for rec in sched.q[engname]:
                waits = {}
                if rec.selfwait is not None:
                    s, v, key = rec.selfwait
                    waits[key] = (s, v)
                for d in rec.deps:
                    s, v, key = d.ev
                    if key in waits:
                        if waits[key][1] < v:
                            waits[key] = (s, v)
                    else:
                        waits[key] = (s, v)
                for key, (s, v) in waits.items():
                    if waited.get(key, 0) >= v:
                        continue
                    eobj.wait_ge(s, v)
                    waited[key] = v
                ins = rec.fn(eobj)
                if rec.dma:
                    ins.then_inc(rec.ev[0], 16)
                elif rec.signaled:
                    ins.then_inc(rec.ev[0], 1)
            if engname == "sp":
                for s, v, key in sched.final_waits:
                    eobj.wait_ge(s, v)

        @block.tensor
        def _(e):
            run("pe", e)

        @block.scalar
        def _(e):
            run("act", e)

        @block.vector
        def _(e):
            run("dve", e)

        @block.gpsimd
        def _(e):
            run("pool", e)

        @block.sync
        def _(e):
            run("sp", e)

    def close(self):
        self.stack.close()

    def dma(self, out, in_, eng="sp", **kw):
        return self.add(eng, lambda e: e.dma_start(out=out.ap, in_=in_.ap, **kw),
                        reads=[in_], writes=[out], dma=True)

    def mm(self, out, lhsT, rhs, start=True, stop=True, extra_reads=(), **kw):
        return self.add("pe", lambda e: e.matmul(out.ap, lhsT.ap, rhs.ap, start=start, stop=stop, **kw),
                        reads=[lhsT, rhs, *extra_reads], writes=[out])

    def tr(self, out, in_, ident, **kw):
        return self.add("pe", lambda e: e.transpose(out.ap, in_.ap, ident.ap, **kw),
                        reads=[in_, ident], writes=[out])

    def actf(self, out, in_, func, bias=None, scale=None, accum=None, eng="act"):
        kw = {}
        reads = [in_]
        writes = [out]
        if bias is not None:
            if isinstance(bias, T):
                kw["bias"] = bias.ap
                reads.append(bias)
            else:
                kw["bias"] = bias
        if scale is not None:
            if isinstance(scale, T):
                kw["scale"] = scale.ap
                reads.append(scale)
            else:
                kw["scale"] = scale
        if accum is not None:
            kw["accum_out"] = accum.ap
            writes.append(accum)
        return self.add("act", lambda e: e.activation(out.ap, in_.ap, func, **kw), reads=reads, writes=writes)

    def tt(self, out, a, b, op, eng="dve"):
        return self.add(eng, lambda e: e.tensor_tensor(out.ap, a.ap, b.ap, op), reads=[a, b], writes=[out])

    def ts(self, out, a, s1, op0, s2=None, op1=None, accum=None, eng="dve"):
        reads = [a]
        writes = [out]
        s1v = s1.ap if isinstance(s1, T) else s1
        s2v = s2.ap if isinstance(s2, T) else s2
        if isinstance(s1, T):
            reads.append(s1)
        if isinstance(s2, T):
            reads.append(s2)
        kw = {}
        if op1 is not None:
            kw["op1"] = op1
        if accum is not None:
            kw["accum_out"] = accum.ap
            writes.append(accum)
        return self.add(eng, lambda e: e.tensor_scalar(out.ap, a.ap, s1v, s2v, op0, **kw), reads=reads, writes=writes)

    def stt(self, out, a, s, b, op0, op1, eng="dve"):
        reads = [a, b]
        sv = s.ap if isinstance(s, T) else s
        if isinstance(s, T):
            reads.append(s)
        return self.add(eng, lambda e: e.scalar_tensor_tensor(out.ap, a.ap, sv, b.ap, op0, op1), reads=reads, writes=[out])

    def copy(self, out, in_, eng="dve"):
        if eng == "act":
            return self.add("act", lambda e: e.copy(out.ap, in_.ap), reads=[in_], writes=[out])
        return self.add(eng, lambda e: e.tensor_copy(out.ap, in_.ap), reads=[in_], writes=[out])

    def memset(self, out, val, eng="pool"):
        return self.add(eng, lambda e: e.memset(out.ap, val), reads=[], writes=[out])

    def recip(self, out, in_):
        return self.add("dve", lambda e: e.reciprocal(out.ap, in_.ap), reads=[in_], writes=[out])

    def scan(self, out, d0, d1, init, op0, op1):
        reads = [d0, d1]
        iv = init.ap if isinstance(init, T) else init
        if isinstance(init, T):
            reads.append(init)
        return self.add("dve", lambda e: e.tensor_tensor_scan(out.ap, d0.ap, d1.ap, iv, op0, op1), reads=reads, writes=[out])


def _barrier(self):
    deps = []
    for e in ENGS:
        q = self.q[e]
        last_c = None
        for rec in reversed(q):
            if not rec.dma:
                last_c = rec
                break
        if last_c is not None:
            deps.append(last_c)
        for rec in q[self._bar_pos.get(e, 0):]:
            if rec.dma:
                deps.append(rec)
        self._bar_pos[e] = len(q)
    for e in ENGS:
        self._pending[e] = list(deps)


Sched.barrier = _barrier


from concourse.bass_utils import run_bass_kernel_spmd

D = 1024
TP = 2048
TSV = 32
NTOK = 2176
NTB = 17
P_DA = 3072
P_RW = 3328
PTOT = 8448
DFF = 2816
EPS = 1e-6
GN_EPS = 64e-5
ROPE_THETA = 500000.0
TQ = [(0, 512), (512, 512), (1024, 512), (1536, 512), (2048, 128)]
PF_NMIX, PF_NFFN, PF_MU, PF_W0, PF_A0, PF_KK, PF_KA, PF_RK, PF_LNW, PF_LNB = 0, 8, 16, 42, 50, 58, 66, 74, 82, 90
PF_CONV, PF_CONVB, PF_SUBLN, NPF = 98, 230, 274, 275
C_ID, C_BD, C_MU, C_ML, C_MUI, C_RST, C_COS, C_SIN, NCONST = 0, 128, 256, 384, 512, 576, 832, 1104, 1376
DEC_C = -0.6065306597126334


def make_consts():
    c = np.zeros((128, NCONST), np.float32)
    c[:, C_ID:C_ID + 128] = np.eye(128, dtype=np.float32)
    p = np.arange(128)
    h = p // 64
    s = p % 64
    bd = (h[:, None] == h[None, :]).astype(np.float32)
    c[:, C_BD:C_BD + 128] = bd
    c[:, C_MU:C_MU + 128] = bd * (s[:, None] < s[None, :])
    c[:, C_ML:C_ML + 128] = bd * (s[:, None] > s[None, :])
    c[:, C_MUI:C_MUI + 64] = (s[:, None] <= np.arange(64)[None, :])
    rst = np.ones(256, np.float32)
    rst[::64] = 0
    c[:, C_RST:C_RST + 256] = rst[None, :]
    inv = (np.float32(ROPE_THETA) ** (-np.arange(0, 16, 2, dtype=np.float32) / np.float32(16))).astype(np.float32)
    for tb in range(NTB):
        pos = (tb * 128 + p) if tb < 16 else (TP + p)
        ang = pos.astype(np.float32)[:, None] * inv[None, :]
        co = np.cos(ang).astype(np.float32)
        si = np.sin(ang).astype(np.float32)
        c[:, C_COS + tb * 16:C_COS + tb * 16 + 8] = co
        c[:, C_COS + tb * 16 + 8:C_COS + tb * 16 + 16] = co
        c[:, C_SIN + tb * 16:C_SIN + tb * 16 + 8] = si
        c[:, C_SIN + tb * 16 + 8:C_SIN + tb * 16 + 16] = si
    return c


CHAIN_BF16 = True


def r32(t):
    if CHAIN_BF16:
        return t
    return T(t.ap.bitcast(F32R), t.bufs)


class Arena:
    def __init__(self, t, width):
        self.t = t
        self.W = width
        self.off = 0

    def seek(self, off):
        self.off = off

    def alloc(self, free_shape, dtype):
        n = 1
        for d in free_shape:
            n *= d
        words = n if dtype != BF16 else (n + 1) // 2
        words = (words + 7) // 8 * 8
        assert self.off + words <= self.W, ("arena overflow", self.off, words, self.W)
        ap = self.t.ap[:, self.off:self.off + words]
        if dtype == BF16:
            ap = ap.bitcast(BF16)
        ap = ap[:, 0:n]
        if len(free_shape) > 1:
            names = [f"d{i}" for i in range(len(free_shape))]
            pat = "p (" + " ".join(names) + ") -> p " + " ".join(names)
            kw = {nm: sz for nm, sz in zip(names[:-1], free_shape[:-1])}
            ap = ap.rearrange(pat, **kw)
        self.off += words
        return T(ap, [Buf()])


class Ring:
    def __init__(self, items):
        self.items = items
        self.i = 0

    def next(self):
        t = self.items[self.i % len(self.items)]
        self.i += 1
        return t


class StopBuild(Exception):
    pass


def build(dbg=None, stop_after=None):
    def chk(tag, l=0):
        if stop_after == (tag, l):
            raise StopBuild()

    nc = bass.Bass("TRN2", target_bir_lowering=False)
    S = Sched(nc)

    def din(name, shape):
        return S.dram(name, shape, F32, "ExternalInput")

    def dout(name, shape):
        return S.dram(name, shape, F32, "ExternalOutput")

    xp = din("xp", [TP, D])
    xs = din("xs", [TSV, D])
    ck = din("ck", [2, TP, D])
    cv = din("cv", [2, TP, D])
    swkv = din("swkv", [2, 8, 128, 64])
    sfm = din("sfm", [2, 128, 26])
    cfm = din("cfm", [2, 128, 88])
    pfm = din("pfm", [2, 128, NPF])
    consts = din("consts", [128, NCONST])
    w_in = din("w_in", [2, D, PTOT])
    da_lambda = din("da_lambda", [512])
    w_o_da = din("w_o_da", [2, D, D])
    rw_w2 = din("rw_w2", [2, 64, D])
    rw_a2 = din("rw_a2", [2, 64, D])
    rw_g2 = din("rw_g2", [2, 128, D])
    w_o_rw = din("w_o_rw", [2, D, D])
    w_out = din("w_out", [2, D, D])
    w_up = din("w_up", [2, D, 2 * DFF])
    w_down = din("w_down", [2, DFF, D])
    nfin = din("nfin", [D])

    yp = dout("yp", [TP, D])
    ys = dout("ys", [TSV, D])
    kp = dout("kp", [2, TP, D])
    vp = dout("vp", [2, TP, D])
    wkvp = dout("wkvp", [2, 8, 128, 64])
    shp = dout("shp", [2, 128, 26])
    cvp = dout("cvp", [2, 128, 88])
    ks = dout("ks", [2, TSV, D])
    vs = dout("vs", [2, TSV, D])
    wkvs = dout("wkvs", [2, 8, 128, 64])
    shs = dout("shs", [2, 128, 26])
    cvs = dout("cvs", [2, 128, 88])
    X1 = S.dram("X1", [NTOK, D], F32, "Internal")
    X2 = S.dram("X2", [NTOK, D], F32, "Internal")
    dbg_out = {}
    if dbg:
        for name, shape in dbg.items():
            dbg_out[name] = dout("dbg_" + name, shape)

    cst = S.sbuf("cst", [128, NCONST], F32)
    S.dma(cst, consts)
    ident_bf = S.sbuf("ident_bf", [128, 128], BF16)
    S.copy(ident_bf, cst[:, C_ID:C_ID + 128])
    ones_bf = S.sbuf("ones_bf", [128, 128], BF16)
    S.memset(ones_bf, 1.0)
    ones_f = S.sbuf("ones_f", [128, 128], F32)
    S.memset(ones_f, 1.0)
    ident_f = cst[:, C_ID:C_ID + 128]
    bdmask = cst[:, C_BD:C_BD + 128]
    bd_bf = S.sbuf("bd_bf", [128, 128], BF16)
    S.copy(bd_bf, cst[:, C_BD:C_BD + 128])
    fr = None if CHAIN_BF16 else S.sbuf("fr", [128, 21, 256], F32)
    pf = S.sbuf("pf", [128, 2, NPF], F32)
    for l in range(2):
        S.dma(pf[:, l, :], pfm[l])
    gfin = S.sbuf("gfin", [128, D], F32)
    S.dma(gfin, nfin.v(nfin.ap.partition_broadcast(128)))
    lamb = S.sbuf("lamb", [128, 512], F32)
    S.dma(lamb, da_lambda.v(da_lambda.ap.partition_broadcast(128)))
    lamv = S.sbuf("lamv", [128, 16], F32)
    epsb = S.sbuf("epsb", [128, 2], F32)
    S.memset(epsb[:, 0:1], EPS)
    S.memset(epsb[:, 1:2], GN_EPS)
    ljunk = S.sbuf("ljunk", [128, 64], F32)
    shp_t = S.sbuf("shp_t", [128, 26], F32)
    shs_t = S.sbuf("shs_t", [128, 26], F32)
    carry_p = S.sbuf("carry_p", [128, 44, 2], F32)
    carry_s = S.sbuf("carry_s", [128, 44, 2], F32)

    ps = [S.psum(f"ps{i}", [128, 512], F32) for i in range(8)]
    psr = Ring(ps)
    halves = []
    for i in range(8):
        halves.append(T(ps[i].ap[:, 0:256], ps[i].bufs))
        halves.append(T(ps[i].ap[:, 256:512], ps[i].bufs))

    AW = ((nc.sbuf_bytes_remaining - 256) // 4) // 8 * 8
    ar = S.sbuf("arena", [128, AW], F32)
    A = Arena(ar, AW)
    hT = A.alloc([8, NTOK], BF16)
    OZ = A.alloc([8, NTOK], BF16)
    MT_OFF = A.off
    MT = A.alloc([8, NTOK], BF16)
    LOC = A.off
    OZ_OFF = MT_OFF - (MT_OFF - 0) // 2 if False else None
    S.memset(hT, 0.0)
    S.memset(OZ, 0.0, eng="dve")
    S.memset(MT, 0.0)

    for l in range(2):
        lam_init = 0.8 - 0.6 * float(np.exp(-0.3 * l))
        b = l * 256
        pr = ljunk
        S.tt(pr, lamb[:, b:b + 64], lamb[:, b + 64:b + 128], ALU.mult)
        S.ts(pr, pr, 1.0, ALU.mult, None, ALU.add, accum=lamv[:, l * 8 + 4:l * 8 + 5])
        S.tt(pr, lamb[:, b + 128:b + 192], lamb[:, b + 192:b + 256], ALU.mult)
        S.ts(pr, pr, 1.0, ALU.mult, None, ALU.add, accum=lamv[:, l * 8 + 5:l * 8 + 6])
        S.actf(lamv[:, l * 8 + 2:l * 8 + 4], lamv[:, l * 8 + 4:l * 8 + 6], AF.Exp)
        S.tt(lamv[:, l * 8:l * 8 + 1], lamv[:, l * 8 + 3:l * 8 + 4], lamv[:, l * 8 + 2:l * 8 + 3], ALU.subtract)
        S.ts(lamv[:, l * 8:l * 8 + 1], lamv[:, l * 8:l * 8 + 1], -lam_init, ALU.add)
        S.ts(lamv[:, l * 8 + 1:l * 8 + 2], pf[:, l, PF_SUBLN:PF_SUBLN + 1], 1.0 - lam_init, ALU.mult)

    def wslab(dst, src_ap):
        return S.dma(dst, T(src_ap.rearrange("(kt p) c -> p kt c", p=128), ()), eng="pool")

    def make_norm_scratch():
        d = {}
        d["junk"] = A.alloc([D], BF16)
        d["xn"] = Ring([A.alloc([D], BF16) for _ in range(2)])
        d["st"] = Ring([A.alloc([4], F32) for _ in range(3)])
        return d

    def norm_stats(x_t, nsc):
        st = nsc["st"].next()
        S.actf(nsc["junk"], x_t, AF.Square, accum=st[:, 0:1])
        S.actf(st[:, 1:2], st[:, 0:1], AF.Sqrt, scale=1.0 / D, bias=epsb[:, 0:1])
        S.recip(st[:, 2:3], st[:, 1:2])
        return st[:, 2:3]

    def norm_to_hT(x_t, l, pfoff, tb, nsc):
        rstd = norm_stats(x_t, nsc)
        xn = nsc["xn"].next()
        S.ts(xn, x_t, rstd, ALU.mult)
        pb = psr.next()
        pst = T(pb.ap.bitcast(BF16).rearrange("p (k t) -> p k t", k=8), pb.bufs)
        for kt in range(8):
            S.tr(pst[:, kt, :], xn[:, kt * 128:(kt + 1) * 128], ident_bf)
        g = pf[:, l, pfoff:pfoff + 8]
        S.tt(hT[:, :, tb * 128:(tb + 1) * 128], pst, g.v(g.ap.unsqueeze(2).broadcast_to([128, 8, 128])), ALU.mult)

    def x_rows(tb):
        return (xp[tb * 128:(tb + 1) * 128, :], 128) if tb < 16 else (xs, TSV)

    def phase_norm0():
        A.seek(LOC)
        nsc = make_norm_scratch()
        xr = Ring([A.alloc([D], F32) for _ in range(3)])
        for tb in range(NTB):
            xt = xr.next()
            src, nr = x_rows(tb)
            if nr < 128:
                S.memset(xt, 0.0)
            S.dma(xt[0:nr, :], src)
            norm_to_hT(xt, 0, PF_NMIX, tb, nsc)

    def phase_rwkv(l):
        A.seek(MT_OFF)
        w_l = w_in[l]
        mu = lambda c: pf[:, l, PF_MU + c:PF_MU + c + 1]
        W2p = A.alloc([D], BF16)
        A2p = A.alloc([D], BF16)
        G2 = A.alloc([D], BF16)
        S.memset(W2p[64:128, :], 0.0)
        S.memset(A2p[0:64, :], 0.0)
        S.dma(W2p[0:64, :], rw_w2[l], eng="pool")
        S.dma(A2p[64:128, :], rw_a2[l], eng="pool")
        S.dma(G2, rw_g2[l], eng="pool")
        tanh_w = A.alloc([NTOK], BF16)
        raw_w = A.alloc([NTOK], BF16)
        sig_g = A.alloc([NTOK], BF16)
        sfm_t = A.alloc([26], F32)
        S.dma(sfm_t, sfm[l])
        NB = 256
        GRP_OFF = A.off
        Wl = A.alloc([8, 256], BF16)
        wslab(Wl, w_l.ap[:, P_DA + 3072:P_DA + 3328])
        u_ring = Ring([A.alloc([513], F32) for _ in range(3)])
        tmp = Ring([A.alloc([512], F32) for _ in range(6)])

        def shifted(psb, n, carry_src, mucol, u_out_last=None, last_idx=None):
            U = u_ring.next()
            if carry_src is None:
                S.memset(U[:, 0:1], 0.0, eng="dve")
            else:
                S.copy(U[:, 0:1], carry_src, eng="dve")
            S.copy(U[:, 1:n + 1], psb[:, 0:n], eng="act")
            d = tmp.next()
            S.tt(d[:, 0:n], U[:, 0:n], U[:, 1:n + 1], ALU.subtract)
            return d, U

        prev = {0: None, 1: None}
        for qi, (c0, n) in enumerate(TQ):
            for which in range(2):
                pb = psr.next()
                for kt in range(8):
                    S.mm(pb[:, 0:n], Wl[:, kt, which * 128:(which + 1) * 128], hT[:, kt, c0:c0 + n],
                         start=(kt == 0), stop=(kt == 7))
                mc = 24 + which
                if qi == 0:
                    carry = None
                elif qi == 4:
                    carry = sfm_t[:, mc:mc + 1]
                else:
                    carry = prev[which]
                d, U = shifted(pb, n, carry, mc)
                us = tmp.next()
                S.stt(us[:, 0:n], d[:, 0:n], mu(mc), U[:, 1:n + 1], ALU.mult, ALU.add)
                prev[which] = U[:, n:n + 1]
                if qi == 3:
                    S.copy(shp_t[:, mc:mc + 1], U[:, n:n + 1], eng="pool")
                if qi == 4:
                    S.copy(shs_t[:, mc:mc + 1], U[:, TSV:TSV + 1], eng="pool")
                if which == 0:
                    S.actf(tanh_w[:, c0:c0 + n], us[:, 0:n], AF.Tanh)
                    S.copy(raw_w[:, c0:c0 + n], us[:, 0:n], eng="pool")
                else:
                    S.actf(sig_g[:, c0:c0 + n], us[:, 0:n], AF.Sigmoid)

        chk("rwkv_lora", l)
        S.barrier()
        A.seek(GRP_OFF)
        Wg = A.alloc([8, 768], BF16)
        AT = A.alloc([2, NB], BF16)
        BT = A.alloc([2, NB], BF16)
        KT_ = A.alloc([2, NB], BF16)
        RT = A.alloc([2, NB], BF16)
        VT = A.alloc([2, NB], BF16)
        WT = A.alloc([2, NB], F32)
        BON = A.alloc([2, NB], F32)
        YT = A.alloc([2, NB], F32)
        S32 = A.alloc([2, 128], F32)
        Sb = A.alloc([2, 128], BF16)
        swt = A.alloc([2, 64], F32)
        carr = A.alloc([8], F32)
        t256 = Ring([A.alloc([NB], F32) for _ in range(11)])
        b256 = Ring([A.alloc([NB], BF16) for _ in range(3)])
        usrk = Ring([A.alloc([NB], F32) for _ in range(2)])
        u257 = Ring([A.alloc([NB + 1], F32) for _ in range(2)])
        NCH = 4
        wide = lambda: A.alloc([2, 128], BF16)
        BDa = [wide() for _ in range(NCH)]
        BDb = [wide() for _ in range(NCH)]
        BDk = [wide() for _ in range(NCH)]
        BDx = Ring([wide() for _ in range(3)])
        AKm = [wide() for _ in range(NCH)]
        Vm = [wide() for _ in range(NCH)]
        Bhm = [wide() for _ in range(NCH)]
        Khm = [wide() for _ in range(NCH)]
        Rm = [wide() for _ in range(NCH)]
        Um = Ring([wide() for _ in range(1)])
        fi = [0]

        def frt():
            if CHAIN_BF16:
                return wide()
            t = T(fr.ap[:, fi[0], :].rearrange("p (g c) -> p g c", g=2), [Buf()])
            fi[0] += 1
            return t

        Nm = [[frt() for _ in range(NCH)] for _ in range(2)]
        NTm = [[frt() for _ in range(NCH)] for _ in range(2)]
        Pm = [frt() for _ in range(NCH)]
        Hm = Ring([frt() for _ in range(1)])
        class _HB:
            def __init__(self):
                self.k = 0
                self.pend = None

            def next(self):
                if self.pend is not None:
                    t = self.pend
                    self.pend = None
                    return t
                i = self.k % 8
                self.k += 1
                self.pend = halves[2 * i + 1]
                return halves[2 * i]

            def newbank(self):
                self.pend = None

        hbr = _HB()

        def w4(t):
            return T(t.ap.rearrange("p g (h t) -> p g h t", h=2), t.bufs)

        def v3(t):
            return T(t.ap.rearrange("p (g c) -> p g c", g=2), t.bufs)

        def v3b(t):
            return T(t.ap.bitcast(BF16)[:, 0:256].rearrange("p (g c) -> p g c", g=2), t.bufs)

        def bcast_mask(coff):
            m = cst[:, coff:coff + 128]
            return T(m.ap.unsqueeze(1).broadcast_to([128, 2, 128]), m.bufs)

        mU_b = bcast_mask(C_MU)
        mL_b = bcast_mask(C_ML)
        id_b = bcast_mask(C_ID)
        bd4 = T(bdmask.ap.rearrange("p (h t) -> p h t", h=2).unsqueeze(1).broadcast_to([128, 2, 2, 64]), bdmask.bufs)
        mui = cst[:, C_MUI:C_MUI + 64]
        mui4 = T(mui.ap.unsqueeze(1).unsqueeze(1).broadcast_to([128, 2, 2, 64]), mui.bufs)
        rstm = cst[:, C_RST:C_RST + NB]

        for grp in range(4):
            for x in range(3):
                wslab(Wg[:, :, x * 256:(x + 1) * 256],
                      w_l.ap[:, P_DA + x * 1024 + grp * 256:P_DA + x * 1024 + (grp + 1) * 256])
            for seq in range(2):
                blocks = [(i * NB, NB) for i in range(8)] if seq == 0 else [(TP, 64)]
                nvalid_last = NB if seq == 0 else TSV
                if seq == 0:
                    S.memset(S32, 0.0, eng="dve")
                    S.memset(Sb, 0.0, eng="dve")
                else:
                    for g in range(2):
                        S.dma(swt[:, g, :], swkv[l, 2 * grp + g])
                    S.tt(w4(S32), swt.v(swt.ap.unsqueeze(2).broadcast_to([128, 2, 2, 64])), bd4, ALU.mult)
                    S.copy(Sb, S32)
                carry = {}
                for bi, (c0, n) in enumerate(blocks):
                    nch = n // 64
                    for g in range(2):
                        hp = 2 * grp + g
                        us = {}
                        for x in range(3):
                            pb = psr.next()
                            off = x * 256 + g * 128
                            for kt in range(8):
                                S.mm(pb[:, 0:n], Wg[:, kt, off:off + 128], hT[:, kt, c0:c0 + n],
                                     start=(kt == 0), stop=(kt == 7))
                            mc = x * 8 + hp
                            U = u257.next()
                            if bi == 0:
                                if seq == 0:
                                    S.memset(U[:, 0:1], 0.0, eng="dve")
                                else:
                                    S.copy(U[:, 0:1], sfm_t[:, mc:mc + 1], eng="dve")
                            else:
                                S.copy(U[:, 0:1], carry[(g, x)], eng="dve")
                            S.copy(U[:, 1:n + 1], pb[:, 0:n], eng="act")
                            d = t256.next()
                            S.tt(d[:, 0:n], U[:, 0:n], U[:, 1:n + 1], ALU.subtract)
                            ut = usrk.next() if x < 2 else t256.next()
                            us[x] = ut[:, 0:n]
                            S.stt(ut[:, 0:n], d[:, 0:n], mu(mc), U[:, 1:n + 1], ALU.mult, ALU.add)
                            S.copy(carr[:, g * 3 + x:g * 3 + x + 1], U[:, n:n + 1], eng="pool")
                            carry[(g, x)] = carr[:, g * 3 + x:g * 3 + x + 1]
                            if bi == len(blocks) - 1:
                                sht = shp_t if seq == 0 else shs_t
                                S.copy(sht[:, mc:mc + 1], U[:, nvalid_last:nvalid_last + 1], eng="pool")
                        us_r, us_k, us_v = us[0], us[1], us[2]
                        S.copy(VT[:, g, 0:n], us_v, eng="pool")
                        pcol = lambda o: pf[:, l, o + hp:o + hp + 1]
                        pb = psr.next()
                        S.mm(pb[:, 0:n], W2p[:, hp * 128:(hp + 1) * 128], tanh_w[:, c0:c0 + n])
                        sg = t256.next()
                        S.actf(sg[:, 0:n], pb[:, 0:n], AF.Sigmoid, bias=pcol(PF_W0))
                        lw = t256.next()
                        S.ts(lw[:, 0:n], sg[:, 0:n], DEC_C, ALU.mult, eng="pool")
                        pb = psr.next()
                        S.mm(pb[:, 0:n], A2p[:, hp * 128:(hp + 1) * 128], raw_w[:, c0:c0 + n])
                        a_t = t256.next()
                        S.actf(a_t[:, 0:n], pb[:, 0:n], AF.Sigmoid, bias=pcol(PF_A0))
                        kk = t256.next()
                        S.ts(kk[:, 0:n], us_k, pcol(PF_KK), ALU.mult)
                        sq = b256.next()
                        S.actf(sq[:, 0:n], kk[:, 0:n], AF.Square)
                        pb = psr.next()
                        S.mm(pb[:, 0:n], bd_bf, sq[:, 0:n])
                        rn = t256.next()
                        S.actf(rn[:, 0:n], pb[:, 0:n], AF.Sqrt)
                        S.ts(rn[:, 0:n], rn[:, 0:n], 1e-12, ALU.max)
                        S.recip(rn[:, 0:n], rn[:, 0:n])
                        kkn = kk
                        S.tt(kkn[:, 0:n], kk[:, 0:n], rn[:, 0:n], ALU.mult)
                        t1 = t256.next()
                        S.ts(t1[:, 0:n], a_t[:, 0:n], -1.0, ALU.add, pcol(PF_KA), ALU.mult)
                        kmod = t256.next()
                        S.stt(kmod[:, 0:n], t1[:, 0:n], 1.0, us_k, ALU.add, ALU.mult)
                        bb = t1
                        S.tt(bb[:, 0:n], kkn[:, 0:n], a_t[:, 0:n], ALU.mult, eng="pool")
                        cum = t256.next()
                        S.scan(cum[:, 0:n], rstm[:, 0:n], lw[:, 0:n], 0.0, ALU.mult, ALU.add)
                        cumex = sg
                        S.tt(cumex[:, 0:n], cum[:, 0:n], lw[:, 0:n], ALU.subtract, eng="pool")
                        S.actf(WT[:, g, 0:n], cum[:, 0:n], AF.Exp)
                        winv = lw
                        S.actf(winv[:, 0:n], cum[:, 0:n], AF.Exp, scale=-1.0)
                        wex = cum
                        S.actf(wex[:, 0:n], cumex[:, 0:n], AF.Exp)
                        S.stt(AT[:, g, 0:n], kkn[:, 0:n], -1.0, wex[:, 0:n], ALU.mult, ALU.mult)
                        S.tt(BT[:, g, 0:n], bb[:, 0:n], winv[:, 0:n], ALU.mult)
                        S.tt(KT_[:, g, 0:n], kmod[:, 0:n], winv[:, 0:n], ALU.mult)
                        S.tt(RT[:, g, 0:n], us_r, WT[:, g, 0:n], ALU.mult)
                        rk = b256.next()
                        S.stt(rk[:, 0:n], us_r, pcol(PF_RK), kmod[:, 0:n], ALU.mult, ALU.mult)
                        pb = psr.next()
                        S.mm(pb[:, 0:n], bd_bf, rk[:, 0:n])
                        S.tt(BON[:, g, 0:n], pb[:, 0:n], us_v, ALU.mult)
                        if seq == 1:
                            for arr in (AT, BT, KT_, RT, VT):
                                S.memset(arr[:, g, TSV:64], 0.0)
                    chk("rwkv_A", l)
                    def chunk_src(arr, ci):
                        a = arr[:, :, ci * 64:(ci + 1) * 64]
                        return T(a.ap.unsqueeze(2).broadcast_to([128, 2, 2, 64]), a.bufs)

                    wcols = []
                    for ci in range(nch):
                        lastc = ci * 64 + (63 if seq == 0 else TSV - 1)
                        wc = WT[:, :, lastc:lastc + 1]
                        wcols.append(wc)
                        wcb = T(wc.ap.broadcast_to([128, 2, 128]), wc.bufs)
                        S.tt(w4(BDa[ci]), chunk_src(AT, ci), bd4, ALU.mult)
                        S.tt(w4(BDb[ci]), chunk_src(BT, ci), bd4, ALU.mult, eng="pool")
                        S.tt(w4(BDk[ci]), chunk_src(KT_, ci), bd4, ALU.mult)
                        bdv = BDx.next()
                        S.tt(w4(bdv), chunk_src(VT, ci), bd4, ALU.mult, eng="pool")
                        bdbh = BDx.next()
                        S.tt(bdbh, BDb[ci], wcb, ALU.mult)
                        bdkh = BDx.next()
                        S.tt(bdkh, BDk[ci], wcb, ALU.mult, eng="pool")
                        chk("B1a", l)
                        hbr.newbank()
                        pN, pNT, pAK, pR, pV, pBh, pKh = [hbr.next() for _ in range(7)]
                        hbr.newbank()
                        for g in range(2):
                            S.mm(v3(pN)[:, g, :], BDb[ci][:, g, :], BDa[ci][:, g, :])
                        for g in range(2):
                            S.mm(v3(pNT)[:, g, :], BDa[ci][:, g, :], BDb[ci][:, g, :])
                        for g in range(2):
                            S.mm(v3(pAK)[:, g, :], BDk[ci][:, g, :], BDa[ci][:, g, :])
                        chk("B1b", l)
                        pR4 = T(pR.ap.rearrange("p (g x t) -> p g x t", g=2, x=2), pR.bufs)
                        for g in range(2):
                            S.mm(pR4[:, g, 0, :], BDb[ci][:, g, :], RT[:, g, ci * 64:(ci + 1) * 64])
                            S.mm(pR4[:, g, 1, :], BDk[ci][:, g, :], RT[:, g, ci * 64:(ci + 1) * 64])
                        chk("B1c", l)
                        for g in range(2):
                            S.tr(v3b(pV)[:, g, :], bdv[:, g, :], ident_bf)
                            S.tr(v3b(pBh)[:, g, :], bdbh[:, g, :], ident_bf)
                            S.tr(v3b(pKh)[:, g, :], bdkh[:, g, :], ident_bf)
                        chk("B1d", l)
                        S.tt(r32(Nm[0][ci]), v3(pN), mU_b, ALU.mult)
                        S.tt(r32(NTm[0][ci]), v3(pNT), mL_b, ALU.mult)
                        S.tt(AKm[ci], v3(pAK), mU_b, ALU.mult)
                        S.tt(w4(Rm[ci]), pR4, mui4, ALU.mult)
                        chk("B1e", l)
                        S.copy(Vm[ci], v3b(pV), eng="act")
                        S.copy(Bhm[ci], v3b(pBh), eng="act")
                        S.copy(Khm[ci], v3b(pKh), eng="act")
                        chk("B1f", l)
                        S.tt(r32(Pm[ci]), Nm[0][ci], id_b, ALU.add)
                        chk("B1g", l)
                        if ci == 1:
                            chk("B1h", l)
                    chk("rwkv_B1", l)
                    cur = 0
                    for rd in range(1, 6):
                        nxt = 1 - cur
                        pairs = []
                        for ci in range(nch):
                            hbr.newbank()
                            pN2 = hbr.next() if rd < 5 else None
                            pNT2 = hbr.next()
                            for g in range(2):
                                if rd < 5:
                                    S.mm(v3(pN2)[:, g, :], r32(NTm[cur][ci][:, g, :]), r32(Nm[cur][ci][:, g, :]))
                                S.mm(v3(pNT2)[:, g, :], r32(Nm[cur][ci][:, g, :]), r32(NTm[cur][ci][:, g, :]))
                            pairs.append((pN2, pNT2))
                        for ci in range(nch):
                            pN2, pNT2 = pairs[ci]
                            S.copy(r32(NTm[nxt][ci]), v3(pNT2), eng="act")
                            if rd < 5:
                                S.copy(r32(Nm[nxt][ci]), v3(pN2), eng="act")
                        pps = []
                        for ci in range(nch):
                            hbr.newbank()
                            pP = hbr.next()
                            for g in range(2):
                                S.mm(v3(pP)[:, g, :], r32(NTm[nxt][ci][:, g, :]), r32(Pm[ci][:, g, :]), start=True, stop=False)
                                S.mm(v3(pP)[:, g, :], ident_bf, r32(Pm[ci][:, g, :]), start=False, stop=True)
                            pps.append(pP)
                        for ci in range(nch):
                            S.copy(r32(Pm[ci]), v3(pps[ci]), eng="act")
                        cur = nxt
                    chk("rwkv_inv", l)
                    for ci in range(nch):
                        hbr.newbank()
                        pH = hbr.next()
                        for g in range(2):
                            S.mm(v3(pH)[:, g, :], BDa[ci][:, g, :], Sb[:, g, :], start=True, stop=False)
                            S.mm(v3(pH)[:, g, :], AKm[ci][:, g, :], Vm[ci][:, g, :], start=False, stop=True)
                        H = Hm.next()
                        S.copy(r32(H), v3(pH), eng="act")
                        hbr.newbank()
                        pU = hbr.next()
                        for g in range(2):
                            S.mm(v3(pU)[:, g, :], r32(Pm[ci][:, g, :]), r32(H[:, g, :]))
                        U = Um.next()
                        S.copy(U, v3(pU), eng="dve")
                        hbr.newbank()
                        pY = hbr.next()
                        pY3 = T(pY.ap[:, 0:128].rearrange("p (g t) -> p g t", g=2), pY.bufs)
                        R4 = w4(Rm[ci])
                        for g in range(2):
                            S.mm(pY3[:, g, :], Sb[:, g, :], RT[:, g, ci * 64:(ci + 1) * 64], start=True, stop=False)
                            S.mm(pY3[:, g, :], U[:, g, :], R4[:, g, 0, :], start=False, stop=False)
                            S.mm(pY3[:, g, :], Vm[ci][:, g, :], R4[:, g, 1, :], start=False, stop=True)
                        S.copy(YT[:, :, ci * 64:(ci + 1) * 64], pY3, eng="act")
                        hbr.newbank()
                        pS = hbr.next()
                        for g in range(2):
                            S.mm(v3(pS)[:, g, :], Bhm[ci][:, g, :], U[:, g, :], start=True, stop=False)
                            S.mm(v3(pS)[:, g, :], Khm[ci][:, g, :], Vm[ci][:, g, :], start=False, stop=True)
                        for g in range(2):
                            S.stt(S32[:, g, :], S32[:, g, :], wcols[ci][:, g, :], v3(pS)[:, g, :], ALU.mult, ALU.add)
                        S.copy(Sb, S32, eng="pool")
                    chk("rwkv_chain", l)
                    for g in range(2):
                        hp = 2 * grp + g
                        pcol = lambda o: pf[:, l, o + hp:o + hp + 1]
                        yb = b256.next()
                        S.copy(yb[:, 0:n], YT[:, g, 0:n], eng="pool")
                        pb = psr.next()
                        S.mm(pb[:, 0:n], bd_bf, yb[:, 0:n])
                        yc = t256.next()
                        S.stt(yc[:, 0:n], pb[:, 0:n], -1.0 / 64, YT[:, g, 0:n], ALU.mult, ALU.add)
                        sq = b256.next()
                        S.actf(sq[:, 0:n], yc[:, 0:n], AF.Square)
                        pb = psr.next()
                        S.mm(pb[:, 0:n], bd_bf, sq[:, 0:n])
                        sd = t256.next()
                        S.actf(sd[:, 0:n], pb[:, 0:n], AF.Sqrt, scale=1.0 / 64, bias=epsb[:, 1:2])
                        S.recip(sd[:, 0:n], sd[:, 0:n])
                        S.tt(yc[:, 0:n], yc[:, 0:n], sd[:, 0:n], ALU.mult)
                        S.ts(yc[:, 0:n], yc[:, 0:n], pcol(PF_LNW), ALU.mult, pcol(PF_LNB), ALU.add)
                        S.tt(yc[:, 0:n], yc[:, 0:n], BON[:, g, 0:n], ALU.add, eng="pool")
                        pb = psr.next()
                        S.mm(pb[:, 0:n], G2[:, hp * 128:(hp + 1) * 128], sig_g[:, c0:c0 + n])
                        S.tt(OZ[:, hp, c0:c0 + n], yc[:, 0:n], pb[:, 0:n], ALU.mult)
                    chk("rwkv_C", l)
                chk("rwkv_seq", l)
                dst = wkvp if seq == 0 else wkvs
                for g in range(2):
                    for h in range(2):
                        S.dma(dst[l, 2 * grp + g, h * 64:(h + 1) * 64, :], S32[h * 64:(h + 1) * 64, g, h * 64:(h + 1) * 64])
        S.dma(shp[l], shp_t)
        S.dma(shs[l], shs_t)

    def phase_gproj(l, w_o, gate_col0, accumulate):
        A.seek(LOC)
        slabs = Ring([A.alloc([8, 256], BF16) for _ in range(2)])
        sgr = Ring([A.alloc([512], F32) for _ in range(3)])
        tr_ = Ring([A.alloc([512], F32) for _ in range(2)])
        gsl = {}

        def issue_g(nt_):
            if nt_ >= 8:
                return
            sl_ = slabs.next()
            wslab(sl_[:, :, 0:128], w_o[l].ap[:, nt_ * 128:(nt_ + 1) * 128])
            wslab(sl_[:, :, 128:256], w_in[l].ap[:, gate_col0 + nt_ * 128:gate_col0 + (nt_ + 1) * 128])
            gsl[nt_] = sl_

        issue_g(0)
        for nt in range(8):
            issue_g(nt + 1)
            sl = gsl.pop(nt)
            for (c0, n) in TQ:
                pa = psr.next()
                for kt in range(8):
                    S.mm(pa[:, 0:n], sl[:, kt, 0:128], OZ[:, kt, c0:c0 + n], start=(kt == 0), stop=(kt == 7))
                pg = psr.next()
                for kt in range(8):
                    S.mm(pg[:, 0:n], sl[:, kt, 128:256], hT[:, kt, c0:c0 + n], start=(kt == 0), stop=(kt == 7))
                sg = sgr.next()
                S.actf(sg[:, 0:n], pg[:, 0:n], AF.Sigmoid)
                if not accumulate:
                    S.tt(MT[:, nt, c0:c0 + n], pa[:, 0:n], sg[:, 0:n], ALU.mult)
                else:
                    t = tr_.next()
                    S.tt(t[:, 0:n], pa[:, 0:n], sg[:, 0:n], ALU.mult)
                    S.tt(MT[:, nt, c0:c0 + n], t[:, 0:n], MT[:, nt, c0:c0 + n], ALU.add, eng="pool")

    def phase_da(l):
        A.seek(LOC)
        Wd_r = Ring([A.alloc([8, 384], BF16) for _ in range(2)])
        qkv_tok = A.alloc([NTB, 384], BF16)
        q_tok = qkv_tok[:, :, 0:128]
        k_tok = qkv_tok[:, :, 128:256]
        v_tok = qkv_tok[:, :, 256:384]
        kc_r = Ring([A.alloc([16, 128], BF16) for _ in range(2)])
        vc_r = Ring([A.alloc([16, 128], BF16) for _ in range(2)])
        KTh = A.alloc([NTOK + TP], BF16)
        Q1p = A.alloc([512], BF16)
        Q2p = A.alloc([512], BF16)
        Pr = Ring([A.alloc([512], BF16) for _ in range(4)])
        f512 = Ring([A.alloc([512], F32) for _ in range(4)])
        qkvr = Ring([A.alloc([384], F32) for _ in range(2)])
        rtmp = Ring([A.alloc([4, 16], F32) for _ in range(4)])
        Lacc = [A.alloc([512], F32) for _ in range(2)]
        S.memset(Q1p, 0.0)
        S.memset(Q2p, 0.0)
        nlam = lamv[:, l * 8:l * 8 + 1]
        subs = lamv[:, l * 8 + 1:l * 8 + 2]
        psOL = [(ps[0], ps[1]), (ps[2], ps[3])]
        pr6 = Ring(ps[4:8])

        def rope_inplace(x, tb):
            x4 = T(x.ap.rearrange("p (m d) -> p m d", m=4), x.bufs)
            cc = cst[:, C_COS + tb * 16:C_COS + tb * 16 + 16]
            ss = cst[:, C_SIN + tb * 16:C_SIN + tb * 16 + 16]
            ccb = T(cc.ap.unsqueeze(1).broadcast_to([128, 4, 16]), cc.bufs)
            ssb = T(ss.ap.unsqueeze(1).broadcast_to([128, 4, 16]), ss.bufs)
            tc_ = rtmp.next()
            ts_ = rtmp.next()
            S.tt(tc_, x4[:, :, 0:16], ccb, ALU.mult)
            S.tt(ts_, x4[:, :, 0:16], ssb, ALU.mult)
            S.tt(x4[:, :, 0:8], tc_[:, :, 0:8], ts_[:, :, 8:16], ALU.subtract)
            S.tt(x4[:, :, 8:16], tc_[:, :, 8:16], ts_[:, :, 0:8], ALU.add)

        def attend(c0, nq, tbs, keyblocks):
            pb = pr6.next()
            pst = T(pb.ap.bitcast(BF16), pb.bufs)
            for i, tb in enumerate(tbs):
                S.tr(pst[:, i * 128:(i + 1) * 128], q_tok[:, tb, :], ident_bf)
            S.copy(Q1p[0:64, 0:nq], pst[0:64, 0:nq], eng="act")
            S.copy(Q2p[64:128, 0:nq], pst[64:128, 0:nq], eng="act")
            o1 = f512.next()
            nk = len(keyblocks)
            steps = [(mp, j) for mp in range(2) for j in range(nk)]
            Ps = {}
            LOOK = 2
            tlast = [None]

            def issue_s(i):
                mp, j = steps[i]
                kap, vap, c_lo, zspec = keyblocks[j]
                Qp = Q1p if mp == 0 else Q2p
                pS = pr6.next()
                S.mm(pS[:, c_lo:nq], kap, Qp[:, c_lo:nq])
                P = Pr.next()
                S.actf(P[:, c_lo:nq], pS[:, c_lo:nq], AF.Exp, scale=0.125)
                if zspec == "diag":
                    S.memset(P[64:128, c_lo:c_lo + 64], 0.0)
                elif zspec == "rows":
                    S.memset(P[32:64, 0:nq], 0.0)
                    S.memset(P[64:128, 0:nq], 0.0)
                Ps[i] = P

            def issue_ol(i):
                mp, j = steps[i]
                kap, vap, c_lo, zspec = keyblocks[j]
                P = Ps.pop(i)
                pO, pL = psOL[mp]
                S.mm(pO[:, c_lo:nq], vap, P[:, c_lo:nq], start=(j == 0), stop=(j == nk - 1))
                eng = "pool" if j % 2 == 0 else "dve"
                La = Lacc[j % 2]
                if j < 2:
                    if c_lo > 0:
                        S.memset(La[:, 0:c_lo], 0.0, eng=eng)
                    S.copy(La[:, c_lo:nq], P[:, c_lo:nq], eng=eng)
                else:
                    S.tt(La[:, c_lo:nq], La[:, c_lo:nq], P[:, c_lo:nq], ALU.add, eng=eng)
                if j == nk - 1:
                    S.mm(pL[:, 0:nq], ones_f, Lacc[0][:, 0:nq], start=True, stop=False)
                    S.mm(pL[:, 0:nq], ones_f, Lacc[1][:, 0:nq], start=False, stop=True)
                    rd = f512.next()
                    S.recip(rd[:, 0:nq], pL[:, 0:nq])
                    if mp == 0:
                        S.tt(o1[:, 0:nq], pO[:, 0:nq], rd[:, 0:nq], ALU.mult)
                    else:
                        t = f512.next()
                        S.tt(t[:, 0:nq], pO[:, 0:nq], rd[:, 0:nq], ALU.mult)
                        S.stt(o1[:, 0:nq], t[:, 0:nq], nlam, o1[:, 0:nq], ALU.mult, ALU.add)
                        tlast[0] = t

            for i in range(len(steps) + LOOK):
                if i < len(steps):
                    issue_s(i)
                if i - LOOK >= 0:
                    issue_ol(i - LOOK)
            t = tlast[0]
            sq = Pr.next()
            S.actf(sq[:, 0:nq], o1[:, 0:nq], AF.Square)
            pb = pr6.next()
            S.mm(pb[:, 0:nq], ones_bf, sq[:, 0:nq])
            sd = t
            S.actf(sd[:, 0:nq], pb[:, 0:nq], AF.Sqrt, scale=1.0 / 128, bias=epsb[:, 0:1])
            S.recip(sd[:, 0:nq], sd[:, 0:nq])
            S.tt(o1[:, 0:nq], o1[:, 0:nq], sd[:, 0:nq], ALU.mult, eng="pool")
            return o1

        hl = {}

        def issue_head(hd_):
            if hd_ >= 8:
                return
            Wd_, kc_, vc_ = Wd_r.next(), kc_r.next(), vc_r.next()
            for x in range(3):
                wslab(Wd_[:, :, x * 128:(x + 1) * 128], w_in[l].ap[:, x * 1024 + hd_ * 128:x * 1024 + (hd_ + 1) * 128])
            S.dma(kc_, T(ck[l].ap[:, hd_ * 128:(hd_ + 1) * 128].rearrange("(j p) c -> p j c", p=128), ()), eng="pool")
            S.dma(vc_, T(cv[l].ap[:, hd_ * 128:(hd_ + 1) * 128].rearrange("(j p) c -> p j c", p=128), ()), eng="pool")
            hl[hd_] = (Wd_, kc_, vc_)

        issue_head(0)
        for hd in range(8):
            issue_head(hd + 1)
            Wd, kc_tok, vc_tok = hl.pop(hd)
            chk("da_load", l)
            for tb in range(NTB):
                if tb == 1:
                    chk("da_tb0", l)
                if tb == 16:
                    chk("da_tb15", l)
                pb = pr6.next()
                for kt in range(8):
                    S.mm(pb[:, 0:384], hT[:, kt, tb * 128:(tb + 1) * 128], Wd[:, kt, :],
                         start=(kt == 0), stop=(kt == 7))
                qkv = qkvr.next()
                S.copy(qkv, pb[:, 0:384], eng="act")
                rope_inplace(qkv[:, 0:256], tb)
                S.copy(qkv_tok[:, tb, :], qkv, eng="pool")
                if tb < 16:
                    S.dma(kp[l, tb * 128:(tb + 1) * 128, hd * 128:(hd + 1) * 128], qkv[:, 128:256])
                    S.dma(vp[l, tb * 128:(tb + 1) * 128, hd * 128:(hd + 1) * 128], qkv[:, 256:384])
                else:
                    S.dma(ks[l, :, hd * 128:(hd + 1) * 128], qkv[0:TSV, 128:256])
                    S.dma(vs[l, :, hd * 128:(hd + 1) * 128], qkv[0:TSV, 256:384])
            chk("da_proj", l)
            srcs = [(k_tok, tb) for tb in range(NTB)] + [(kc_tok, j) for j in range(16)]
            base = 0
            ei = 0
            while base < len(srcs):
                grp_ = srcs[base:base + 8]
                pb = pr6.next()
                pst = T(pb.ap.bitcast(BF16), pb.bufs)
                for i, (src, idx) in enumerate(grp_):
                    S.tr(pst[:, i * 128:(i + 1) * 128], src[:, idx, :], ident_bf)
                S.copy(KTh[:, base * 128:(base + len(grp_)) * 128], pst[:, 0:len(grp_) * 128],
                       eng=("act" if ei % 2 == 0 else "dve"))
                base += len(grp_)
                ei += 1
            chk("da_kt", l)
            for i in range(4):
                if i == 1:
                    chk("da_att0", l)
                kb = []
                for j in range(4 * i + 4):
                    m = j - 4 * i
                    c_lo = 128 * m if m > 0 else 0
                    kb.append((KTh[:, j * 128:(j + 1) * 128], v_tok[:, j, :], c_lo, "diag" if m >= 0 else None))
                o = attend(i * 512, 512, [4 * i + m for m in range(4)], kb)
                S.ts(OZ[:, hd, i * 512:(i + 1) * 512], o[:, 0:512], subs, ALU.mult)
            chk("da_attp", l)
            kb = [(KTh[:, NTOK + j * 128:NTOK + (j + 1) * 128], vc_tok[:, j, :], 0, None) for j in range(16)]
            kb.append((KTh[:, TP:TP + 128], v_tok[:, 16, :], 0, "rows"))
            o = attend(TP, TSV, [16], kb)
            S.ts(OZ[:, hd, TP:TP + TSV], o[:, 0:TSV], subs, ALU.mult)
            chk("da_head0", l)

    def phase_out(l):
        A.seek(LOC)
        nsc = make_norm_scratch()
        wo = A.alloc([8, D], BF16)
        wslab(wo[:, :, 0:512], w_out[l].ap[:, 0:512])
        wslab(wo[:, :, 512:1024], w_out[l].ap[:, 512:1024])
        xr = Ring([A.alloc([D], F32) for _ in range(2)])
        x1r = Ring([A.alloc([D], F32) for _ in range(2)])
        for tb in range(NTB):
            xt = xr.next()
            if l == 0:
                src, nr = x_rows(tb)
                if nr < 128:
                    S.memset(xt, 0.0)
                S.dma(xt[0:nr, :], src)
            else:
                S.dma(xt, X2[tb * 128:(tb + 1) * 128, :])
            x1 = x1r.next()
            for half in range(2):
                pb = psr.next()
                for kt in range(8):
                    S.mm(pb, MT[:, kt, tb * 128:(tb + 1) * 128], wo[:, kt, half * 512:(half + 1) * 512],
                         start=(kt == 0), stop=(kt == 7))
                S.tt(x1[:, half * 512:(half + 1) * 512], pb, xt[:, half * 512:(half + 1) * 512], ALU.add)
            S.dma(X1[tb * 128:(tb + 1) * 128, :], x1)
            norm_to_hT(x1, l, PF_NFFN, tb, nsc)

    def phase_ffn(l):
        A.seek(0)
        A.alloc([8, NTOK], BF16)
        wdn = A.alloc([22, D], BF16)
        GT = A.alloc([22, 512], BF16)
        assert A.off <= LOC
        A.seek(LOC)
        nsc = make_norm_scratch()
        for c in range(2):
            for hh in range(2):
                S.dma(wdn[:, 11 * hh:11 * (hh + 1), c * 512:(c + 1) * 512],
                      T(w_down[l].ap[11 * hh * 128:11 * (hh + 1) * 128, c * 512:(c + 1) * 512].rearrange("(kt p) c -> p kt c", p=128), ()), eng="pool")
        slabs = Ring([A.alloc([8, 512], BF16) for _ in range(3)])
        hpr = Ring([A.alloc([514], F32) for _ in range(4)])
        cr = Ring([A.alloc([512], F32) for _ in range(4)])
        xr = Ring([A.alloc([D], F32) for _ in range(2)])
        x2r = Ring([A.alloc([D], F32) for _ in range(2)])
        yr = Ring([A.alloc([D], F32) for _ in range(2)])
        S.memset(carry_p, 0.0)
        S.dma(T(carry_s.ap.rearrange("p a b -> p (a b)"), carry_s.bufs), cfm[l])
        jobs = [(qi, f2) for qi in range(len(TQ)) for f2 in range(11)]
        slab_of = {}

        def issue_slab(k):
            if k >= len(jobs):
                return
            _, f2_ = jobs[k]
            sl_ = slabs.next()
            wslab(sl_[:, :, 0:256], w_up[l].ap[:, f2_ * 256:(f2_ + 1) * 256])
            wslab(sl_[:, :, 256:512], w_up[l].ap[:, DFF + f2_ * 256:DFF + (f2_ + 1) * 256])
            slab_of[k] = sl_

        issue_slab(0)
        issue_slab(1)
        for qi, (c0, n) in enumerate(TQ):
            carry = carry_p if qi < 4 else carry_s
            nval = n if qi < 4 else TSV
            for f2 in range(11):
                k_ = qi * 11 + f2
                issue_slab(k_ + 2)
                sl = slab_of.pop(k_)
                for fi in range(2):
                    ft = 2 * f2 + fi
                    cs = []
                    for which in range(2):
                        fidx = which * 22 + ft
                        pb = psr.next()
                        off = which * 256 + fi * 128
                        for kt in range(8):
                            S.mm(pb[:, 0:n], sl[:, kt, off:off + 128], hT[:, kt, c0:c0 + n], start=(kt == 0), stop=(kt == 7))
                        hp_ = hpr.next()
                        S.copy(hp_[:, 0:2], carry[:, fidx, :], eng="pool")
                        S.copy(hp_[:, 2:n + 2], pb[:, 0:n], eng="act")
                        S.copy(carry[:, fidx, :], hp_[:, nval:nval + 2], eng="pool")
                        cw = lambda j: pf[:, l, PF_CONV + j * 44 + fidx:PF_CONV + j * 44 + fidx + 1]
                        cb = pf[:, l, PF_CONVB + fidx:PF_CONVB + fidx + 1]
                        c_ = cr.next()
                        eng = "dve" if which == 0 else "pool"
                        S.ts(c_[:, 0:n], hp_[:, 0:n], cw(0), ALU.mult, cb, ALU.add, eng=eng)
                        S.stt(c_[:, 0:n], hp_[:, 1:n + 1], cw(1), c_[:, 0:n], ALU.mult, ALU.add)
                        S.stt(c_[:, 0:n], hp_[:, 2:n + 2], cw(2), c_[:, 0:n], ALU.mult, ALU.add)
                        cs.append(c_)
                    S.actf(cs[0][:, 0:n], cs[0][:, 0:n], AF.Silu)
                    S.tt(GT[:, ft, 0:n], cs[0][:, 0:n], cs[1][:, 0:n], ALU.mult, eng="pool")
            for tbl in range(n // 128):
                tb = c0 // 128 + tbl
                xt = xr.next()
                S.dma(xt, X1[tb * 128:(tb + 1) * 128, :])
                x2 = x2r.next()
                for half in range(2):
                    pb = psr.next()
                    for ft in range(22):
                        S.mm(pb, GT[:, ft, tbl * 128:(tbl + 1) * 128], wdn[:, ft, half * 512:(half + 1) * 512],
                             start=(ft == 0), stop=(ft == 21))
                    S.tt(x2[:, half * 512:(half + 1) * 512], pb, xt[:, half * 512:(half + 1) * 512], ALU.add)
                if l == 1:
                    rstd = norm_stats(x2, nsc)
                    y = yr.next()
                    S.stt(y, x2, rstd, gfin, ALU.mult, ALU.mult)
                    if tb < 16:
                        S.dma(yp[tb * 128:(tb + 1) * 128, :], y)
                    else:
                        S.dma(ys, y[0:TSV, :])
                else:
                    S.dma(X2[tb * 128:(tb + 1) * 128, :], x2)
                    norm_to_hT(x2, 1, PF_NMIX, tb, nsc)
        S.dma(cvp[l], T(carry_p.ap.rearrange("p a b -> p (a b)"), carry_p.bufs))
        S.dma(cvs[l], T(carry_s.ap.rearrange("p a b -> p (a b)"), carry_s.bufs))

    def dump(name, src):
        if name in dbg_out:
            S.barrier()
            S.dma(dbg_out[name], src)
            S.barrier()

    try:
        _program(S, locals())
    except StopBuild:
        pass
    S.emit()
    S.close()
    return nc


def _program(S, L):
    phase_norm0, phase_rwkv, phase_gproj, phase_da, phase_out, phase_ffn = (
        L["phase_norm0"], L["phase_rwkv"], L["phase_gproj"], L["phase_da"], L["phase_out"], L["phase_ffn"])
    dump, stop_after, hT, OZ, MT = L["dump"], L["stop_after"], L["hT"], L["OZ"], L["MT"]
    w_o_rw, w_o_da = L["w_o_rw"], L["w_o_da"]
    S.barrier()
    phase_norm0()
    S.barrier()
    dump("hT0", hT)
    done = False
    for l in range(2):
        if stop_after == ("norm", l):
            break
        phase_rwkv(l)
        S.barrier()
        dump(f"Z{l}", OZ)
        if stop_after == ("rwkv", l):
            break
        phase_gproj(l, w_o_rw, P_DA + P_RW + D, False)
        S.barrier()
        if stop_after == ("gproj1", l):
            break
        phase_da(l)
        S.barrier()
        dump(f"O{l}", OZ)
        if stop_after == ("da", l):
            break
        phase_gproj(l, w_o_da, P_DA + P_RW, True)
        S.barrier()
        dump(f"M{l}", MT)
        phase_out(l)
        S.barrier()
        dump(f"h2T{l}", hT)
        if stop_after == ("out", l):
            break
        phase_ffn(l)
        S.barrier()
        if stop_after == ("ffn", l):
            break


_NC_CACHE = {}


def _prep_inputs(inp):
    f = lambda a: np.ascontiguousarray(np.asarray(a, dtype=np.float32))
    g = {k: f(v) for k, v in inp.items()}
    consts = make_consts()

    def fm(v, nt):
        return v.reshape(nt, 128).T

    pfm = np.zeros((2, 128, NPF), np.float32)
    for l in range(2):
        pfm[l, :, PF_NMIX:PF_NMIX + 8] = fm(g["norm_mix"][l], 8)
        pfm[l, :, PF_NFFN:PF_NFFN + 8] = fm(g["norm_ffn"][l], 8)
        pfm[l, :, PF_MU:PF_MU + 26] = fm(g["rw_mu"][l], 26)
        pfm[l, :, PF_W0:PF_W0 + 8] = fm(g["rw_w0"][l], 8)
        pfm[l, :, PF_A0:PF_A0 + 8] = fm(g["rw_a0"][l], 8)
        pfm[l, :, PF_KK:PF_KK + 8] = fm(g["rw_k_k"][l], 8)
        pfm[l, :, PF_KA:PF_KA + 8] = fm(g["rw_k_a"][l], 8)
        pfm[l, :, PF_RK:PF_RK + 8] = fm(g["rw_r_k"][l].reshape(-1), 8)
        pfm[l, :, PF_LNW:PF_LNW + 8] = fm(g["rw_ln_w"][l], 8)
        pfm[l, :, PF_LNB:PF_LNB + 8] = fm(g["rw_ln_b"][l], 8)
        for j in range(3):
            pfm[l, :, PF_CONV + j * 44:PF_CONV + (j + 1) * 44] = fm(g["ffn_conv"][l, j], 44)
        pfm[l, :, PF_CONVB:PF_CONVB + 44] = fm(g["ffn_conv_b"][l], 44)
        pfm[l, :, PF_SUBLN] = g["da_subln"][l]
    shared = {
        "pfm": pfm, "consts": consts, "w_in": g["w_in"], "da_lambda": g["da_lambda"].reshape(512),
        "w_o_da": g["w_o_da"], "rw_w2": g["rw_w2"], "rw_a2": g["rw_a2"], "rw_g2": g["rw_g2"],
        "w_o_rw": g["w_o_rw"], "w_out": g["w_out"], "w_up": g["w_up"], "w_down": g["w_down"],
        "nfin": g["norm_final"],
    }
    maps = []
    for b in range(8):
        m = dict(shared)
        m["xp"] = g["x_prompt"][b]
        m["xs"] = g["x_sample"][b]
        m["ck"] = np.ascontiguousarray(g["cache_k"][:, b].reshape(2, TP, D))
        m["cv"] = np.ascontiguousarray(g["cache_v"][:, b].reshape(2, TP, D))
        sw = g["state_wkv"][:, b].reshape(2, 8, 2, 64, 64).transpose(0, 1, 2, 4, 3).reshape(2, 8, 128, 64)
        m["swkv"] = np.ascontiguousarray(sw)
        m["sfm"] = np.ascontiguousarray(g["state_shift"][:, b, 0].reshape(2, 26, 128).transpose(0, 2, 1))
        cf = g["state_ffn_conv"][:, b].reshape(2, 2, 44, 128).transpose(0, 3, 2, 1).reshape(2, 128, 88)
        m["cfm"] = np.ascontiguousarray(cf)
        maps.append(m)
    return maps


def _assemble(results):
    def st(name):
        return np.stack([np.asarray(r[name]) for r in results], axis=0)

    y_prompt = st("yp")
    y_sample = st("ys")
    k_prompt = st("kp").transpose(1, 0, 2, 3).reshape(2, 8, TP, 8, 128)
    v_prompt = st("vp").transpose(1, 0, 2, 3).reshape(2, 8, TP, 8, 128)
    k_sample = st("ks").transpose(1, 0, 2, 3).reshape(2, 8, TSV, 8, 128)
    v_sample = st("vs").transpose(1, 0, 2, 3).reshape(2, 8, TSV, 8, 128)

    def wkv(name):
        a = st(name).reshape(8, 2, 8, 2, 64, 64).transpose(1, 0, 2, 3, 5, 4).reshape(2, 8, 16, 64, 64)
        return np.ascontiguousarray(a)

    def shift(name):
        a = st(name).transpose(1, 0, 3, 2).reshape(2, 8, 1, P_RW)
        return np.ascontiguousarray(a)

    def conv(name):
        a = st(name).reshape(8, 2, 128, 44, 2).transpose(1, 0, 4, 3, 2).reshape(2, 8, 2, 2 * DFF)
        return np.ascontiguousarray(a)

    return (np.ascontiguousarray(y_prompt), np.ascontiguousarray(y_sample),
            np.ascontiguousarray(k_prompt), np.ascontiguousarray(v_prompt),
            wkv("wkvp"), shift("shp"), conv("cvp"),
            np.ascontiguousarray(k_sample), np.ascontiguousarray(v_sample),
            wkv("wkvs"), shift("shs"), conv("cvs"))


def kernel(**inputs):
    maps = _prep_inputs(inputs)
    nc = build()
    res = run_bass_kernel_spmd(nc, maps, core_ids=list(range(8)))
    return _assemble(res.results)
```

```python
import numpy as np
from contextlib import ExitStack

import concourse.bass as bass
import concourse.mybir as mybir

F32 = mybir.dt.float32
BF16 = mybir.dt.bfloat16
F32R = mybir.dt.float32r
AF = mybir.ActivationFunctionType
ALU = mybir.AluOpType
AX = mybir.AxisListType

ENGS = ("pe", "act", "dve", "pool", "sp")
SEM_EPOCH = 20000
DMA_ROT = 8


class Buf:
    __slots__ = ("w", "r", "name")

    def __init__(self, name=""):
        self.w = None
        self.r = []
        self.name = name


class T:
    __slots__ = ("ap", "bufs")

    def __init__(self, ap, bufs):
        self.ap = ap
        self.bufs = tuple(bufs)

    def __getitem__(self, key):
        return T(self.ap[key], self.bufs)

    def v(self, ap):
        return T(ap, self.bufs)

    def bitcast(self, dt):
        return T(self.ap.bitcast(dt), self.bufs)


class Rec:
    __slots__ = ("eng", "idx", "fn", "deps", "dma", "signaled", "ev", "selfwait")

    def __init__(self, eng, idx, fn, dma):
        self.eng = eng
        self.idx = idx
        self.fn = fn
        self.deps = []
        self.dma = dma
        self.signaled = False
        self.ev = None
        self.selfwait = None


class Sched:
    def __init__(self, nc):
        self.nc = nc
        self.q = {e: [] for e in ENGS}
        self.stack = ExitStack()
        self.nbuf = 0
        self.same_engine_sync = True
        self._bar_pos = {}
        self._pending = {e: [] for e in ENGS}

    def sbuf(self, name, shape, dtype, nbufs=1):
        h = self.stack.enter_context(self.nc.sbuf_tensor(name, list(shape), dtype))
        return T(h[:], [Buf(name)])

    def psum(self, name, shape, dtype=F32):
        h = self.stack.enter_context(self.nc.psum_tensor(name, list(shape), dtype))
        return T(h[:], [Buf(name)])

    def dram(self, name, shape, dtype, kind):
        h = self.nc.dram_tensor(name, list(shape), dtype, kind=kind)
        return T(h.ap(), [Buf(name)])

    def newbuf(self, name=""):
        return Buf(name)

    def add(self, eng, fn, reads=(), writes=(), dma=False):
        q = self.q[eng]
        rec = Rec(eng, len(q), fn, dma)
        deps = {}
        for t in reads:
            for b in t.bufs:
                if b.w is not None:
                    deps[id(b.w)] = b.w
        for t in writes:
            for b in t.bufs:
                if b.w is not None:
                    deps[id(b.w)] = b.w
                for r in b.r:
                    deps[id(r)] = r
        if self._pending[eng]:
            for d in self._pending[eng]:
                deps[id(d)] = d
            self._pending[eng] = []
            barrier_deps = True
        else:
            barrier_deps = False
        for d in deps.values():
            if d is rec:
                continue
            if d.eng == eng and not d.dma and not dma:
                if eng == "pe" or eng == "sp" or not self.same_engine_sync:
                    continue
            rec.deps.append(d)
        for t in reads:
            for b in t.bufs:
                b.r.append(rec)
        for t in writes:
            for b in t.bufs:
                b.w = rec
                b.r = []
        q.append(rec)
        return rec

    def emit(self):
        nc = self.nc
        for e in ENGS:
            for rec in self.q[e]:
                for d in rec.deps:
                    d.signaled = True
        sems = {}
        for e in ENGS:
            cnt = 0
            for rec in self.q[e]:
                if rec.dma:
                    continue
                if rec.signaled:
                    ep = cnt // SEM_EPOCH
                    key = (e, ep)
                    if key not in sems:
                        sems[key] = self.stack.enter_context(nc.semaphore(f"s_{e}_{ep}"))
                    rec.ev = (sems[key], cnt % SEM_EPOCH + 1, key)
                    cnt += 1
        self.final_waits = []
        for e in ENGS:
            j = 0
            last = {}
            for rec in self.q[e]:
                if not rec.dma:
                    continue
                slot = j % DMA_ROT
                key = ("dma", e, slot)
                if key not in sems:
                    sems[key] = self.stack.enter_context(nc.semaphore(f"d_{e}_{slot}"))
                val = 16 * (j // DMA_ROT + 1)
                rec.ev = (sems[key], val, key)
                if j >= DMA_ROT:
                    rec.selfwait = (sems[key], val - 16, key)
                last[key] = (sems[key], val, key)
                j += 1
            self.final_waits.extend(last.values())

        block = self.stack.enter_context(nc.Block())
        sched = self

        def run(engname, eobj):
            waited = {}
            for rec in sched.q[engname]:
                waits = {}
                if rec.selfwait is not None:
                    s, v, key = rec.selfwait
                    waits[key] = (s, v)
                for d in rec.deps:
                    s, v, key = d.ev
                    if key in waits:
                        if waits[key][1] < v:
                            waits[key] = (s, v)
                    else:
                        waits[key] = (s, v)
                for key, (s, v) in waits.items():
                    if waited.get(key, 0) >= v:
                        continue
                    eobj.wait_ge(s, v)
                    waited[key] = v
                ins = rec.fn(eobj)
                if rec.dma:
                    ins.then_inc(rec.ev[0], 16)
                elif rec.signaled:
                    ins.then_inc(rec.ev[0], 1)
            if engname == "sp":
                for s, v, key in sched.final_waits:
                    eobj.wait_ge(s, v)

        @block.tensor
        def _(e):
            run("pe", e)

        @block.scalar
        def _(e):
            run("act", e)

        @block.vector
        def _(e):
            run("dve", e)

        @block.gpsimd
        def _(e):
            run("pool", e)

        @block.sync
        def _(e):
            run("sp", e)

    def close(self):
        self.stack.close()

    def dma(self, out, in_, eng="sp", **kw):
        return self.add(eng, lambda e: e.dma_start(out=out.ap, in_=in_.ap, **kw),
                        reads=[in_], writes=[out], dma=True)

    def mm(self, out, lhsT, rhs, start=True, stop=True, extra_reads=(), **kw):
        return self.add("pe", lambda e: e.matmul(out.ap, lhsT.ap, rhs.ap, start=start, stop=stop, **kw),
                        reads=[lhsT, rhs, *extra_reads], writes=[out])

    def tr(self, out, in_, ident, **kw):
        return self.add("pe", lambda e: e.transpose(out.ap, in_.ap, ident.ap, **kw),
                        reads=[in_, ident], writes=[out])

    def actf(self, out, in_, func, bias=None, scale=None, accum=None, eng="act"):
        kw = {}
        reads = [in_]
        writes = [out]
        if bias is not None:
            if isinstance(bias, T):
                kw["bias"] = bias.ap
                reads.append(bias)
            else:
                kw["bias"] = bias
        if scale is not None:
            if isinstance(scale, T):
                kw["scale"] = scale.ap
                reads.append(scale)
            else:
                kw["scale"] = scale
        if accum is not None:
            kw["accum_out"] = accum.ap
            writes.append(accum)
        return self.add("act", lambda e: e.activation(out.ap, in_.ap, func, **kw), reads=reads, writes=writes)

    def tt(self, out, a, b, op, eng="dve"):
        return self.add(eng, lambda e: e.tensor_tensor(out.ap, a.ap, b.ap, op), reads=[a, b], writes=[out])

    def ts(self, out, a, s1, op0, s2=None, op1=None, accum=None, eng="dve"):
        reads = [a]
        writes = [out]
        s1v = s1.ap if isinstance(s1, T) else s1
        s2v = s2.ap if isinstance(s2, T) else s2
        if isinstance(s1, T):
            reads.append(s1)
        if isinstance(s2, T):
            reads.append(s2)
        kw = {}
        if op1 is not None:
            kw["op1"] = op1
        if accum is not None:
            kw["accum_out"] = accum.ap
            writes.append(accum)
        return self.add(eng, lambda e: e.tensor_scalar(out.ap, a.ap, s1v, s2v, op0, **kw), reads=reads, writes=writes)

    def stt(self, out, a, s, b, op0, op1, eng="dve"):
        reads = [a, b]
        sv = s.ap if isinstance(s, T) else s
        if isinstance(s, T):
            reads.append(s)
        return self.add(eng, lambda e: e.scalar_tensor_tensor(out.ap, a.ap, sv, b.ap, op0, op1), reads=reads, writes=[out])

    def copy(self, out, in_, eng="dve"):
        if eng == "act":
            return self.add("act", lambda e: e.copy(out.ap, in_.ap), reads=[in_], writes=[out])
        return self.add(eng, lambda e: e.tensor_copy(out.ap, in_.ap), reads=[in_], writes=[out])

    def memset(self, out, val, eng="pool"):
        return self.add(eng, lambda e: e.memset(out.ap, val), reads=[], writes=[out])

    def recip(self, out, in_):
        return self.add("dve", lambda e: e.reciprocal(out.ap, in_.ap), reads=[in_], writes=[out])

    def scan(self, out, d0, d1, init, op0, op1):
        reads = [d0, d1]
        iv = init.ap if isinstance(init, T) else init
        if isinstance(init, T):
            reads.append(init)
        return self.add("dve", lambda e: e.tensor_tensor_scan(out.ap, d0.ap, d1.ap, iv, op0, op1), reads=reads, writes=[out])


def _barrier(self):
    deps = []
    for e in ENGS:
        q = self.q[e]
        last_c = None
        for rec in reversed(q):
            if not rec.dma:
                last_c = rec
                break
        if last_c is not None:
            deps.append(last_c)
        for rec in q[self._bar_pos.get(e, 0):]:
            if rec.dma:
                deps.append(rec)
        self._bar_pos[e] = len(q)
    for e in ENGS:
        self._pending[e] = list(deps)


Sched.barrier = _barrier


from concourse.bass_utils import run_bass_kernel_spmd

D = 1024
TP = 2048
TSV = 32
NTOK = 2176
NTB = 17
P_DA = 3072
P_RW = 3328
PTOT = 8448
DFF = 2816
EPS = 1e-6
GN_EPS = 64e-5
ROPE_THETA = 500000.0
TQ = [(0, 512), (512, 512), (1024, 512), (1536, 512), (2048, 128)]
PF_NMIX, PF_NFFN, PF_MU, PF_W0, PF_A0, PF_KK, PF_KA, PF_RK, PF_LNW, PF_LNB = 0, 8, 16, 42, 50, 58, 66, 74, 82, 90
PF_CONV, PF_CONVB, PF_SUBLN, NPF = 98, 230, 274, 275
C_ID, C_BD, C_MU, C_ML, C_MUI, C_RST, C_COS, C_SIN, NCONST = 0, 128, 256, 384, 512, 576, 832, 1104, 1376
DEC_C = -0.6065306597126334


def make_consts():
    c = np.zeros((128, NCONST), np.float32)
    c[:, C_ID:C_ID + 128] = np.eye(128, dtype=np.float32)
    p = np.arange(128)
    h = p // 64
    s = p % 64
    bd = (h[:, None] == h[None, :]).astype(np.float32)
    c[:, C_BD:C_BD + 128] = bd
    c[:, C_MU:C_MU + 128] = bd * (s[:, None] < s[None, :])
    c[:, C_ML:C_ML + 128] = bd * (s[:, None] > s[None, :])
    c[:, C_MUI:C_MUI + 64] = (s[:, None] <= np.arange(64)[None, :])
    rst = np.ones(256, np.float32)
    rst[::64] = 0
    c[:, C_RST:C_RST + 256] = rst[None, :]
    inv = (np.float32(ROPE_THETA) ** (-np.arange(0, 16, 2, dtype=np.float32) / np.float32(16))).astype(np.float32)
    for tb in range(NTB):
        pos = (tb * 128 + p) if tb < 16 else (TP + p)
        ang = pos.astype(np.float32)[:, None] * inv[None, :]
        co = np.cos(ang).astype(np.float32)
        si = np.sin(ang).astype(np.float32)
        c[:, C_COS + tb * 16:C_COS + tb * 16 + 8] = co
        c[:, C_COS + tb * 16 + 8:C_COS + tb * 16 + 16] = co
        c[:, C_SIN + tb * 16:C_SIN + tb * 16 + 8] = si
        c[:, C_SIN + tb * 16 + 8:C_SIN + tb * 16 + 16] = si
    return c


CHAIN_BF16 = True
LACC_ENG = ("pool", "dve")


def r32(t):
    if CHAIN_BF16:
        return t
    return T(t.ap.bitcast(F32R), t.bufs)


class Arena:
    def __init__(self, t, width):
        self.t = t
        self.W = width
        self.off = 0

    def seek(self, off):
        self.off = off

    def alloc(self, free_shape, dtype):
        n = 1
        for d in free_shape:
            n *= d
        words = n if dtype != BF16 else (n + 1) // 2
        words = (words + 7) // 8 * 8
        assert self.off + words <= self.W, ("arena overflow", self.off, words, self.W)
        ap = self.t.ap[:, self.off:self.off + words]
        if dtype == BF16:
            ap = ap.bitcast(BF16)
        ap = ap[:, 0:n]
        if len(free_shape) > 1:
            names = [f"d{i}" for i in range(len(free_shape))]
            pat = "p (" + " ".join(names) + ") -> p " + " ".join(names)
            kw = {nm: sz for nm, sz in zip(names[:-1], free_shape[:-1])}
            ap = ap.rearrange(pat, **kw)
        self.off += words
        return T(ap, [Buf()])


def _run_interleaved(gens):
    alive = list(gens)
    while alive:
        for gen in list(alive):
            try:
                next(gen)
            except StopIteration:
                alive.remove(gen)


class Ring:
    def __init__(self, items):
        self.items = items
        self.i = 0

    def next(self):
        t = self.items[self.i % len(self.items)]
        self.i += 1
        return t


class StopBuild(Exception):
    pass


def build(dbg=None, stop_after=None):
    def chk(tag, l=0):
        if stop_after == (tag, l):
            raise StopBuild()

    nc = bass.Bass("TRN2", target_bir_lowering=False)
    S = Sched(nc)

    def din(name, shape):
        return S.dram(name, shape, F32, "ExternalInput")

    def dout(name, shape):
        return S.dram(name, shape, F32, "ExternalOutput")

    xp = din("xp", [TP, D])
    xs = din("xs", [TSV, D])
    ck = din("ck", [2, TP, D])
    cv = din("cv", [2, TP, D])
    swkv = din("swkv", [2, 8, 128, 64])
    sfm = din("sfm", [2, 128, 26])
    cfm = din("cfm", [2, 128, 88])
    pfm = din("pfm", [2, 128, NPF])
    consts = din("consts", [128, NCONST])
    w_in = din("w_in", [2, D, PTOT])
    da_lambda = din("da_lambda", [512])
    w_o_da = din("w_o_da", [2, D, D])
    rw_w2 = din("rw_w2", [2, 64, D])
    rw_a2 = din("rw_a2", [2, 64, D])
    rw_g2 = din("rw_g2", [2, 128, D])
    w_o_rw = din("w_o_rw", [2, D, D])
    w_out = din("w_out", [2, D, D])
    w_up = din("w_up", [2, D, 2 * DFF])
    w_down = din("w_down", [2, DFF, D])
    nfin = din("nfin", [D])

    yp = dout("yp", [TP, D])
    ys = dout("ys", [TSV, D])
    kp = dout("kp", [2, TP, D])
    vp = dout("vp", [2, TP, D])
    wkvp = dout("wkvp", [2, 8, 128, 64])
    shp = dout("shp", [2, 128, 26])
    cvp = dout("cvp", [2, 128, 88])
    ks = dout("ks", [2, TSV, D])
    vs = dout("vs", [2, TSV, D])
    wkvs = dout("wkvs", [2, 8, 128, 64])
    shs = dout("shs", [2, 128, 26])
    cvs = dout("cvs", [2, 128, 88])
    X1 = S.dram("X1", [NTOK, D], F32, "Internal")
    X2 = S.dram("X2", [NTOK, D], F32, "Internal")
    dbg_out = {}
    if dbg:
        for name, shape in dbg.items():
            dbg_out[name] = dout("dbg_" + name, shape)

    cst = S.sbuf("cst", [128, NCONST], F32)
    S.dma(cst, consts)
    ident_bf = S.sbuf("ident_bf", [128, 128], BF16)
    S.copy(ident_bf, cst[:, C_ID:C_ID + 128])
    ones_bf = S.sbuf("ones_bf", [128, 128], BF16)
    S.memset(ones_bf, 1.0)
    ones_f = S.sbuf("ones_f", [128, 128], F32)
    S.memset(ones_f, 1.0)
    ident_f = cst[:, C_ID:C_ID + 128]
    bdmask = cst[:, C_BD:C_BD + 128]
    bd_bf = S.sbuf("bd_bf", [128, 128], BF16)
    S.copy(bd_bf, cst[:, C_BD:C_BD + 128])
    fr = None if CHAIN_BF16 else S.sbuf("fr", [128, 21, 256], F32)
    pf = S.sbuf("pf", [128, 2, NPF], F32)
    for l in range(2):
        S.dma(pf[:, l, :], pfm[l])
    gfin = S.sbuf("gfin", [128, D], F32)
    S.dma(gfin, nfin.v(nfin.ap.partition_broadcast(128)))
    lamb = S.sbuf("lamb", [128, 512], F32)
    S.dma(lamb, da_lambda.v(da_lambda.ap.partition_broadcast(128)))
    lamv = S.sbuf("lamv", [128, 16], F32)
    epsb = S.sbuf("epsb", [128, 2], F32)
    S.memset(epsb[:, 0:1], EPS)
    S.memset(epsb[:, 1:2], GN_EPS)
    ljunk = S.sbuf("ljunk", [128, 64], F32)
    shp_t = S.sbuf("shp_t", [128, 26], F32)
    shs_t = S.sbuf("shs_t", [128, 26], F32)
    carry_p = S.sbuf("carry_p", [128, 44, 2], F32)
    carry_s = S.sbuf("carry_s", [128, 44, 2], F32)

    ps = [S.psum(f"ps{i}", [128, 512], F32) for i in range(8)]
    psr = Ring(ps)
    halves = []
    for i in range(8):
        halves.append(T(ps[i].ap[:, 0:256], ps[i].bufs))
        halves.append(T(ps[i].ap[:, 256:512], ps[i].bufs))

    AW = ((nc.sbuf_bytes_remaining - 256) // 4) // 8 * 8
    ar = S.sbuf("arena", [128, AW], F32)
    A = Arena(ar, AW)
    hT = A.alloc([8, NTOK], BF16)
    OZ = A.alloc([8, NTOK], BF16)
    MT_OFF = A.off
    MT = A.alloc([8, NTOK], BF16)
    LOC = A.off
    OZ_OFF = MT_OFF - (MT_OFF - 0) // 2 if False else None
    S.memset(hT, 0.0)
    S.memset(OZ, 0.0, eng="dve")
    S.memset(MT, 0.0)

    for l in range(2):
        lam_init = 0.8 - 0.6 * float(np.exp(-0.3 * l))
        b = l * 256
        pr = ljunk
        S.tt(pr, lamb[:, b:b + 64], lamb[:, b + 64:b + 128], ALU.mult)
        S.ts(pr, pr, 1.0, ALU.mult, None, ALU.add, accum=lamv[:, l * 8 + 4:l * 8 + 5])
        S.tt(pr, lamb[:, b + 128:b + 192], lamb[:, b + 192:b + 256], ALU.mult)
        S.ts(pr, pr, 1.0, ALU.mult, None, ALU.add, accum=lamv[:, l * 8 + 5:l * 8 + 6])
        S.actf(lamv[:, l * 8 + 2:l * 8 + 4], lamv[:, l * 8 + 4:l * 8 + 6], AF.Exp)
        S.tt(lamv[:, l * 8:l * 8 + 1], lamv[:, l * 8 + 3:l * 8 + 4], lamv[:, l * 8 + 2:l * 8 + 3], ALU.subtract)
        S.ts(lamv[:, l * 8:l * 8 + 1], lamv[:, l * 8:l * 8 + 1], -lam_init, ALU.add)
        S.ts(lamv[:, l * 8 + 1:l * 8 + 2], pf[:, l, PF_SUBLN:PF_SUBLN + 1], 1.0 - lam_init, ALU.mult)

    def wslab(dst, src_ap):
        return S.dma(dst, T(src_ap.rearrange("(kt p) c -> p kt c", p=128), ()), eng="pool")

    def make_norm_scratch():
        d = {}
        d["junk"] = A.alloc([D], BF16)
        d["xn"] = Ring([A.alloc([D], BF16) for _ in range(2)])
        d["st"] = Ring([A.alloc([4], F32) for _ in range(3)])
        return d

    def norm_stats(x_t, nsc):
        st = nsc["st"].next()
        S.actf(nsc["junk"], x_t, AF.Square, accum=st[:, 0:1])
        S.actf(st[:, 1:2], st[:, 0:1], AF.Sqrt, scale=1.0 / D, bias=epsb[:, 0:1])
        S.recip(st[:, 2:3], st[:, 1:2])
        return st[:, 2:3]

    def norm_to_hT(x_t, l, pfoff, tb, nsc):
        rstd = norm_stats(x_t, nsc)
        xn = nsc["xn"].next()
        S.ts(xn, x_t, rstd, ALU.mult)
        pb = psr.next()
        pst = T(pb.ap.bitcast(BF16).rearrange("p (k t) -> p k t", k=8), pb.bufs)
        for kt in range(8):
            S.tr(pst[:, kt, :], xn[:, kt * 128:(kt + 1) * 128], ident_bf)
        g = pf[:, l, pfoff:pfoff + 8]
        S.tt(hT[:, :, tb * 128:(tb + 1) * 128], pst, g.v(g.ap.unsqueeze(2).broadcast_to([128, 8, 128])), ALU.mult)

    def x_rows(tb):
        return (xp[tb * 128:(tb + 1) * 128, :], 128) if tb < 16 else (xs, TSV)

    def phase_norm0():
        A.seek(LOC)
        nsc = make_norm_scratch()
        xr = Ring([A.alloc([D], F32) for _ in range(3)])
        for tb in range(NTB):
            xt = xr.next()
            src, nr = x_rows(tb)
            if nr < 128:
                S.memset(xt, 0.0)
            S.dma(xt[0:nr, :], src)
            norm_to_hT(xt, 0, PF_NMIX, tb, nsc)

    def phase_rwkv(l):
        A.seek(MT_OFF)
        w_l = w_in[l]
        mu = lambda c: pf[:, l, PF_MU + c:PF_MU + c + 1]
        W2p = A.alloc([D], BF16)
        A2p = A.alloc([D], BF16)
        G2 = A.alloc([D], BF16)
        S.memset(W2p[64:128, :], 0.0)
        S.memset(A2p[0:64, :], 0.0)
        S.dma(W2p[0:64, :], rw_w2[l], eng="pool")
        S.dma(A2p[64:128, :], rw_a2[l], eng="pool")
        S.dma(G2, rw_g2[l], eng="pool")
        tanh_w = A.alloc([NTOK], BF16)
        raw_w = A.alloc([NTOK], BF16)
        sig_g = A.alloc([NTOK], BF16)
        sfm_t = A.alloc([26], F32)
        S.dma(sfm_t, sfm[l])
        NB = 256
        GRP_OFF = A.off
        Wl = A.alloc([8, 256], BF16)
        wslab(Wl, w_l.ap[:, P_DA + 3072:P_DA + 3328])
        u_ring = Ring([A.alloc([513], F32) for _ in range(3)])
        tmp = Ring([A.alloc([512], F32) for _ in range(6)])

        def shifted(psb, n, carry_src, mucol, u_out_last=None, last_idx=None):
            U = u_ring.next()
            if carry_src is None:
                S.memset(U[:, 0:1], 0.0, eng="dve")
            else:
                S.copy(U[:, 0:1], carry_src, eng="dve")
            S.copy(U[:, 1:n + 1], psb[:, 0:n], eng="act")
            d = tmp.next()
            S.tt(d[:, 0:n], U[:, 0:n], U[:, 1:n + 1], ALU.subtract)
            return d, U

        prev = {0: None, 1: None}
        for qi, (c0, n) in enumerate(TQ):
            for which in range(2):
                pb = psr.next()
                for kt in range(8):
                    S.mm(pb[:, 0:n], Wl[:, kt, which * 128:(which + 1) * 128], hT[:, kt, c0:c0 + n],
                         start=(kt == 0), stop=(kt == 7))
                mc = 24 + which
                if qi == 0:
                    carry = None
                elif qi == 4:
                    carry = sfm_t[:, mc:mc + 1]
                else:
                    carry = prev[which]
                d, U = shifted(pb, n, carry, mc)
                us = tmp.next()
                S.stt(us[:, 0:n], d[:, 0:n], mu(mc), U[:, 1:n + 1], ALU.mult, ALU.add)
                prev[which] = U[:, n:n + 1]
                if qi == 3:
                    S.copy(shp_t[:, mc:mc + 1], U[:, n:n + 1], eng="pool")
                if qi == 4:
                    S.copy(shs_t[:, mc:mc + 1], U[:, TSV:TSV + 1], eng="pool")
                if which == 0:
                    S.actf(tanh_w[:, c0:c0 + n], us[:, 0:n], AF.Tanh)
                    S.copy(raw_w[:, c0:c0 + n], us[:, 0:n], eng="pool")
                else:
                    S.actf(sig_g[:, c0:c0 + n], us[:, 0:n], AF.Sigmoid)

        chk("rwkv_lora", l)
        S.barrier()
        A.seek(GRP_OFF)
        Wg = A.alloc([8, 768], BF16)
        AT = A.alloc([2, NB], BF16)
        BT = A.alloc([2, NB], BF16)
        KT_ = A.alloc([2, NB], BF16)
        RT = A.alloc([2, NB], BF16)
        VT = A.alloc([2, NB], BF16)
        WT = A.alloc([2, NB], F32)
        BON = A.alloc([2, NB], F32)
        YT = A.alloc([2, NB], F32)
        S32 = A.alloc([2, 128], F32)
        Sb = A.alloc([2, 128], BF16)
        swt = A.alloc([2, 64], F32)
        carr = A.alloc([8], F32)
        t256g = [Ring([A.alloc([NB], F32) for _ in range(11)]) for _ in range(2)]
        b256g = [Ring([A.alloc([NB], BF16) for _ in range(3)]) for _ in range(2)]
        usrkg = [Ring([A.alloc([NB], F32) for _ in range(2)]) for _ in range(2)]
        u257g = [Ring([A.alloc([NB + 1], F32) for _ in range(2)]) for _ in range(2)]
        NCH = 4
        wide = lambda: A.alloc([2, 128], BF16)
        BDa = [wide() for _ in range(NCH)]
        BDb = [wide() for _ in range(NCH)]
        BDk = [wide() for _ in range(NCH)]
        BDx = Ring([wide() for _ in range(3)])
        AKm = [wide() for _ in range(NCH)]
        Vm = [wide() for _ in range(NCH)]
        Bhm = [wide() for _ in range(NCH)]
        Khm = [wide() for _ in range(NCH)]
        Rm = [wide() for _ in range(NCH)]
        Um = Ring([wide() for _ in range(1)])
        fi = [0]

        def frt():
            if CHAIN_BF16:
                return wide()
            t = T(fr.ap[:, fi[0], :].rearrange("p (g c) -> p g c", g=2), [Buf()])
            fi[0] += 1
            return t

        Nm = [[frt() for _ in range(NCH)] for _ in range(2)]
        NTm = [[frt() for _ in range(NCH)] for _ in range(2)]
        Pm = [frt() for _ in range(NCH)]
        Hm = Ring([frt() for _ in range(1)])
        class _HB:
            def __init__(self):
                self.k = 0
                self.pend = None

            def next(self):
                if self.pend is not None:
                    t = self.pend
                    self.pend = None
                    return t
                i = self.k % 8
                self.k += 1
                self.pend = halves[2 * i + 1]
                return halves[2 * i]

            def newbank(self):
                self.pend = None

        hbr = _HB()

        def w4(t):
            return T(t.ap.rearrange("p g (h t) -> p g h t", h=2), t.bufs)

        def v3(t):
            return T(t.ap.rearrange("p (g c) -> p g c", g=2), t.bufs)

        def v3b(t):
            return T(t.ap.bitcast(BF16)[:, 0:256].rearrange("p (g c) -> p g c", g=2), t.bufs)

        def bcast_mask(coff):
            m = cst[:, coff:coff + 128]
            return T(m.ap.unsqueeze(1).broadcast_to([128, 2, 128]), m.bufs)

        mU_b = bcast_mask(C_MU)
        mL_b = bcast_mask(C_ML)
        id_b = bcast_mask(C_ID)
        bd4 = T(bdmask.ap.rearrange("p (h t) -> p h t", h=2).unsqueeze(1).broadcast_to([128, 2, 2, 64]), bdmask.bufs)
        mui = cst[:, C_MUI:C_MUI + 64]
        mui4 = T(mui.ap.unsqueeze(1).unsqueeze(1).broadcast_to([128, 2, 2, 64]), mui.bufs)
        rstm = cst[:, C_RST:C_RST + NB]

        for grp in range(4):
            for x in range(3):
                wslab(Wg[:, :, x * 256:(x + 1) * 256],
                      w_l.ap[:, P_DA + x * 1024 + grp * 256:P_DA + x * 1024 + (grp + 1) * 256])
            for seq in range(2):
                blocks = [(i * NB, NB) for i in range(8)] if seq == 0 else [(TP, 64)]
                nvalid_last = NB if seq == 0 else TSV
                if seq == 0:
                    S.memset(S32, 0.0, eng="dve")
                    S.memset(Sb, 0.0, eng="dve")
                else:
                    for g in range(2):
                        S.dma(swt[:, g, :], swkv[l, 2 * grp + g])
                    S.tt(w4(S32), swt.v(swt.ap.unsqueeze(2).broadcast_to([128, 2, 2, 64])), bd4, ALU.mult)
                    S.copy(Sb, S32)
                carry = {}
                for bi, (c0, n) in enumerate(blocks):
                    nch = n // 64
                    def stage_a(g, t256=None, b256=None, usrk=None, u257=None):
                        t256, b256, usrk, u257 = t256g[g], b256g[g], usrkg[g], u257g[g]
                        hp = 2 * grp + g
                        us = {}
                        for x in range(3):
                            pb = psr.next()
                            off = x * 256 + g * 128
                            for kt in range(8):
                                S.mm(pb[:, 0:n], Wg[:, kt, off:off + 128], hT[:, kt, c0:c0 + n],
                                     start=(kt == 0), stop=(kt == 7))
                            mc = x * 8 + hp
                            U = u257.next()
                            if bi == 0:
                                if seq == 0:
                                    S.memset(U[:, 0:1], 0.0, eng="dve")
                                else:
                                    S.copy(U[:, 0:1], sfm_t[:, mc:mc + 1], eng="dve")
                            else:
                                S.copy(U[:, 0:1], carry[(g, x)], eng="dve")
                            S.copy(U[:, 1:n + 1], pb[:, 0:n], eng="act")
                            d = t256.next()
                            S.tt(d[:, 0:n], U[:, 0:n], U[:, 1:n + 1], ALU.subtract)
                            ut = usrk.next() if x < 2 else t256.next()
                            us[x] = ut[:, 0:n]
                            S.stt(ut[:, 0:n], d[:, 0:n], mu(mc), U[:, 1:n + 1], ALU.mult, ALU.add)
                            S.copy(carr[:, g * 3 + x:g * 3 + x + 1], U[:, n:n + 1], eng="pool")
                            carry[(g, x)] = carr[:, g * 3 + x:g * 3 + x + 1]
                            yield
                            if bi == len(blocks) - 1:
                                sht = shp_t if seq == 0 else shs_t
                                S.copy(sht[:, mc:mc + 1], U[:, nvalid_last:nvalid_last + 1], eng="pool")
                        us_r, us_k, us_v = us[0], us[1], us[2]
                        S.copy(VT[:, g, 0:n], us_v, eng="pool")
                        yield
                        pcol = lambda o: pf[:, l, o + hp:o + hp + 1]
                        pb = psr.next()
                        S.mm(pb[:, 0:n], W2p[:, hp * 128:(hp + 1) * 128], tanh_w[:, c0:c0 + n])
                        yield
                        sg = t256.next()
                        S.actf(sg[:, 0:n], pb[:, 0:n], AF.Sigmoid, bias=pcol(PF_W0))
                        yield
                        lw = t256.next()
                        S.ts(lw[:, 0:n], sg[:, 0:n], DEC_C, ALU.mult, eng="pool")
                        yield
                        pb = psr.next()
                        S.mm(pb[:, 0:n], A2p[:, hp * 128:(hp + 1) * 128], raw_w[:, c0:c0 + n])
                        yield
                        a_t = t256.next()
                        S.actf(a_t[:, 0:n], pb[:, 0:n], AF.Sigmoid, bias=pcol(PF_A0))
                        yield
                        kk = t256.next()
                        S.ts(kk[:, 0:n], us_k, pcol(PF_KK), ALU.mult)
                        yield
                        sq = b256.next()
                        S.actf(sq[:, 0:n], kk[:, 0:n], AF.Square)
                        yield
                        pb = psr.next()
                        S.mm(pb[:, 0:n], bd_bf, sq[:, 0:n])
                        yield
                        rn = t256.next()
                        S.actf(rn[:, 0:n], pb[:, 0:n], AF.Sqrt)
                        yield
                        S.ts(rn[:, 0:n], rn[:, 0:n], 1e-12, ALU.max)
                        yield
                        S.recip(rn[:, 0:n], rn[:, 0:n])
                        yield
                        kkn = kk
                        S.tt(kkn[:, 0:n], kk[:, 0:n], rn[:, 0:n], ALU.mult)
                        yield
                        t1 = t256.next()
                        S.ts(t1[:, 0:n], a_t[:, 0:n], -1.0, ALU.add, pcol(PF_KA), ALU.mult)
                        yield
                        kmod = t256.next()
                        S.stt(kmod[:, 0:n], t1[:, 0:n], 1.0, us_k, ALU.add, ALU.mult)
                        yield
                        bb = t1
                        S.tt(bb[:, 0:n], kkn[:, 0:n], a_t[:, 0:n], ALU.mult, eng="pool")
                        yield
                        cum = t256.next()
                        S.scan(cum[:, 0:n], rstm[:, 0:n], lw[:, 0:n], 0.0, ALU.mult, ALU.add)
                        yield
                        cumex = sg
                        S.tt(cumex[:, 0:n], cum[:, 0:n], lw[:, 0:n], ALU.subtract, eng="pool")
                        yield
                        S.actf(WT[:, g, 0:n], cum[:, 0:n], AF.Exp)
                        yield
                        winv = lw
                        S.actf(winv[:, 0:n], cum[:, 0:n], AF.Exp, scale=-1.0)
                        yield
                        wex = cum
                        S.actf(wex[:, 0:n], cumex[:, 0:n], AF.Exp)
                        yield
                        S.stt(AT[:, g, 0:n], kkn[:, 0:n], -1.0, wex[:, 0:n], ALU.mult, ALU.mult)
                        yield
                        S.tt(BT[:, g, 0:n], bb[:, 0:n], winv[:, 0:n], ALU.mult)
                        yield
                        S.tt(KT_[:, g, 0:n], kmod[:, 0:n], winv[:, 0:n], ALU.mult)
                        yield
                        S.tt(RT[:, g, 0:n], us_r, WT[:, g, 0:n], ALU.mult)
                        yield
                        rk = b256.next()
                        S.stt(rk[:, 0:n], us_r, pcol(PF_RK), kmod[:, 0:n], ALU.mult, ALU.mult)
                        yield
                        pb = psr.next()
                        S.mm(pb[:, 0:n], bd_bf, rk[:, 0:n])
                        yield
                        S.tt(BON[:, g, 0:n], pb[:, 0:n], us_v, ALU.mult)
                        yield
                        if seq == 1:
                            for arr in (AT, BT, KT_, RT, VT):
                                S.memset(arr[:, g, TSV:64], 0.0)

                    _run_interleaved([stage_a(0), stage_a(1)])
                    chk("rwkv_A", l)
                    def chunk_src(arr, ci):
                        a = arr[:, :, ci * 64:(ci + 1) * 64]
                        return T(a.ap.unsqueeze(2).broadcast_to([128, 2, 2, 64]), a.bufs)

                    wcols = []
                    for ci in range(nch):
                        lastc = ci * 64 + (63 if seq == 0 else TSV - 1)
                        wc = WT[:, :, lastc:lastc + 1]
                        wcols.append(wc)
                        wcb = T(wc.ap.broadcast_to([128, 2, 128]), wc.bufs)
                        S.tt(w4(BDa[ci]), chunk_src(AT, ci), bd4, ALU.mult)
                        S.tt(w4(BDb[ci]), chunk_src(BT, ci), bd4, ALU.mult, eng="pool")
                        S.tt(w4(BDk[ci]), chunk_src(KT_, ci), bd4, ALU.mult)
                        bdv = BDx.next()
                        S.tt(w4(bdv), chunk_src(VT, ci), bd4, ALU.mult, eng="pool")
                        bdbh = BDx.next()
                        S.tt(bdbh, BDb[ci], wcb, ALU.mult)
                        bdkh = BDx.next()
                        S.tt(bdkh, BDk[ci], wcb, ALU.mult, eng="pool")
                        chk("B1a", l)
                        hbr.newbank()
                        pN, pNT, pAK, pR, pV, pBh, pKh = [hbr.next() for _ in range(7)]
                        hbr.newbank()
                        for g in range(2):
                            S.mm(v3(pN)[:, g, :], BDb[ci][:, g, :], BDa[ci][:, g, :])
                        for g in range(2):
                            S.mm(v3(pNT)[:, g, :], BDa[ci][:, g, :], BDb[ci][:, g, :])
                        for g in range(2):
                            S.mm(v3(pAK)[:, g, :], BDk[ci][:, g, :], BDa[ci][:, g, :])
                        chk("B1b", l)
                        pR4 = T(pR.ap.rearrange("p (g x t) -> p g x t", g=2, x=2), pR.bufs)
                        for g in range(2):
                            S.mm(pR4[:, g, 0, :], BDb[ci][:, g, :], RT[:, g, ci * 64:(ci + 1) * 64])
                            S.mm(pR4[:, g, 1, :], BDk[ci][:, g, :], RT[:, g, ci * 64:(ci + 1) * 64])
                        chk("B1c", l)
                        for g in range(2):
                            S.tr(v3b(pV)[:, g, :], bdv[:, g, :], ident_bf)
                            S.tr(v3b(pBh)[:, g, :], bdbh[:, g, :], ident_bf)
                            S.tr(v3b(pKh)[:, g, :], bdkh[:, g, :], ident_bf)
                        chk("B1d", l)
                        S.tt(r32(Nm[0][ci]), v3(pN), mU_b, ALU.mult)
                        S.tt(r32(NTm[0][ci]), v3(pNT), mL_b, ALU.mult)
                        S.tt(AKm[ci], v3(pAK), mU_b, ALU.mult)
                        S.tt(w4(Rm[ci]), pR4, mui4, ALU.mult)
                        chk("B1e", l)
                        S.copy(Vm[ci], v3b(pV), eng="act")
                        S.copy(Bhm[ci], v3b(pBh), eng="act")
                        S.copy(Khm[ci], v3b(pKh), eng="act")
                        chk("B1f", l)
                        S.tt(r32(Pm[ci]), Nm[0][ci], id_b, ALU.add)
                        chk("B1g", l)
                        if ci == 1:
                            chk("B1h", l)
                    chk("rwkv_B1", l)
                    cur = 0
                    for rd in range(1, 6):
                        nxt = 1 - cur
                        pairs = []
                        for ci in range(nch):
                            hbr.newbank()
                            pN2 = hbr.next() if rd < 5 else None
                            pNT2 = hbr.next()
                            for g in range(2):
                                if rd < 5:
                                    S.mm(v3(pN2)[:, g, :], r32(NTm[cur][ci][:, g, :]), r32(Nm[cur][ci][:, g, :]))
                                S.mm(v3(pNT2)[:, g, :], r32(Nm[cur][ci][:, g, :]), r32(NTm[cur][ci][:, g, :]))
                            pairs.append((pN2, pNT2))
                        for ci in range(nch):
                            pN2, pNT2 = pairs[ci]
                            S.copy(r32(NTm[nxt][ci]), v3(pNT2), eng="dve")
                            if rd < 5:
                                S.copy(r32(Nm[nxt][ci]), v3(pN2), eng="dve")
                        pps = []
                        for ci in range(nch):
                            hbr.newbank()
                            pP = hbr.next()
                            for g in range(2):
                                S.mm(v3(pP)[:, g, :], r32(NTm[nxt][ci][:, g, :]), r32(Pm[ci][:, g, :]), start=True, stop=False)
                                S.mm(v3(pP)[:, g, :], ident_bf, r32(Pm[ci][:, g, :]), start=False, stop=True)
                            pps.append(pP)
                        for ci in range(nch):
                            S.copy(r32(Pm[ci]), v3(pps[ci]), eng="act")
                        cur = nxt
                    chk("rwkv_inv", l)
                    for ci in range(nch):
                        hbr.newbank()
                        pH = hbr.next()
                        for g in range(2):
                            S.mm(v3(pH)[:, g, :], BDa[ci][:, g, :], Sb[:, g, :], start=True, stop=False)
                            S.mm(v3(pH)[:, g, :], AKm[ci][:, g, :], Vm[ci][:, g, :], start=False, stop=True)
                        H = Hm.next()
                        S.copy(r32(H), v3(pH), eng="act")
                        hbr.newbank()
                        pU = hbr.next()
                        for g in range(2):
                            S.mm(v3(pU)[:, g, :], r32(Pm[ci][:, g, :]), r32(H[:, g, :]))
                        U = Um.next()
                        S.copy(U, v3(pU), eng="dve")
                        hbr.newbank()
                        pY = hbr.next()
                        pY3 = T(pY.ap[:, 0:128].rearrange("p (g t) -> p g t", g=2), pY.bufs)
                        R4 = w4(Rm[ci])
                        for g in range(2):
                            S.mm(pY3[:, g, :], Sb[:, g, :], RT[:, g, ci * 64:(ci + 1) * 64], start=True, stop=False)
                            S.mm(pY3[:, g, :], U[:, g, :], R4[:, g, 0, :], start=False, stop=False)
                            S.mm(pY3[:, g, :], Vm[ci][:, g, :], R4[:, g, 1, :], start=False, stop=True)
                        S.copy(YT[:, :, ci * 64:(ci + 1) * 64], pY3, eng="act")
                        hbr.newbank()
                        pS = hbr.next()
                        for g in range(2):
                            S.mm(v3(pS)[:, g, :], Bhm[ci][:, g, :], U[:, g, :], start=True, stop=False)
                            S.mm(v3(pS)[:, g, :], Khm[ci][:, g, :], Vm[ci][:, g, :], start=False, stop=True)
                        for g in range(2):
                            S.stt(S32[:, g, :], S32[:, g, :], wcols[ci][:, g, :], v3(pS)[:, g, :], ALU.mult, ALU.add)
                        S.copy(Sb, S32, eng="pool")
                    chk("rwkv_chain", l)
                    def stage_c(g):
                        t256, b256 = t256g[g], b256g[g]
                        hp = 2 * grp + g
                        pcol = lambda o: pf[:, l, o + hp:o + hp + 1]
                        yb = b256.next()
                        S.copy(yb[:, 0:n], YT[:, g, 0:n], eng="pool")
                        yield
                        pb = psr.next()
                        S.mm(pb[:, 0:n], bd_bf, yb[:, 0:n])
                        yield
                        yc = t256.next()
                        S.stt(yc[:, 0:n], pb[:, 0:n], -1.0 / 64, YT[:, g, 0:n], ALU.mult, ALU.add)
                        yield
                        sq = b256.next()
                        S.actf(sq[:, 0:n], yc[:, 0:n], AF.Square)
                        yield
                        pb = psr.next()
                        S.mm(pb[:, 0:n], bd_bf, sq[:, 0:n])
                        yield
                        sd = t256.next()
                        S.actf(sd[:, 0:n], pb[:, 0:n], AF.Sqrt, scale=1.0 / 64, bias=epsb[:, 1:2])
                        yield
                        S.recip(sd[:, 0:n], sd[:, 0:n])
                        yield
                        S.tt(yc[:, 0:n], yc[:, 0:n], sd[:, 0:n], ALU.mult)
                        yield
                        S.ts(yc[:, 0:n], yc[:, 0:n], pcol(PF_LNW), ALU.mult, pcol(PF_LNB), ALU.add)
                        yield
                        S.tt(yc[:, 0:n], yc[:, 0:n], BON[:, g, 0:n], ALU.add, eng="pool")
                        yield
                        pb = psr.next()
                        S.mm(pb[:, 0:n], G2[:, hp * 128:(hp + 1) * 128], sig_g[:, c0:c0 + n])
                        yield
                        S.tt(OZ[:, hp, c0:c0 + n], yc[:, 0:n], pb[:, 0:n], ALU.mult)
                        yield

                    _run_interleaved([stage_c(0), stage_c(1)])
                    chk("rwkv_C", l)
                chk("rwkv_seq", l)
                dst = wkvp if seq == 0 else wkvs
                for g in range(2):
                    for h in range(2):
                        S.dma(dst[l, 2 * grp + g, h * 64:(h + 1) * 64, :], S32[h * 64:(h + 1) * 64, g, h * 64:(h + 1) * 64])
        S.dma(shp[l], shp_t)
        S.dma(shs[l], shs_t)

    def phase_gproj(l, w_o, gate_col0, accumulate):
        A.seek(LOC)
        slabs = Ring([A.alloc([8, 256], BF16) for _ in range(2)])
        sgr = Ring([A.alloc([512], F32) for _ in range(3)])
        tr_ = Ring([A.alloc([512], F32) for _ in range(2)])
        gsl = {}

        def issue_g(nt_):
            if nt_ >= 8:
                return
            sl_ = slabs.next()
            wslab(sl_[:, :, 0:128], w_o[l].ap[:, nt_ * 128:(nt_ + 1) * 128])
            wslab(sl_[:, :, 128:256], w_in[l].ap[:, gate_col0 + nt_ * 128:gate_col0 + (nt_ + 1) * 128])
            gsl[nt_] = sl_

        issue_g(0)
        for nt in range(8):
            issue_g(nt + 1)
            sl = gsl.pop(nt)
            for (c0, n) in TQ:
                pa = psr.next()
                for kt in range(8):
                    S.mm(pa[:, 0:n], sl[:, kt, 0:128], OZ[:, kt, c0:c0 + n], start=(kt == 0), stop=(kt == 7))
                pg = psr.next()
                for kt in range(8):
                    S.mm(pg[:, 0:n], sl[:, kt, 128:256], hT[:, kt, c0:c0 + n], start=(kt == 0), stop=(kt == 7))
                sg = sgr.next()
                S.actf(sg[:, 0:n], pg[:, 0:n], AF.Sigmoid)
                if not accumulate:
                    S.tt(MT[:, nt, c0:c0 + n], pa[:, 0:n], sg[:, 0:n], ALU.mult)
                else:
                    t = tr_.next()
                    S.tt(t[:, 0:n], pa[:, 0:n], sg[:, 0:n], ALU.mult)
                    S.tt(MT[:, nt, c0:c0 + n], t[:, 0:n], MT[:, nt, c0:c0 + n], ALU.add, eng="pool")

    def phase_da(l):
        A.seek(LOC)
        Wd_r = Ring([A.alloc([8, 384], BF16) for _ in range(2)])
        qkv_tok = A.alloc([NTB, 384], BF16)
        q_tok = qkv_tok[:, :, 0:128]
        k_tok = qkv_tok[:, :, 128:256]
        v_tok = qkv_tok[:, :, 256:384]
        kc_r = Ring([A.alloc([16, 128], BF16) for _ in range(2)])
        vc_r = Ring([A.alloc([16, 128], BF16) for _ in range(2)])
        KTh = A.alloc([NTOK + TP], BF16)
        Q1p = A.alloc([512], BF16)
        Q2p = A.alloc([512], BF16)
        Pr = Ring([A.alloc([512], BF16) for _ in range(4)])
        f512 = Ring([A.alloc([512], F32) for _ in range(4)])
        qkvr = Ring([A.alloc([384], F32) for _ in range(2)])
        rtmp = Ring([A.alloc([4, 16], F32) for _ in range(4)])
        Lacc = [A.alloc([512], F32) for _ in range(2)]
        Pd = [A.alloc([512], BF16) for _ in range(4)]
        Pz = A.alloc([512], BF16)
        for m_ in range(4):
            S.memset(Pd[m_], 0.0)
        S.memset(Pz, 0.0)
        S.memset(Q1p, 0.0)
        S.memset(Q2p, 0.0)
        nlam = lamv[:, l * 8:l * 8 + 1]
        subs = lamv[:, l * 8 + 1:l * 8 + 2]
        psOL = [(ps[0], ps[1]), (ps[2], ps[3])]
        pr6 = Ring(ps[4:8])

        def rope_inplace(x, tb):
            x4 = T(x.ap.rearrange("p (m d) -> p m d", m=4), x.bufs)
            cc = cst[:, C_COS + tb * 16:C_COS + tb * 16 + 16]
            ss = cst[:, C_SIN + tb * 16:C_SIN + tb * 16 + 16]
            ccb = T(cc.ap.unsqueeze(1).broadcast_to([128, 4, 16]), cc.bufs)
            ssb = T(ss.ap.unsqueeze(1).broadcast_to([128, 4, 16]), ss.bufs)
            tc_ = rtmp.next()
            ts_ = rtmp.next()
            S.tt(tc_, x4[:, :, 0:16], ccb, ALU.mult)
            S.tt(ts_, x4[:, :, 0:16], ssb, ALU.mult)
            S.tt(x4[:, :, 0:8], tc_[:, :, 0:8], ts_[:, :, 8:16], ALU.subtract)
            S.tt(x4[:, :, 8:16], tc_[:, :, 8:16], ts_[:, :, 0:8], ALU.add)

        def attend(c0, nq, tbs, keyblocks):
            pb = pr6.next()
            pst = T(pb.ap.bitcast(BF16), pb.bufs)
            for i, tb in enumerate(tbs):
                S.tr(pst[:, i * 128:(i + 1) * 128], q_tok[:, tb, :], ident_bf)
            S.copy(Q1p[0:64, 0:nq], pst[0:64, 0:nq], eng="act")
            S.copy(Q2p[64:128, 0:nq], pst[64:128, 0:nq], eng="act")
            o1 = f512.next()
            nk = len(keyblocks)
            steps = [(mp, j) for mp in range(2) for j in range(nk)]
            Ps = {}
            LOOK = 2
            tlast = [None]

            def issue_s(i):
                mp, j = steps[i]
                kap, vap, c_lo, zspec = keyblocks[j]
                Qp = Q1p if mp == 0 else Q2p
                pS = pr6.next()
                S.mm(pS[:, c_lo:nq], kap, Qp[:, c_lo:nq])
                if zspec == "diag":
                    P = Pd[c_lo // 128]
                    S.actf(P[0:64, c_lo:nq], pS[0:64, c_lo:nq], AF.Exp, scale=0.125)
                    S.actf(P[64:128, c_lo + 64:nq], pS[64:128, c_lo + 64:nq], AF.Exp, scale=0.125)
                elif zspec == "rows":
                    P = Pz
                    S.actf(P[0:32, 0:nq], pS[0:32, 0:nq], AF.Exp, scale=0.125)
                else:
                    P = Pr.next()
                    S.actf(P[:, c_lo:nq], pS[:, c_lo:nq], AF.Exp, scale=0.125)
                Ps[i] = P

            def issue_ol(i):
                mp, j = steps[i]
                kap, vap, c_lo, zspec = keyblocks[j]
                P = Ps.pop(i)
                pO, pL = psOL[mp]
                S.mm(pO[:, c_lo:nq], vap, P[:, c_lo:nq], start=(j == 0), stop=(j == nk - 1))
                eng = LACC_ENG[j % 2]
                La = Lacc[j % 2]
                if j < 2:
                    if c_lo > 0:
                        S.memset(La[:, 0:c_lo], 0.0, eng=eng)
                    S.copy(La[:, c_lo:nq], P[:, c_lo:nq], eng=eng)
                else:
                    S.tt(La[:, c_lo:nq], La[:, c_lo:nq], P[:, c_lo:nq], ALU.add, eng=eng)
                if j == nk - 1:
                    Lb = Pr.next()
                    S.tt(Lb[:, 0:nq], Lacc[0][:, 0:nq], Lacc[1][:, 0:nq], ALU.add, eng="pool")
                    S.mm(pL[:, 0:nq], ones_bf, Lb[:, 0:nq])
                    rd = f512.next()
                    S.recip(rd[:, 0:nq], pL[:, 0:nq])
                    if mp == 0:
                        S.tt(o1[:, 0:nq], pO[:, 0:nq], rd[:, 0:nq], ALU.mult)
                    else:
                        t = f512.next()
                        S.tt(t[:, 0:nq], pO[:, 0:nq], rd[:, 0:nq], ALU.mult)
                        S.stt(o1[:, 0:nq], t[:, 0:nq], nlam, o1[:, 0:nq], ALU.mult, ALU.add)
                        tlast[0] = t

            for i in range(len(steps) + LOOK):
                if i < len(steps):
                    issue_s(i)
                if i - LOOK >= 0:
                    issue_ol(i - LOOK)
            t = tlast[0]
            sq = Pr.next()
            S.actf(sq[:, 0:nq], o1[:, 0:nq], AF.Square)
            pb = pr6.next()
            S.mm(pb[:, 0:nq], ones_bf, sq[:, 0:nq])
            sd = t
            S.actf(sd[:, 0:nq], pb[:, 0:nq], AF.Sqrt, scale=1.0 / 128, bias=epsb[:, 0:1])
            S.recip(sd[:, 0:nq], sd[:, 0:nq])
            S.tt(o1[:, 0:nq], o1[:, 0:nq], sd[:, 0:nq], ALU.mult, eng="pool")
            return o1

        hl = {}

        def issue_head(hd_):
            if hd_ >= 8:
                return
            Wd_, kc_, vc_ = Wd_r.next(), kc_r.next(), vc_r.next()
            for x in range(3):
                wslab(Wd_[:, :, x * 128:(x + 1) * 128], w_in[l].ap[:, x * 1024 + hd_ * 128:x * 1024 + (hd_ + 1) * 128])
            S.dma(kc_, T(ck[l].ap[:, hd_ * 128:(hd_ + 1) * 128].rearrange("(j p) c -> p j c", p=128), ()), eng="pool")
            S.dma(vc_, T(cv[l].ap[:, hd_ * 128:(hd_ + 1) * 128].rearrange("(j p) c -> p j c", p=128), ()), eng="pool")
            hl[hd_] = (Wd_, kc_, vc_)

        issue_head(0)
        for hd in range(8):
            issue_head(hd + 1)
            Wd, kc_tok, vc_tok = hl.pop(hd)
            chk("da_load", l)
            for tb in range(NTB):
                if tb == 1:
                    chk("da_tb0", l)
                if tb == 16:
                    chk("da_tb15", l)
                pb = pr6.next()
                for kt in range(8):
                    S.mm(pb[:, 0:384], hT[:, kt, tb * 128:(tb + 1) * 128], Wd[:, kt, :],
                         start=(kt == 0), stop=(kt == 7))
                qkv = qkvr.next()
                S.copy(qkv, pb[:, 0:384], eng="act")
                rope_inplace(qkv[:, 0:256], tb)
                S.copy(qkv_tok[:, tb, :], qkv, eng="pool")
                if tb < 16:
                    S.dma(kp[l, tb * 128:(tb + 1) * 128, hd * 128:(hd + 1) * 128], qkv[:, 128:256])
                    S.dma(vp[l, tb * 128:(tb + 1) * 128, hd * 128:(hd + 1) * 128], qkv[:, 256:384])
                else:
                    S.dma(ks[l, :, hd * 128:(hd + 1) * 128], qkv[0:TSV, 128:256])
                    S.dma(vs[l, :, hd * 128:(hd + 1) * 128], qkv[0:TSV, 256:384])
            chk("da_proj", l)
            srcs = [(k_tok, tb) for tb in range(NTB)] + [(kc_tok, j) for j in range(16)]
            base = 0
            ei = 0
            while base < len(srcs):
                grp_ = srcs[base:base + 8]
                pb = pr6.next()
                pst = T(pb.ap.bitcast(BF16), pb.bufs)
                for i, (src, idx) in enumerate(grp_):
                    S.tr(pst[:, i * 128:(i + 1) * 128], src[:, idx, :], ident_bf)
                S.copy(KTh[:, base * 128:(base + len(grp_)) * 128], pst[:, 0:len(grp_) * 128],
                       eng=("act" if ei % 2 == 0 else "dve"))
                base += len(grp_)
                ei += 1
            chk("da_kt", l)
            for i in range(4):
                if i == 1:
                    chk("da_att0", l)
                kb = []
                for j in range(4 * i + 4):
                    m = j - 4 * i
                    c_lo = 128 * m if m > 0 else 0
                    kb.append((KTh[:, j * 128:(j + 1) * 128], v_tok[:, j, :], c_lo, "diag" if m >= 0 else None))
                o = attend(i * 512, 512, [4 * i + m for m in range(4)], kb)
                S.ts(OZ[:, hd, i * 512:(i + 1) * 512], o[:, 0:512], subs, ALU.mult)
            chk("da_attp", l)
            kb = [(KTh[:, NTOK + j * 128:NTOK + (j + 1) * 128], vc_tok[:, j, :], 0, None) for j in range(16)]
            kb.append((KTh[:, TP:TP + 128], v_tok[:, 16, :], 0, "rows"))
            o = attend(TP, TSV, [16], kb)
            S.ts(OZ[:, hd, TP:TP + TSV], o[:, 0:TSV], subs, ALU.mult)
            chk("da_head0", l)

    def phase_out(l):
        A.seek(LOC)
        nsc = make_norm_scratch()
        wo = A.alloc([8, D], BF16)
        wslab(wo[:, :, 0:512], w_out[l].ap[:, 0:512])
        wslab(wo[:, :, 512:1024], w_out[l].ap[:, 512:1024])
        xr = Ring([A.alloc([D], F32) for _ in range(2)])
        x1r = Ring([A.alloc([D], F32) for _ in range(2)])
        for tb in range(NTB):
            xt = xr.next()
            if l == 0:
                src, nr = x_rows(tb)
                if nr < 128:
                    S.memset(xt, 0.0)
                S.dma(xt[0:nr, :], src)
            else:
                S.dma(xt, X2[tb * 128:(tb + 1) * 128, :])
            x1 = x1r.next()
            for half in range(2):
                pb = psr.next()
                for kt in range(8):
                    S.mm(pb, MT[:, kt, tb * 128:(tb + 1) * 128], wo[:, kt, half * 512:(half + 1) * 512],
                         start=(kt == 0), stop=(kt == 7))
                S.tt(x1[:, half * 512:(half + 1) * 512], pb, xt[:, half * 512:(half + 1) * 512], ALU.add)
            S.dma(X1[tb * 128:(tb + 1) * 128, :], x1)
            norm_to_hT(x1, l, PF_NFFN, tb, nsc)

    def phase_ffn(l):
        A.seek(0)
        A.alloc([8, NTOK], BF16)
        wdn = A.alloc([22, D], BF16)
        GT = A.alloc([22, 512], BF16)
        assert A.off <= LOC
        A.seek(LOC)
        nsc = make_norm_scratch()
        for c in range(2):
            for hh in range(2):
                S.dma(wdn[:, 11 * hh:11 * (hh + 1), c * 512:(c + 1) * 512],
                      T(w_down[l].ap[11 * hh * 128:11 * (hh + 1) * 128, c * 512:(c + 1) * 512].rearrange("(kt p) c -> p kt c", p=128), ()), eng="pool")
        slabs = Ring([A.alloc([8, 512], BF16) for _ in range(3)])
        hpr = Ring([A.alloc([514], F32) for _ in range(4)])
        cr = Ring([A.alloc([512], F32) for _ in range(4)])
        xr = Ring([A.alloc([D], F32) for _ in range(2)])
        x2r = Ring([A.alloc([D], F32) for _ in range(2)])
        yr = Ring([A.alloc([D], F32) for _ in range(2)])
        S.memset(carry_p, 0.0)
        S.dma(T(carry_s.ap.rearrange("p a b -> p (a b)"), carry_s.bufs), cfm[l])
        jobs = [(qi, f2) for qi in range(len(TQ)) for f2 in range(11)]
        slab_of = {}

        def issue_slab(k):
            if k >= len(jobs):
                return
            _, f2_ = jobs[k]
            sl_ = slabs.next()
            wslab(sl_[:, :, 0:256], w_up[l].ap[:, f2_ * 256:(f2_ + 1) * 256])
            wslab(sl_[:, :, 256:512], w_up[l].ap[:, DFF + f2_ * 256:DFF + (f2_ + 1) * 256])
            slab_of[k] = sl_

        issue_slab(0)
        issue_slab(1)
        for qi, (c0, n) in enumerate(TQ):
            carry = carry_p if qi < 4 else carry_s
            nval = n if qi < 4 else TSV
            for f2 in range(11):
                k_ = qi * 11 + f2
                issue_slab(k_ + 2)
                sl = slab_of.pop(k_)
                for fi in range(2):
                    ft = 2 * f2 + fi
                    cs = []
                    for which in range(2):
                        fidx = which * 22 + ft
                        pb = psr.next()
                        off = which * 256 + fi * 128
                        for kt in range(8):
                            S.mm(pb[:, 0:n], sl[:, kt, off:off + 128], hT[:, kt, c0:c0 + n], start=(kt == 0), stop=(kt == 7))
                        hp_ = hpr.next()
                        S.copy(hp_[:, 0:2], carry[:, fidx, :], eng="pool")
                        S.copy(hp_[:, 2:n + 2], pb[:, 0:n], eng="act")
                        S.copy(carry[:, fidx, :], hp_[:, nval:nval + 2], eng="pool")
                        cw = lambda j: pf[:, l, PF_CONV + j * 44 + fidx:PF_CONV + j * 44 + fidx + 1]
                        cb = pf[:, l, PF_CONVB + fidx:PF_CONVB + fidx + 1]
                        c_ = cr.next()
                        eng = "dve" if which == 0 else "pool"
                        S.ts(c_[:, 0:n], hp_[:, 0:n], cw(0), ALU.mult, cb, ALU.add, eng=eng)
                        S.stt(c_[:, 0:n], hp_[:, 1:n + 1], cw(1), c_[:, 0:n], ALU.mult, ALU.add)
                        S.stt(c_[:, 0:n], hp_[:, 2:n + 2], cw(2), c_[:, 0:n], ALU.mult, ALU.add)
                        cs.append(c_)
                    S.actf(cs[0][:, 0:n], cs[0][:, 0:n], AF.Silu)
                    S.tt(GT[:, ft, 0:n], cs[0][:, 0:n], cs[1][:, 0:n], ALU.mult, eng="pool")
            for tbl in range(n // 128):
                tb = c0 // 128 + tbl
                xt = xr.next()
                S.dma(xt, X1[tb * 128:(tb + 1) * 128, :])
                x2 = x2r.next()
                for half in range(2):
                    pb = psr.next()
                    for ft in range(22):
                        S.mm(pb, GT[:, ft, tbl * 128:(tbl + 1) * 128], wdn[:, ft, half * 512:(half + 1) * 512],
                             start=(ft == 0), stop=(ft == 21))
                    S.tt(x2[:, half * 512:(half + 1) * 512], pb, xt[:, half * 512:(half + 1) * 512], ALU.add)
                if l == 1:
                    rstd = norm_stats(x2, nsc)
                    y = yr.next()
                    S.stt(y, x2, rstd, gfin, ALU.mult, ALU.mult)
                    if tb < 16:
                        S.dma(yp[tb * 128:(tb + 1) * 128, :], y)
                    else:
                        S.dma(ys, y[0:TSV, :])
                else:
                    S.dma(X2[tb * 128:(tb + 1) * 128, :], x2)
                    norm_to_hT(x2, 1, PF_NMIX, tb, nsc)
        S.dma(cvp[l], T(carry_p.ap.rearrange("p a b -> p (a b)"), carry_p.bufs))
        S.dma(cvs[l], T(carry_s.ap.rearrange("p a b -> p (a b)"), carry_s.bufs))

    def dump(name, src):
        if name in dbg_out:
            S.barrier()
            S.dma(dbg_out[name], src)
            S.barrier()

    try:
        _program(S, locals())
    except StopBuild:
        pass
    S.emit()
    S.close()
    return nc


def _program(S, L):
    phase_norm0, phase_rwkv, phase_gproj, phase_da, phase_out, phase_ffn = (
        L["phase_norm0"], L["phase_rwkv"], L["phase_gproj"], L["phase_da"], L["phase_out"], L["phase_ffn"])
    dump, stop_after, hT, OZ, MT = L["dump"], L["stop_after"], L["hT"], L["OZ"], L["MT"]
    w_o_rw, w_o_da = L["w_o_rw"], L["w_o_da"]
    S.barrier()
    phase_norm0()
    S.barrier()
    dump("hT0", hT)
    done = False
    for l in range(2):
        if stop_after == ("norm", l):
            break
        phase_rwkv(l)
        S.barrier()
        dump(f"Z{l}", OZ)
        if stop_after == ("rwkv", l):
            break
        phase_gproj(l, w_o_rw, P_DA + P_RW + D, False)
        S.barrier()
        if stop_after == ("gproj1", l):
            break
        phase_da(l)
        S.barrier()
        dump(f"O{l}", OZ)
        if stop_after == ("da", l):
            break
        phase_gproj(l, w_o_da, P_DA + P_RW, True)
        S.barrier()
        dump(f"M{l}", MT)
        phase_out(l)
        S.barrier()
        dump(f"h2T{l}", hT)
        if stop_after == ("out", l):
            break
        phase_ffn(l)
        S.barrier()
        if stop_after == ("ffn", l):
            break


_NC_CACHE = {}


def _prep_inputs(inp):
    f = lambda a: np.ascontiguousarray(np.asarray(a, dtype=np.float32))
    g = {k: f(v) for k, v in inp.items()}
    consts = make_consts()

    def fm(v, nt):
        return v.reshape(nt, 128).T

    pfm = np.zeros((2, 128, NPF), np.float32)
    for l in range(2):
        pfm[l, :, PF_NMIX:PF_NMIX + 8] = fm(g["norm_mix"][l], 8)
        pfm[l, :, PF_NFFN:PF_NFFN + 8] = fm(g["norm_ffn"][l], 8)
        pfm[l, :, PF_MU:PF_MU + 26] = fm(g["rw_mu"][l], 26)
        pfm[l, :, PF_W0:PF_W0 + 8] = fm(g["rw_w0"][l], 8)
        pfm[l, :, PF_A0:PF_A0 + 8] = fm(g["rw_a0"][l], 8)
        pfm[l, :, PF_KK:PF_KK + 8] = fm(g["rw_k_k"][l], 8)
        pfm[l, :, PF_KA:PF_KA + 8] = fm(g["rw_k_a"][l], 8)
        pfm[l, :, PF_RK:PF_RK + 8] = fm(g["rw_r_k"][l].reshape(-1), 8)
        pfm[l, :, PF_LNW:PF_LNW + 8] = fm(g["rw_ln_w"][l], 8)
        pfm[l, :, PF_LNB:PF_LNB + 8] = fm(g["rw_ln_b"][l], 8)
        for j in range(3):
            pfm[l, :, PF_CONV + j * 44:PF_CONV + (j + 1) * 44] = fm(g["ffn_conv"][l, j], 44)
        pfm[l, :, PF_CONVB:PF_CONVB + 44] = fm(g["ffn_conv_b"][l], 44)
        pfm[l, :, PF_SUBLN] = g["da_subln"][l]
    shared = {
        "pfm": pfm, "consts": consts, "w_in": g["w_in"], "da_lambda": g["da_lambda"].reshape(512),
        "w_o_da": g["w_o_da"], "rw_w2": g["rw_w2"], "rw_a2": g["rw_a2"], "rw_g2": g["rw_g2"],
        "w_o_rw": g["w_o_rw"], "w_out": g["w_out"], "w_up": g["w_up"], "w_down": g["w_down"],
        "nfin": g["norm_final"],
    }
    maps = []
    for b in range(8):
        m = dict(shared)
        m["xp"] = g["x_prompt"][b]
        m["xs"] = g["x_sample"][b]
        m["ck"] = np.ascontiguousarray(g["cache_k"][:, b].reshape(2, TP, D))
        m["cv"] = np.ascontiguousarray(g["cache_v"][:, b].reshape(2, TP, D))
        sw = g["state_wkv"][:, b].reshape(2, 8, 2, 64, 64).transpose(0, 1, 2, 4, 3).reshape(2, 8, 128, 64)
        m["swkv"] = np.ascontiguousarray(sw)
        m["sfm"] = np.ascontiguousarray(g["state_shift"][:, b, 0].reshape(2, 26, 128).transpose(0, 2, 1))
        cf = g["state_ffn_conv"][:, b].reshape(2, 2, 44, 128).transpose(0, 3, 2, 1).reshape(2, 128, 88)
        m["cfm"] = np.ascontiguousarray(cf)
        maps.append(m)
    return maps


def _assemble(results):
    def st(name):
        return np.stack([np.asarray(r[name]) for r in results], axis=0)

    y_prompt = st("yp")
    y_sample = st("ys")
    k_prompt = st("kp").transpose(1, 0, 2, 3).reshape(2, 8, TP, 8, 128)
    v_prompt = st("vp").transpose(1, 0, 2, 3).reshape(2, 8, TP, 8, 128)
    k_sample = st("ks").transpose(1, 0, 2, 3).reshape(2, 8, TSV, 8, 128)
    v_sample = st("vs").transpose(1, 0, 2, 3).reshape(2, 8, TSV, 8, 128)

    def wkv(name):
        a = st(name).reshape(8, 2, 8, 2, 64, 64).transpose(1, 0, 2, 3, 5, 4).reshape(2, 8, 16, 64, 64)
        return np.ascontiguousarray(a)

    def shift(name):
        a = st(name).transpose(1, 0, 3, 2).reshape(2, 8, 1, P_RW)
        return np.ascontiguousarray(a)

    def conv(name):
        a = st(name).reshape(8, 2, 128, 44, 2).transpose(1, 0, 4, 3, 2).reshape(2, 8, 2, 2 * DFF)
        return np.ascontiguousarray(a)

    return (np.ascontiguousarray(y_prompt), np.ascontiguousarray(y_sample),
            np.ascontiguousarray(k_prompt), np.ascontiguousarray(v_prompt),
            wkv("wkvp"), shift("shp"), conv("cvp"),
            np.ascontiguousarray(k_sample), np.ascontiguousarray(v_sample),
            wkv("wkvs"), shift("shs"), conv("cvs"))


def kernel(**inputs):
    maps = _prep_inputs(inputs)
    nc = build()
    res = run_bass_kernel_spmd(nc, maps, core_ids=list(range(8)))
    return _assemble(res.results)
```

```python
import numpy as np
from contextlib import ExitStack

import concourse.bass as bass
import concourse.mybir as mybir

F32 = mybir.dt.float32
BF16 = mybir.dt.bfloat16
F32R = mybir.dt.float32r
AF = mybir.ActivationFunctionType
ALU = mybir.AluOpType
AX = mybir.AxisListType

ENGS = ("pe", "act", "dve", "pool", "sp")
SEM_EPOCH = 20000
DMA_ROT = 8


class Buf:
    __slots__ = ("w", "r", "name")

    def __init__(self, name=""):
        self.w = None
        self.r = []
        self.name = name


class T:
    __slots__ = ("ap", "bufs")

    def __init__(self, ap, bufs):
        self.ap = ap
        self.bufs = tuple(bufs)

    def __getitem__(self, key):
        return T(self.ap[key], self.bufs)

    def v(self, ap):
        return T(ap, self.bufs)

    def bitcast(self, dt):
        return T(self.ap.bitcast(dt), self.bufs)


class Rec:
    __slots__ = ("eng", "idx", "fn", "deps", "dma", "signaled", "ev", "selfwait")

    def __init__(self, eng, idx, fn, dma):
        self.eng = eng
        self.idx = idx
        self.fn = fn
        self.deps = []
        self.dma = dma
        self.signaled = False
        self.ev = None
        self.selfwait = None


class Sched:
    def __init__(self, nc):
        self.nc = nc
        self.q = {e: [] for e in ENGS}
        self.stack = ExitStack()
        self.nbuf = 0
        self.same_engine_sync = True
        self._bar_pos = {}
        self._pending = {e: [] for e in ENGS}

    def sbuf(self, name, shape, dtype, nbufs=1):
        h = self.stack.enter_context(self.nc.sbuf_tensor(name, list(shape), dtype))
        return T(h[:], [Buf(name)])

    def psum(self, name, shape, dtype=F32):
        h = self.stack.enter_context(self.nc.psum_tensor(name, list(shape), dtype))
        return T(h[:], [Buf(name)])

    def dram(self, name, shape, dtype, kind):
        h = self.nc.dram_tensor(name, list(shape), dtype, kind=kind)
        return T(h.ap(), [Buf(name)])

    def newbuf(self, name=""):
        return Buf(name)

    def add(self, eng, fn, reads=(), writes=(), dma=False):
        q = self.q[eng]
        rec = Rec(eng, len(q), fn, dma)
        deps = {}
        for t in reads:
            for b in t.bufs:
                if b.w is not None:
                    deps[id(b.w)] = b.w
        for t in writes:
            for b in t.bufs:
                if b.w is not None:
                    deps[id(b.w)] = b.w
                for r in b.r:
                    deps[id(r)] = r
        if self._pending[eng]:
            for d in self._pending[eng]:
                deps[id(d)] = d
            self._pending[eng] = []
            barrier_deps = True
        else:
            barrier_deps = False
        for d in deps.values():
            if d is rec:
                continue
            if d.eng == eng and not d.dma and not dma:
                if eng == "pe" or eng == "sp" or not self.same_engine_sync:
                    continue
            rec.deps.append(d)
        for t in reads:
            for b in t.bufs:
                b.r.append(rec)
        for t in writes:
            for b in t.bufs:
                b.w = rec
                b.r = []
        q.append(rec)
        return rec

    def emit(self):
        nc = self.nc
        for e in ENGS:
            for rec in self.q[e]:
                for d in rec.deps:
                    d.signaled = True
        sems = {}
        for e in ENGS:
            cnt = 0
            for rec in self.q[e]:
                if rec.dma:
                    continue
                if rec.signaled:
                    ep = cnt // SEM_EPOCH
                    key = (e, ep)
                    if key not in sems:
                        sems[key] = self.stack.enter_context(nc.semaphore(f"s_{e}_{ep}"))
                    rec.ev = (sems[key], cnt % SEM_EPOCH + 1, key)
                    cnt += 1
        self.final_waits = []
        for e in ENGS:
            j = 0
            last = {}
            for rec in self.q[e]:
                if not rec.dma:
                    continue
                slot = j % DMA_ROT
                key = ("dma", e, slot)
                if key not in sems:
                    sems[key] = self.stack.enter_context(nc.semaphore(f"d_{e}_{slot}"))
                val = 16 * (j // DMA_ROT + 1)
                rec.ev = (sems[key], val, key)
                if j >= DMA_ROT:
                    rec.selfwait = (sems[key], val - 16, key)
                last[key] = (sems[key], val, key)
                j += 1
            self.final_waits.extend(last.values())

        block = self.stack.enter_context(nc.Block())
        sched = self

        def run(engname, eobj):
            waited = {}
            for rec in sched.q[engname]:
                waits = {}
                if rec.selfwait is not None:
                    s, v, key = rec.selfwait
                    waits[key] = (s, v)
                for d in rec.deps:
                    s, v, key = d.ev
                    if key in waits:
                        if waits[key][1] < v:
                            waits[key] = (s, v)
                    else:
                        waits[key] = (s, v)
                for key, (s, v) in waits.items():
                    if waited.get(key, 0) >= v:
                        continue
                    eobj.wait_ge(s, v)
                    waited[key] = v
                ins = rec.fn(eobj)
                if rec.dma:
                    ins.then_inc(rec.ev[0], 16)
                elif rec.signaled:
                    ins.then_inc(rec.ev[0], 1)
            if engname == "sp":
                for s, v, key in sched.final_waits:
                    eobj.wait_ge(s, v)

        @block.tensor
        def _(e):
            run("pe", e)

        @block.scalar
        def _(e):
            run("act", e)

        @block.vector
        def _(e):
            run("dve", e)

        @block.gpsimd
        def _(e):
            run("pool", e)

        @block.sync
        def _(e):
            run("sp", e)

    def close(self):
        self.stack.close()

    def dma(self, out, in_, eng="sp", **kw):
        return self.add(eng, lambda e: e.dma_start(out=out.ap, in_=in_.ap, **kw),
                        reads=[in_], writes=[out], dma=True)

    def mm(self, out, lhsT, rhs, start=True, stop=True, extra_reads=(), **kw):
        return self.add("pe", lambda e: e.matmul(out.ap, lhsT.ap, rhs.ap, start=start, stop=stop, **kw),
                        reads=[lhsT, rhs, *extra_reads], writes=[out])

    def tr(self, out, in_, ident, **kw):
        return self.add("pe", lambda e: e.transpose(out.ap, in_.ap, ident.ap, **kw),
                        reads=[in_, ident], writes=[out])

    def actf(self, out, in_, func, bias=None, scale=None, accum=None, eng="act"):
        kw = {}
        reads = [in_]
        writes = [out]
        if bias is not None:
            if isinstance(bias, T):
                kw["bias"] = bias.ap
                reads.append(bias)
            else:
                kw["bias"] = bias
        if scale is not None:
            if isinstance(scale, T):
                kw["scale"] = scale.ap
                reads.append(scale)
            else:
                kw["scale"] = scale
        if accum is not None:
            kw["accum_out"] = accum.ap
            writes.append(accum)
        return self.add("act", lambda e: e.activation(out.ap, in_.ap, func, **kw), reads=reads, writes=writes)

    def tt(self, out, a, b, op, eng="dve"):
        return self.add(eng, lambda e: e.tensor_tensor(out.ap, a.ap, b.ap, op), reads=[a, b], writes=[out])

    def ts(self, out, a, s1, op0, s2=None, op1=None, accum=None, eng="dve"):
        reads = [a]
        writes = [out]
        s1v = s1.ap if isinstance(s1, T) else s1
        s2v = s2.ap if isinstance(s2, T) else s2
        if isinstance(s1, T):
            reads.append(s1)
        if isinstance(s2, T):
            reads.append(s2)
        kw = {}
        if op1 is not None:
            kw["op1"] = op1
        if accum is not None:
            kw["accum_out"] = accum.ap
            writes.append(accum)
        return self.add(eng, lambda e: e.tensor_scalar(out.ap, a.ap, s1v, s2v, op0, **kw), reads=reads, writes=writes)

    def stt(self, out, a, s, b, op0, op1, eng="dve"):
        reads = [a, b]
        sv = s.ap if isinstance(s, T) else s
        if isinstance(s, T):
            reads.append(s)
        return self.add(eng, lambda e: e.scalar_tensor_tensor(out.ap, a.ap, sv, b.ap, op0, op1), reads=reads, writes=[out])

    def copy(self, out, in_, eng="dve"):
        if eng == "act":
            return self.add("act", lambda e: e.copy(out.ap, in_.ap), reads=[in_], writes=[out])
        return self.add(eng, lambda e: e.tensor_copy(out.ap, in_.ap), reads=[in_], writes=[out])

    def memset(self, out, val, eng="pool"):
        return self.add(eng, lambda e: e.memset(out.ap, val), reads=[], writes=[out])

    def recip(self, out, in_):
        return self.add("dve", lambda e: e.reciprocal(out.ap, in_.ap), reads=[in_], writes=[out])

    def scan(self, out, d0, d1, init, op0, op1):
        reads = [d0, d1]
        iv = init.ap if isinstance(init, T) else init
        if isinstance(init, T):
            reads.append(init)
        return self.add("dve", lambda e: e.tensor_tensor_scan(out.ap, d0.ap, d1.ap, iv, op0, op1), reads=reads, writes=[out])


def _barrier(self):
    deps = []
    for e in ENGS:
        q = self.q[e]
        last_c = None
        for rec in reversed(q):
            if not rec.dma:
                last_c = rec
                break
        if last_c is not None:
            deps.append(last_c)
        for rec in q[self._bar_pos.get(e, 0):]:
            if rec.dma:
                deps.append(rec)
        self._bar_pos[e] = len(q)
    for e in ENGS:
        self._pending[e] = list(deps)


Sched.barrier = _barrier


from concourse.bass_utils import run_bass_kernel_spmd

D = 1024
TP = 2048
TSV = 32
NTOK = 2176
NTB = 17
P_DA = 3072
P_RW = 3328
PTOT = 8448
DFF = 2816
EPS = 1e-6
GN_EPS = 64e-5
ROPE_THETA = 500000.0
TQ = [(0, 512), (512, 512), (1024, 512), (1536, 512), (2048, 128)]
PF_NMIX, PF_NFFN, PF_MU, PF_W0, PF_A0, PF_KK, PF_KA, PF_RK, PF_LNW, PF_LNB = 0, 8, 16, 42, 50, 58, 66, 74, 82, 90
PF_CONV, PF_CONVB, PF_SUBLN, NPF = 98, 230, 274, 275
C_ID, C_BD, C_MU, C_ML, C_MUI, C_RST, C_COS, C_SIN, NCONST = 0, 128, 256, 384, 512, 576, 832, 1104, 1376
DEC_C = -0.6065306597126334


def make_consts():
    c = np.zeros((128, NCONST), np.float32)
    c[:, C_ID:C_ID + 128] = np.eye(128, dtype=np.float32)
    p = np.arange(128)
    h = p // 64
    s = p % 64
    bd = (h[:, None] == h[None, :]).astype(np.float32)
    c[:, C_BD:C_BD + 128] = bd
    c[:, C_MU:C_MU + 128] = bd * (s[:, None] < s[None, :])
    c[:, C_ML:C_ML + 128] = bd * (s[:, None] > s[None, :])
    c[:, C_MUI:C_MUI + 64] = (s[:, None] <= np.arange(64)[None, :])
    rst = np.ones(256, np.float32)
    rst[::64] = 0
    c[:, C_RST:C_RST + 256] = rst[None, :]
    inv = (np.float32(ROPE_THETA) ** (-np.arange(0, 16, 2, dtype=np.float32) / np.float32(16))).astype(np.float32)
    for tb in range(NTB):
        pos = (tb * 128 + p) if tb < 16 else (TP + p)
        ang = pos.astype(np.float32)[:, None] * inv[None, :]
        co = np.cos(ang).astype(np.float32)
        si = np.sin(ang).astype(np.float32)
        c[:, C_COS + tb * 16:C_COS + tb * 16 + 8] = co
        c[:, C_COS + tb * 16 + 8:C_COS + tb * 16 + 16] = co
        c[:, C_SIN + tb * 16:C_SIN + tb * 16 + 8] = si
        c[:, C_SIN + tb * 16 + 8:C_SIN + tb * 16 + 16] = si
    return c


CHAIN_BF16 = True
LACC_ENG = ("pool", "dve")


def r32(t):
    if CHAIN_BF16:
        return t
    return T(t.ap.bitcast(F32R), t.bufs)


class Arena:
    def __init__(self, t, width):
        self.t = t
        self.W = width
        self.off = 0

    def seek(self, off):
        self.off = off

    def alloc(self, free_shape, dtype):
        n = 1
        for d in free_shape:
            n *= d
        words = n if dtype != BF16 else (n + 1) // 2
        words = (words + 7) // 8 * 8
        assert self.off + words <= self.W, ("arena overflow", self.off, words, self.W)
        ap = self.t.ap[:, self.off:self.off + words]
        if dtype == BF16:
            ap = ap.bitcast(BF16)
        ap = ap[:, 0:n]
        if len(free_shape) > 1:
            names = [f"d{i}" for i in range(len(free_shape))]
            pat = "p (" + " ".join(names) + ") -> p " + " ".join(names)
            kw = {nm: sz for nm, sz in zip(names[:-1], free_shape[:-1])}
            ap = ap.rearrange(pat, **kw)
        self.off += words
        return T(ap, [Buf()])


def _run_interleaved(gens):
    alive = list(gens)
    while alive:
        for gen in list(alive):
            try:
                next(gen)
            except StopIteration:
                alive.remove(gen)


B_WEIGHT = 7


def _run_weighted(gens, weights):
    alive = [(g, w) for g, w in zip(gens, weights)]
    while alive:
        for item in list(alive):
            g, w = item
            for _ in range(w):
                try:
                    next(g)
                except StopIteration:
                    alive.remove(item)
                    break


class Ring:
    def __init__(self, items):
        self.items = items
        self.i = 0

    def next(self):
        t = self.items[self.i % len(self.items)]
        self.i += 1
        return t


class StopBuild(Exception):
    pass


def build(dbg=None, stop_after=None):
    def chk(tag, l=0):
        if stop_after == (tag, l):
            raise StopBuild()

    nc = bass.Bass("TRN2", target_bir_lowering=False)
    S = Sched(nc)

    def din(name, shape):
        return S.dram(name, shape, F32, "ExternalInput")

    def dout(name, shape):
        return S.dram(name, shape, F32, "ExternalOutput")

    xp = din("xp", [TP, D])
    xs = din("xs", [TSV, D])
    ck = din("ck", [2, TP, D])
    cv = din("cv", [2, TP, D])
    swkv = din("swkv", [2, 8, 128, 64])
    sfm = din("sfm", [2, 128, 26])
    cfm = din("cfm", [2, 128, 88])
    pfm = din("pfm", [2, 128, NPF])
    consts = din("consts", [128, NCONST])
    w_in = din("w_in", [2, D, PTOT])
    da_lambda = din("da_lambda", [512])
    w_o_da = din("w_o_da", [2, D, D])
    rw_w2 = din("rw_w2", [2, 64, D])
    rw_a2 = din("rw_a2", [2, 64, D])
    rw_g2 = din("rw_g2", [2, 128, D])
    w_o_rw = din("w_o_rw", [2, D, D])
    w_out = din("w_out", [2, D, D])
    w_up = din("w_up", [2, D, 2 * DFF])
    w_down = din("w_down", [2, DFF, D])
    nfin = din("nfin", [D])

    yp = dout("yp", [TP, D])
    ys = dout("ys", [TSV, D])
    kp = dout("kp", [2, TP, D])
    vp = dout("vp", [2, TP, D])
    wkvp = dout("wkvp", [2, 8, 128, 64])
    shp = dout("shp", [2, 128, 26])
    cvp = dout("cvp", [2, 128, 88])
    ks = dout("ks", [2, TSV, D])
    vs = dout("vs", [2, TSV, D])
    wkvs = dout("wkvs", [2, 8, 128, 64])
    shs = dout("shs", [2, 128, 26])
    cvs = dout("cvs", [2, 128, 88])
    X1 = S.dram("X1", [NTOK, D], F32, "Internal")
    X2 = S.dram("X2", [NTOK, D], F32, "Internal")
    dbg_out = {}
    if dbg:
        for name, shape in dbg.items():
            dbg_out[name] = dout("dbg_" + name, shape)

    cst = S.sbuf("cst", [128, NCONST], F32)
    S.dma(cst, consts)
    ident_bf = S.sbuf("ident_bf", [128, 128], BF16)
    S.copy(ident_bf, cst[:, C_ID:C_ID + 128])
    ones_bf = S.sbuf("ones_bf", [128, 128], BF16)
    S.memset(ones_bf, 1.0)
    ones_f = S.sbuf("ones_f", [128, 128], F32)
    S.memset(ones_f, 1.0)
    ident_f = cst[:, C_ID:C_ID + 128]
    bdmask = cst[:, C_BD:C_BD + 128]
    bd_bf = S.sbuf("bd_bf", [128, 128], BF16)
    S.copy(bd_bf, cst[:, C_BD:C_BD + 128])
    fr = None if CHAIN_BF16 else S.sbuf("fr", [128, 21, 256], F32)
    pf = S.sbuf("pf", [128, 2, NPF], F32)
    for l in range(2):
        S.dma(pf[:, l, :], pfm[l])
    gfin = S.sbuf("gfin", [128, D], F32)
    S.dma(gfin, nfin.v(nfin.ap.partition_broadcast(128)))
    lamb = S.sbuf("lamb", [128, 512], F32)
    S.dma(lamb, da_lambda.v(da_lambda.ap.partition_broadcast(128)))
    lamv = S.sbuf("lamv", [128, 16], F32)
    epsb = S.sbuf("epsb", [128, 2], F32)
    S.memset(epsb[:, 0:1], EPS)
    S.memset(epsb[:, 1:2], GN_EPS)
    ljunk = S.sbuf("ljunk", [128, 64], F32)
    shp_t = S.sbuf("shp_t", [128, 26], F32)
    shs_t = S.sbuf("shs_t", [128, 26], F32)
    carry_p = S.sbuf("carry_p", [128, 44, 2], F32)
    carry_s = S.sbuf("carry_s", [128, 44, 2], F32)

    ps = [S.psum(f"ps{i}", [128, 512], F32) for i in range(8)]
    psr = Ring(ps)
    halves = []
    for i in range(8):
        halves.append(T(ps[i].ap[:, 0:256], ps[i].bufs))
        halves.append(T(ps[i].ap[:, 256:512], ps[i].bufs))

    AW = ((nc.sbuf_bytes_remaining - 256) // 4) // 8 * 8
    ar = S.sbuf("arena", [128, AW], F32)
    A = Arena(ar, AW)
    hT = A.alloc([8, NTOK], BF16)
    OZ = A.alloc([8, NTOK], BF16)
    MT_OFF = A.off
    MT = A.alloc([8, NTOK], BF16)
    LOC = A.off
    OZ_OFF = MT_OFF - (MT_OFF - 0) // 2 if False else None
    S.memset(hT, 0.0)
    S.memset(OZ, 0.0, eng="dve")
    S.memset(MT, 0.0)

    for l in range(2):
        lam_init = 0.8 - 0.6 * float(np.exp(-0.3 * l))
        b = l * 256
        pr = ljunk
        S.tt(pr, lamb[:, b:b + 64], lamb[:, b + 64:b + 128], ALU.mult)
        S.ts(pr, pr, 1.0, ALU.mult, None, ALU.add, accum=lamv[:, l * 8 + 4:l * 8 + 5])
        S.tt(pr, lamb[:, b + 128:b + 192], lamb[:, b + 192:b + 256], ALU.mult)
        S.ts(pr, pr, 1.0, ALU.mult, None, ALU.add, accum=lamv[:, l * 8 + 5:l * 8 + 6])
        S.actf(lamv[:, l * 8 + 2:l * 8 + 4], lamv[:, l * 8 + 4:l * 8 + 6], AF.Exp)
        S.tt(lamv[:, l * 8:l * 8 + 1], lamv[:, l * 8 + 3:l * 8 + 4], lamv[:, l * 8 + 2:l * 8 + 3], ALU.subtract)
        S.ts(lamv[:, l * 8:l * 8 + 1], lamv[:, l * 8:l * 8 + 1], -lam_init, ALU.add)
        S.ts(lamv[:, l * 8 + 1:l * 8 + 2], pf[:, l, PF_SUBLN:PF_SUBLN + 1], 1.0 - lam_init, ALU.mult)

    def wslab(dst, src_ap):
        return S.dma(dst, T(src_ap.rearrange("(kt p) c -> p kt c", p=128), ()), eng="pool")

    def make_norm_scratch():
        d = {}
        d["junk"] = A.alloc([D], BF16)
        d["xn"] = Ring([A.alloc([D], BF16) for _ in range(2)])
        d["st"] = Ring([A.alloc([4], F32) for _ in range(3)])
        return d

    def norm_stats(x_t, nsc):
        st = nsc["st"].next()
        S.actf(nsc["junk"], x_t, AF.Square, accum=st[:, 0:1])
        S.actf(st[:, 1:2], st[:, 0:1], AF.Sqrt, scale=1.0 / D, bias=epsb[:, 0:1])
        S.recip(st[:, 2:3], st[:, 1:2])
        return st[:, 2:3]

    def norm_to_hT(x_t, l, pfoff, tb, nsc):
        rstd = norm_stats(x_t, nsc)
        xn = nsc["xn"].next()
        S.ts(xn, x_t, rstd, ALU.mult)
        pb = psr.next()
        pst = T(pb.ap.bitcast(BF16).rearrange("p (k t) -> p k t", k=8), pb.bufs)
        for kt in range(8):
            S.tr(pst[:, kt, :], xn[:, kt * 128:(kt + 1) * 128], ident_bf)
        g = pf[:, l, pfoff:pfoff + 8]
        S.tt(hT[:, :, tb * 128:(tb + 1) * 128], pst, g.v(g.ap.unsqueeze(2).broadcast_to([128, 8, 128])), ALU.mult)

    def x_rows(tb):
        return (xp[tb * 128:(tb + 1) * 128, :], 128) if tb < 16 else (xs, TSV)

    def phase_norm0():
        A.seek(LOC)
        nsc = make_norm_scratch()
        xr = Ring([A.alloc([D], F32) for _ in range(3)])
        for tb in range(NTB):
            xt = xr.next()
            src, nr = x_rows(tb)
            if nr < 128:
                S.memset(xt, 0.0)
            S.dma(xt[0:nr, :], src)
            norm_to_hT(xt, 0, PF_NMIX, tb, nsc)

    def phase_rwkv(l):
        A.seek(MT_OFF)
        w_l = w_in[l]
        mu = lambda c: pf[:, l, PF_MU + c:PF_MU + c + 1]
        W2p = A.alloc([D], BF16)
        A2p = A.alloc([D], BF16)
        G2 = A.alloc([D], BF16)
        S.memset(W2p[64:128, :], 0.0)
        S.memset(A2p[0:64, :], 0.0)
        S.dma(W2p[0:64, :], rw_w2[l], eng="pool")
        S.dma(A2p[64:128, :], rw_a2[l], eng="pool")
        S.dma(G2, rw_g2[l], eng="pool")
        tanh_w = A.alloc([NTOK], BF16)
        raw_w = A.alloc([NTOK], BF16)
        sig_g = A.alloc([NTOK], BF16)
        sfm_t = A.alloc([26], F32)
        S.dma(sfm_t, sfm[l])
        NB = 256
        GRP_OFF = A.off
        Wl = A.alloc([8, 256], BF16)
        wslab(Wl, w_l.ap[:, P_DA + 3072:P_DA + 3328])
        u_ring = Ring([A.alloc([513], F32) for _ in range(3)])
        tmp = Ring([A.alloc([512], F32) for _ in range(6)])

        def shifted(psb, n, carry_src, mucol, u_out_last=None, last_idx=None):
            U = u_ring.next()
            if carry_src is None:
                S.memset(U[:, 0:1], 0.0, eng="dve")
            else:
                S.copy(U[:, 0:1], carry_src, eng="dve")
            S.copy(U[:, 1:n + 1], psb[:, 0:n], eng="act")
            d = tmp.next()
            S.tt(d[:, 0:n], U[:, 0:n], U[:, 1:n + 1], ALU.subtract)
            return d, U

        prev = {0: None, 1: None}
        for qi, (c0, n) in enumerate(TQ):
            for which in range(2):
                pb = psr.next()
                for kt in range(8):
                    S.mm(pb[:, 0:n], Wl[:, kt, which * 128:(which + 1) * 128], hT[:, kt, c0:c0 + n],
                         start=(kt == 0), stop=(kt == 7))
                mc = 24 + which
                if qi == 0:
                    carry = None
                elif qi == 4:
                    carry = sfm_t[:, mc:mc + 1]
                else:
                    carry = prev[which]
                d, U = shifted(pb, n, carry, mc)
                us = tmp.next()
                S.stt(us[:, 0:n], d[:, 0:n], mu(mc), U[:, 1:n + 1], ALU.mult, ALU.add)
                prev[which] = U[:, n:n + 1]
                if qi == 3:
                    S.copy(shp_t[:, mc:mc + 1], U[:, n:n + 1], eng="pool")
                if qi == 4:
                    S.copy(shs_t[:, mc:mc + 1], U[:, TSV:TSV + 1], eng="pool")
                if which == 0:
                    S.actf(tanh_w[:, c0:c0 + n], us[:, 0:n], AF.Tanh)
                    S.copy(raw_w[:, c0:c0 + n], us[:, 0:n], eng="pool")
                else:
                    S.actf(sig_g[:, c0:c0 + n], us[:, 0:n], AF.Sigmoid)

        chk("rwkv_lora", l)
        S.barrier()
        A.seek(GRP_OFF)
        Wg = A.alloc([8, 768], BF16)
        ASET = []
        for _ in range(2):
            ASET.append((A.alloc([2, NB], BF16), A.alloc([2, NB], BF16), A.alloc([2, NB], BF16), A.alloc([2, NB], BF16),
                         A.alloc([2, NB], BF16), A.alloc([2, NB], F32), A.alloc([2, NB], F32)))
        YT = A.alloc([2, NB], F32)
        S32 = A.alloc([2, 128], F32)
        Sb = A.alloc([2, 128], BF16)
        swt = A.alloc([2, 64], F32)
        carr = A.alloc([8], F32)
        t256g = [Ring([A.alloc([NB], F32) for _ in range(11)]) for _ in range(2)]
        b256g = [Ring([A.alloc([NB], BF16) for _ in range(3)]) for _ in range(2)]
        usrkg = [Ring([A.alloc([NB], F32) for _ in range(2)]) for _ in range(2)]
        u257g = [Ring([A.alloc([NB + 1], F32) for _ in range(2)]) for _ in range(2)]
        NCH = 4
        wide = lambda: A.alloc([2, 128], BF16)
        BDa = [wide() for _ in range(NCH)]
        BDb = [wide() for _ in range(NCH)]
        BDk = [wide() for _ in range(NCH)]
        BDx = Ring([wide() for _ in range(3)])
        AKm = [wide() for _ in range(NCH)]
        Vm = [wide() for _ in range(NCH)]
        Bhm = [wide() for _ in range(NCH)]
        Khm = [wide() for _ in range(NCH)]
        Rm = [wide() for _ in range(NCH)]
        Um = Ring([wide() for _ in range(1)])
        fi = [0]

        def frt():
            if CHAIN_BF16:
                return wide()
            t = T(fr.ap[:, fi[0], :].rearrange("p (g c) -> p g c", g=2), [Buf()])
            fi[0] += 1
            return t

        Nm = [[frt() for _ in range(NCH)] for _ in range(2)]
        NTm = [[frt() for _ in range(NCH)] for _ in range(2)]
        Pm = [frt() for _ in range(NCH)]
        Hm = Ring([frt() for _ in range(1)])
        class _HB:
            def __init__(self):
                self.k = 0
                self.pend = None

            def next(self):
                if self.pend is not None:
                    t = self.pend
                    self.pend = None
                    return t
                i = 2 + self.k % 6
                self.k += 1
                self.pend = halves[2 * i + 1]
                return halves[2 * i]

            def newbank(self):
                self.pend = None

        hbr = _HB()
        psrA = Ring(ps[0:2])

        def w4(t):
            return T(t.ap.rearrange("p g (h t) -> p g h t", h=2), t.bufs)

        def v3(t):
            return T(t.ap.rearrange("p (g c) -> p g c", g=2), t.bufs)

        def v3b(t):
            return T(t.ap.bitcast(BF16)[:, 0:256].rearrange("p (g c) -> p g c", g=2), t.bufs)

        def bcast_mask(coff):
            m = cst[:, coff:coff + 128]
            return T(m.ap.unsqueeze(1).broadcast_to([128, 2, 128]), m.bufs)

        mU_b = bcast_mask(C_MU)
        mL_b = bcast_mask(C_ML)
        id_b = bcast_mask(C_ID)
        bd4 = T(bdmask.ap.rearrange("p (h t) -> p h t", h=2).unsqueeze(1).broadcast_to([128, 2, 2, 64]), bdmask.bufs)
        mui = cst[:, C_MUI:C_MUI + 64]
        mui4 = T(mui.ap.unsqueeze(1).unsqueeze(1).broadcast_to([128, 2, 2, 64]), mui.bufs)
        rstm = cst[:, C_RST:C_RST + NB]

        for grp in range(4):
            for x in range(3):
                wslab(Wg[:, :, x * 256:(x + 1) * 256],
                      w_l.ap[:, P_DA + x * 1024 + grp * 256:P_DA + x * 1024 + (grp + 1) * 256])
            for seq in range(2):
                blocks = [(i * NB, NB) for i in range(8)] if seq == 0 else [(TP, 64)]
                nvalid_last = NB if seq == 0 else TSV
                if seq == 0:
                    S.memset(S32, 0.0, eng="dve")
                    S.memset(Sb, 0.0, eng="dve")
                else:
                    for g in range(2):
                        S.dma(swt[:, g, :], swkv[l, 2 * grp + g])
                    S.tt(w4(S32), swt.v(swt.ap.unsqueeze(2).broadcast_to([128, 2, 2, 64])), bd4, ALU.mult)
                    S.copy(Sb, S32)
                carry = {}
                for _once in range(1):
                    def stage_a(g, bi, c0, n):
                        t256, b256, usrk, u257 = t256g[g], b256g[g], usrkg[g], u257g[g]
                        AT, BT, KT_, RT, VT, WT, BON = ASET[bi % 2]
                        hp = 2 * grp + g
                        us = {}
                        for x in range(3):
                            pb = psrA.next()
                            off = x * 256 + g * 128
                            for kt in range(8):
                                S.mm(pb[:, 0:n], Wg[:, kt, off:off + 128], hT[:, kt, c0:c0 + n],
                                     start=(kt == 0), stop=(kt == 7))
                            mc = x * 8 + hp
                            U = u257.next()
                            if bi == 0:
                                if seq == 0:
                                    S.memset(U[:, 0:1], 0.0, eng="dve")
                                else:
                                    S.copy(U[:, 0:1], sfm_t[:, mc:mc + 1], eng="dve")
                            else:
                                S.copy(U[:, 0:1], carry[(g, x)], eng="dve")
                            S.copy(U[:, 1:n + 1], pb[:, 0:n], eng="act")
                            d = t256.next()
                            S.tt(d[:, 0:n], U[:, 0:n], U[:, 1:n + 1], ALU.subtract)
                            ut = usrk.next() if x < 2 else t256.next()
                            us[x] = ut[:, 0:n]
                            S.stt(ut[:, 0:n], d[:, 0:n], mu(mc), U[:, 1:n + 1], ALU.mult, ALU.add)
                            S.copy(carr[:, g * 3 + x:g * 3 + x + 1], U[:, n:n + 1], eng="pool")
                            carry[(g, x)] = carr[:, g * 3 + x:g * 3 + x + 1]
                            yield
                            if bi == len(blocks) - 1:
                                sht = shp_t if seq == 0 else shs_t
                                S.copy(sht[:, mc:mc + 1], U[:, nvalid_last:nvalid_last + 1], eng="pool")
                        us_r, us_k, us_v = us[0], us[1], us[2]
                        S.copy(VT[:, g, 0:n], us_v, eng="act")
                        yield
                        pcol = lambda o: pf[:, l, o + hp:o + hp + 1]
                        pb = psrA.next()
                        S.mm(pb[:, 0:n], W2p[:, hp * 128:(hp + 1) * 128], tanh_w[:, c0:c0 + n])
                        yield
                        sg = t256.next()
                        S.actf(sg[:, 0:n], pb[:, 0:n], AF.Sigmoid, bias=pcol(PF_W0))
                        yield
                        lw = t256.next()
                        S.ts(lw[:, 0:n], sg[:, 0:n], DEC_C, ALU.mult, 0.0, ALU.add, eng="pool")
                        yield
                        pb = psrA.next()
                        S.mm(pb[:, 0:n], A2p[:, hp * 128:(hp + 1) * 128], raw_w[:, c0:c0 + n])
                        yield
                        a_t = t256.next()
                        S.actf(a_t[:, 0:n], pb[:, 0:n], AF.Sigmoid, bias=pcol(PF_A0))
                        yield
                        kk = t256.next()
                        S.ts(kk[:, 0:n], us_k, pcol(PF_KK), ALU.mult)
                        yield
                        sq = b256.next()
                        S.actf(sq[:, 0:n], kk[:, 0:n], AF.Square)
                        yield
                        pb = psrA.next()
                        S.mm(pb[:, 0:n], bd_bf, sq[:, 0:n])
                        yield
                        rn = t256.next()
                        S.actf(rn[:, 0:n], pb[:, 0:n], AF.Sqrt)
                        yield
                        S.ts(rn[:, 0:n], rn[:, 0:n], 1e-12, ALU.max)
                        yield
                        S.recip(rn[:, 0:n], rn[:, 0:n])
                        yield
                        kkn = kk
                        S.tt(kkn[:, 0:n], kk[:, 0:n], rn[:, 0:n], ALU.mult)
                        yield
                        t1 = t256.next()
                        S.ts(t1[:, 0:n], a_t[:, 0:n], -1.0, ALU.add, pcol(PF_KA), ALU.mult)
                        yield
                        kmod = t256.next()
                        S.stt(kmod[:, 0:n], t1[:, 0:n], 1.0, us_k, ALU.add, ALU.mult)
                        yield
                        bb = t1
                        S.tt(bb[:, 0:n], kkn[:, 0:n], a_t[:, 0:n], ALU.mult, eng="pool")
                        yield
                        cum = t256.next()
                        S.scan(cum[:, 0:n], rstm[:, 0:n], lw[:, 0:n], 0.0, ALU.mult, ALU.add)
                        yield
                        cumex = sg
                        S.tt(cumex[:, 0:n], cum[:, 0:n], lw[:, 0:n], ALU.subtract, eng="pool")
                        yield
                        S.actf(WT[:, g, 0:n], cum[:, 0:n], AF.Exp)
                        yield
                        winv = lw
                        S.actf(winv[:, 0:n], cum[:, 0:n], AF.Exp, scale=-1.0)
                        yield
                        wex = cum
                        S.actf(wex[:, 0:n], cumex[:, 0:n], AF.Exp)
                        yield
                        S.stt(AT[:, g, 0:n], kkn[:, 0:n], -1.0, wex[:, 0:n], ALU.mult, ALU.mult)
                        yield
                        S.tt(BT[:, g, 0:n], bb[:, 0:n], winv[:, 0:n], ALU.mult)
                        yield
                        S.tt(KT_[:, g, 0:n], kmod[:, 0:n], winv[:, 0:n], ALU.mult)
                        yield
                        S.tt(RT[:, g, 0:n], us_r, WT[:, g, 0:n], ALU.mult)
                        yield
                        rk = b256.next()
                        S.stt(rk[:, 0:n], us_r, pcol(PF_RK), kmod[:, 0:n], ALU.mult, ALU.mult)
                        yield
                        pb = psrA.next()
                        S.mm(pb[:, 0:n], bd_bf, rk[:, 0:n])
                        yield
                        S.tt(BON[:, g, 0:n], pb[:, 0:n], us_v, ALU.mult)
                        yield
                        if seq == 1:
                            for arr in (AT, BT, KT_, RT, VT):
                                S.memset(arr[:, g, TSV:64], 0.0)

                    def stage_b(bi, c0, n):
                        nch = n // 64
                        AT, BT, KT_, RT, VT, WT, BON = ASET[bi % 2]
                        def chunk_src(arr, ci):
                            a = arr[:, :, ci * 64:(ci + 1) * 64]
                            return T(a.ap.unsqueeze(2).broadcast_to([128, 2, 2, 64]), a.bufs)

                        wcols = []
                        for ci in range(nch):
                            lastc = ci * 64 + (63 if seq == 0 else TSV - 1)
                            wc = WT[:, :, lastc:lastc + 1]
                            wcols.append(wc)
                            wcb = T(wc.ap.broadcast_to([128, 2, 128]), wc.bufs)
                            S.tt(w4(BDa[ci]), chunk_src(AT, ci), bd4, ALU.mult)
                            yield
                            S.tt(w4(BDb[ci]), chunk_src(BT, ci), bd4, ALU.mult, eng="pool")
                            yield
                            S.tt(w4(BDk[ci]), chunk_src(KT_, ci), bd4, ALU.mult)
                            yield
                            bdv = BDx.next()
                            S.tt(w4(bdv), chunk_src(VT, ci), bd4, ALU.mult)
                            yield
                            bdbh = BDx.next()
                            S.tt(bdbh, BDb[ci], wcb, ALU.mult)
                            yield
                            bdkh = BDx.next()
                            S.tt(bdkh, BDk[ci], wcb, ALU.mult, eng="pool")
                            yield
                            hbr.newbank()
                            pN, pNT, pAK, pR, pV, pBh, pKh = [hbr.next() for _ in range(7)]
                            hbr.newbank()
                            for g in range(2):
                                S.mm(v3(pN)[:, g, :], BDb[ci][:, g, :], BDa[ci][:, g, :])
                                yield
                            for g in range(2):
                                S.mm(v3(pNT)[:, g, :], BDa[ci][:, g, :], BDb[ci][:, g, :])
                                yield
                            for g in range(2):
                                S.mm(v3(pAK)[:, g, :], BDk[ci][:, g, :], BDa[ci][:, g, :])
                                yield
                            pR4 = T(pR.ap.rearrange("p (g x t) -> p g x t", g=2, x=2), pR.bufs)
                            for g in range(2):
                                S.mm(pR4[:, g, 0, :], BDb[ci][:, g, :], RT[:, g, ci * 64:(ci + 1) * 64])
                                yield
                                S.mm(pR4[:, g, 1, :], BDk[ci][:, g, :], RT[:, g, ci * 64:(ci + 1) * 64])
                                yield
                            for g in range(2):
                                S.tr(v3b(pV)[:, g, :], bdv[:, g, :], ident_bf)
                                yield
                                S.tr(v3b(pBh)[:, g, :], bdbh[:, g, :], ident_bf)
                                yield
                                S.tr(v3b(pKh)[:, g, :], bdkh[:, g, :], ident_bf)
                                yield
                            S.tt(r32(Nm[0][ci]), v3(pN), mU_b, ALU.mult)
                            yield
                            S.tt(r32(NTm[0][ci]), v3(pNT), mL_b, ALU.mult)
                            yield
                            S.tt(AKm[ci], v3(pAK), mU_b, ALU.mult)
                            yield
                            S.tt(w4(Rm[ci]), pR4, mui4, ALU.mult)
                            yield
                            S.copy(Vm[ci], v3b(pV), eng="act")
                            yield
                            S.copy(Bhm[ci], v3b(pBh), eng="act")
                            yield
                            S.copy(Khm[ci], v3b(pKh), eng="act")
                            yield
                            S.tt(r32(Pm[ci]), Nm[0][ci], id_b, ALU.add)
                            yield
                        cur = 0
                        for rd in range(1, 6):
                            nxt = 1 - cur
                            pairs = []
                            for ci in range(nch):
                                hbr.newbank()
                                pN2 = hbr.next() if rd < 5 else None
                                pNT2 = hbr.next()
                                for g in range(2):
                                    if rd < 5:
                                        S.mm(v3(pN2)[:, g, :], r32(NTm[cur][ci][:, g, :]), r32(Nm[cur][ci][:, g, :]))
                                    S.mm(v3(pNT2)[:, g, :], r32(Nm[cur][ci][:, g, :]), r32(NTm[cur][ci][:, g, :]))
                                pairs.append((pN2, pNT2))
                            for ci in range(nch):
                                pN2, pNT2 = pairs[ci]
                                S.copy(r32(NTm[nxt][ci]), v3(pNT2), eng="act")
                                yield
                                if rd < 5:
                                    S.copy(r32(Nm[nxt][ci]), v3(pN2), eng="act")
                            pps = []
                            for ci in range(nch):
                                hbr.newbank()
                                pP = hbr.next()
                                for g in range(2):
                                    if ci % 2 == 1:
                                        S.mm(v3(pP)[:, g, :], r32(NTm[nxt][ci][:, g, :]), r32(Pm[ci][:, g, :]), start=True, stop=False)
                                        S.mm(v3(pP)[:, g, :], ident_bf, r32(Pm[ci][:, g, :]), start=False, stop=True)
                                    else:
                                        S.mm(v3(pP)[:, g, :], r32(NTm[nxt][ci][:, g, :]), r32(Pm[ci][:, g, :]))
                                pps.append(pP)
                            for ci in range(nch):
                                if ci % 2 == 1:
                                    S.copy(r32(Pm[ci]), v3(pps[ci]), eng="act")
                                else:
                                    S.tt(r32(Pm[ci]), v3(pps[ci]), Pm[ci], ALU.add)
                                yield
                            cur = nxt
                        for ci in range(nch):
                            hbr.newbank()
                            pH = hbr.next()
                            for g in range(2):
                                S.mm(v3(pH)[:, g, :], BDa[ci][:, g, :], Sb[:, g, :], start=True, stop=False)
                                yield
                                S.mm(v3(pH)[:, g, :], AKm[ci][:, g, :], Vm[ci][:, g, :], start=False, stop=True)
                                yield
                            H = Hm.next()
                            S.copy(r32(H), v3(pH), eng="act")
                            yield
                            hbr.newbank()
                            pU = hbr.next()
                            for g in range(2):
                                S.mm(v3(pU)[:, g, :], r32(Pm[ci][:, g, :]), r32(H[:, g, :]))
                                yield
                            U = Um.next()
                            S.copy(U, v3(pU), eng="act")
                            yield
                            hbr.newbank()
                            pY = hbr.next()
                            pY3 = T(pY.ap[:, 0:128].rearrange("p (g t) -> p g t", g=2), pY.bufs)
                            R4 = w4(Rm[ci])
                            for g in range(2):
                                S.mm(pY3[:, g, :], Sb[:, g, :], RT[:, g, ci * 64:(ci + 1) * 64], start=True, stop=False)
                                yield
                                S.mm(pY3[:, g, :], U[:, g, :], R4[:, g, 0, :], start=False, stop=False)
                                yield
                                S.mm(pY3[:, g, :], Vm[ci][:, g, :], R4[:, g, 1, :], start=False, stop=True)
                                yield
                            S.copy(YT[:, :, ci * 64:(ci + 1) * 64], pY3, eng="act")
                            yield
                            hbr.newbank()
                            pS = hbr.next()
                            for g in range(2):
                                S.mm(v3(pS)[:, g, :], Bhm[ci][:, g, :], U[:, g, :], start=True, stop=False)
                                yield
                                S.mm(v3(pS)[:, g, :], Khm[ci][:, g, :], Vm[ci][:, g, :], start=False, stop=True)
                                yield
                            for g in range(2):
                                S.stt(S32[:, g, :], S32[:, g, :], wcols[ci][:, g, :], v3(pS)[:, g, :], ALU.mult, ALU.add)
                                yield
                            S.copy(Sb, S32, eng="act")
                            yield
                    def stage_c(g, bi, c0, n):
                        t256, b256 = t256g[g], b256g[g]
                        AT, BT, KT_, RT, VT, WT, BON = ASET[bi % 2]
                        hp = 2 * grp + g
                        pcol = lambda o: pf[:, l, o + hp:o + hp + 1]
                        yb = b256.next()
                        S.copy(yb[:, 0:n], YT[:, g, 0:n], eng="act")
                        yield
                        pb = psrA.next()
                        S.mm(pb[:, 0:n], bd_bf, yb[:, 0:n])
                        yield
                        yc = t256.next()
                        S.stt(yc[:, 0:n], pb[:, 0:n], -1.0 / 64, YT[:, g, 0:n], ALU.mult, ALU.add)
                        yield
                        sq = b256.next()
                        S.actf(sq[:, 0:n], yc[:, 0:n], AF.Square)
                        yield
                        pb = psrA.next()
                        S.mm(pb[:, 0:n], bd_bf, sq[:, 0:n])
                        yield
                        sd = t256.next()
                        S.actf(sd[:, 0:n], pb[:, 0:n], AF.Sqrt, scale=1.0 / 64, bias=epsb[:, 1:2])
                        yield
                        S.recip(sd[:, 0:n], sd[:, 0:n])
                        yield
                        S.tt(yc[:, 0:n], yc[:, 0:n], sd[:, 0:n], ALU.mult)
                        yield
                        S.ts(yc[:, 0:n], yc[:, 0:n], pcol(PF_LNW), ALU.mult, pcol(PF_LNB), ALU.add)
                        yield
                        S.tt(yc[:, 0:n], yc[:, 0:n], BON[:, g, 0:n], ALU.add, eng="pool")
                        yield
                        pb = psrA.next()
                        S.mm(pb[:, 0:n], G2[:, hp * 128:(hp + 1) * 128], sig_g[:, c0:c0 + n])
                        yield
                        S.tt(OZ[:, hp, c0:c0 + n], yc[:, 0:n], pb[:, 0:n], ALU.mult)
                        yield

                    pass
                _run_interleaved([stage_a(0, 0, *blocks[0]), stage_a(1, 0, *blocks[0])])
                for bi, (c0, n) in enumerate(blocks):
                    gens = [stage_b(bi, c0, n)]
                    if bi + 1 < len(blocks):
                        gens += [stage_a(0, bi + 1, *blocks[bi + 1]), stage_a(1, bi + 1, *blocks[bi + 1])]
                    _run_weighted(gens, [B_WEIGHT, 1, 1])
                    _run_interleaved([stage_c(0, bi, c0, n), stage_c(1, bi, c0, n)])
                    chk("rwkv_C", l)
                chk("rwkv_seq", l)
                dst = wkvp if seq == 0 else wkvs
                for g in range(2):
                    for h in range(2):
                        S.dma(dst[l, 2 * grp + g, h * 64:(h + 1) * 64, :], S32[h * 64:(h + 1) * 64, g, h * 64:(h + 1) * 64])
        S.dma(shp[l], shp_t)
        S.dma(shs[l], shs_t)

    def phase_gproj(l, w_o, gate_col0, accumulate):
        A.seek(LOC)
        slabs = Ring([A.alloc([8, 256], BF16) for _ in range(2)])
        sgr = Ring([A.alloc([512], F32) for _ in range(3)])
        tr_ = Ring([A.alloc([512], F32) for _ in range(2)])
        gsl = {}

        def issue_g(nt_):
            if nt_ >= 8:
                return
            sl_ = slabs.next()
            wslab(sl_[:, :, 0:128], w_o[l].ap[:, nt_ * 128:(nt_ + 1) * 128])
            wslab(sl_[:, :, 128:256], w_in[l].ap[:, gate_col0 + nt_ * 128:gate_col0 + (nt_ + 1) * 128])
            gsl[nt_] = sl_

        issue_g(0)
        for nt in range(8):
            issue_g(nt + 1)
            sl = gsl.pop(nt)
            for (c0, n) in TQ:
                pa = psr.next()
                for kt in range(8):
                    S.mm(pa[:, 0:n], sl[:, kt, 0:128], OZ[:, kt, c0:c0 + n], start=(kt == 0), stop=(kt == 7))
                pg = psr.next()
                for kt in range(8):
                    S.mm(pg[:, 0:n], sl[:, kt, 128:256], hT[:, kt, c0:c0 + n], start=(kt == 0), stop=(kt == 7))
                sg = sgr.next()
                S.actf(sg[:, 0:n], pg[:, 0:n], AF.Sigmoid)
                if not accumulate:
                    S.tt(MT[:, nt, c0:c0 + n], pa[:, 0:n], sg[:, 0:n], ALU.mult)
                else:
                    t = tr_.next()
                    S.tt(t[:, 0:n], pa[:, 0:n], sg[:, 0:n], ALU.mult)
                    S.tt(MT[:, nt, c0:c0 + n], t[:, 0:n], MT[:, nt, c0:c0 + n], ALU.add, eng="pool")

    def phase_da(l):
        A.seek(LOC)
        Wd_r = Ring([A.alloc([8, 384], BF16) for _ in range(2)])
        qkv_tok = A.alloc([NTB, 384], BF16)
        q_tok = qkv_tok[:, :, 0:128]
        k_tok = qkv_tok[:, :, 128:256]
        v_tok = qkv_tok[:, :, 256:384]
        kc_r = Ring([A.alloc([16, 128], BF16) for _ in range(2)])
        vc_r = Ring([A.alloc([16, 128], BF16) for _ in range(2)])
        KTh = A.alloc([NTOK + TP], BF16)
        Q1p = A.alloc([512], BF16)
        Q2p = A.alloc([512], BF16)
        Pr = Ring([A.alloc([512], BF16) for _ in range(4)])
        f512 = Ring([A.alloc([512], F32) for _ in range(4)])
        qkvr = Ring([A.alloc([384], F32) for _ in range(2)])
        rtmp = Ring([A.alloc([4, 16], F32) for _ in range(4)])
        Lacc = [A.alloc([512], F32) for _ in range(2)]
        Pd = [A.alloc([512], BF16) for _ in range(4)]
        Pz = A.alloc([512], BF16)
        for m_ in range(4):
            S.memset(Pd[m_], 0.0)
        S.memset(Pz, 0.0)
        S.memset(Q1p, 0.0)
        S.memset(Q2p, 0.0)
        nlam = lamv[:, l * 8:l * 8 + 1]
        subs = lamv[:, l * 8 + 1:l * 8 + 2]
        psOL = [(ps[0], ps[1]), (ps[2], ps[3])]
        pr6 = Ring(ps[4:8])

        def rope_inplace(x, tb):
            x4 = T(x.ap.rearrange("p (m d) -> p m d", m=4), x.bufs)
            cc = cst[:, C_COS + tb * 16:C_COS + tb * 16 + 16]
            ss = cst[:, C_SIN + tb * 16:C_SIN + tb * 16 + 16]
            ccb = T(cc.ap.unsqueeze(1).broadcast_to([128, 4, 16]), cc.bufs)
            ssb = T(ss.ap.unsqueeze(1).broadcast_to([128, 4, 16]), ss.bufs)
            tc_ = rtmp.next()
            ts_ = rtmp.next()
            S.tt(tc_, x4[:, :, 0:16], ccb, ALU.mult)
            S.tt(ts_, x4[:, :, 0:16], ssb, ALU.mult)
            S.tt(x4[:, :, 0:8], tc_[:, :, 0:8], ts_[:, :, 8:16], ALU.subtract)
            S.tt(x4[:, :, 8:16], tc_[:, :, 8:16], ts_[:, :, 0:8], ALU.add)

        def attend(c0, nq, tbs, keyblocks):
            pb = pr6.next()
            pst = T(pb.ap.bitcast(BF16), pb.bufs)
            for i, tb in enumerate(tbs):
                S.tr(pst[:, i * 128:(i + 1) * 128], q_tok[:, tb, :], ident_bf)
            S.copy(Q1p[0:64, 0:nq], pst[0:64, 0:nq], eng="act")
            S.copy(Q2p[64:128, 0:nq], pst[64:128, 0:nq], eng="act")
            o1 = f512.next()
            nk = len(keyblocks)
            steps = [(mp, j) for mp in range(2) for j in range(nk)]
            Ps = {}
            LOOK = 2
            tlast = [None]

            def issue_s(i):
                mp, j = steps[i]
                kap, vap, c_lo, zspec = keyblocks[j]
                Qp = Q1p if mp == 0 else Q2p
                pS = pr6.next()
                S.mm(pS[:, c_lo:nq], kap, Qp[:, c_lo:nq])
                if zspec == "diag":
                    P = Pd[c_lo // 128]
                    S.actf(P[0:64, c_lo:nq], pS[0:64, c_lo:nq], AF.Exp, scale=0.125)
                    S.actf(P[64:128, c_lo + 64:nq], pS[64:128, c_lo + 64:nq], AF.Exp, scale=0.125)
                elif zspec == "rows":
                    P = Pz
                    S.actf(P[0:32, 0:nq], pS[0:32, 0:nq], AF.Exp, scale=0.125)
                else:
                    P = Pr.next()
                    S.actf(P[:, c_lo:nq], pS[:, c_lo:nq], AF.Exp, scale=0.125)
                Ps[i] = P

            def issue_ol(i):
                mp, j = steps[i]
                kap, vap, c_lo, zspec = keyblocks[j]
                P = Ps.pop(i)
                pO, pL = psOL[mp]
                S.mm(pO[:, c_lo:nq], vap, P[:, c_lo:nq], start=(j == 0), stop=(j == nk - 1))
                eng = LACC_ENG[j % 2]
                La = Lacc[j % 2]
                if j < 2:
                    if c_lo > 0:
                        S.memset(La[:, 0:c_lo], 0.0, eng=eng)
                    S.copy(La[:, c_lo:nq], P[:, c_lo:nq], eng=eng)
                else:
                    S.tt(La[:, c_lo:nq], La[:, c_lo:nq], P[:, c_lo:nq], ALU.add, eng=eng)
                if j == nk - 1:
                    Lb = Pr.next()
                    S.tt(Lb[:, 0:nq], Lacc[0][:, 0:nq], Lacc[1][:, 0:nq], ALU.add)
                    S.mm(pL[:, 0:nq], ones_bf, Lb[:, 0:nq])
                    rd = f512.next()
                    S.recip(rd[:, 0:nq], pL[:, 0:nq])
                    if mp == 0:
                        S.tt(o1[:, 0:nq], pO[:, 0:nq], rd[:, 0:nq], ALU.mult)
                    else:
                        t = f512.next()
                        S.tt(t[:, 0:nq], pO[:, 0:nq], rd[:, 0:nq], ALU.mult)
                        S.stt(o1[:, 0:nq], t[:, 0:nq], nlam, o1[:, 0:nq], ALU.mult, ALU.add)
                        tlast[0] = t

            for i in range(len(steps) + LOOK):
                if i < len(steps):
                    issue_s(i)
                if i - LOOK >= 0:
                    issue_ol(i - LOOK)
            t = tlast[0]
            sq = Pr.next()
            S.actf(sq[:, 0:nq], o1[:, 0:nq], AF.Square)
            pb = pr6.next()
            S.mm(pb[:, 0:nq], ones_bf, sq[:, 0:nq])
            sd = t
            S.actf(sd[:, 0:nq], pb[:, 0:nq], AF.Sqrt, scale=1.0 / 128, bias=epsb[:, 0:1])
            S.recip(sd[:, 0:nq], sd[:, 0:nq])
            S.tt(o1[:, 0:nq], o1[:, 0:nq], sd[:, 0:nq], ALU.mult, eng="pool")
            return o1

        hl = {}

        def issue_head(hd_):
            if hd_ >= 8:
                return
            Wd_, kc_, vc_ = Wd_r.next(), kc_r.next(), vc_r.next()
            for x in range(3):
                wslab(Wd_[:, :, x * 128:(x + 1) * 128], w_in[l].ap[:, x * 1024 + hd_ * 128:x * 1024 + (hd_ + 1) * 128])
            S.dma(kc_, T(ck[l].ap[:, hd_ * 128:(hd_ + 1) * 128].rearrange("(j p) c -> p j c", p=128), ()), eng="pool")
            S.dma(vc_, T(cv[l].ap[:, hd_ * 128:(hd_ + 1) * 128].rearrange("(j p) c -> p j c", p=128), ()), eng="pool")
            hl[hd_] = (Wd_, kc_, vc_)

        issue_head(0)
        for hd in range(8):
            issue_head(hd + 1)
            Wd, kc_tok, vc_tok = hl.pop(hd)
            chk("da_load", l)
            for tb in range(NTB):
                if tb == 1:
                    chk("da_tb0", l)
                if tb == 16:
                    chk("da_tb15", l)
                pb = pr6.next()
                for kt in range(8):
                    S.mm(pb[:, 0:384], hT[:, kt, tb * 128:(tb + 1) * 128], Wd[:, kt, :],
                         start=(kt == 0), stop=(kt == 7))
                qkv = qkvr.next()
                S.copy(qkv, pb[:, 0:384], eng="act")
                rope_inplace(qkv[:, 0:256], tb)
                S.copy(qkv_tok[:, tb, :], qkv, eng="act")
                if tb < 16:
                    S.dma(kp[l, tb * 128:(tb + 1) * 128, hd * 128:(hd + 1) * 128], qkv[:, 128:256])
                    S.dma(vp[l, tb * 128:(tb + 1) * 128, hd * 128:(hd + 1) * 128], qkv[:, 256:384])
                else:
                    S.dma(ks[l, :, hd * 128:(hd + 1) * 128], qkv[0:TSV, 128:256])
                    S.dma(vs[l, :, hd * 128:(hd + 1) * 128], qkv[0:TSV, 256:384])
            chk("da_proj", l)
            srcs = [(k_tok, tb) for tb in range(NTB)] + [(kc_tok, j) for j in range(16)]
            base = 0
            ei = 0
            while base < len(srcs):
                grp_ = srcs[base:base + 8]
                pb = pr6.next()
                pst = T(pb.ap.bitcast(BF16), pb.bufs)
                for i, (src, idx) in enumerate(grp_):
                    S.tr(pst[:, i * 128:(i + 1) * 128], src[:, idx, :], ident_bf)
                S.copy(KTh[:, base * 128:(base + len(grp_)) * 128], pst[:, 0:len(grp_) * 128],
                       eng=("act" if ei % 2 == 0 else "dve"))
                base += len(grp_)
                ei += 1
            chk("da_kt", l)
            for i in range(4):
                if i == 1:
                    chk("da_att0", l)
                kb = []
                for j in range(4 * i + 4):
                    m = j - 4 * i
                    c_lo = 128 * m if m > 0 else 0
                    kb.append((KTh[:, j * 128:(j + 1) * 128], v_tok[:, j, :], c_lo, "diag" if m >= 0 else None))
                o = attend(i * 512, 512, [4 * i + m for m in range(4)], kb)
                S.ts(OZ[:, hd, i * 512:(i + 1) * 512], o[:, 0:512], subs, ALU.mult)
            chk("da_attp", l)
            kb = [(KTh[:, NTOK + j * 128:NTOK + (j + 1) * 128], vc_tok[:, j, :], 0, None) for j in range(16)]
            kb.append((KTh[:, TP:TP + 128], v_tok[:, 16, :], 0, "rows"))
            o = attend(TP, TSV, [16], kb)
            S.ts(OZ[:, hd, TP:TP + TSV], o[:, 0:TSV], subs, ALU.mult)
            chk("da_head0", l)

    def phase_out(l):
        A.seek(LOC)
        nsc = make_norm_scratch()
        wo = A.alloc([8, D], BF16)
        wslab(wo[:, :, 0:512], w_out[l].ap[:, 0:512])
        wslab(wo[:, :, 512:1024], w_out[l].ap[:, 512:1024])
        xr = Ring([A.alloc([D], F32) for _ in range(2)])
        x1r = Ring([A.alloc([D], F32) for _ in range(2)])
        for tb in range(NTB):
            xt = xr.next()
            if l == 0:
                src, nr = x_rows(tb)
                if nr < 128:
                    S.memset(xt, 0.0)
                S.dma(xt[0:nr, :], src)
            else:
                S.dma(xt, X2[tb * 128:(tb + 1) * 128, :])
            x1 = x1r.next()
            for half in range(2):
                pb = psr.next()
                for kt in range(8):
                    S.mm(pb, MT[:, kt, tb * 128:(tb + 1) * 128], wo[:, kt, half * 512:(half + 1) * 512],
                         start=(kt == 0), stop=(kt == 7))
                S.tt(x1[:, half * 512:(half + 1) * 512], pb, xt[:, half * 512:(half + 1) * 512], ALU.add)
            S.dma(X1[tb * 128:(tb + 1) * 128, :], x1)
            norm_to_hT(x1, l, PF_NFFN, tb, nsc)

    def phase_ffn(l):
        A.seek(0)
        A.alloc([8, NTOK], BF16)
        wdn = A.alloc([22, D], BF16)
        GT = A.alloc([22, 512], BF16)
        assert A.off <= LOC
        A.seek(LOC)
        nsc = make_norm_scratch()
        for c in range(2):
            for hh in range(2):
                S.dma(wdn[:, 11 * hh:11 * (hh + 1), c * 512:(c + 1) * 512],
                      T(w_down[l].ap[11 * hh * 128:11 * (hh + 1) * 128, c * 512:(c + 1) * 512].rearrange("(kt p) c -> p kt c", p=128), ()), eng="pool")
        slabs = Ring([A.alloc([8, 512], BF16) for _ in range(3)])
        hpr = Ring([A.alloc([514], F32) for _ in range(4)])
        cr = Ring([A.alloc([512], F32) for _ in range(4)])
        xr = Ring([A.alloc([D], F32) for _ in range(2)])
        x2r = Ring([A.alloc([D], F32) for _ in range(2)])
        yr = Ring([A.alloc([D], F32) for _ in range(2)])
        S.memset(carry_p, 0.0)
        S.dma(T(carry_s.ap.rearrange("p a b -> p (a b)"), carry_s.bufs), cfm[l])
        jobs = [(qi, f2) for qi in range(len(TQ)) for f2 in range(11)]
        slab_of = {}

        def issue_slab(k):
            if k >= len(jobs):
                return
            _, f2_ = jobs[k]
            sl_ = slabs.next()
            wslab(sl_[:, :, 0:256], w_up[l].ap[:, f2_ * 256:(f2_ + 1) * 256])
            wslab(sl_[:, :, 256:512], w_up[l].ap[:, DFF + f2_ * 256:DFF + (f2_ + 1) * 256])
            slab_of[k] = sl_

        issue_slab(0)
        issue_slab(1)
        for qi, (c0, n) in enumerate(TQ):
            carry = carry_p if qi < 4 else carry_s
            nval = n if qi < 4 else TSV
            for f2 in range(11):
                k_ = qi * 11 + f2
                issue_slab(k_ + 2)
                sl = slab_of.pop(k_)
                for fi in range(2):
                    ft = 2 * f2 + fi
                    cs = []
                    for which in range(2):
                        fidx = which * 22 + ft
                        pb = psr.next()
                        off = which * 256 + fi * 128
                        for kt in range(8):
                            S.mm(pb[:, 0:n], sl[:, kt, off:off + 128], hT[:, kt, c0:c0 + n], start=(kt == 0), stop=(kt == 7))
                        hp_ = hpr.next()
                        S.copy(hp_[:, 0:2], carry[:, fidx, :], eng="act")
                        S.copy(hp_[:, 2:n + 2], pb[:, 0:n], eng="act")
                        S.copy(carry[:, fidx, :], hp_[:, nval:nval + 2], eng="act")
                        cw = lambda j: pf[:, l, PF_CONV + j * 44 + fidx:PF_CONV + j * 44 + fidx + 1]
                        cb = pf[:, l, PF_CONVB + fidx:PF_CONVB + fidx + 1]
                        c_ = cr.next()
                        eng = "dve"
                        S.ts(c_[:, 0:n], hp_[:, 0:n], cw(0), ALU.mult, cb, ALU.add, eng=eng)
                        S.stt(c_[:, 0:n], hp_[:, 1:n + 1], cw(1), c_[:, 0:n], ALU.mult, ALU.add)
                        S.stt(c_[:, 0:n], hp_[:, 2:n + 2], cw(2), c_[:, 0:n], ALU.mult, ALU.add)
                        cs.append(c_)
                    S.actf(cs[0][:, 0:n], cs[0][:, 0:n], AF.Silu)
                    S.tt(GT[:, ft, 0:n], cs[0][:, 0:n], cs[1][:, 0:n], ALU.mult)
            for tbl in range(n // 128):
                tb = c0 // 128 + tbl
                xt = xr.next()
                S.dma(xt, X1[tb * 128:(tb + 1) * 128, :])
                x2 = x2r.next()
                for half in range(2):
                    pb = psr.next()
                    for ft in range(22):
                        S.mm(pb, GT[:, ft, tbl * 128:(tbl + 1) * 128], wdn[:, ft, half * 512:(half + 1) * 512],
                             start=(ft == 0), stop=(ft == 21))
                    S.tt(x2[:, half * 512:(half + 1) * 512], pb, xt[:, half * 512:(half + 1) * 512], ALU.add)
                if l == 1:
                    rstd = norm_stats(x2, nsc)
                    y = yr.next()
                    S.stt(y, x2, rstd, gfin, ALU.mult, ALU.mult)
                    if tb < 16:
                        S.dma(yp[tb * 128:(tb + 1) * 128, :], y)
                    else:
                        S.dma(ys, y[0:TSV, :])
                else:
                    S.dma(X2[tb * 128:(tb + 1) * 128, :], x2)
                    norm_to_hT(x2, 1, PF_NMIX, tb, nsc)
        S.dma(cvp[l], T(carry_p.ap.rearrange("p a b -> p (a b)"), carry_p.bufs))
        S.dma(cvs[l], T(carry_s.ap.rearrange("p a b -> p (a b)"), carry_s.bufs))

    def dump(name, src):
        if name in dbg_out:
            S.barrier()
            S.dma(dbg_out[name], src)
            S.barrier()

    try:
        _program(S, locals())
    except StopBuild:
        pass
    S.emit()
    S.close()
    return nc


def _program(S, L):
    phase_norm0, phase_rwkv, phase_gproj, phase_da, phase_out, phase_ffn = (
        L["phase_norm0"], L["phase_rwkv"], L["phase_gproj"], L["phase_da"], L["phase_out"], L["phase_ffn"])
    dump, stop_after, hT, OZ, MT = L["dump"], L["stop_after"], L["hT"], L["OZ"], L["MT"]
    w_o_rw, w_o_da = L["w_o_rw"], L["w_o_da"]
    S.barrier()
    phase_norm0()
    S.barrier()
    dump("hT0", hT)
    done = False
    for l in range(2):
        if stop_after == ("norm", l):
            break
        phase_rwkv(l)
        S.barrier()
        dump(f"Z{l}", OZ)
        if stop_after == ("rwkv", l):
            break
        phase_gproj(l, w_o_rw, P_DA + P_RW + D, False)
        S.barrier()
        if stop_after == ("gproj1", l):
            break
        phase_da(l)
        S.barrier()
        dump(f"O{l}", OZ)
        if stop_after == ("da", l):
            break
        phase_gproj(l, w_o_da, P_DA + P_RW, True)
        S.barrier()
        dump(f"M{l}", MT)
        phase_out(l)
        S.barrier()
        dump(f"h2T{l}", hT)
        if stop_after == ("out", l):
            break
        phase_ffn(l)
        S.barrier()
        if stop_after == ("ffn", l):
            break


_NC_CACHE = {}


def _prep_inputs(inp):
    f = lambda a: np.ascontiguousarray(np.asarray(a, dtype=np.float32))
    g = {k: f(v) for k, v in inp.items()}
    consts = make_consts()

    def fm(v, nt):
        return v.reshape(nt, 128).T

    pfm = np.zeros((2, 128, NPF), np.float32)
    for l in range(2):
        pfm[l, :, PF_NMIX:PF_NMIX + 8] = fm(g["norm_mix"][l], 8)
        pfm[l, :, PF_NFFN:PF_NFFN + 8] = fm(g["norm_ffn"][l], 8)
        pfm[l, :, PF_MU:PF_MU + 26] = fm(g["rw_mu"][l], 26)
        pfm[l, :, PF_W0:PF_W0 + 8] = fm(g["rw_w0"][l], 8)
        pfm[l, :, PF_A0:PF_A0 + 8] = fm(g["rw_a0"][l], 8)
        pfm[l, :, PF_KK:PF_KK + 8] = fm(g["rw_k_k"][l], 8)
        pfm[l, :, PF_KA:PF_KA + 8] = fm(g["rw_k_a"][l], 8)
        pfm[l, :, PF_RK:PF_RK + 8] = fm(g["rw_r_k"][l].reshape(-1), 8)
        pfm[l, :, PF_LNW:PF_LNW + 8] = fm(g["rw_ln_w"][l], 8)
        pfm[l, :, PF_LNB:PF_LNB + 8] = fm(g["rw_ln_b"][l], 8)
        for j in range(3):
            pfm[l, :, PF_CONV + j * 44:PF_CONV + (j + 1) * 44] = fm(g["ffn_conv"][l, j], 44)
        pfm[l, :, PF_CONVB:PF_CONVB + 44] = fm(g["ffn_conv_b"][l], 44)
        pfm[l, :, PF_SUBLN] = g["da_subln"][l]
    shared = {
        "pfm": pfm, "consts": consts, "w_in": g["w_in"], "da_lambda": g["da_lambda"].reshape(512),
        "w_o_da": g["w_o_da"], "rw_w2": g["rw_w2"], "rw_a2": g["rw_a2"], "rw_g2": g["rw_g2"],
        "w_o_rw": g["w_o_rw"], "w_out": g["w_out"], "w_up": g["w_up"], "w_down": g["w_down"],
        "nfin": g["norm_final"],
    }
    maps = []
    for b in range(8):
        m = dict(shared)
        m["xp"] = g["x_prompt"][b]
        m["xs"] = g["x_sample"][b]
        m["ck"] = np.ascontiguousarray(g["cache_k"][:, b].reshape(2, TP, D))
        m["cv"] = np.ascontiguousarray(g["cache_v"][:, b].reshape(2, TP, D))
        sw = g["state_wkv"][:, b].reshape(2, 8, 2, 64, 64).transpose(0, 1, 2, 4, 3).reshape(2, 8, 128, 64)
        m["swkv"] = np.ascontiguousarray(sw)
        m["sfm"] = np.ascontiguousarray(g["state_shift"][:, b, 0].reshape(2, 26, 128).transpose(0, 2, 1))
        cf = g["state_ffn_conv"][:, b].reshape(2, 2, 44, 128).transpose(0, 3, 2, 1).reshape(2, 128, 88)
        m["cfm"] = np.ascontiguousarray(cf)
        maps.append(m)
    return maps


def _assemble(results):
    def st(name):
        return np.stack([np.asarray(r[name]) for r in results], axis=0)

    y_prompt = st("yp")
    y_sample = st("ys")
    k_prompt = st("kp").transpose(1, 0, 2, 3).reshape(2, 8, TP, 8, 128)
    v_prompt = st("vp").transpose(1, 0, 2, 3).reshape(2, 8, TP, 8, 128)
    k_sample = st("ks").transpose(1, 0, 2, 3).reshape(2, 8, TSV, 8, 128)
    v_sample = st("vs").transpose(1, 0, 2, 3).reshape(2, 8, TSV, 8, 128)

    def wkv(name):
        a = st(name).reshape(8, 2, 8, 2, 64, 64).transpose(1, 0, 2, 3, 5, 4).reshape(2, 8, 16, 64, 64)
        return np.ascontiguousarray(a)

    def shift(name):
        a = st(name).transpose(1, 0, 3, 2).reshape(2, 8, 1, P_RW)
        return np.ascontiguousarray(a)

    def conv(name):
        a = st(name).reshape(8, 2, 128, 44, 2).transpose(1, 0, 4, 3, 2).reshape(2, 8, 2, 2 * DFF)
        return np.ascontiguousarray(a)

    return (np.ascontiguousarray(y_prompt), np.ascontiguousarray(y_sample),
            np.ascontiguousarray(k_prompt), np.ascontiguousarray(v_prompt),
            wkv("wkvp"), shift("shp"), conv("cvp"),
            np.ascontiguousarray(k_sample), np.ascontiguousarray(v_sample),
            wkv("wkvs"), shift("shs"), conv("cvs"))


def kernel(**inputs):
    maps = _prep_inputs(inputs)
    nc = build()
    res = run_bass_kernel_spmd(nc, maps, core_ids=list(range(8)))
    return _assemble(res.results)
```

```python
import numpy as np
from contextlib import ExitStack

import concourse.bass as bass
import concourse.mybir as mybir

F32 = mybir.dt.float32
BF16 = mybir.dt.bfloat16
F32R = mybir.dt.float32r
AF = mybir.ActivationFunctionType
ALU = mybir.AluOpType
AX = mybir.AxisListType

ENGS = ("pe", "act", "dve", "pool", "sp")
SEM_EPOCH = 20000
DMA_ROT = 8


class Buf:
    __slots__ = ("w", "r", "name")

    def __init__(self, name=""):
        self.w = None
        self.r = []
        self.name = name


class T:
    __slots__ = ("ap", "bufs")

    def __init__(self, ap, bufs):
        self.ap = ap
        self.bufs = tuple(bufs)

    def __getitem__(self, key):
        return T(self.ap[key], self.bufs)

    def v(self, ap):
        return T(ap, self.bufs)

    def bitcast(self, dt):
        return T(self.ap.bitcast(dt), self.bufs)


class Rec:
    __slots__ = ("eng", "idx", "fn", "deps", "dma", "signaled", "ev", "selfwait")

    def __init__(self, eng, idx, fn, dma):
        self.eng = eng
        self.idx = idx
        self.fn = fn
        self.deps = []
        self.dma = dma
        self.signaled = False
        self.ev = None
        self.selfwait = None


class Sched:
    def __init__(self, nc):
        self.nc = nc
        self.q = {e: [] for e in ENGS}
        self.stack = ExitStack()
        self.nbuf = 0
        self.same_engine_sync = True
        self._bar_pos = {}
        self._pending = {e: [] for e in ENGS}

    def sbuf(self, name, shape, dtype, nbufs=1):
        h = self.stack.enter_context(self.nc.sbuf_tensor(name, list(shape), dtype))
        return T(h[:], [Buf(name)])

    def psum(self, name, shape, dtype=F32):
        h = self.stack.enter_context(self.nc.psum_tensor(name, list(shape), dtype))
        return T(h[:], [Buf(name)])

    def dram(self, name, shape, dtype, kind):
        h = self.nc.dram_tensor(name, list(shape), dtype, kind=kind)
        return T(h.ap(), [Buf(name)])

    def newbuf(self, name=""):
        return Buf(name)

    def add(self, eng, fn, reads=(), writes=(), dma=False):
        q = self.q[eng]
        rec = Rec(eng, len(q), fn, dma)
        deps = {}
        for t in reads:
            for b in t.bufs:
                if b.w is not None:
                    deps[id(b.w)] = b.w
        for t in writes:
            for b in t.bufs:
                if b.w is not None:
                    deps[id(b.w)] = b.w
                for r in b.r:
                    deps[id(r)] = r
        if self._pending[eng]:
            for d in self._pending[eng]:
                deps[id(d)] = d
            self._pending[eng] = []
            barrier_deps = True
        else:
            barrier_deps = False
        for d in deps.values():
            if d is rec:
                continue
            if d.eng == eng and not d.dma and not dma:
                if eng == "pe" or eng == "sp" or not self.same_engine_sync:
                    continue
            rec.deps.append(d)
        for t in reads:
            for b in t.bufs:
                b.r.append(rec)
        for t in writes:
            for b in t.bufs:
                b.w = rec
                b.r = []
        q.append(rec)
        return rec

    def emit(self):
        nc = self.nc
        for e in ENGS:
            for rec in self.q[e]:
                for d in rec.deps:
                    d.signaled = True
        sems = {}
        for e in ENGS:
            cnt = 0
            for rec in self.q[e]:
                if rec.dma:
                    continue
                if rec.signaled:
                    ep = cnt // SEM_EPOCH
                    key = (e, ep)
                    if key not in sems:
                        sems[key] = self.stack.enter_context(nc.semaphore(f"s_{e}_{ep}"))
                    rec.ev = (sems[key], cnt % SEM_EPOCH + 1, key)
                    cnt += 1
        self.final_waits = []
        for e in ENGS:
            j = 0
            last = {}
            for rec in self.q[e]:
                if not rec.dma:
                    continue
                slot = j % DMA_ROT
                key = ("dma", e, slot)
                if key not in sems:
                    sems[key] = self.stack.enter_context(nc.semaphore(f"d_{e}_{slot}"))
                val = 16 * (j // DMA_ROT + 1)
                rec.ev = (sems[key], val, key)
                if j >= DMA_ROT:
                    rec.selfwait = (sems[key], val - 16, key)
                last[key] = (sems[key], val, key)
                j += 1
            self.final_waits.extend(last.values())

        block = self.stack.enter_context(nc.Block())
        sched = self

        def run(engname, eobj):
            waited = {}
            for rec in sched.q[engname]:
                waits = {}
                if rec.selfwait is not None:
                    s, v, key = rec.selfwait
                    waits[key] = (s, v)
                for d in rec.deps:
                    s, v, key = d.ev
                    if key in waits:
                        if waits[key][1] < v:
                            waits[key] = (s, v)
                    else:
                        waits[key] = (s, v)
                for key, (s, v) in waits.items():
                    if waited.get(key, 0) >= v:
                        continue
                    eobj.wait_ge(s, v)
                    waited[key] = v
                ins = rec.fn(eobj)
                if rec.dma:
                    ins.then_inc(rec.ev[0], 16)
                elif rec.signaled:
                    ins.then_inc(rec.ev[0], 1)
            if engname == "sp":
                for s, v, key in sched.final_waits:
                    eobj.wait_ge(s, v)

        @block.tensor
        def _(e):
            run("pe", e)

        @block.scalar
        def _(e):
            run("act", e)

        @block.vector
        def _(e):
            run("dve", e)

        @block.gpsimd
        def _(e):
            run("pool", e)

        @block.sync
        def _(e):
            run("sp", e)

    def close(self):
        self.stack.close()

    def dma(self, out, in_, eng="sp", **kw):
        return self.add(eng, lambda e: e.dma_start(out=out.ap, in_=in_.ap, **kw),
                        reads=[in_], writes=[out], dma=True)

    def mm(self, out, lhsT, rhs, start=True, stop=True, extra_reads=(), **kw):
        return self.add("pe", lambda e: e.matmul(out.ap, lhsT.ap, rhs.ap, start=start, stop=stop, **kw),
                        reads=[lhsT, rhs, *extra_reads], writes=[out])

    def tr(self, out, in_, ident, **kw):
        return self.add("pe", lambda e: e.transpose(out.ap, in_.ap, ident.ap, **kw),
                        reads=[in_, ident], writes=[out])

    def actf(self, out, in_, func, bias=None, scale=None, accum=None, eng="act"):
        kw = {}
        reads = [in_]
        writes = [out]
        if bias is not None:
            if isinstance(bias, T):
                kw["bias"] = bias.ap
                reads.append(bias)
            else:
                kw["bias"] = bias
        if scale is not None:
            if isinstance(scale, T):
                kw["scale"] = scale.ap
                reads.append(scale)
            else:
                kw["scale"] = scale
        if accum is not None:
            kw["accum_out"] = accum.ap
            writes.append(accum)
        return self.add("act", lambda e: e.activation(out.ap, in_.ap, func, **kw), reads=reads, writes=writes)

    def tt(self, out, a, b, op, eng="dve"):
        return self.add(eng, lambda e: e.tensor_tensor(out.ap, a.ap, b.ap, op), reads=[a, b], writes=[out])

    def ts(self, out, a, s1, op0, s2=None, op1=None, accum=None, eng="dve"):
        reads = [a]
        writes = [out]
        s1v = s1.ap if isinstance(s1, T) else s1
        s2v = s2.ap if isinstance(s2, T) else s2
        if isinstance(s1, T):
            reads.append(s1)
        if isinstance(s2, T):
            reads.append(s2)
        kw = {}
        if op1 is not None:
            kw["op1"] = op1
        if accum is not None:
            kw["accum_out"] = accum.ap
            writes.append(accum)
        return self.add(eng, lambda e: e.tensor_scalar(out.ap, a.ap, s1v, s2v, op0, **kw), reads=reads, writes=writes)

    def stt(self, out, a, s, b, op0, op1, eng="dve"):
        reads = [a, b]
        sv = s.ap if isinstance(s, T) else s
        if isinstance(s, T):
            reads.append(s)
        return self.add(eng, lambda e: e.scalar_tensor_tensor(out.ap, a.ap, sv, b.ap, op0, op1), reads=reads, writes=[out])

    def copy(self, out, in_, eng="dve"):
        if eng == "act":
            return self.add("act", lambda e: e.copy(out.ap, in_.ap), reads=[in_], writes=[out])
        return self.add(eng, lambda e: e.tensor_copy(out.ap, in_.ap), reads=[in_], writes=[out])

    def memset(self, out, val, eng="pool"):
        return self.add(eng, lambda e: e.memset(out.ap, val), reads=[], writes=[out])

    def recip(self, out, in_):
        return self.add("dve", lambda e: e.reciprocal(out.ap, in_.ap), reads=[in_], writes=[out])

    def scan(self, out, d0, d1, init, op0, op1):
        reads = [d0, d1]
        iv = init.ap if isinstance(init, T) else init
        if isinstance(init, T):
            reads.append(init)
        return self.add("dve", lambda e: e.tensor_tensor_scan(out.ap, d0.ap, d1.ap, iv, op0, op1), reads=reads, writes=[out])


def _barrier(self):
    deps = []
    for e in ENGS:
        q = self.q[e]
        last_c = None
        for rec in reversed(q):
            if not rec.dma:
                last_c = rec
                break
        if last_c is not None:
            deps.append(last_c)
        for rec in q[self._bar_pos.get(e, 0):]:
            if rec.dma:
                deps.append(rec)
        self._bar_pos[e] = len(q)
    for e in ENGS:
        self._pending[e] = list(deps)


Sched.barrier = _barrier


from concourse.bass_utils import run_bass_kernel_spmd

D = 1024
TP = 2048
TSV = 32
NTOK = 2176
NTB = 17
P_DA = 3072
P_RW = 3328
PTOT = 8448
DFF = 2816
EPS = 1e-6
GN_EPS = 64e-5
ROPE_THETA = 500000.0
TQ = [(0, 512), (512, 512), (1024, 512), (1536, 512), (2048, 128)]
PF_NMIX, PF_NFFN, PF_MU, PF_W0, PF_A0, PF_KK, PF_KA, PF_RK, PF_LNW, PF_LNB = 0, 8, 16, 42, 50, 58, 66, 74, 82, 90
PF_CONV, PF_CONVB, PF_SUBLN, NPF = 98, 230, 274, 275
C_ID, C_BD, C_MU, C_ML, C_MUI, C_RST, C_COS, C_SIN, NCONST = 0, 128, 256, 384, 512, 576, 832, 1104, 1376
DEC_C = -0.6065306597126334


def make_consts():
    c = np.zeros((128, NCONST), np.float32)
    c[:, C_ID:C_ID + 128] = np.eye(128, dtype=np.float32)
    p = np.arange(128)
    h = p // 64
    s = p % 64
    bd = (h[:, None] == h[None, :]).astype(np.float32)
    c[:, C_BD:C_BD + 128] = bd
    c[:, C_MU:C_MU + 128] = bd * (s[:, None] < s[None, :])
    c[:, C_ML:C_ML + 128] = bd * (s[:, None] > s[None, :])
    c[:, C_MUI:C_MUI + 64] = (s[:, None] <= np.arange(64)[None, :])
    rst = np.ones(256, np.float32)
    rst[::64] = 0
    c[:, C_RST:C_RST + 256] = rst[None, :]
    inv = (np.float32(ROPE_THETA) ** (-np.arange(0, 16, 2, dtype=np.float32) / np.float32(16))).astype(np.float32)
    for tb in range(NTB):
        pos = (tb * 128 + p) if tb < 16 else (TP + p)
        ang = pos.astype(np.float32)[:, None] * inv[None, :]
        co = np.cos(ang).astype(np.float32)
        si = np.sin(ang).astype(np.float32)
        c[:, C_COS + tb * 16:C_COS + tb * 16 + 8] = co
        c[:, C_COS + tb * 16 + 8:C_COS + tb * 16 + 16] = co
        c[:, C_SIN + tb * 16:C_SIN + tb * 16 + 8] = si
        c[:, C_SIN + tb * 16 + 8:C_SIN + tb * 16 + 16] = si
    return c


CHAIN_BF16 = True
LACC_ENG = ("dve", "dve")


def r32(t):
    if CHAIN_BF16:
        return t
    return T(t.ap.bitcast(F32R), t.bufs)


class Arena:
    def __init__(self, t, width):
        self.t = t
        self.W = width
        self.off = 0

    def seek(self, off):
        self.off = off

    def alloc(self, free_shape, dtype):
        n = 1
        for d in free_shape:
            n *= d
        words = n if dtype != BF16 else (n + 1) // 2
        words = (words + 7) // 8 * 8
        assert self.off + words <= self.W, ("arena overflow", self.off, words, self.W)
        ap = self.t.ap[:, self.off:self.off + words]
        if dtype == BF16:
            ap = ap.bitcast(BF16)
        ap = ap[:, 0:n]
        if len(free_shape) > 1:
            names = [f"d{i}" for i in range(len(free_shape))]
            pat = "p (" + " ".join(names) + ") -> p " + " ".join(names)
            kw = {nm: sz for nm, sz in zip(names[:-1], free_shape[:-1])}
            ap = ap.rearrange(pat, **kw)
        self.off += words
        return T(ap, [Buf()])


def _run_interleaved(gens):
    alive = list(gens)
    while alive:
        for gen in list(alive):
            try:
                next(gen)
            except StopIteration:
                alive.remove(gen)


B_WEIGHT = 7


def _run_weighted(gens, weights):
    alive = [(g, w) for g, w in zip(gens, weights)]
    while alive:
        for item in list(alive):
            g, w = item
            for _ in range(w):
                try:
                    next(g)
                except StopIteration:
                    alive.remove(item)
                    break


class Ring:
    def __init__(self, items):
        self.items = items
        self.i = 0

    def next(self):
        t = self.items[self.i % len(self.items)]
        self.i += 1
        return t


class StopBuild(Exception):
    pass


def build(dbg=None, stop_after=None):
    def chk(tag, l=0):
        if stop_after == (tag, l):
            raise StopBuild()

    nc = bass.Bass("TRN2", target_bir_lowering=False)
    S = Sched(nc)

    def din(name, shape):
        return S.dram(name, shape, F32, "ExternalInput")

    def dout(name, shape):
        return S.dram(name, shape, F32, "ExternalOutput")

    xp = din("xp", [TP, D])
    xs = din("xs", [TSV, D])
    ck = din("ck", [2, TP, D])
    cv = din("cv", [2, TP, D])
    swkv = din("swkv", [2, 8, 128, 64])
    sfm = din("sfm", [2, 128, 26])
    cfm = din("cfm", [2, 128, 88])
    pfm = din("pfm", [2, 128, NPF])
    consts = din("consts", [128, NCONST])
    w_in = din("w_in", [2, D, PTOT])
    da_lambda = din("da_lambda", [512])
    w_o_da = din("w_o_da", [2, D, D])
    rw_w2 = din("rw_w2", [2, 64, D])
    rw_a2 = din("rw_a2", [2, 64, D])
    rw_g2 = din("rw_g2", [2, 128, D])
    w_o_rw = din("w_o_rw", [2, D, D])
    w_out = din("w_out", [2, D, D])
    w_up = din("w_up", [2, D, 2 * DFF])
    w_down = din("w_down", [2, DFF, D])
    nfin = din("nfin", [D])

    yp = dout("yp", [TP, D])
    ys = dout("ys", [TSV, D])
    kp = dout("kp", [2, TP, D])
    vp = dout("vp", [2, TP, D])
    wkvp = dout("wkvp", [2, 8, 128, 64])
    shp = dout("shp", [2, 128, 26])
    cvp = dout("cvp", [2, 128, 88])
    ks = dout("ks", [2, TSV, D])
    vs = dout("vs", [2, TSV, D])
    wkvs = dout("wkvs", [2, 8, 128, 64])
    shs = dout("shs", [2, 128, 26])
    cvs = dout("cvs", [2, 128, 88])
    X1 = S.dram("X1", [NTOK, D], F32, "Internal")
    X2 = S.dram("X2", [NTOK, D], F32, "Internal")
    dbg_out = {}
    if dbg:
        for name, shape in dbg.items():
            dbg_out[name] = dout("dbg_" + name, shape)

    cst = S.sbuf("cst", [128, NCONST], F32)
    S.dma(cst, consts)
    ident_bf = S.sbuf("ident_bf", [128, 128], BF16)
    S.copy(ident_bf, cst[:, C_ID:C_ID + 128])
    ones_bf = S.sbuf("ones_bf", [128, 128], BF16)
    S.memset(ones_bf, 1.0)
    ones_f = S.sbuf("ones_f", [128, 128], F32)
    S.memset(ones_f, 1.0)
    ident_f = cst[:, C_ID:C_ID + 128]
    bdmask = cst[:, C_BD:C_BD + 128]
    bd_bf = S.sbuf("bd_bf", [128, 128], BF16)
    S.copy(bd_bf, cst[:, C_BD:C_BD + 128])
    fr = None if CHAIN_BF16 else S.sbuf("fr", [128, 21, 256], F32)
    pf = S.sbuf("pf", [128, 2, NPF], F32)
    for l in range(2):
        S.dma(pf[:, l, :], pfm[l])
    gfin = S.sbuf("gfin", [128, D], F32)
    S.dma(gfin, nfin.v(nfin.ap.partition_broadcast(128)))
    lamb = S.sbuf("lamb", [128, 512], F32)
    S.dma(lamb, da_lambda.v(da_lambda.ap.partition_broadcast(128)))
    lamv = S.sbuf("lamv", [128, 16], F32)
    epsb = S.sbuf("epsb", [128, 2], F32)
    S.memset(epsb[:, 0:1], EPS)
    S.memset(epsb[:, 1:2], GN_EPS)
    ljunk = S.sbuf("ljunk", [128, 64], F32)
    shp_t = S.sbuf("shp_t", [128, 26], F32)
    shs_t = S.sbuf("shs_t", [128, 26], F32)
    carry_p = S.sbuf("carry_p", [128, 44, 2], F32)
    carry_s = S.sbuf("carry_s", [128, 44, 2], F32)

    ps = [S.psum(f"ps{i}", [128, 512], F32) for i in range(8)]
    psr = Ring(ps)
    halves = []
    for i in range(8):
        halves.append(T(ps[i].ap[:, 0:256], ps[i].bufs))
        halves.append(T(ps[i].ap[:, 256:512], ps[i].bufs))

    AW = ((nc.sbuf_bytes_remaining - 256) // 4) // 8 * 8
    ar = S.sbuf("arena", [128, AW], F32)
    A = Arena(ar, AW)
    hT = A.alloc([8, NTOK], BF16)
    OZ = A.alloc([8, NTOK], BF16)
    MT_OFF = A.off
    MT = A.alloc([8, NTOK], BF16)
    LOC = A.off
    OZ_OFF = MT_OFF - (MT_OFF - 0) // 2 if False else None
    S.memset(hT, 0.0)
    S.memset(OZ, 0.0, eng="dve")
    S.memset(MT, 0.0)

    for l in range(2):
        lam_init = 0.8 - 0.6 * float(np.exp(-0.3 * l))
        b = l * 256
        pr = ljunk
        S.tt(pr, lamb[:, b:b + 64], lamb[:, b + 64:b + 128], ALU.mult)
        S.ts(pr, pr, 1.0, ALU.mult, None, ALU.add, accum=lamv[:, l * 8 + 4:l * 8 + 5])
        S.tt(pr, lamb[:, b + 128:b + 192], lamb[:, b + 192:b + 256], ALU.mult)
        S.ts(pr, pr, 1.0, ALU.mult, None, ALU.add, accum=lamv[:, l * 8 + 5:l * 8 + 6])
        S.actf(lamv[:, l * 8 + 2:l * 8 + 4], lamv[:, l * 8 + 4:l * 8 + 6], AF.Exp)
        S.tt(lamv[:, l * 8:l * 8 + 1], lamv[:, l * 8 + 3:l * 8 + 4], lamv[:, l * 8 + 2:l * 8 + 3], ALU.subtract)
        S.ts(lamv[:, l * 8:l * 8 + 1], lamv[:, l * 8:l * 8 + 1], -lam_init, ALU.add)
        S.ts(lamv[:, l * 8 + 1:l * 8 + 2], pf[:, l, PF_SUBLN:PF_SUBLN + 1], 1.0 - lam_init, ALU.mult)

    def wslab(dst, src_ap):
        return S.dma(dst, T(src_ap.rearrange("(kt p) c -> p kt c", p=128), ()), eng="pool")

    def make_norm_scratch():
        d = {}
        d["junk"] = A.alloc([D], BF16)
        d["xn"] = Ring([A.alloc([D], BF16) for _ in range(2)])
        d["st"] = Ring([A.alloc([4], F32) for _ in range(3)])
        return d

    def norm_stats(x_t, nsc):
        st = nsc["st"].next()
        S.actf(nsc["junk"], x_t, AF.Square, accum=st[:, 0:1])
        S.actf(st[:, 1:2], st[:, 0:1], AF.Sqrt, scale=1.0 / D, bias=epsb[:, 0:1])
        S.recip(st[:, 2:3], st[:, 1:2])
        return st[:, 2:3]

    def norm_to_hT(x_t, l, pfoff, tb, nsc):
        rstd = norm_stats(x_t, nsc)
        xn = nsc["xn"].next()
        S.ts(xn, x_t, rstd, ALU.mult)
        pb = psr.next()
        pst = T(pb.ap.bitcast(BF16).rearrange("p (k t) -> p k t", k=8), pb.bufs)
        for kt in range(8):
            S.tr(pst[:, kt, :], xn[:, kt * 128:(kt + 1) * 128], ident_bf)
        g = pf[:, l, pfoff:pfoff + 8]
        S.tt(hT[:, :, tb * 128:(tb + 1) * 128], pst, g.v(g.ap.unsqueeze(2).broadcast_to([128, 8, 128])), ALU.mult)

    def x_rows(tb):
        return (xp[tb * 128:(tb + 1) * 128, :], 128) if tb < 16 else (xs, TSV)

    def phase_norm0():
        A.seek(LOC)
        nsc = make_norm_scratch()
        xr = Ring([A.alloc([D], F32) for _ in range(3)])
        for tb in range(NTB):
            xt = xr.next()
            src, nr = x_rows(tb)
            if nr < 128:
                S.memset(xt, 0.0)
            S.dma(xt[0:nr, :], src)
            norm_to_hT(xt, 0, PF_NMIX, tb, nsc)

    def phase_rwkv(l):
        A.seek(MT_OFF)
        w_l = w_in[l]
        mu = lambda c: pf[:, l, PF_MU + c:PF_MU + c + 1]
        W2p = A.alloc([D], BF16)
        A2p = A.alloc([D], BF16)
        G2 = A.alloc([D], BF16)
        S.memset(W2p[64:128, :], 0.0)
        S.memset(A2p[0:64, :], 0.0)
        S.dma(W2p[0:64, :], rw_w2[l], eng="pool")
        S.dma(A2p[64:128, :], rw_a2[l], eng="pool")
        S.dma(G2, rw_g2[l], eng="pool")
        tanh_w = A.alloc([NTOK], BF16)
        raw_w = A.alloc([NTOK], BF16)
        sig_g = A.alloc([NTOK], BF16)
        sfm_t = A.alloc([26], F32)
        S.dma(sfm_t, sfm[l])
        NB = 256
        GRP_OFF = A.off
        Wl = A.alloc([8, 256], BF16)
        wslab(Wl, w_l.ap[:, P_DA + 3072:P_DA + 3328])
        u_ring = Ring([A.alloc([513], F32) for _ in range(3)])
        tmp = Ring([A.alloc([512], F32) for _ in range(6)])

        def shifted(psb, n, carry_src, mucol, u_out_last=None, last_idx=None):
            U = u_ring.next()
            if carry_src is None:
                S.memset(U[:, 0:1], 0.0, eng="dve")
            else:
                S.copy(U[:, 0:1], carry_src, eng="dve")
            S.copy(U[:, 1:n + 1], psb[:, 0:n], eng="act")
            d = tmp.next()
            S.tt(d[:, 0:n], U[:, 0:n], U[:, 1:n + 1], ALU.subtract)
            return d, U

        prev = {0: None, 1: None}
        for qi, (c0, n) in enumerate(TQ):
            for which in range(2):
                pb = psr.next()
                for kt in range(8):
                    S.mm(pb[:, 0:n], Wl[:, kt, which * 128:(which + 1) * 128], hT[:, kt, c0:c0 + n],
                         start=(kt == 0), stop=(kt == 7))
                mc = 24 + which
                if qi == 0:
                    carry = None
                elif qi == 4:
                    carry = sfm_t[:, mc:mc + 1]
                else:
                    carry = prev[which]
                d, U = shifted(pb, n, carry, mc)
                us = tmp.next()
                S.stt(us[:, 0:n], d[:, 0:n], mu(mc), U[:, 1:n + 1], ALU.mult, ALU.add)
                prev[which] = U[:, n:n + 1]
                if qi == 3:
                    S.copy(shp_t[:, mc:mc + 1], U[:, n:n + 1], eng="pool")
                if qi == 4:
                    S.copy(shs_t[:, mc:mc + 1], U[:, TSV:TSV + 1], eng="pool")
                if which == 0:
                    S.actf(tanh_w[:, c0:c0 + n], us[:, 0:n], AF.Tanh)
                    S.copy(raw_w[:, c0:c0 + n], us[:, 0:n], eng="act")
                else:
                    S.actf(sig_g[:, c0:c0 + n], us[:, 0:n], AF.Sigmoid)

        chk("rwkv_lora", l)
        S.barrier()
        A.seek(GRP_OFF)
        Wg = A.alloc([8, 768], BF16)
        ASET = []
        for _ in range(2):
            ASET.append((A.alloc([2, NB], BF16), A.alloc([2, NB], BF16), A.alloc([2, NB], BF16), A.alloc([2, NB], BF16),
                         A.alloc([2, NB], BF16), A.alloc([2, NB], F32), A.alloc([2, NB], F32)))
        YT = A.alloc([2, NB], F32)
        S32 = A.alloc([2, 128], F32)
        Sb = A.alloc([2, 128], BF16)
        swt = A.alloc([2, 64], F32)
        carr = A.alloc([8], F32)
        t256g = [Ring([A.alloc([NB], F32) for _ in range(11)]) for _ in range(2)]
        b256g = [Ring([A.alloc([NB], BF16) for _ in range(3)]) for _ in range(2)]
        usrkg = [Ring([A.alloc([NB], F32) for _ in range(2)]) for _ in range(2)]
        u257g = [Ring([A.alloc([NB + 1], F32) for _ in range(2)]) for _ in range(2)]
        NCH = 4
        wide = lambda: A.alloc([2, 128], BF16)
        BDa = [wide() for _ in range(NCH)]
        BDb = [wide() for _ in range(NCH)]
        BDk = [wide() for _ in range(NCH)]
        BDx = Ring([wide() for _ in range(3)])
        AKm = [wide() for _ in range(NCH)]
        Vm = [wide() for _ in range(NCH)]
        Bhm = [wide() for _ in range(NCH)]
        Khm = [wide() for _ in range(NCH)]
        Rm = [wide() for _ in range(NCH)]
        Um = Ring([wide() for _ in range(1)])
        fi = [0]

        def frt():
            if CHAIN_BF16:
                return wide()
            t = T(fr.ap[:, fi[0], :].rearrange("p (g c) -> p g c", g=2), [Buf()])
            fi[0] += 1
            return t

        Nm = [[frt() for _ in range(NCH)] for _ in range(2)]
        NTm = [[frt() for _ in range(NCH)] for _ in range(2)]
        Pm = [frt() for _ in range(NCH)]
        Hm = Ring([frt() for _ in range(1)])
        class _HB:
            def __init__(self):
                self.k = 0
                self.pend = None

            def next(self):
                if self.pend is not None:
                    t = self.pend
                    self.pend = None
                    return t
                i = 2 + self.k % 6
                self.k += 1
                self.pend = halves[2 * i + 1]
                return halves[2 * i]

            def newbank(self):
                self.pend = None

        hbr = _HB()
        psrA = Ring(ps[0:2])

        def w4(t):
            return T(t.ap.rearrange("p g (h t) -> p g h t", h=2), t.bufs)

        def v3(t):
            return T(t.ap.rearrange("p (g c) -> p g c", g=2), t.bufs)

        def v3b(t):
            return T(t.ap.bitcast(BF16)[:, 0:256].rearrange("p (g c) -> p g c", g=2), t.bufs)

        def bcast_mask(coff):
            m = cst[:, coff:coff + 128]
            return T(m.ap.unsqueeze(1).broadcast_to([128, 2, 128]), m.bufs)

        mU_b = bcast_mask(C_MU)
        mL_b = bcast_mask(C_ML)
        id_b = bcast_mask(C_ID)
        bd4 = T(bdmask.ap.rearrange("p (h t) -> p h t", h=2).unsqueeze(1).broadcast_to([128, 2, 2, 64]), bdmask.bufs)
        mui = cst[:, C_MUI:C_MUI + 64]
        mui4 = T(mui.ap.unsqueeze(1).unsqueeze(1).broadcast_to([128, 2, 2, 64]), mui.bufs)
        rstm = cst[:, C_RST:C_RST + NB]

        for grp in range(4):
            for x in range(3):
                wslab(Wg[:, :, x * 256:(x + 1) * 256],
                      w_l.ap[:, P_DA + x * 1024 + grp * 256:P_DA + x * 1024 + (grp + 1) * 256])
            for seq in range(2):
                blocks = [(i * NB, NB) for i in range(8)] if seq == 0 else [(TP, 64)]
                nvalid_last = NB if seq == 0 else TSV
                if seq == 0:
                    S.memset(S32, 0.0, eng="dve")
                    S.memset(Sb, 0.0, eng="dve")
                else:
                    for g in range(2):
                        S.dma(swt[:, g, :], swkv[l, 2 * grp + g])
                    S.tt(w4(S32), swt.v(swt.ap.unsqueeze(2).broadcast_to([128, 2, 2, 64])), bd4, ALU.mult)
                    S.copy(Sb, S32)
                carry = {}
                for _once in range(1):
                    def stage_a(g, bi, c0, n):
                        t256, b256, usrk, u257 = t256g[g], b256g[g], usrkg[g], u257g[g]
                        AT, BT, KT_, RT, VT, WT, BON = ASET[bi % 2]
                        hp = 2 * grp + g
                        us = {}
                        for x in range(3):
                            pb = psrA.next()
                            off = x * 256 + g * 128
                            for kt in range(8):
                                S.mm(pb[:, 0:n], Wg[:, kt, off:off + 128], hT[:, kt, c0:c0 + n],
                                     start=(kt == 0), stop=(kt == 7))
                            mc = x * 8 + hp
                            U = u257.next()
                            if bi == 0:
                                if seq == 0:
                                    S.memset(U[:, 0:1], 0.0, eng="dve")
                                else:
                                    S.copy(U[:, 0:1], sfm_t[:, mc:mc + 1], eng="dve")
                            else:
                                S.copy(U[:, 0:1], carry[(g, x)], eng="dve")
                            S.copy(U[:, 1:n + 1], pb[:, 0:n], eng="act")
                            d = t256.next()
                            S.tt(d[:, 0:n], U[:, 0:n], U[:, 1:n + 1], ALU.subtract)
                            ut = usrk.next() if x < 2 else t256.next()
                            us[x] = ut[:, 0:n]
                            S.stt(ut[:, 0:n], d[:, 0:n], mu(mc), U[:, 1:n + 1], ALU.mult, ALU.add)
                            S.copy(carr[:, g * 3 + x:g * 3 + x + 1], U[:, n:n + 1], eng="act")
                            carry[(g, x)] = carr[:, g * 3 + x:g * 3 + x + 1]
                            yield
                            if bi == len(blocks) - 1:
                                sht = shp_t if seq == 0 else shs_t
                                S.copy(sht[:, mc:mc + 1], U[:, nvalid_last:nvalid_last + 1], eng="pool")
                        us_r, us_k, us_v = us[0], us[1], us[2]
                        S.copy(VT[:, g, 0:n], us_v, eng="act")
                        yield
                        pcol = lambda o: pf[:, l, o + hp:o + hp + 1]
                        pb = psrA.next()
                        S.mm(pb[:, 0:n], W2p[:, hp * 128:(hp + 1) * 128], tanh_w[:, c0:c0 + n])
                        yield
                        sg = t256.next()
                        S.actf(sg[:, 0:n], pb[:, 0:n], AF.Sigmoid, bias=pcol(PF_W0))
                        yield
                        lw = t256.next()
                        S.add("act", (lambda o_, i_: (lambda e: e.mul(o_.ap, i_.ap, DEC_C)))(lw[:, 0:n], sg[:, 0:n]), reads=[sg], writes=[lw])
                        yield
                        pb = psrA.next()
                        S.mm(pb[:, 0:n], A2p[:, hp * 128:(hp + 1) * 128], raw_w[:, c0:c0 + n])
                        yield
                        a_t = t256.next()
                        S.actf(a_t[:, 0:n], pb[:, 0:n], AF.Sigmoid, bias=pcol(PF_A0))
                        yield
                        kk = t256.next()
                        S.ts(kk[:, 0:n], us_k, pcol(PF_KK), ALU.mult)
                        yield
                        sq = b256.next()
                        S.actf(sq[:, 0:n], kk[:, 0:n], AF.Square)
                        yield
                        pb = psrA.next()
                        S.mm(pb[:, 0:n], bd_bf, sq[:, 0:n])
                        yield
                        rn = t256.next()
                        S.actf(rn[:, 0:n], pb[:, 0:n], AF.Sqrt)
                        yield
                        S.ts(rn[:, 0:n], rn[:, 0:n], 1e-12, ALU.max)
                        yield
                        S.recip(rn[:, 0:n], rn[:, 0:n])
                        yield
                        kkn = kk
                        S.tt(kkn[:, 0:n], kk[:, 0:n], rn[:, 0:n], ALU.mult)
                        yield
                        t1 = t256.next()
                        S.ts(t1[:, 0:n], a_t[:, 0:n], -1.0, ALU.add, pcol(PF_KA), ALU.mult)
                        yield
                        kmod = t256.next()
                        S.stt(kmod[:, 0:n], t1[:, 0:n], 1.0, us_k, ALU.add, ALU.mult)
                        yield
                        bb = t1
                        S.tt(bb[:, 0:n], kkn[:, 0:n], a_t[:, 0:n], ALU.mult)
                        yield
                        cum = t256.next()
                        S.scan(cum[:, 0:n], rstm[:, 0:n], lw[:, 0:n], 0.0, ALU.mult, ALU.add)
                        yield
                        cumex = sg
                        S.tt(cumex[:, 0:n], cum[:, 0:n], lw[:, 0:n], ALU.subtract)
                        yield
                        S.actf(WT[:, g, 0:n], cum[:, 0:n], AF.Exp)
                        yield
                        winv = lw
                        S.actf(winv[:, 0:n], cum[:, 0:n], AF.Exp, scale=-1.0)
                        yield
                        wex = cum
                        S.actf(wex[:, 0:n], cumex[:, 0:n], AF.Exp)
                        yield
                        S.stt(AT[:, g, 0:n], kkn[:, 0:n], -1.0, wex[:, 0:n], ALU.mult, ALU.mult)
                        yield
                        S.tt(BT[:, g, 0:n], bb[:, 0:n], winv[:, 0:n], ALU.mult)
                        yield
                        S.tt(KT_[:, g, 0:n], kmod[:, 0:n], winv[:, 0:n], ALU.mult)
                        yield
                        S.tt(RT[:, g, 0:n], us_r, WT[:, g, 0:n], ALU.mult)
                        yield
                        rk = b256.next()
                        S.stt(rk[:, 0:n], us_r, pcol(PF_RK), kmod[:, 0:n], ALU.mult, ALU.mult)
                        yield
                        pb = psrA.next()
                        S.mm(pb[:, 0:n], bd_bf, rk[:, 0:n])
                        yield
                        S.tt(BON[:, g, 0:n], pb[:, 0:n], us_v, ALU.mult)
                        yield
                        if seq == 1:
                            for arr in (AT, BT, KT_, RT, VT):
                                S.memset(arr[:, g, TSV:64], 0.0)

                    def stage_b(bi, c0, n):
                        nch = n // 64
                        AT, BT, KT_, RT, VT, WT, BON = ASET[bi % 2]
                        def chunk_src(arr, ci):
                            a = arr[:, :, ci * 64:(ci + 1) * 64]
                            return T(a.ap.unsqueeze(2).broadcast_to([128, 2, 2, 64]), a.bufs)

                        wcols = []
                        for ci in range(nch):
                            lastc = ci * 64 + (63 if seq == 0 else TSV - 1)
                            wc = WT[:, :, lastc:lastc + 1]
                            wcols.append(wc)
                            wcb = T(wc.ap.broadcast_to([128, 2, 128]), wc.bufs)
                            S.tt(w4(BDa[ci]), chunk_src(AT, ci), bd4, ALU.mult)
                            yield
                            S.tt(w4(BDb[ci]), chunk_src(BT, ci), bd4, ALU.mult, eng="pool")
                            yield
                            S.tt(w4(BDk[ci]), chunk_src(KT_, ci), bd4, ALU.mult)
                            yield
                            bdv = BDx.next()
                            S.tt(w4(bdv), chunk_src(VT, ci), bd4, ALU.mult)
                            yield
                            bdbh = BDx.next()
                            S.tt(bdbh, BDb[ci], wcb, ALU.mult)
                            yield
                            bdkh = BDx.next()
                            S.tt(bdkh, BDk[ci], wcb, ALU.mult, eng="pool")
                            yield
                            hbr.newbank()
                            pN, pNT, pAK, pR, pV, pBh, pKh = [hbr.next() for _ in range(7)]
                            hbr.newbank()
                            for g in range(2):
                                S.mm(v3(pN)[:, g, :], BDb[ci][:, g, :], BDa[ci][:, g, :])
                                yield
                            for g in range(2):
                                S.mm(v3(pNT)[:, g, :], BDa[ci][:, g, :], BDb[ci][:, g, :])
                                yield
                            for g in range(2):
                                S.mm(v3(pAK)[:, g, :], BDk[ci][:, g, :], BDa[ci][:, g, :])
                                yield
                            pR4 = T(pR.ap.rearrange("p (g x t) -> p g x t", g=2, x=2), pR.bufs)
                            for g in range(2):
                                S.mm(pR4[:, g, 0, :], BDb[ci][:, g, :], RT[:, g, ci * 64:(ci + 1) * 64])
                                yield
                                S.mm(pR4[:, g, 1, :], BDk[ci][:, g, :], RT[:, g, ci * 64:(ci + 1) * 64])
                                yield
                            for g in range(2):
                                S.tr(v3b(pV)[:, g, :], bdv[:, g, :], ident_bf)
                                yield
                                S.tr(v3b(pBh)[:, g, :], bdbh[:, g, :], ident_bf)
                                yield
                                S.tr(v3b(pKh)[:, g, :], bdkh[:, g, :], ident_bf)
                                yield
                            S.tt(r32(Nm[0][ci]), v3(pN), mU_b, ALU.mult)
                            yield
                            S.tt(r32(NTm[0][ci]), v3(pNT), mL_b, ALU.mult)
                            yield
                            S.tt(AKm[ci], v3(pAK), mU_b, ALU.mult)
                            yield
                            S.tt(w4(Rm[ci]), pR4, mui4, ALU.mult)
                            yield
                            S.copy(Vm[ci], v3b(pV), eng="act")
                            yield
                            S.copy(Bhm[ci], v3b(pBh), eng="act")
                            yield
                            S.copy(Khm[ci], v3b(pKh), eng="act")
                            yield
                            S.tt(r32(Pm[ci]), Nm[0][ci], id_b, ALU.add)
                            yield
                        cur = 0
                        for rd in range(1, 6):
                            nxt = 1 - cur
                            pairs = []
                            for ci in range(nch):
                                hbr.newbank()
                                pN2 = hbr.next() if rd < 5 else None
                                pNT2 = hbr.next()
                                for g in range(2):
                                    if rd < 5:
                                        S.mm(v3(pN2)[:, g, :], r32(NTm[cur][ci][:, g, :]), r32(Nm[cur][ci][:, g, :]))
                                    S.mm(v3(pNT2)[:, g, :], r32(Nm[cur][ci][:, g, :]), r32(NTm[cur][ci][:, g, :]))
                                pairs.append((pN2, pNT2))
                            for ci in range(nch):
                                pN2, pNT2 = pairs[ci]
                                S.copy(r32(NTm[nxt][ci]), v3(pNT2), eng="act")
                                yield
                                if rd < 5:
                                    S.copy(r32(Nm[nxt][ci]), v3(pN2), eng="act")
                            pps = []
                            for ci in range(nch):
                                hbr.newbank()
                                pP = hbr.next()
                                for g in range(2):
                                    if ci % 2 == 1:
                                        S.mm(v3(pP)[:, g, :], r32(NTm[nxt][ci][:, g, :]), r32(Pm[ci][:, g, :]), start=True, stop=False)
                                        S.mm(v3(pP)[:, g, :], ident_bf, r32(Pm[ci][:, g, :]), start=False, stop=True)
                                    else:
                                        S.mm(v3(pP)[:, g, :], r32(NTm[nxt][ci][:, g, :]), r32(Pm[ci][:, g, :]))
                                pps.append(pP)
                            for ci in range(nch):
                                if ci % 2 == 1:
                                    S.copy(r32(Pm[ci]), v3(pps[ci]), eng="act")
                                else:
                                    S.tt(r32(Pm[ci]), v3(pps[ci]), Pm[ci], ALU.add)
                                yield
                            cur = nxt
                        for ci in range(nch):
                            hbr.newbank()
                            pH = hbr.next()
                            for g in range(2):
                                S.mm(v3(pH)[:, g, :], BDa[ci][:, g, :], Sb[:, g, :], start=True, stop=False)
                                yield
                                S.mm(v3(pH)[:, g, :], AKm[ci][:, g, :], Vm[ci][:, g, :], start=False, stop=True)
                                yield
                            H = Hm.next()
                            S.copy(r32(H), v3(pH), eng="act")
                            yield
                            hbr.newbank()
                            pU = hbr.next()
                            for g in range(2):
                                S.mm(v3(pU)[:, g, :], r32(Pm[ci][:, g, :]), r32(H[:, g, :]))
                                yield
                            U = Um.next()
                            S.copy(U, v3(pU), eng="act")
                            yield
                            hbr.newbank()
                            pY = hbr.next()
                            pY3 = T(pY.ap[:, 0:128].rearrange("p (g t) -> p g t", g=2), pY.bufs)
                            R4 = w4(Rm[ci])
                            for g in range(2):
                                S.mm(pY3[:, g, :], Sb[:, g, :], RT[:, g, ci * 64:(ci + 1) * 64], start=True, stop=False)
                                yield
                                S.mm(pY3[:, g, :], U[:, g, :], R4[:, g, 0, :], start=False, stop=False)
                                yield
                                S.mm(pY3[:, g, :], Vm[ci][:, g, :], R4[:, g, 1, :], start=False, stop=True)
                                yield
                            S.copy(YT[:, :, ci * 64:(ci + 1) * 64], pY3, eng="act")
                            yield
                            hbr.newbank()
                            pS = hbr.next()
                            for g in range(2):
                                S.mm(v3(pS)[:, g, :], Bhm[ci][:, g, :], U[:, g, :], start=True, stop=False)
                                yield
                                S.mm(v3(pS)[:, g, :], Khm[ci][:, g, :], Vm[ci][:, g, :], start=False, stop=True)
                                yield
                            for g in range(2):
                                S.stt(S32[:, g, :], S32[:, g, :], wcols[ci][:, g, :], v3(pS)[:, g, :], ALU.mult, ALU.add)
                                yield
                            S.copy(Sb, S32, eng="act")
                            yield
                    def stage_c(g, bi, c0, n):
                        t256, b256 = t256g[g], b256g[g]
                        AT, BT, KT_, RT, VT, WT, BON = ASET[bi % 2]
                        hp = 2 * grp + g
                        pcol = lambda o: pf[:, l, o + hp:o + hp + 1]
                        yb = b256.next()
                        S.copy(yb[:, 0:n], YT[:, g, 0:n], eng="act")
                        yield
                        pb = psrA.next()
                        S.mm(pb[:, 0:n], bd_bf, yb[:, 0:n])
                        yield
                        yc = t256.next()
                        S.stt(yc[:, 0:n], pb[:, 0:n], -1.0 / 64, YT[:, g, 0:n], ALU.mult, ALU.add)
                        yield
                        sq = b256.next()
                        S.actf(sq[:, 0:n], yc[:, 0:n], AF.Square)
                        yield
                        pb = psrA.next()
                        S.mm(pb[:, 0:n], bd_bf, sq[:, 0:n])
                        yield
                        sd = t256.next()
                        S.actf(sd[:, 0:n], pb[:, 0:n], AF.Sqrt, scale=1.0 / 64, bias=epsb[:, 1:2])
                        yield
                        S.recip(sd[:, 0:n], sd[:, 0:n])
                        yield
                        S.tt(yc[:, 0:n], yc[:, 0:n], sd[:, 0:n], ALU.mult)
                        yield
                        S.ts(yc[:, 0:n], yc[:, 0:n], pcol(PF_LNW), ALU.mult, pcol(PF_LNB), ALU.add)
                        yield
                        S.tt(yc[:, 0:n], yc[:, 0:n], BON[:, g, 0:n], ALU.add, eng="pool")
                        yield
                        pb = psrA.next()
                        S.mm(pb[:, 0:n], G2[:, hp * 128:(hp + 1) * 128], sig_g[:, c0:c0 + n])
                        yield
                        S.tt(OZ[:, hp, c0:c0 + n], yc[:, 0:n], pb[:, 0:n], ALU.mult)
                        yield

                    pass
                _run_interleaved([stage_a(0, 0, *blocks[0]), stage_a(1, 0, *blocks[0])])
                for bi, (c0, n) in enumerate(blocks):
                    gens = [stage_b(bi, c0, n)]
                    if bi + 1 < len(blocks):
                        gens += [stage_a(0, bi + 1, *blocks[bi + 1]), stage_a(1, bi + 1, *blocks[bi + 1])]
                    _run_weighted(gens, [B_WEIGHT, 1, 1])
                    _run_interleaved([stage_c(0, bi, c0, n), stage_c(1, bi, c0, n)])
                    chk("rwkv_C", l)
                chk("rwkv_seq", l)
                dst = wkvp if seq == 0 else wkvs
                for g in range(2):
                    for h in range(2):
                        S.dma(dst[l, 2 * grp + g, h * 64:(h + 1) * 64, :], S32[h * 64:(h + 1) * 64, g, h * 64:(h + 1) * 64])
        S.dma(shp[l], shp_t)
        S.dma(shs[l], shs_t)

    def phase_gproj(l, w_o, gate_col0, accumulate):
        A.seek(LOC)
        slabs = Ring([A.alloc([8, 256], BF16) for _ in range(2)])
        sgr = Ring([A.alloc([512], F32) for _ in range(3)])
        tr_ = Ring([A.alloc([512], F32) for _ in range(2)])
        gsl = {}

        def issue_g(nt_):
            if nt_ >= 8:
                return
            sl_ = slabs.next()
            wslab(sl_[:, :, 0:128], w_o[l].ap[:, nt_ * 128:(nt_ + 1) * 128])
            wslab(sl_[:, :, 128:256], w_in[l].ap[:, gate_col0 + nt_ * 128:gate_col0 + (nt_ + 1) * 128])
            gsl[nt_] = sl_

        issue_g(0)
        for nt in range(8):
            issue_g(nt + 1)
            sl = gsl.pop(nt)
            for (c0, n) in TQ:
                pa = psr.next()
                for kt in range(8):
                    S.mm(pa[:, 0:n], sl[:, kt, 0:128], OZ[:, kt, c0:c0 + n], start=(kt == 0), stop=(kt == 7))
                pg = psr.next()
                for kt in range(8):
                    S.mm(pg[:, 0:n], sl[:, kt, 128:256], hT[:, kt, c0:c0 + n], start=(kt == 0), stop=(kt == 7))
                sg = sgr.next()
                S.actf(sg[:, 0:n], pg[:, 0:n], AF.Sigmoid)
                if not accumulate:
                    S.tt(MT[:, nt, c0:c0 + n], pa[:, 0:n], sg[:, 0:n], ALU.mult)
                else:
                    t = tr_.next()
                    S.tt(t[:, 0:n], pa[:, 0:n], sg[:, 0:n], ALU.mult)
                    S.tt(MT[:, nt, c0:c0 + n], t[:, 0:n], MT[:, nt, c0:c0 + n], ALU.add)

    def phase_da(l):
        A.seek(LOC)
        Wd_r = Ring([A.alloc([8, 384], BF16) for _ in range(2)])
        qkv_tok = A.alloc([NTB, 384], BF16)
        q_tok = qkv_tok[:, :, 0:128]
        k_tok = qkv_tok[:, :, 128:256]
        v_tok = qkv_tok[:, :, 256:384]
        kc_r = Ring([A.alloc([16, 128], BF16) for _ in range(2)])
        vc_r = Ring([A.alloc([16, 128], BF16) for _ in range(2)])
        KTh = A.alloc([NTOK + TP], BF16)
        Q1p = A.alloc([512], BF16)
        Q2p = A.alloc([512], BF16)
        Pr = Ring([A.alloc([512], BF16) for _ in range(4)])
        f512 = Ring([A.alloc([512], F32) for _ in range(4)])
        qkvr = Ring([A.alloc([384], F32) for _ in range(2)])
        rtmp = Ring([A.alloc([4, 16], F32) for _ in range(4)])
        Lacc = [A.alloc([512], F32) for _ in range(2)]
        Pd = [A.alloc([512], BF16) for _ in range(4)]
        Pz = A.alloc([512], BF16)
        for m_ in range(4):
            S.memset(Pd[m_], 0.0)
        S.memset(Pz, 0.0)
        S.memset(Q1p, 0.0)
        S.memset(Q2p, 0.0)
        nlam = lamv[:, l * 8:l * 8 + 1]
        subs = lamv[:, l * 8 + 1:l * 8 + 2]
        psOL = [(ps[0], ps[1]), (ps[2], ps[3])]
        pr6 = Ring(ps[4:8])

        def rope_inplace(x, tb):
            x4 = T(x.ap.rearrange("p (m d) -> p m d", m=4), x.bufs)
            cc = cst[:, C_COS + tb * 16:C_COS + tb * 16 + 16]
            ss = cst[:, C_SIN + tb * 16:C_SIN + tb * 16 + 16]
            ccb = T(cc.ap.unsqueeze(1).broadcast_to([128, 4, 16]), cc.bufs)
            ssb = T(ss.ap.unsqueeze(1).broadcast_to([128, 4, 16]), ss.bufs)
            tc_ = rtmp.next()
            ts_ = rtmp.next()
            S.tt(tc_, x4[:, :, 0:16], ccb, ALU.mult)
            S.tt(ts_, x4[:, :, 0:16], ssb, ALU.mult)
            S.tt(x4[:, :, 0:8], tc_[:, :, 0:8], ts_[:, :, 8:16], ALU.subtract)
            S.tt(x4[:, :, 8:16], tc_[:, :, 8:16], ts_[:, :, 0:8], ALU.add)

        def attend(c0, nq, tbs, keyblocks):
            pb = pr6.next()
            pst = T(pb.ap.bitcast(BF16), pb.bufs)
            for i, tb in enumerate(tbs):
                S.tr(pst[:, i * 128:(i + 1) * 128], q_tok[:, tb, :], ident_bf)
            S.copy(Q1p[0:64, 0:nq], pst[0:64, 0:nq], eng="act")
            S.copy(Q2p[64:128, 0:nq], pst[64:128, 0:nq], eng="act")
            o1 = f512.next()
            nk = len(keyblocks)
            steps = [(mp, j) for mp in range(2) for j in range(nk)]
            Ps = {}
            LOOK = 2
            tlast = [None]

            def issue_s(i):
                mp, j = steps[i]
                kap, vap, c_lo, zspec = keyblocks[j]
                Qp = Q1p if mp == 0 else Q2p
                pS = pr6.next()
                S.mm(pS[:, c_lo:nq], kap, Qp[:, c_lo:nq])
                if zspec == "diag":
                    P = Pd[c_lo // 128]
                    S.actf(P[0:64, c_lo:nq], pS[0:64, c_lo:nq], AF.Exp, scale=0.125)
                    S.actf(P[64:128, c_lo + 64:nq], pS[64:128, c_lo + 64:nq], AF.Exp, scale=0.125)
                elif zspec == "rows":
                    P = Pz
                    S.actf(P[0:32, 0:nq], pS[0:32, 0:nq], AF.Exp, scale=0.125)
                else:
                    P = Pr.next()
                    S.actf(P[:, c_lo:nq], pS[:, c_lo:nq], AF.Exp, scale=0.125)
                Ps[i] = P

            def issue_ol(i):
                mp, j = steps[i]
                kap, vap, c_lo, zspec = keyblocks[j]
                P = Ps.pop(i)
                pO, pL = psOL[mp]
                S.mm(pO[:, c_lo:nq], vap, P[:, c_lo:nq], start=(j == 0), stop=(j == nk - 1))
                eng = LACC_ENG[j % 2]
                La = Lacc[j % 2]
                if j < 2:
                    if c_lo > 0:
                        S.memset(La[:, 0:c_lo], 0.0, eng=eng)
                    S.copy(La[:, c_lo:nq], P[:, c_lo:nq], eng=eng)
                else:
                    S.tt(La[:, c_lo:nq], La[:, c_lo:nq], P[:, c_lo:nq], ALU.add, eng=eng)
                if j == nk - 1:
                    Lb = Pr.next()
                    S.tt(Lb[:, 0:nq], Lacc[0][:, 0:nq], Lacc[1][:, 0:nq], ALU.add)
                    S.mm(pL[:, 0:nq], ones_bf, Lb[:, 0:nq])
                    rd = f512.next()
                    S.recip(rd[:, 0:nq], pL[:, 0:nq])
                    if mp == 0:
                        S.tt(o1[:, 0:nq], pO[:, 0:nq], rd[:, 0:nq], ALU.mult)
                    else:
                        t = f512.next()
                        S.tt(t[:, 0:nq], pO[:, 0:nq], rd[:, 0:nq], ALU.mult)
                        S.stt(o1[:, 0:nq], t[:, 0:nq], nlam, o1[:, 0:nq], ALU.mult, ALU.add)
                        tlast[0] = t

            for i in range(len(steps) + LOOK):
                if i < len(steps):
                    issue_s(i)
                if i - LOOK >= 0:
                    issue_ol(i - LOOK)
            t = tlast[0]
            sq = Pr.next()
            S.actf(sq[:, 0:nq], o1[:, 0:nq], AF.Square)
            pb = pr6.next()
            S.mm(pb[:, 0:nq], ones_bf, sq[:, 0:nq])
            sd = t
            S.actf(sd[:, 0:nq], pb[:, 0:nq], AF.Sqrt, scale=1.0 / 128, bias=epsb[:, 0:1])
            S.recip(sd[:, 0:nq], sd[:, 0:nq])
            S.tt(o1[:, 0:nq], o1[:, 0:nq], sd[:, 0:nq], ALU.mult)
            return o1

        hl = {}

        def issue_head(hd_):
            if hd_ >= 8:
                return
            Wd_, kc_, vc_ = Wd_r.next(), kc_r.next(), vc_r.next()
            for x in range(3):
                wslab(Wd_[:, :, x * 128:(x + 1) * 128], w_in[l].ap[:, x * 1024 + hd_ * 128:x * 1024 + (hd_ + 1) * 128])
            S.dma(kc_, T(ck[l].ap[:, hd_ * 128:(hd_ + 1) * 128].rearrange("(j p) c -> p j c", p=128), ()), eng="pool")
            S.dma(vc_, T(cv[l].ap[:, hd_ * 128:(hd_ + 1) * 128].rearrange("(j p) c -> p j c", p=128), ()), eng="pool")
            hl[hd_] = (Wd_, kc_, vc_)

        issue_head(0)
        for hd in range(8):
            issue_head(hd + 1)
            Wd, kc_tok, vc_tok = hl.pop(hd)
            chk("da_load", l)
            for tb in range(NTB):
                if tb == 1:
                    chk("da_tb0", l)
                if tb == 16:
                    chk("da_tb15", l)
                pb = pr6.next()
                for kt in range(8):
                    S.mm(pb[:, 0:384], hT[:, kt, tb * 128:(tb + 1) * 128], Wd[:, kt, :],
                         start=(kt == 0), stop=(kt == 7))
                qkv = qkvr.next()
                S.copy(qkv, pb[:, 0:384], eng="act")
                rope_inplace(qkv[:, 0:256], tb)
                S.copy(qkv_tok[:, tb, :], qkv, eng="act")
                if tb < 16:
                    S.dma(kp[l, tb * 128:(tb + 1) * 128, hd * 128:(hd + 1) * 128], qkv[:, 128:256])
                    S.dma(vp[l, tb * 128:(tb + 1) * 128, hd * 128:(hd + 1) * 128], qkv[:, 256:384])
                else:
                    S.dma(ks[l, :, hd * 128:(hd + 1) * 128], qkv[0:TSV, 128:256])
                    S.dma(vs[l, :, hd * 128:(hd + 1) * 128], qkv[0:TSV, 256:384])
            chk("da_proj", l)
            srcs = [(k_tok, tb) for tb in range(NTB)] + [(kc_tok, j) for j in range(16)]
            base = 0
            ei = 0
            while base < len(srcs):
                grp_ = srcs[base:base + 8]
                pb = pr6.next()
                pst = T(pb.ap.bitcast(BF16), pb.bufs)
                for i, (src, idx) in enumerate(grp_):
                    S.tr(pst[:, i * 128:(i + 1) * 128], src[:, idx, :], ident_bf)
                S.copy(KTh[:, base * 128:(base + len(grp_)) * 128], pst[:, 0:len(grp_) * 128],
                       eng=("act" if ei % 2 == 0 else "dve"))
                base += len(grp_)
                ei += 1
            chk("da_kt", l)
            for i in range(4):
                if i == 1:
                    chk("da_att0", l)
                kb = []
                for j in range(4 * i + 4):
                    m = j - 4 * i
                    c_lo = 128 * m if m > 0 else 0
                    kb.append((KTh[:, j * 128:(j + 1) * 128], v_tok[:, j, :], c_lo, "diag" if m >= 0 else None))
                o = attend(i * 512, 512, [4 * i + m for m in range(4)], kb)
                S.ts(OZ[:, hd, i * 512:(i + 1) * 512], o[:, 0:512], subs, ALU.mult)
            chk("da_attp", l)
            kb = [(KTh[:, NTOK + j * 128:NTOK + (j + 1) * 128], vc_tok[:, j, :], 0, None) for j in range(16)]
            kb.append((KTh[:, TP:TP + 128], v_tok[:, 16, :], 0, "rows"))
            o = attend(TP, TSV, [16], kb)
            S.ts(OZ[:, hd, TP:TP + TSV], o[:, 0:TSV], subs, ALU.mult)
            chk("da_head0", l)

    def phase_out(l):
        A.seek(LOC)
        nsc = make_norm_scratch()
        wo = A.alloc([8, D], BF16)
        wslab(wo[:, :, 0:512], w_out[l].ap[:, 0:512])
        wslab(wo[:, :, 512:1024], w_out[l].ap[:, 512:1024])
        xr = Ring([A.alloc([D], F32) for _ in range(2)])
        x1r = Ring([A.alloc([D], F32) for _ in range(2)])
        for tb in range(NTB):
            xt = xr.next()
            if l == 0:
                src, nr = x_rows(tb)
                if nr < 128:
                    S.memset(xt, 0.0)
                S.dma(xt[0:nr, :], src)
            else:
                S.dma(xt, X2[tb * 128:(tb + 1) * 128, :])
            x1 = x1r.next()
            for half in range(2):
                pb = psr.next()
                for kt in range(8):
                    S.mm(pb, MT[:, kt, tb * 128:(tb + 1) * 128], wo[:, kt, half * 512:(half + 1) * 512],
                         start=(kt == 0), stop=(kt == 7))
                S.tt(x1[:, half * 512:(half + 1) * 512], pb, xt[:, half * 512:(half + 1) * 512], ALU.add)
            S.dma(X1[tb * 128:(tb + 1) * 128, :], x1)
            norm_to_hT(x1, l, PF_NFFN, tb, nsc)

    def phase_ffn(l):
        A.seek(0)
        A.alloc([8, NTOK], BF16)
        wdn = A.alloc([22, D], BF16)
        GT = A.alloc([22, 512], BF16)
        assert A.off <= LOC
        A.seek(LOC)
        nsc = make_norm_scratch()
        for c in range(2):
            for hh in range(2):
                S.dma(wdn[:, 11 * hh:11 * (hh + 1), c * 512:(c + 1) * 512],
                      T(w_down[l].ap[11 * hh * 128:11 * (hh + 1) * 128, c * 512:(c + 1) * 512].rearrange("(kt p) c -> p kt c", p=128), ()), eng="pool")
        slabs = Ring([A.alloc([8, 512], BF16) for _ in range(3)])
        hpr = Ring([A.alloc([514], F32) for _ in range(4)])
        cr = Ring([A.alloc([512], F32) for _ in range(4)])
        xr = Ring([A.alloc([D], F32) for _ in range(2)])
        x2r = Ring([A.alloc([D], F32) for _ in range(2)])
        yr = Ring([A.alloc([D], F32) for _ in range(2)])
        S.memset(carry_p, 0.0)
        S.dma(T(carry_s.ap.rearrange("p a b -> p (a b)"), carry_s.bufs), cfm[l])
        jobs = [(qi, f2) for qi in range(len(TQ)) for f2 in range(11)]
        slab_of = {}

        def issue_slab(k):
            if k >= len(jobs):
                return
            _, f2_ = jobs[k]
            sl_ = slabs.next()
            wslab(sl_[:, :, 0:256], w_up[l].ap[:, f2_ * 256:(f2_ + 1) * 256])
            wslab(sl_[:, :, 256:512], w_up[l].ap[:, DFF + f2_ * 256:DFF + (f2_ + 1) * 256])
            slab_of[k] = sl_

        issue_slab(0)
        issue_slab(1)
        for qi, (c0, n) in enumerate(TQ):
            carry = carry_p if qi < 4 else carry_s
            nval = n if qi < 4 else TSV
            for f2 in range(11):
                k_ = qi * 11 + f2
                issue_slab(k_ + 2)
                sl = slab_of.pop(k_)
                for fi in range(2):
                    ft = 2 * f2 + fi
                    cs = []
                    for which in range(2):
                        fidx = which * 22 + ft
                        pb = psr.next()
                        off = which * 256 + fi * 128
                        for kt in range(8):
                            S.mm(pb[:, 0:n], sl[:, kt, off:off + 128], hT[:, kt, c0:c0 + n], start=(kt == 0), stop=(kt == 7))
                        hp_ = hpr.next()
                        S.copy(hp_[:, 0:2], carry[:, fidx, :], eng="act")
                        S.copy(hp_[:, 2:n + 2], pb[:, 0:n], eng="act")
                        S.copy(carry[:, fidx, :], hp_[:, nval:nval + 2], eng="act")
                        cw = lambda j: pf[:, l, PF_CONV + j * 44 + fidx:PF_CONV + j * 44 + fidx + 1]
                        cb = pf[:, l, PF_CONVB + fidx:PF_CONVB + fidx + 1]
                        c_ = cr.next()
                        eng = "dve"
                        S.ts(c_[:, 0:n], hp_[:, 0:n], cw(0), ALU.mult, cb, ALU.add, eng=eng)
                        S.stt(c_[:, 0:n], hp_[:, 1:n + 1], cw(1), c_[:, 0:n], ALU.mult, ALU.add)
                        S.stt(c_[:, 0:n], hp_[:, 2:n + 2], cw(2), c_[:, 0:n], ALU.mult, ALU.add)
                        cs.append(c_)
                    S.actf(cs[0][:, 0:n], cs[0][:, 0:n], AF.Silu)
                    S.tt(GT[:, ft, 0:n], cs[0][:, 0:n], cs[1][:, 0:n], ALU.mult)
            for tbl in range(n // 128):
                tb = c0 // 128 + tbl
                xt = xr.next()
                S.dma(xt, X1[tb * 128:(tb + 1) * 128, :])
                x2 = x2r.next()
                for half in range(2):
                    pb = psr.next()
                    for ft in range(22):
                        S.mm(pb, GT[:, ft, tbl * 128:(tbl + 1) * 128], wdn[:, ft, half * 512:(half + 1) * 512],
                             start=(ft == 0), stop=(ft == 21))
                    S.tt(x2[:, half * 512:(half + 1) * 512], pb, xt[:, half * 512:(half + 1) * 512], ALU.add)
                if l == 1:
                    rstd = norm_stats(x2, nsc)
                    y = yr.next()
                    S.stt(y, x2, rstd, gfin, ALU.mult, ALU.mult)
                    if tb < 16:
                        S.dma(yp[tb * 128:(tb + 1) * 128, :], y)
                    else:
                        S.dma(ys, y[0:TSV, :])
                else:
                    S.dma(X2[tb * 128:(tb + 1) * 128, :], x2)
                    norm_to_hT(x2, 1, PF_NMIX, tb, nsc)
        S.dma(cvp[l], T(carry_p.ap.rearrange("p a b -> p (a b)"), carry_p.bufs))
        S.dma(cvs[l], T(carry_s.ap.rearrange("p a b -> p (a b)"), carry_s.bufs))

    def dump(name, src):
        if name in dbg_out:
            S.barrier()
            S.dma(dbg_out[name], src)
            S.barrier()

    try:
        _program(S, locals())
    except StopBuild:
        pass
    S.emit()
    S.close()
    return nc


def _program(S, L):
    phase_norm0, phase_rwkv, phase_gproj, phase_da, phase_out, phase_ffn = (
        L["phase_norm0"], L["phase_rwkv"], L["phase_gproj"], L["phase_da"], L["phase_out"], L["phase_ffn"])
    dump, stop_after, hT, OZ, MT = L["dump"], L["stop_after"], L["hT"], L["OZ"], L["MT"]
    w_o_rw, w_o_da = L["w_o_rw"], L["w_o_da"]
    S.barrier()
    phase_norm0()
    S.barrier()
    dump("hT0", hT)
    done = False
    for l in range(2):
        if stop_after == ("norm", l):
            break
        phase_rwkv(l)
        S.barrier()
        dump(f"Z{l}", OZ)
        if stop_after == ("rwkv", l):
            break
        phase_gproj(l, w_o_rw, P_DA + P_RW + D, False)
        S.barrier()
        if stop_after == ("gproj1", l):
            break
        phase_da(l)
        S.barrier()
        dump(f"O{l}", OZ)
        if stop_after == ("da", l):
            break
        phase_gproj(l, w_o_da, P_DA + P_RW, True)
        S.barrier()
        dump(f"M{l}", MT)
        phase_out(l)
        S.barrier()
        dump(f"h2T{l}", hT)
        if stop_after == ("out", l):
            break
        phase_ffn(l)
        S.barrier()
        if stop_after == ("ffn", l):
            break


_NC_CACHE = {}


def _prep_inputs(inp):
    f = lambda a: np.ascontiguousarray(np.asarray(a, dtype=np.float32))
    g = {k: f(v) for k, v in inp.items()}
    consts = make_consts()

    def fm(v, nt):
        return v.reshape(nt, 128).T

    pfm = np.zeros((2, 128, NPF), np.float32)
    for l in range(2):
        pfm[l, :, PF_NMIX:PF_NMIX + 8] = fm(g["norm_mix"][l], 8)
        pfm[l, :, PF_NFFN:PF_NFFN + 8] = fm(g["norm_ffn"][l], 8)
        pfm[l, :, PF_MU:PF_MU + 26] = fm(g["rw_mu"][l], 26)
        pfm[l, :, PF_W0:PF_W0 + 8] = fm(g["rw_w0"][l], 8)
        pfm[l, :, PF_A0:PF_A0 + 8] = fm(g["rw_a0"][l], 8)
        pfm[l, :, PF_KK:PF_KK + 8] = fm(g["rw_k_k"][l], 8)
        pfm[l, :, PF_KA:PF_KA + 8] = fm(g["rw_k_a"][l], 8)
        pfm[l, :, PF_RK:PF_RK + 8] = fm(g["rw_r_k"][l].reshape(-1), 8)
        pfm[l, :, PF_LNW:PF_LNW + 8] = fm(g["rw_ln_w"][l], 8)
        pfm[l, :, PF_LNB:PF_LNB + 8] = fm(g["rw_ln_b"][l], 8)
        for j in range(3):
            pfm[l, :, PF_CONV + j * 44:PF_CONV + (j + 1) * 44] = fm(g["ffn_conv"][l, j], 44)
        pfm[l, :, PF_CONVB:PF_CONVB + 44] = fm(g["ffn_conv_b"][l], 44)
        pfm[l, :, PF_SUBLN] = g["da_subln"][l]
    shared = {
        "pfm": pfm, "consts": consts, "w_in": g["w_in"], "da_lambda": g["da_lambda"].reshape(512),
        "w_o_da": g["w_o_da"], "rw_w2": g["rw_w2"], "rw_a2": g["rw_a2"], "rw_g2": g["rw_g2"],
        "w_o_rw": g["w_o_rw"], "w_out": g["w_out"], "w_up": g["w_up"], "w_down": g["w_down"],
        "nfin": g["norm_final"],
    }
    maps = []
    for b in range(8):
        m = dict(shared)
        m["xp"] = g["x_prompt"][b]
        m["xs"] = g["x_sample"][b]
        m["ck"] = np.ascontiguousarray(g["cache_k"][:, b].reshape(2, TP, D))
        m["cv"] = np.ascontiguousarray(g["cache_v"][:, b].reshape(2, TP, D))
        sw = g["state_wkv"][:, b].reshape(2, 8, 2, 64, 64).transpose(0, 1, 2, 4, 3).reshape(2, 8, 128, 64)
        m["swkv"] = np.ascontiguousarray(sw)
        m["sfm"] = np.ascontiguousarray(g["state_shift"][:, b, 0].reshape(2, 26, 128).transpose(0, 2, 1))
        cf = g["state_ffn_conv"][:, b].reshape(2, 2, 44, 128).transpose(0, 3, 2, 1).reshape(2, 128, 88)
        m["cfm"] = np.ascontiguousarray(cf)
        maps.append(m)
    return maps


def _assemble(results):
    def st(name):
        return np.stack([np.asarray(r[name]) for r in results], axis=0)

    y_prompt = st("yp")
    y_sample = st("ys")
    k_prompt = st("kp").transpose(1, 0, 2, 3).reshape(2, 8, TP, 8, 128)
    v_prompt = st("vp").transpose(1, 0, 2, 3).reshape(2, 8, TP, 8, 128)
    k_sample = st("ks").transpose(1, 0, 2, 3).reshape(2, 8, TSV, 8, 128)
    v_sample = st("vs").transpose(1, 0, 2, 3).reshape(2, 8, TSV, 8, 128)

    def wkv(name):
        a = st(name).reshape(8, 2, 8, 2, 64, 64).transpose(1, 0, 2, 3, 5, 4).reshape(2, 8, 16, 64, 64)
        return np.ascontiguousarray(a)

    def shift(name):
        a = st(name).transpose(1, 0, 3, 2).reshape(2, 8, 1, P_RW)
        return np.ascontiguousarray(a)

    def conv(name):
        a = st(name).reshape(8, 2, 128, 44, 2).transpose(1, 0, 4, 3, 2).reshape(2, 8, 2, 2 * DFF)
        return np.ascontiguousarray(a)

    return (np.ascontiguousarray(y_prompt), np.ascontiguousarray(y_sample),
            np.ascontiguousarray(k_prompt), np.ascontiguousarray(v_prompt),
            wkv("wkvp"), shift("shp"), conv("cvp"),
            np.ascontiguousarray(k_sample), np.ascontiguousarray(v_sample),
            wkv("wkvs"), shift("shs"), conv("cvs"))


def kernel(**inputs):
    maps = _prep_inputs(inputs)
    nc = build()
    res = run_bass_kernel_spmd(nc, maps, core_ids=list(range(8)))
    return _assemble(res.results)
```

```python
import numpy as np
from contextlib import ExitStack

import concourse.bass as bass
import concourse.mybir as mybir

F32 = mybir.dt.float32
BF16 = mybir.dt.bfloat16
F32R = mybir.dt.float32r
AF = mybir.ActivationFunctionType
ALU = mybir.AluOpType
AX = mybir.AxisListType

ENGS = ("pe", "act", "dve", "pool", "sp")
SEM_EPOCH = 20000
DMA_ROT = 8


class Buf:
    __slots__ = ("w", "r", "name")

    def __init__(self, name=""):
        self.w = None
        self.r = []
        self.name = name


class T:
    __slots__ = ("ap", "bufs")

    def __init__(self, ap, bufs):
        self.ap = ap
        self.bufs = tuple(bufs)

    def __getitem__(self, key):
        return T(self.ap[key], self.bufs)

    def v(self, ap):
        return T(ap, self.bufs)

    def bitcast(self, dt):
        return T(self.ap.bitcast(dt), self.bufs)


class Rec:
    __slots__ = ("eng", "idx", "fn", "deps", "dma", "signaled", "ev", "selfwait")

    def __init__(self, eng, idx, fn, dma):
        self.eng = eng
        self.idx = idx
        self.fn = fn
        self.deps = []
        self.dma = dma
        self.signaled = False
        self.ev = None
        self.selfwait = None


class Sched:
    def __init__(self, nc):
        self.nc = nc
        self.q = {e: [] for e in ENGS}
        self.stack = ExitStack()
        self.nbuf = 0
        self.same_engine_sync = True
        self._bar_pos = {}
        self._pending = {e: [] for e in ENGS}

    def sbuf(self, name, shape, dtype, nbufs=1):
        h = self.stack.enter_context(self.nc.sbuf_tensor(name, list(shape), dtype))
        return T(h[:], [Buf(name)])

    def psum(self, name, shape, dtype=F32):
        h = self.stack.enter_context(self.nc.psum_tensor(name, list(shape), dtype))
        return T(h[:], [Buf(name)])

    def dram(self, name, shape, dtype, kind):
        h = self.nc.dram_tensor(name, list(shape), dtype, kind=kind)
        return T(h.ap(), [Buf(name)])

    def newbuf(self, name=""):
        return Buf(name)

    def add(self, eng, fn, reads=(), writes=(), dma=False):
        q = self.q[eng]
        rec = Rec(eng, len(q), fn, dma)
        deps = {}
        for t in reads:
            for b in t.bufs:
                if b.w is not None:
                    deps[id(b.w)] = b.w
        for t in writes:
            for b in t.bufs:
                if b.w is not None:
                    deps[id(b.w)] = b.w
                for r in b.r:
                    deps[id(r)] = r
        if self._pending[eng]:
            for d in self._pending[eng]:
                deps[id(d)] = d
            self._pending[eng] = []
            barrier_deps = True
        else:
            barrier_deps = False
        for d in deps.values():
            if d is rec:
                continue
            if d.eng == eng and not d.dma and not dma:
                if eng == "pe" or eng == "sp" or not self.same_engine_sync:
                    continue
            rec.deps.append(d)
        for t in reads:
            for b in t.bufs:
                b.r.append(rec)
        for t in writes:
            for b in t.bufs:
                b.w = rec
                b.r = []
        q.append(rec)
        return rec

    def emit(self):
        nc = self.nc
        for e in ENGS:
            for rec in self.q[e]:
                for d in rec.deps:
                    d.signaled = True
        sems = {}
        for e in ENGS:
            cnt = 0
            for rec in self.q[e]:
                if rec.dma:
                    continue
                if rec.signaled:
                    ep = cnt // SEM_EPOCH
                    key = (e, ep)
                    if key not in sems:
                        sems[key] = self.stack.enter_context(nc.semaphore(f"s_{e}_{ep}"))
                    rec.ev = (sems[key], cnt % SEM_EPOCH + 1, key)
                    cnt += 1
        self.final_waits = []
        for e in ENGS:
            j = 0
            last = {}
            for rec in self.q[e]:
                if not rec.dma:
                    continue
                slot = j % DMA_ROT
                key = ("dma", e, slot)
                if key not in sems:
                    sems[key] = self.stack.enter_context(nc.semaphore(f"d_{e}_{slot}"))
                val = 16 * (j // DMA_ROT + 1)
                rec.ev = (sems[key], val, key)
                if j >= DMA_ROT:
                    rec.selfwait = (sems[key], val - 16, key)
                last[key] = (sems[key], val, key)
                j += 1
            self.final_waits.extend(last.values())

        block = self.stack.enter_context(nc.Block())
        sched = self

        def run(engname, eobj):
            waited = {}
            for rec in sched.q[engname]:
                waits = {}
                if rec.selfwait is not None:
                    s, v, key = rec.selfwait
                    waits[key] = (s, v)
                for d in rec.deps:
                    s, v, key = d.ev
                    if key in waits:
                        if waits[key][1] < v:
                            waits[key] = (s, v)
                    else:
                        waits[key] = (s, v)
                for key, (s, v) in waits.items():
                    if waited.get(key, 0) >= v:
                        continue
                    eobj.wait_ge(s, v)
                    waited[key] = v
                ins = rec.fn(eobj)
                if rec.dma:
                    ins.then_inc(rec.ev[0], 16)
                elif rec.signaled:
                    ins.then_inc(rec.ev[0], 1)
            if engname == "sp":
                for s, v, key in sched.final_waits:
                    eobj.wait_ge(s, v)

        @block.tensor
        def _(e):
            run("pe", e)

        @block.scalar
        def _(e):
            run("act", e)

        @block.vector
        def _(e):
            run("dve", e)

        @block.gpsimd
        def _(e):
            run("pool", e)

        @block.sync
        def _(e):
            run("sp", e)

    def close(self):
        self.stack.close()

    def dma(self, out, in_, eng="sp", **kw):
        return self.add(eng, lambda e: e.dma_start(out=out.ap, in_=in_.ap, **kw),
                        reads=[in_], writes=[out], dma=True)

    def mm(self, out, lhsT, rhs, start=True, stop=True, extra_reads=(), **kw):
        return self.add("pe", lambda e: e.matmul(out.ap, lhsT.ap, rhs.ap, start=start, stop=stop, **kw),
                        reads=[lhsT, rhs, *extra_reads], writes=[out])

    def tr(self, out, in_, ident, **kw):
        return self.add("pe", lambda e: e.transpose(out.ap, in_.ap, ident.ap, **kw),
                        reads=[in_, ident], writes=[out])

    def actf(self, out, in_, func, bias=None, scale=None, accum=None, eng="act"):
        kw = {}
        reads = [in_]
        writes = [out]
        if bias is not None:
            if isinstance(bias, T):
                kw["bias"] = bias.ap
                reads.append(bias)
            else:
                kw["bias"] = bias
        if scale is not None:
            if isinstance(scale, T):
                kw["scale"] = scale.ap
                reads.append(scale)
            else:
                kw["scale"] = scale
        if accum is not None:
            kw["accum_out"] = accum.ap
            writes.append(accum)
        return self.add("act", lambda e: e.activation(out.ap, in_.ap, func, **kw), reads=reads, writes=writes)

    def tt(self, out, a, b, op, eng="dve"):
        return self.add(eng, lambda e: e.tensor_tensor(out.ap, a.ap, b.ap, op), reads=[a, b], writes=[out])

    def ts(self, out, a, s1, op0, s2=None, op1=None, accum=None, eng="dve"):
        reads = [a]
        writes = [out]
        s1v = s1.ap if isinstance(s1, T) else s1
        s2v = s2.ap if isinstance(s2, T) else s2
        if isinstance(s1, T):
            reads.append(s1)
        if isinstance(s2, T):
            reads.append(s2)
        kw = {}
        if op1 is not None:
            kw["op1"] = op1
        if accum is not None:
            kw["accum_out"] = accum.ap
            writes.append(accum)
        return self.add(eng, lambda e: e.tensor_scalar(out.ap, a.ap, s1v, s2v, op0, **kw), reads=reads, writes=writes)

    def stt(self, out, a, s, b, op0, op1, eng="dve"):
        reads = [a, b]
        sv = s.ap if isinstance(s, T) else s
        if isinstance(s, T):
            reads.append(s)
        return self.add(eng, lambda e: e.scalar_tensor_tensor(out.ap, a.ap, sv, b.ap, op0, op1), reads=reads, writes=[out])

    def copy(self, out, in_, eng="dve"):
        if eng == "act":
            return self.add("act", lambda e: e.copy(out.ap, in_.ap), reads=[in_], writes=[out])
        return self.add(eng, lambda e: e.tensor_copy(out.ap, in_.ap), reads=[in_], writes=[out])

    def memset(self, out, val, eng="pool"):
        return self.add(eng, lambda e: e.memset(out.ap, val), reads=[], writes=[out])

    def recip(self, out, in_):
        return self.add("dve", lambda e: e.reciprocal(out.ap, in_.ap), reads=[in_], writes=[out])

    def scan(self, out, d0, d1, init, op0, op1):
        reads = [d0, d1]
        iv = init.ap if isinstance(init, T) else init
        if isinstance(init, T):
            reads.append(init)
        return self.add("dve", lambda e: e.tensor_tensor_scan(out.ap, d0.ap, d1.ap, iv, op0, op1), reads=reads, writes=[out])


def _barrier(self):
    deps = []
    for e in ENGS:
        q = self.q[e]
        last_c = None
        for rec in reversed(q):
            if not rec.dma:
                last_c = rec
                break
        if last_c is not None:
            deps.append(last_c)
        for rec in q[self._bar_pos.get(e, 0):]:
            if rec.dma:
                deps.append(rec)
        self._bar_pos[e] = len(q)
    for e in ENGS:
        self._pending[e] = list(deps)


Sched.barrier = _barrier


from concourse.bass_utils import run_bass_kernel_spmd

D = 1024
TP = 2048
TSV = 32
NTOK = 2176
NTB = 17
P_DA = 3072
P_RW = 3328
PTOT = 8448
DFF = 2816
EPS = 1e-6
GN_EPS = 64e-5
ROPE_THETA = 500000.0
TQ = [(0, 512), (512, 512), (1024, 512), (1536, 512), (2048, 128)]
PF_NMIX, PF_NFFN, PF_MU, PF_W0, PF_A0, PF_KK, PF_KA, PF_RK, PF_LNW, PF_LNB = 0, 8, 16, 42, 50, 58, 66, 74, 82, 90
PF_CONV, PF_CONVB, PF_SUBLN, NPF = 98, 230, 274, 275
C_ID, C_BD, C_MU, C_ML, C_MUI, C_RST, C_COS, C_SIN, NCONST = 0, 128, 256, 384, 512, 576, 832, 1104, 1376
DEC_C = -0.6065306597126334


def make_consts():
    c = np.zeros((128, NCONST), np.float32)
    c[:, C_ID:C_ID + 128] = np.eye(128, dtype=np.float32)
    p = np.arange(128)
    h = p // 64
    s = p % 64
    bd = (h[:, None] == h[None, :]).astype(np.float32)
    c[:, C_BD:C_BD + 128] = bd
    c[:, C_MU:C_MU + 128] = bd * (s[:, None] < s[None, :])
    c[:, C_ML:C_ML + 128] = bd * (s[:, None] > s[None, :])
    c[:, C_MUI:C_MUI + 64] = (s[:, None] <= np.arange(64)[None, :])
    rst = np.ones(256, np.float32)
    rst[::64] = 0
    c[:, C_RST:C_RST + 256] = rst[None, :]
    inv = (np.float32(ROPE_THETA) ** (-np.arange(0, 16, 2, dtype=np.float32) / np.float32(16))).astype(np.float32)
    for tb in range(NTB):
        pos = (tb * 128 + p) if tb < 16 else (TP + p)
        ang = pos.astype(np.float32)[:, None] * inv[None, :]
        co = np.cos(ang).astype(np.float32)
        si = np.sin(ang).astype(np.float32)
        c[:, C_COS + tb * 16:C_COS + tb * 16 + 8] = co
        c[:, C_COS + tb * 16 + 8:C_COS + tb * 16 + 16] = co
        c[:, C_SIN + tb * 16:C_SIN + tb * 16 + 8] = si
        c[:, C_SIN + tb * 16 + 8:C_SIN + tb * 16 + 16] = si
    return c


CHAIN_BF16 = True
LACC_ENG = ("dve", "dve")


def r32(t):
    if CHAIN_BF16:
        return t
    return T(t.ap.bitcast(F32R), t.bufs)


class Arena:
    def __init__(self, t, width):
        self.t = t
        self.W = width
        self.off = 0

    def seek(self, off):
        self.off = off

    def alloc(self, free_shape, dtype):
        n = 1
        for d in free_shape:
            n *= d
        words = n if dtype != BF16 else (n + 1) // 2
        words = (words + 7) // 8 * 8
        assert self.off + words <= self.W, ("arena overflow", self.off, words, self.W)
        ap = self.t.ap[:, self.off:self.off + words]
        if dtype == BF16:
            ap = ap.bitcast(BF16)
        ap = ap[:, 0:n]
        if len(free_shape) > 1:
            names = [f"d{i}" for i in range(len(free_shape))]
            pat = "p (" + " ".join(names) + ") -> p " + " ".join(names)
            kw = {nm: sz for nm, sz in zip(names[:-1], free_shape[:-1])}
            ap = ap.rearrange(pat, **kw)
        self.off += words
        return T(ap, [Buf()])


def _run_interleaved(gens):
    alive = list(gens)
    while alive:
        for gen in list(alive):
            try:
                next(gen)
            except StopIteration:
                alive.remove(gen)


B_WEIGHT = 7


def _run_weighted(gens, weights):
    alive = [(g, w) for g, w in zip(gens, weights)]
    while alive:
        for item in list(alive):
            g, w = item
            for _ in range(w):
                try:
                    next(g)
                except StopIteration:
                    alive.remove(item)
                    break


class Ring:
    def __init__(self, items):
        self.items = items
        self.i = 0

    def next(self):
        t = self.items[self.i % len(self.items)]
        self.i += 1
        return t


class StopBuild(Exception):
    pass


def build(dbg=None, stop_after=None):
    def chk(tag, l=0):
        if stop_after == (tag, l):
            raise StopBuild()

    nc = bass.Bass("TRN2", target_bir_lowering=False)
    S = Sched(nc)

    def din(name, shape):
        return S.dram(name, shape, F32, "ExternalInput")

    def dout(name, shape):
        return S.dram(name, shape, F32, "ExternalOutput")

    xp = din("xp", [TP, D])
    xs = din("xs", [TSV, D])
    ck = din("ck", [2, TP, D])
    cv = din("cv", [2, TP, D])
    swkv = din("swkv", [2, 8, 128, 64])
    sfm = din("sfm", [2, 128, 26])
    cfm = din("cfm", [2, 128, 88])
    pfm = din("pfm", [2, 128, NPF])
    consts = din("consts", [128, NCONST])
    w_in = din("w_in", [2, D, PTOT])
    da_lambda = din("da_lambda", [512])
    w_o_da = din("w_o_da", [2, D, D])
    rw_w2 = din("rw_w2", [2, 64, D])
    rw_a2 = din("rw_a2", [2, 64, D])
    rw_g2 = din("rw_g2", [2, 128, D])
    w_o_rw = din("w_o_rw", [2, D, D])
    w_out = din("w_out", [2, D, D])
    w_up = din("w_up", [2, D, 2 * DFF])
    w_down = din("w_down", [2, DFF, D])
    nfin = din("nfin", [D])

    yp = dout("yp", [TP, D])
    ys = dout("ys", [TSV, D])
    kp = dout("kp", [2, TP, D])
    vp = dout("vp", [2, TP, D])
    wkvp = dout("wkvp", [2, 8, 128, 64])
    shp = dout("shp", [2, 128, 26])
    cvp = dout("cvp", [2, 128, 88])
    ks = dout("ks", [2, TSV, D])
    vs = dout("vs", [2, TSV, D])
    wkvs = dout("wkvs", [2, 8, 128, 64])
    shs = dout("shs", [2, 128, 26])
    cvs = dout("cvs", [2, 128, 88])
    X1 = S.dram("X1", [NTOK, D], F32, "Internal")
    X2 = S.dram("X2", [NTOK, D], F32, "Internal")
    dbg_out = {}
    if dbg:
        for name, shape in dbg.items():
            dbg_out[name] = dout("dbg_" + name, shape)

    cst = S.sbuf("cst", [128, NCONST], F32)
    S.dma(cst, consts)
    ident_bf = S.sbuf("ident_bf", [128, 128], BF16)
    S.copy(ident_bf, cst[:, C_ID:C_ID + 128])
    ones_bf = S.sbuf("ones_bf", [128, 128], BF16)
    S.memset(ones_bf, 1.0)
    ones_f = S.sbuf("ones_f", [128, 128], F32)
    S.memset(ones_f, 1.0)
    ident_f = cst[:, C_ID:C_ID + 128]
    bdmask = cst[:, C_BD:C_BD + 128]
    bd_bf = S.sbuf("bd_bf", [128, 128], BF16)
    S.copy(bd_bf, cst[:, C_BD:C_BD + 128])
    fr = None if CHAIN_BF16 else S.sbuf("fr", [128, 21, 256], F32)
    pf = S.sbuf("pf", [128, 2, NPF], F32)
    for l in range(2):
        S.dma(pf[:, l, :], pfm[l])
    gfin = S.sbuf("gfin", [128, D], F32)
    S.dma(gfin, nfin.v(nfin.ap.partition_broadcast(128)))
    lamb = S.sbuf("lamb", [128, 512], F32)
    S.dma(lamb, da_lambda.v(da_lambda.ap.partition_broadcast(128)))
    lamv = S.sbuf("lamv", [128, 16], F32)
    epsb = S.sbuf("epsb", [128, 2], F32)
    S.memset(epsb[:, 0:1], EPS)
    S.memset(epsb[:, 1:2], GN_EPS)
    ljunk = S.sbuf("ljunk", [128, 64], F32)
    shp_t = S.sbuf("shp_t", [128, 26], F32)
    shs_t = S.sbuf("shs_t", [128, 26], F32)
    carry_p = S.sbuf("carry_p", [128, 44, 2], F32)
    carry_s = S.sbuf("carry_s", [128, 44, 2], F32)

    ps = [S.psum(f"ps{i}", [128, 512], F32) for i in range(8)]
    psr = Ring(ps)
    halves = []
    for i in range(8):
        halves.append(T(ps[i].ap[:, 0:256], ps[i].bufs))
        halves.append(T(ps[i].ap[:, 256:512], ps[i].bufs))

    AW = ((nc.sbuf_bytes_remaining - 256) // 4) // 8 * 8
    ar = S.sbuf("arena", [128, AW], F32)
    A = Arena(ar, AW)
    hT = A.alloc([8, NTOK], BF16)
    OZ = A.alloc([8, NTOK], BF16)
    MT_OFF = A.off
    MT = A.alloc([8, NTOK], BF16)
    LOC = A.off
    OZ_OFF = MT_OFF - (MT_OFF - 0) // 2 if False else None
    S.memset(hT, 0.0)
    S.memset(OZ, 0.0, eng="dve")
    S.memset(MT, 0.0)

    for l in range(2):
        lam_init = 0.8 - 0.6 * float(np.exp(-0.3 * l))
        b = l * 256
        pr = ljunk
        S.tt(pr, lamb[:, b:b + 64], lamb[:, b + 64:b + 128], ALU.mult)
        S.ts(pr, pr, 1.0, ALU.mult, None, ALU.add, accum=lamv[:, l * 8 + 4:l * 8 + 5])
        S.tt(pr, lamb[:, b + 128:b + 192], lamb[:, b + 192:b + 256], ALU.mult)
        S.ts(pr, pr, 1.0, ALU.mult, None, ALU.add, accum=lamv[:, l * 8 + 5:l * 8 + 6])
        S.actf(lamv[:, l * 8 + 2:l * 8 + 4], lamv[:, l * 8 + 4:l * 8 + 6], AF.Exp)
        S.tt(lamv[:, l * 8:l * 8 + 1], lamv[:, l * 8 + 3:l * 8 + 4], lamv[:, l * 8 + 2:l * 8 + 3], ALU.subtract)
        S.ts(lamv[:, l * 8:l * 8 + 1], lamv[:, l * 8:l * 8 + 1], -lam_init, ALU.add)
        S.ts(lamv[:, l * 8 + 1:l * 8 + 2], pf[:, l, PF_SUBLN:PF_SUBLN + 1], 1.0 - lam_init, ALU.mult)

    def wslab(dst, src_ap):
        return S.dma(dst, T(src_ap.rearrange("(kt p) c -> p kt c", p=128), ()), eng="pool")

    def make_norm_scratch():
        d = {}
        d["junk"] = A.alloc([D], BF16)
        d["xn"] = Ring([A.alloc([D], BF16) for _ in range(2)])
        d["st"] = Ring([A.alloc([4], F32) for _ in range(3)])
        return d

    def norm_stats(x_t, nsc):
        st = nsc["st"].next()
        S.actf(nsc["junk"], x_t, AF.Square, accum=st[:, 0:1])
        S.actf(st[:, 1:2], st[:, 0:1], AF.Sqrt, scale=1.0 / D, bias=epsb[:, 0:1])
        S.recip(st[:, 2:3], st[:, 1:2])
        return st[:, 2:3]

    def norm_to_hT(x_t, l, pfoff, tb, nsc):
        rstd = norm_stats(x_t, nsc)
        xn = nsc["xn"].next()
        S.ts(xn, x_t, rstd, ALU.mult)
        pb = psr.next()
        pst = T(pb.ap.bitcast(BF16).rearrange("p (k t) -> p k t", k=8), pb.bufs)
        for kt in range(8):
            S.tr(pst[:, kt, :], xn[:, kt * 128:(kt + 1) * 128], ident_bf)
        g = pf[:, l, pfoff:pfoff + 8]
        S.tt(hT[:, :, tb * 128:(tb + 1) * 128], pst, g.v(g.ap.unsqueeze(2).broadcast_to([128, 8, 128])), ALU.mult)

    def x_rows(tb):
        return (xp[tb * 128:(tb + 1) * 128, :], 128) if tb < 16 else (xs, TSV)

    def phase_norm0():
        A.seek(LOC)
        nsc = make_norm_scratch()
        xr = Ring([A.alloc([D], F32) for _ in range(3)])
        for tb in range(NTB):
            xt = xr.next()
            src, nr = x_rows(tb)
            if nr < 128:
                S.memset(xt, 0.0)
            S.dma(xt[0:nr, :], src)
            norm_to_hT(xt, 0, PF_NMIX, tb, nsc)

    def phase_rwkv(l):
        A.seek(MT_OFF)
        w_l = w_in[l]
        mu = lambda c: pf[:, l, PF_MU + c:PF_MU + c + 1]
        W2p = A.alloc([D], BF16)
        A2p = A.alloc([D], BF16)
        G2 = A.alloc([D], BF16)
        S.memset(W2p[64:128, :], 0.0)
        S.memset(A2p[0:64, :], 0.0)
        S.dma(W2p[0:64, :], rw_w2[l], eng="pool")
        S.dma(A2p[64:128, :], rw_a2[l], eng="pool")
        S.dma(G2, rw_g2[l], eng="pool")
        tanh_w = A.alloc([NTOK], BF16)
        raw_w = A.alloc([NTOK], BF16)
        sig_g = A.alloc([NTOK], BF16)
        sfm_t = A.alloc([26], F32)
        S.dma(sfm_t, sfm[l])
        NB = 256
        GRP_OFF = A.off
        Wl = A.alloc([8, 256], BF16)
        wslab(Wl, w_l.ap[:, P_DA + 3072:P_DA + 3328])
        u_ring = Ring([A.alloc([513], F32) for _ in range(3)])
        tmp = Ring([A.alloc([512], F32) for _ in range(6)])

        def shifted(psb, n, carry_src, mucol, u_out_last=None, last_idx=None):
            U = u_ring.next()
            if carry_src is None:
                S.memset(U[:, 0:1], 0.0, eng="dve")
            else:
                S.copy(U[:, 0:1], carry_src, eng="dve")
            S.copy(U[:, 1:n + 1], psb[:, 0:n], eng="act")
            d = tmp.next()
            S.tt(d[:, 0:n], U[:, 0:n], U[:, 1:n + 1], ALU.subtract)
            return d, U

        prev = {0: None, 1: None}
        for qi, (c0, n) in enumerate(TQ):
            for which in range(2):
                pb = psr.next()
                for kt in range(8):
                    S.mm(pb[:, 0:n], Wl[:, kt, which * 128:(which + 1) * 128], hT[:, kt, c0:c0 + n],
                         start=(kt == 0), stop=(kt == 7))
                mc = 24 + which
                if qi == 0:
                    carry = None
                elif qi == 4:
                    carry = sfm_t[:, mc:mc + 1]
                else:
                    carry = prev[which]
                d, U = shifted(pb, n, carry, mc)
                us = tmp.next()
                S.stt(us[:, 0:n], d[:, 0:n], mu(mc), U[:, 1:n + 1], ALU.mult, ALU.add)
                prev[which] = U[:, n:n + 1]
                if qi == 3:
                    S.copy(shp_t[:, mc:mc + 1], U[:, n:n + 1], eng="pool")
                if qi == 4:
                    S.copy(shs_t[:, mc:mc + 1], U[:, TSV:TSV + 1], eng="pool")
                if which == 0:
                    S.actf(tanh_w[:, c0:c0 + n], us[:, 0:n], AF.Tanh)
                    S.copy(raw_w[:, c0:c0 + n], us[:, 0:n], eng="act")
                else:
                    S.actf(sig_g[:, c0:c0 + n], us[:, 0:n], AF.Sigmoid)

        chk("rwkv_lora", l)
        S.barrier()
        A.seek(GRP_OFF)
        Wg = A.alloc([8, 768], BF16)
        ASET = []
        for _ in range(2):
            ASET.append((A.alloc([2, NB], BF16), A.alloc([2, NB], BF16), A.alloc([2, NB], BF16), A.alloc([2, NB], BF16),
                         A.alloc([2, NB], BF16), A.alloc([2, NB], F32), A.alloc([2, NB], F32)))
        YT = A.alloc([2, NB], F32)
        S32 = A.alloc([2, 128], F32)
        Sb = A.alloc([2, 128], BF16)
        swt = A.alloc([2, 64], F32)
        carr = A.alloc([8], F32)
        t256g = [Ring([A.alloc([NB], F32) for _ in range(11)]) for _ in range(2)]
        b256g = [Ring([A.alloc([NB], BF16) for _ in range(3)]) for _ in range(2)]
        usrkg = [Ring([A.alloc([NB], F32) for _ in range(2)]) for _ in range(2)]
        u257g = [Ring([A.alloc([NB + 1], F32) for _ in range(2)]) for _ in range(2)]
        NCH = 4
        wide = lambda: A.alloc([2, 128], BF16)
        BDa = [wide() for _ in range(NCH)]
        BDb = [wide() for _ in range(NCH)]
        BDk = [wide() for _ in range(NCH)]
        BDx = Ring([wide() for _ in range(3)])
        AKm = [wide() for _ in range(NCH)]
        Vm = [wide() for _ in range(NCH)]
        Bhm = [wide() for _ in range(NCH)]
        Khm = [wide() for _ in range(NCH)]
        Rm = [wide() for _ in range(NCH)]
        Um = Ring([wide() for _ in range(1)])
        fi = [0]

        def frt():
            if CHAIN_BF16:
                return wide()
            t = T(fr.ap[:, fi[0], :].rearrange("p (g c) -> p g c", g=2), [Buf()])
            fi[0] += 1
            return t

        Nm = [[frt() for _ in range(NCH)] for _ in range(2)]
        NTm = [[frt() for _ in range(NCH)] for _ in range(2)]
        Pm = [frt() for _ in range(NCH)]
        Hm = Ring([frt() for _ in range(1)])
        class _HB:
            def __init__(self):
                self.k = 0
                self.pend = None

            def next(self):
                if self.pend is not None:
                    t = self.pend
                    self.pend = None
                    return t
                i = 2 + self.k % 6
                self.k += 1
                self.pend = halves[2 * i + 1]
                return halves[2 * i]

            def newbank(self):
                self.pend = None

        hbr = _HB()
        psrA = Ring(ps[0:2])

        def w4(t):
            return T(t.ap.rearrange("p g (h t) -> p g h t", h=2), t.bufs)

        def v3(t):
            return T(t.ap.rearrange("p (g c) -> p g c", g=2), t.bufs)

        def v3b(t):
            return T(t.ap.bitcast(BF16)[:, 0:256].rearrange("p (g c) -> p g c", g=2), t.bufs)

        def bcast_mask(coff):
            m = cst[:, coff:coff + 128]
            return T(m.ap.unsqueeze(1).broadcast_to([128, 2, 128]), m.bufs)

        mU_b = bcast_mask(C_MU)
        mL_b = bcast_mask(C_ML)
        id_b = bcast_mask(C_ID)
        bd4 = T(bdmask.ap.rearrange("p (h t) -> p h t", h=2).unsqueeze(1).broadcast_to([128, 2, 2, 64]), bdmask.bufs)
        mui = cst[:, C_MUI:C_MUI + 64]
        mui4 = T(mui.ap.unsqueeze(1).unsqueeze(1).broadcast_to([128, 2, 2, 64]), mui.bufs)
        rstm = cst[:, C_RST:C_RST + NB]

        for grp in range(4):
            for x in range(3):
                wslab(Wg[:, :, x * 256:(x + 1) * 256],
                      w_l.ap[:, P_DA + x * 1024 + grp * 256:P_DA + x * 1024 + (grp + 1) * 256])
            for seq in range(2):
                blocks = [(i * NB, NB) for i in range(8)] if seq == 0 else [(TP, 64)]
                nvalid_last = NB if seq == 0 else TSV
                if seq == 0:
                    S.memset(S32, 0.0, eng="dve")
                    S.memset(Sb, 0.0, eng="dve")
                else:
                    for g in range(2):
                        S.dma(swt[:, g, :], swkv[l, 2 * grp + g])
                    S.tt(w4(S32), swt.v(swt.ap.unsqueeze(2).broadcast_to([128, 2, 2, 64])), bd4, ALU.mult)
                    S.copy(Sb, S32)
                carry = {}
                for _once in range(1):
                    def stage_a(g, bi, c0, n):
                        t256, b256, usrk, u257 = t256g[g], b256g[g], usrkg[g], u257g[g]
                        AT, BT, KT_, RT, VT, WT, BON = ASET[bi % 2]
                        hp = 2 * grp + g
                        us = {}
                        for x in range(3):
                            pb = psrA.next()
                            off = x * 256 + g * 128
                            for kt in range(8):
                                S.mm(pb[:, 0:n], Wg[:, kt, off:off + 128], hT[:, kt, c0:c0 + n],
                                     start=(kt == 0), stop=(kt == 7))
                            mc = x * 8 + hp
                            U = u257.next()
                            if bi == 0:
                                if seq == 0:
                                    S.memset(U[:, 0:1], 0.0, eng="dve")
                                else:
                                    S.copy(U[:, 0:1], sfm_t[:, mc:mc + 1], eng="dve")
                            else:
                                S.copy(U[:, 0:1], carry[(g, x)], eng="dve")
                            S.copy(U[:, 1:n + 1], pb[:, 0:n], eng="act")
                            d = t256.next()
                            S.tt(d[:, 0:n], U[:, 0:n], U[:, 1:n + 1], ALU.subtract)
                            ut = usrk.next() if x < 2 else t256.next()
                            us[x] = ut[:, 0:n]
                            S.stt(ut[:, 0:n], d[:, 0:n], mu(mc), U[:, 1:n + 1], ALU.mult, ALU.add)
                            S.copy(carr[:, g * 3 + x:g * 3 + x + 1], U[:, n:n + 1], eng="act")
                            carry[(g, x)] = carr[:, g * 3 + x:g * 3 + x + 1]
                            yield
                            if bi == len(blocks) - 1:
                                sht = shp_t if seq == 0 else shs_t
                                S.copy(sht[:, mc:mc + 1], U[:, nvalid_last:nvalid_last + 1], eng="pool")
                        us_r, us_k, us_v = us[0], us[1], us[2]
                        S.copy(VT[:, g, 0:n], us_v, eng="act")
                        yield
                        pcol = lambda o: pf[:, l, o + hp:o + hp + 1]
                        pb = psrA.next()
                        S.mm(pb[:, 0:n], W2p[:, hp * 128:(hp + 1) * 128], tanh_w[:, c0:c0 + n])
                        yield
                        sg = t256.next()
                        S.actf(sg[:, 0:n], pb[:, 0:n], AF.Sigmoid, bias=pcol(PF_W0))
                        yield
                        lw = t256.next()
                        S.add("act", (lambda o_, i_: (lambda e: e.mul(o_.ap, i_.ap, DEC_C)))(lw[:, 0:n], sg[:, 0:n]), reads=[sg], writes=[lw])
                        yield
                        pb = psrA.next()
                        S.mm(pb[:, 0:n], A2p[:, hp * 128:(hp + 1) * 128], raw_w[:, c0:c0 + n])
                        yield
                        a_t = t256.next()
                        S.actf(a_t[:, 0:n], pb[:, 0:n], AF.Sigmoid, bias=pcol(PF_A0))
                        yield
                        kk = t256.next()
                        S.ts(kk[:, 0:n], us_k, pcol(PF_KK), ALU.mult)
                        yield
                        sq = b256.next()
                        S.actf(sq[:, 0:n], kk[:, 0:n], AF.Square)
                        yield
                        pb = psrA.next()
                        S.mm(pb[:, 0:n], bd_bf, sq[:, 0:n])
                        yield
                        rn = t256.next()
                        S.actf(rn[:, 0:n], pb[:, 0:n], AF.Sqrt)
                        yield
                        S.ts(rn[:, 0:n], rn[:, 0:n], 1e-12, ALU.max)
                        yield
                        S.recip(rn[:, 0:n], rn[:, 0:n])
                        yield
                        kkn = kk
                        S.tt(kkn[:, 0:n], kk[:, 0:n], rn[:, 0:n], ALU.mult)
                        yield
                        t1 = t256.next()
                        S.ts(t1[:, 0:n], a_t[:, 0:n], -1.0, ALU.add, pcol(PF_KA), ALU.mult)
                        yield
                        kmod = t256.next()
                        S.stt(kmod[:, 0:n], t1[:, 0:n], 1.0, us_k, ALU.add, ALU.mult)
                        yield
                        bb = t1
                        S.tt(bb[:, 0:n], kkn[:, 0:n], a_t[:, 0:n], ALU.mult)
                        yield
                        cum = t256.next()
                        S.scan(cum[:, 0:n], rstm[:, 0:n], lw[:, 0:n], 0.0, ALU.mult, ALU.add)
                        yield
                        cumex = sg
                        S.tt(cumex[:, 0:n], cum[:, 0:n], lw[:, 0:n], ALU.subtract)
                        yield
                        S.actf(WT[:, g, 0:n], cum[:, 0:n], AF.Exp)
                        yield
                        winv = lw
                        S.actf(winv[:, 0:n], cum[:, 0:n], AF.Exp, scale=-1.0)
                        yield
                        wex = cum
                        S.actf(wex[:, 0:n], cumex[:, 0:n], AF.Exp)
                        yield
                        S.stt(AT[:, g, 0:n], kkn[:, 0:n], -1.0, wex[:, 0:n], ALU.mult, ALU.mult)
                        yield
                        S.tt(BT[:, g, 0:n], bb[:, 0:n], winv[:, 0:n], ALU.mult)
                        yield
                        S.tt(KT_[:, g, 0:n], kmod[:, 0:n], winv[:, 0:n], ALU.mult)
                        yield
                        S.tt(RT[:, g, 0:n], us_r, WT[:, g, 0:n], ALU.mult)
                        yield
                        rk = b256.next()
                        S.stt(rk[:, 0:n], us_r, pcol(PF_RK), kmod[:, 0:n], ALU.mult, ALU.mult)
                        yield
                        pb = psrA.next()
                        S.mm(pb[:, 0:n], bd_bf, rk[:, 0:n])
                        yield
                        S.tt(BON[:, g, 0:n], pb[:, 0:n], us_v, ALU.mult)
                        yield
                        if seq == 1:
                            for arr in (AT, BT, KT_, RT, VT):
                                S.memset(arr[:, g, TSV:64], 0.0)

                    def stage_b(bi, c0, n):
                        nch = n // 64
                        AT, BT, KT_, RT, VT, WT, BON = ASET[bi % 2]
                        def chunk_src(arr, ci):
                            a = arr[:, :, ci * 64:(ci + 1) * 64]
                            return T(a.ap.unsqueeze(2).broadcast_to([128, 2, 2, 64]), a.bufs)

                        wcols = []
                        for ci in range(nch):
                            lastc = ci * 64 + (63 if seq == 0 else TSV - 1)
                            wc = WT[:, :, lastc:lastc + 1]
                            wcols.append(wc)
                            wcb = T(wc.ap.broadcast_to([128, 2, 128]), wc.bufs)
                            S.tt(w4(BDa[ci]), chunk_src(AT, ci), bd4, ALU.mult)
                            yield
                            S.tt(w4(BDb[ci]), chunk_src(BT, ci), bd4, ALU.mult)
                            yield
                            S.tt(w4(BDk[ci]), chunk_src(KT_, ci), bd4, ALU.mult)
                            yield
                            bdv = BDx.next()
                            S.tt(w4(bdv), chunk_src(VT, ci), bd4, ALU.mult)
                            yield
                            bdbh = BDx.next()
                            S.tt(bdbh, BDb[ci], wcb, ALU.mult)
                            yield
                            bdkh = BDx.next()
                            S.tt(bdkh, BDk[ci], wcb, ALU.mult, eng="pool")
                            yield
                            hbr.newbank()
                            pN, pNT, pAK, pR, pV, pBh, pKh = [hbr.next() for _ in range(7)]
                            hbr.newbank()
                            for g in range(2):
                                S.mm(v3(pN)[:, g, :], BDb[ci][:, g, :], BDa[ci][:, g, :])
                                yield
                            for g in range(2):
                                S.mm(v3(pNT)[:, g, :], BDa[ci][:, g, :], BDb[ci][:, g, :])
                                yield
                            for g in range(2):
                                S.mm(v3(pAK)[:, g, :], BDk[ci][:, g, :], BDa[ci][:, g, :])
                                yield
                            pR4 = T(pR.ap.rearrange("p (g x t) -> p g x t", g=2, x=2), pR.bufs)
                            for g in range(2):
                                S.mm(pR4[:, g, 0, :], BDb[ci][:, g, :], RT[:, g, ci * 64:(ci + 1) * 64])
                                yield
                                S.mm(pR4[:, g, 1, :], BDk[ci][:, g, :], RT[:, g, ci * 64:(ci + 1) * 64])
                                yield
                            for g in range(2):
                                S.tr(v3b(pV)[:, g, :], bdv[:, g, :], ident_bf)
                                yield
                                S.tr(v3b(pBh)[:, g, :], bdbh[:, g, :], ident_bf)
                                yield
                                S.tr(v3b(pKh)[:, g, :], bdkh[:, g, :], ident_bf)
                                yield
                            S.tt(r32(Nm[0][ci]), v3(pN), mU_b, ALU.mult)
                            yield
                            S.tt(r32(NTm[0][ci]), v3(pNT), mL_b, ALU.mult)
                            yield
                            S.tt(AKm[ci], v3(pAK), mU_b, ALU.mult)
                            yield
                            S.tt(w4(Rm[ci]), pR4, mui4, ALU.mult)
                            yield
                            S.copy(Vm[ci], v3b(pV), eng="act")
                            yield
                            S.copy(Bhm[ci], v3b(pBh), eng="act")
                            yield
                            S.copy(Khm[ci], v3b(pKh), eng="act")
                            yield
                            S.tt(r32(Pm[ci]), Nm[0][ci], id_b, ALU.add)
                            yield
                        cur = 0
                        for rd in range(1, 6):
                            nxt = 1 - cur
                            pairs = []
                            for ci in range(nch):
                                hbr.newbank()
                                pN2 = hbr.next() if rd < 5 else None
                                pNT2 = hbr.next()
                                for g in range(2):
                                    if rd < 5:
                                        S.mm(v3(pN2)[:, g, :], r32(NTm[cur][ci][:, g, :]), r32(Nm[cur][ci][:, g, :]))
                                    S.mm(v3(pNT2)[:, g, :], r32(Nm[cur][ci][:, g, :]), r32(NTm[cur][ci][:, g, :]))
                                pairs.append((pN2, pNT2))
                            for ci in range(nch):
                                pN2, pNT2 = pairs[ci]
                                S.copy(r32(NTm[nxt][ci]), v3(pNT2), eng="act")
                                yield
                                if rd < 5:
                                    S.copy(r32(Nm[nxt][ci]), v3(pN2), eng="act")
                            pps = []
                            for ci in range(nch):
                                hbr.newbank()
                                pP = hbr.next()
                                for g in range(2):
                                    if ci % 2 == 1:
                                        S.mm(v3(pP)[:, g, :], r32(NTm[nxt][ci][:, g, :]), r32(Pm[ci][:, g, :]), start=True, stop=False)
                                        S.mm(v3(pP)[:, g, :], ident_bf, r32(Pm[ci][:, g, :]), start=False, stop=True)
                                    else:
                                        S.mm(v3(pP)[:, g, :], r32(NTm[nxt][ci][:, g, :]), r32(Pm[ci][:, g, :]))
                                pps.append(pP)
                            for ci in range(nch):
                                if ci % 2 == 1:
                                    S.copy(r32(Pm[ci]), v3(pps[ci]), eng="act")
                                else:
                                    S.tt(r32(Pm[ci]), v3(pps[ci]), Pm[ci], ALU.add)
                                yield
                            cur = nxt
                        for ci in range(nch):
                            hbr.newbank()
                            pH = hbr.next()
                            for g in range(2):
                                S.mm(v3(pH)[:, g, :], BDa[ci][:, g, :], Sb[:, g, :], start=True, stop=False)
                                yield
                                S.mm(v3(pH)[:, g, :], AKm[ci][:, g, :], Vm[ci][:, g, :], start=False, stop=True)
                                yield
                            H = Hm.next()
                            S.copy(r32(H), v3(pH), eng="act")
                            yield
                            hbr.newbank()
                            pU = hbr.next()
                            for g in range(2):
                                S.mm(v3(pU)[:, g, :], r32(Pm[ci][:, g, :]), r32(H[:, g, :]))
                                yield
                            U = Um.next()
                            S.copy(U, v3(pU), eng="act")
                            yield
                            hbr.newbank()
                            pY = hbr.next()
                            pY3 = T(pY.ap[:, 0:128].rearrange("p (g t) -> p g t", g=2), pY.bufs)
                            R4 = w4(Rm[ci])
                            for g in range(2):
                                S.mm(pY3[:, g, :], Sb[:, g, :], RT[:, g, ci * 64:(ci + 1) * 64], start=True, stop=False)
                                yield
                                S.mm(pY3[:, g, :], U[:, g, :], R4[:, g, 0, :], start=False, stop=False)
                                yield
                                S.mm(pY3[:, g, :], Vm[ci][:, g, :], R4[:, g, 1, :], start=False, stop=True)
                                yield
                            S.copy(YT[:, :, ci * 64:(ci + 1) * 64], pY3, eng="act")
                            yield
                            hbr.newbank()
                            pS = hbr.next()
                            for g in range(2):
                                S.mm(v3(pS)[:, g, :], Bhm[ci][:, g, :], U[:, g, :], start=True, stop=False)
                                yield
                                S.mm(v3(pS)[:, g, :], Khm[ci][:, g, :], Vm[ci][:, g, :], start=False, stop=True)
                                yield
                            for g in range(2):
                                S.stt(S32[:, g, :], S32[:, g, :], wcols[ci][:, g, :], v3(pS)[:, g, :], ALU.mult, ALU.add)
                                yield
                            S.copy(Sb, S32, eng="act")
                            yield
                    def stage_c(g, bi, c0, n):
                        t256, b256 = t256g[g], b256g[g]
                        AT, BT, KT_, RT, VT, WT, BON = ASET[bi % 2]
                        hp = 2 * grp + g
                        pcol = lambda o: pf[:, l, o + hp:o + hp + 1]
                        yb = b256.next()
                        S.copy(yb[:, 0:n], YT[:, g, 0:n], eng="act")
                        yield
                        pb = psrA.next()
                        S.mm(pb[:, 0:n], bd_bf, yb[:, 0:n])
                        yield
                        yc = t256.next()
                        S.stt(yc[:, 0:n], pb[:, 0:n], -1.0 / 64, YT[:, g, 0:n], ALU.mult, ALU.add)
                        yield
                        sq = b256.next()
                        S.actf(sq[:, 0:n], yc[:, 0:n], AF.Square)
                        yield
                        pb = psrA.next()
                        S.mm(pb[:, 0:n], bd_bf, sq[:, 0:n])
                        yield
                        sd = t256.next()
                        S.actf(sd[:, 0:n], pb[:, 0:n], AF.Sqrt, scale=1.0 / 64, bias=epsb[:, 1:2])
                        yield
                        S.recip(sd[:, 0:n], sd[:, 0:n])
                        yield
                        S.tt(yc[:, 0:n], yc[:, 0:n], sd[:, 0:n], ALU.mult)
                        yield
                        S.ts(yc[:, 0:n], yc[:, 0:n], pcol(PF_LNW), ALU.mult, pcol(PF_LNB), ALU.add)
                        yield
                        S.tt(yc[:, 0:n], yc[:, 0:n], BON[:, g, 0:n], ALU.add)
                        yield
                        pb = psrA.next()
                        S.mm(pb[:, 0:n], G2[:, hp * 128:(hp + 1) * 128], sig_g[:, c0:c0 + n])
                        yield
                        S.tt(OZ[:, hp, c0:c0 + n], yc[:, 0:n], pb[:, 0:n], ALU.mult)
                        yield

                    pass
                _run_interleaved([stage_a(0, 0, *blocks[0]), stage_a(1, 0, *blocks[0])])
                for bi, (c0, n) in enumerate(blocks):
                    gens = [stage_b(bi, c0, n)]
                    if bi + 1 < len(blocks):
                        gens += [stage_a(0, bi + 1, *blocks[bi + 1]), stage_a(1, bi + 1, *blocks[bi + 1])]
                    _run_weighted(gens, [B_WEIGHT, 1, 1])
                    _run_interleaved([stage_c(0, bi, c0, n), stage_c(1, bi, c0, n)])
                    chk("rwkv_C", l)
                chk("rwkv_seq", l)
                dst = wkvp if seq == 0 else wkvs
                for g in range(2):
                    for h in range(2):
                        S.dma(dst[l, 2 * grp + g, h * 64:(h + 1) * 64, :], S32[h * 64:(h + 1) * 64, g, h * 64:(h + 1) * 64])
        S.dma(shp[l], shp_t)
        S.dma(shs[l], shs_t)

    def phase_gproj(l, w_o, gate_col0, accumulate):
        A.seek(LOC)
        slabs = Ring([A.alloc([8, 256], BF16) for _ in range(2)])
        sgr = Ring([A.alloc([512], F32) for _ in range(3)])
        tr_ = Ring([A.alloc([512], F32) for _ in range(2)])
        gsl = {}

        def issue_g(nt_):
            if nt_ >= 8:
                return
            sl_ = slabs.next()
            wslab(sl_[:, :, 0:128], w_o[l].ap[:, nt_ * 128:(nt_ + 1) * 128])
            wslab(sl_[:, :, 128:256], w_in[l].ap[:, gate_col0 + nt_ * 128:gate_col0 + (nt_ + 1) * 128])
            gsl[nt_] = sl_

        issue_g(0)
        for nt in range(8):
            issue_g(nt + 1)
            sl = gsl.pop(nt)
            for (c0, n) in TQ:
                pa = psr.next()
                for kt in range(8):
                    S.mm(pa[:, 0:n], sl[:, kt, 0:128], OZ[:, kt, c0:c0 + n], start=(kt == 0), stop=(kt == 7))
                pg = psr.next()
                for kt in range(8):
                    S.mm(pg[:, 0:n], sl[:, kt, 128:256], hT[:, kt, c0:c0 + n], start=(kt == 0), stop=(kt == 7))
                sg = sgr.next()
                S.actf(sg[:, 0:n], pg[:, 0:n], AF.Sigmoid)
                if not accumulate:
                    S.tt(MT[:, nt, c0:c0 + n], pa[:, 0:n], sg[:, 0:n], ALU.mult)
                else:
                    t = tr_.next()
                    S.tt(t[:, 0:n], pa[:, 0:n], sg[:, 0:n], ALU.mult)
                    S.tt(MT[:, nt, c0:c0 + n], t[:, 0:n], MT[:, nt, c0:c0 + n], ALU.add)

    def phase_da(l):
        A.seek(LOC)
        Wd_r = Ring([A.alloc([8, 384], BF16) for _ in range(2)])
        qkv_tok = A.alloc([NTB, 384], BF16)
        q_tok = qkv_tok[:, :, 0:128]
        k_tok = qkv_tok[:, :, 128:256]
        v_tok = qkv_tok[:, :, 256:384]
        kc_r = Ring([A.alloc([16, 128], BF16) for _ in range(2)])
        vc_r = Ring([A.alloc([16, 128], BF16) for _ in range(2)])
        KTh = A.alloc([NTOK + TP], BF16)
        Q1p = A.alloc([512], BF16)
        Q2p = A.alloc([512], BF16)
        Pr = Ring([A.alloc([512], BF16) for _ in range(4)])
        f512 = Ring([A.alloc([512], F32) for _ in range(4)])
        qkvr = Ring([A.alloc([384], F32) for _ in range(2)])
        rtmp = Ring([A.alloc([4, 16], F32) for _ in range(4)])
        Lacc = [A.alloc([512], F32) for _ in range(2)]
        Pd = [A.alloc([512], BF16) for _ in range(4)]
        Pz = A.alloc([512], BF16)
        for m_ in range(4):
            S.memset(Pd[m_], 0.0)
        S.memset(Pz, 0.0)
        S.memset(Q1p, 0.0)
        S.memset(Q2p, 0.0)
        nlam = lamv[:, l * 8:l * 8 + 1]
        subs = lamv[:, l * 8 + 1:l * 8 + 2]
        psOL = [(ps[0], ps[1]), (ps[2], ps[3])]
        pr6 = Ring(ps[4:8])

        def rope_inplace(x, tb):
            x4 = T(x.ap.rearrange("p (m d) -> p m d", m=4), x.bufs)
            cc = cst[:, C_COS + tb * 16:C_COS + tb * 16 + 16]
            ss = cst[:, C_SIN + tb * 16:C_SIN + tb * 16 + 16]
            ccb = T(cc.ap.unsqueeze(1).broadcast_to([128, 4, 16]), cc.bufs)
            ssb = T(ss.ap.unsqueeze(1).broadcast_to([128, 4, 16]), ss.bufs)
            tc_ = rtmp.next()
            ts_ = rtmp.next()
            S.tt(tc_, x4[:, :, 0:16], ccb, ALU.mult)
            S.tt(ts_, x4[:, :, 0:16], ssb, ALU.mult)
            S.tt(x4[:, :, 0:8], tc_[:, :, 0:8], ts_[:, :, 8:16], ALU.subtract)
            S.tt(x4[:, :, 8:16], tc_[:, :, 8:16], ts_[:, :, 0:8], ALU.add)

        def attend(c0, nq, tbs, keyblocks):
            pb = pr6.next()
            pst = T(pb.ap.bitcast(BF16), pb.bufs)
            for i, tb in enumerate(tbs):
                S.tr(pst[:, i * 128:(i + 1) * 128], q_tok[:, tb, :], ident_bf)
            S.copy(Q1p[0:64, 0:nq], pst[0:64, 0:nq], eng="act")
            S.copy(Q2p[64:128, 0:nq], pst[64:128, 0:nq], eng="act")
            o1 = f512.next()
            nk = len(keyblocks)
            steps = [(mp, j) for mp in range(2) for j in range(nk)]
            Ps = {}
            LOOK = 2
            tlast = [None]

            def issue_s(i):
                mp, j = steps[i]
                kap, vap, c_lo, zspec = keyblocks[j]
                Qp = Q1p if mp == 0 else Q2p
                pS = pr6.next()
                S.mm(pS[:, c_lo:nq], kap, Qp[:, c_lo:nq])
                if zspec == "diag":
                    P = Pd[c_lo // 128]
                    S.actf(P[0:64, c_lo:nq], pS[0:64, c_lo:nq], AF.Exp, scale=0.125)
                    S.actf(P[64:128, c_lo + 64:nq], pS[64:128, c_lo + 64:nq], AF.Exp, scale=0.125)
                elif zspec == "rows":
                    P = Pz
                    S.actf(P[0:32, 0:nq], pS[0:32, 0:nq], AF.Exp, scale=0.125)
                else:
                    P = Pr.next()
                    S.actf(P[:, c_lo:nq], pS[:, c_lo:nq], AF.Exp, scale=0.125)
                Ps[i] = P

            def issue_ol(i):
                mp, j = steps[i]
                kap, vap, c_lo, zspec = keyblocks[j]
                P = Ps.pop(i)
                pO, pL = psOL[mp]
                S.mm(pO[:, c_lo:nq], vap, P[:, c_lo:nq], start=(j == 0), stop=(j == nk - 1))
                eng = LACC_ENG[j % 2]
                La = Lacc[j % 2]
                if j < 2:
                    if c_lo > 0:
                        S.memset(La[:, 0:c_lo], 0.0, eng=eng)
                    S.copy(La[:, c_lo:nq], P[:, c_lo:nq], eng=eng)
                else:
                    S.tt(La[:, c_lo:nq], La[:, c_lo:nq], P[:, c_lo:nq], ALU.add, eng=eng)
                if j == nk - 1:
                    Lb = Pr.next()
                    S.tt(Lb[:, 0:nq], Lacc[0][:, 0:nq], Lacc[1][:, 0:nq], ALU.add)
                    S.mm(pL[:, 0:nq], ones_bf, Lb[:, 0:nq])
                    rd = f512.next()
                    S.recip(rd[:, 0:nq], pL[:, 0:nq])
                    if mp == 0:
                        S.tt(o1[:, 0:nq], pO[:, 0:nq], rd[:, 0:nq], ALU.mult)
                    else:
                        t = f512.next()
                        S.tt(t[:, 0:nq], pO[:, 0:nq], rd[:, 0:nq], ALU.mult)
                        S.stt(o1[:, 0:nq], t[:, 0:nq], nlam, o1[:, 0:nq], ALU.mult, ALU.add)
                        tlast[0] = t

            for i in range(len(steps) + LOOK):
                if i < len(steps):
                    issue_s(i)
                if i - LOOK >= 0:
                    issue_ol(i - LOOK)
            t = tlast[0]
            sq = Pr.next()
            S.actf(sq[:, 0:nq], o1[:, 0:nq], AF.Square)
            pb = pr6.next()
            S.mm(pb[:, 0:nq], ones_bf, sq[:, 0:nq])
            sd = t
            S.actf(sd[:, 0:nq], pb[:, 0:nq], AF.Sqrt, scale=1.0 / 128, bias=epsb[:, 0:1])
            S.recip(sd[:, 0:nq], sd[:, 0:nq])
            S.tt(o1[:, 0:nq], o1[:, 0:nq], sd[:, 0:nq], ALU.mult)
            return o1

        hl = {}

        def issue_head(hd_):
            if hd_ >= 8:
                return
            Wd_, kc_, vc_ = Wd_r.next(), kc_r.next(), vc_r.next()
            for x in range(3):
                wslab(Wd_[:, :, x * 128:(x + 1) * 128], w_in[l].ap[:, x * 1024 + hd_ * 128:x * 1024 + (hd_ + 1) * 128])
            S.dma(kc_, T(ck[l].ap[:, hd_ * 128:(hd_ + 1) * 128].rearrange("(j p) c -> p j c", p=128), ()), eng="pool")
            S.dma(vc_, T(cv[l].ap[:, hd_ * 128:(hd_ + 1) * 128].rearrange("(j p) c -> p j c", p=128), ()), eng="pool")
            hl[hd_] = (Wd_, kc_, vc_)

        issue_head(0)
        for hd in range(8):
            issue_head(hd + 1)
            Wd, kc_tok, vc_tok = hl.pop(hd)
            chk("da_load", l)
            for tb in range(NTB):
                if tb == 1:
                    chk("da_tb0", l)
                if tb == 16:
                    chk("da_tb15", l)
                pb = pr6.next()
                for kt in range(8):
                    S.mm(pb[:, 0:384], hT[:, kt, tb * 128:(tb + 1) * 128], Wd[:, kt, :],
                         start=(kt == 0), stop=(kt == 7))
                qkv = qkvr.next()
                S.copy(qkv, pb[:, 0:384], eng="act")
                rope_inplace(qkv[:, 0:256], tb)
                S.copy(qkv_tok[:, tb, :], qkv, eng="act")
                if tb < 16:
                    S.dma(kp[l, tb * 128:(tb + 1) * 128, hd * 128:(hd + 1) * 128], qkv[:, 128:256])
                    S.dma(vp[l, tb * 128:(tb + 1) * 128, hd * 128:(hd + 1) * 128], qkv[:, 256:384])
                else:
                    S.dma(ks[l, :, hd * 128:(hd + 1) * 128], qkv[0:TSV, 128:256])
                    S.dma(vs[l, :, hd * 128:(hd + 1) * 128], qkv[0:TSV, 256:384])
            chk("da_proj", l)
            srcs = [(k_tok, tb) for tb in range(NTB)] + [(kc_tok, j) for j in range(16)]
            base = 0
            ei = 0
            while base < len(srcs):
                grp_ = srcs[base:base + 8]
                pb = pr6.next()
                pst = T(pb.ap.bitcast(BF16), pb.bufs)
                for i, (src, idx) in enumerate(grp_):
                    S.tr(pst[:, i * 128:(i + 1) * 128], src[:, idx, :], ident_bf)
                S.copy(KTh[:, base * 128:(base + len(grp_)) * 128], pst[:, 0:len(grp_) * 128],
                       eng=("act" if ei % 2 == 0 else "dve"))
                base += len(grp_)
                ei += 1
            chk("da_kt", l)
            for i in range(4):
                if i == 1:
                    chk("da_att0", l)
                kb = []
                for j in range(4 * i + 4):
                    m = j - 4 * i
                    c_lo = 128 * m if m > 0 else 0
                    kb.append((KTh[:, j * 128:(j + 1) * 128], v_tok[:, j, :], c_lo, "diag" if m >= 0 else None))
                o = attend(i * 512, 512, [4 * i + m for m in range(4)], kb)
                S.ts(OZ[:, hd, i * 512:(i + 1) * 512], o[:, 0:512], subs, ALU.mult)
            chk("da_attp", l)
            kb = [(KTh[:, NTOK + j * 128:NTOK + (j + 1) * 128], vc_tok[:, j, :], 0, None) for j in range(16)]
            kb.append((KTh[:, TP:TP + 128], v_tok[:, 16, :], 0, "rows"))
            o = attend(TP, TSV, [16], kb)
            S.ts(OZ[:, hd, TP:TP + TSV], o[:, 0:TSV], subs, ALU.mult)
            chk("da_head0", l)

    def phase_out(l):
        A.seek(LOC)
        nsc = make_norm_scratch()
        wo = A.alloc([8, D], BF16)
        wslab(wo[:, :, 0:512], w_out[l].ap[:, 0:512])
        wslab(wo[:, :, 512:1024], w_out[l].ap[:, 512:1024])
        xr = Ring([A.alloc([D], F32) for _ in range(2)])
        x1r = Ring([A.alloc([D], F32) for _ in range(2)])
        for tb in range(NTB):
            xt = xr.next()
            if l == 0:
                src, nr = x_rows(tb)
                if nr < 128:
                    S.memset(xt, 0.0)
                S.dma(xt[0:nr, :], src)
            else:
                S.dma(xt, X2[tb * 128:(tb + 1) * 128, :])
            x1 = x1r.next()
            for half in range(2):
                pb = psr.next()
                for kt in range(8):
                    S.mm(pb, MT[:, kt, tb * 128:(tb + 1) * 128], wo[:, kt, half * 512:(half + 1) * 512],
                         start=(kt == 0), stop=(kt == 7))
                S.tt(x1[:, half * 512:(half + 1) * 512], pb, xt[:, half * 512:(half + 1) * 512], ALU.add)
            S.dma(X1[tb * 128:(tb + 1) * 128, :], x1)
            norm_to_hT(x1, l, PF_NFFN, tb, nsc)

    def phase_ffn(l):
        A.seek(0)
        A.alloc([8, NTOK], BF16)
        wdn = A.alloc([22, D], BF16)
        GT = A.alloc([22, 512], BF16)
        assert A.off <= LOC
        A.seek(LOC)
        nsc = make_norm_scratch()
        for c in range(2):
            for hh in range(2):
                S.dma(wdn[:, 11 * hh:11 * (hh + 1), c * 512:(c + 1) * 512],
                      T(w_down[l].ap[11 * hh * 128:11 * (hh + 1) * 128, c * 512:(c + 1) * 512].rearrange("(kt p) c -> p kt c", p=128), ()), eng="pool")
        slabs = Ring([A.alloc([8, 512], BF16) for _ in range(3)])
        hpr = Ring([A.alloc([514], F32) for _ in range(4)])
        cr = Ring([A.alloc([512], F32) for _ in range(4)])
        xr = Ring([A.alloc([D], F32) for _ in range(2)])
        x2r = Ring([A.alloc([D], F32) for _ in range(2)])
        yr = Ring([A.alloc([D], F32) for _ in range(2)])
        S.memset(carry_p, 0.0)
        S.dma(T(carry_s.ap.rearrange("p a b -> p (a b)"), carry_s.bufs), cfm[l])
        jobs = [(qi, f2) for qi in range(len(TQ)) for f2 in range(11)]
        slab_of = {}

        def issue_slab(k):
            if k >= len(jobs):
                return
            _, f2_ = jobs[k]
            sl_ = slabs.next()
            wslab(sl_[:, :, 0:256], w_up[l].ap[:, f2_ * 256:(f2_ + 1) * 256])
            wslab(sl_[:, :, 256:512], w_up[l].ap[:, DFF + f2_ * 256:DFF + (f2_ + 1) * 256])
            slab_of[k] = sl_

        issue_slab(0)
        issue_slab(1)
        for qi, (c0, n) in enumerate(TQ):
            carry = carry_p if qi < 4 else carry_s
            nval = n if qi < 4 else TSV
            for f2 in range(11):
                k_ = qi * 11 + f2
                issue_slab(k_ + 2)
                sl = slab_of.pop(k_)
                for fi in range(2):
                    ft = 2 * f2 + fi
                    cs = []
                    for which in range(2):
                        fidx = which * 22 + ft
                        pb = psr.next()
                        off = which * 256 + fi * 128
                        for kt in range(8):
                            S.mm(pb[:, 0:n], sl[:, kt, off:off + 128], hT[:, kt, c0:c0 + n], start=(kt == 0), stop=(kt == 7))
                        hp_ = hpr.next()
                        S.copy(hp_[:, 0:2], carry[:, fidx, :], eng="act")
                        S.copy(hp_[:, 2:n + 2], pb[:, 0:n], eng="act")
                        S.copy(carry[:, fidx, :], hp_[:, nval:nval + 2], eng="act")
                        cw = lambda j: pf[:, l, PF_CONV + j * 44 + fidx:PF_CONV + j * 44 + fidx + 1]
                        cb = pf[:, l, PF_CONVB + fidx:PF_CONVB + fidx + 1]
                        c_ = cr.next()
                        eng = "dve"
                        S.ts(c_[:, 0:n], hp_[:, 0:n], cw(0), ALU.mult, cb, ALU.add, eng=eng)
                        S.stt(c_[:, 0:n], hp_[:, 1:n + 1], cw(1), c_[:, 0:n], ALU.mult, ALU.add)
                        S.stt(c_[:, 0:n], hp_[:, 2:n + 2], cw(2), c_[:, 0:n], ALU.mult, ALU.add)
                        cs.append(c_)
                    S.actf(cs[0][:, 0:n], cs[0][:, 0:n], AF.Silu)
                    S.tt(GT[:, ft, 0:n], cs[0][:, 0:n], cs[1][:, 0:n], ALU.mult)
            for tbl in range(n // 128):
                tb = c0 // 128 + tbl
                xt = xr.next()
                S.dma(xt, X1[tb * 128:(tb + 1) * 128, :])
                x2 = x2r.next()
                for half in range(2):
                    pb = psr.next()
                    for ft in range(22):
                        S.mm(pb, GT[:, ft, tbl * 128:(tbl + 1) * 128], wdn[:, ft, half * 512:(half + 1) * 512],
                             start=(ft == 0), stop=(ft == 21))
                    S.tt(x2[:, half * 512:(half + 1) * 512], pb, xt[:, half * 512:(half + 1) * 512], ALU.add)
                if l == 1:
                    rstd = norm_stats(x2, nsc)
                    y = yr.next()
                    S.stt(y, x2, rstd, gfin, ALU.mult, ALU.mult)
                    if tb < 16:
                        S.dma(yp[tb * 128:(tb + 1) * 128, :], y)
                    else:
                        S.dma(ys, y[0:TSV, :])
                else:
                    S.dma(X2[tb * 128:(tb + 1) * 128, :], x2)
                    norm_to_hT(x2, 1, PF_NMIX, tb, nsc)
        S.dma(cvp[l], T(carry_p.ap.rearrange("p a b -> p (a b)"), carry_p.bufs))
        S.dma(cvs[l], T(carry_s.ap.rearrange("p a b -> p (a b)"), carry_s.bufs))

    def dump(name, src):
        if name in dbg_out:
            S.barrier()
            S.dma(dbg_out[name], src)
            S.barrier()

    try:
        _program(S, locals())
    except StopBuild:
        pass
    S.emit()
    S.close()
    return nc


def _program(S, L):
    phase_norm0, phase_rwkv, phase_gproj, phase_da, phase_out, phase_ffn = (
        L["phase_norm0"], L["phase_rwkv"], L["phase_gproj"], L["phase_da"], L["phase_out"], L["phase_ffn"])
    dump, stop_after, hT, OZ, MT = L["dump"], L["stop_after"], L["hT"], L["OZ"], L["MT"]
    w_o_rw, w_o_da = L["w_o_rw"], L["w_o_da"]
    S.barrier()
    phase_norm0()
    S.barrier()
    dump("hT0", hT)
    done = False
    for l in range(2):
        if stop_after == ("norm", l):
            break
        phase_rwkv(l)
        S.barrier()
        dump(f"Z{l}", OZ)
        if stop_after == ("rwkv", l):
            break
        phase_gproj(l, w_o_rw, P_DA + P_RW + D, False)
        S.barrier()
        if stop_after == ("gproj1", l):
            break
        phase_da(l)
        S.barrier()
        dump(f"O{l}", OZ)
        if stop_after == ("da", l):
            break
        phase_gproj(l, w_o_da, P_DA + P_RW, True)
        S.barrier()
        dump(f"M{l}", MT)
        phase_out(l)
        S.barrier()
        dump(f"h2T{l}", hT)
        if stop_after == ("out", l):
            break
        phase_ffn(l)
        S.barrier()
        if stop_after == ("ffn", l):
            break


_NC_CACHE = {}


def _prep_inputs(inp):
    f = lambda a: np.ascontiguousarray(np.asarray(a, dtype=np.float32))
    g = {k: f(v) for k, v in inp.items()}
    consts = make_consts()

    def fm(v, nt):
        return v.reshape(nt, 128).T

    pfm = np.zeros((2, 128, NPF), np.float32)
    for l in range(2):
        pfm[l, :, PF_NMIX:PF_NMIX + 8] = fm(g["norm_mix"][l], 8)
        pfm[l, :, PF_NFFN:PF_NFFN + 8] = fm(g["norm_ffn"][l], 8)
        pfm[l, :, PF_MU:PF_MU + 26] = fm(g["rw_mu"][l], 26)
        pfm[l, :, PF_W0:PF_W0 + 8] = fm(g["rw_w0"][l], 8)
        pfm[l, :, PF_A0:PF_A0 + 8] = fm(g["rw_a0"][l], 8)
        pfm[l, :, PF_KK:PF_KK + 8] = fm(g["rw_k_k"][l], 8)
        pfm[l, :, PF_KA:PF_KA + 8] = fm(g["rw_k_a"][l], 8)
        pfm[l, :, PF_RK:PF_RK + 8] = fm(g["rw_r_k"][l].reshape(-1), 8)
        pfm[l, :, PF_LNW:PF_LNW + 8] = fm(g["rw_ln_w"][l], 8)
        pfm[l, :, PF_LNB:PF_LNB + 8] = fm(g["rw_ln_b"][l], 8)
        for j in range(3):
            pfm[l, :, PF_CONV + j * 44:PF_CONV + (j + 1) * 44] = fm(g["ffn_conv"][l, j], 44)
        pfm[l, :, PF_CONVB:PF_CONVB + 44] = fm(g["ffn_conv_b"][l], 44)
        pfm[l, :, PF_SUBLN] = g["da_subln"][l]
    shared = {
        "pfm": pfm, "consts": consts, "w_in": g["w_in"], "da_lambda": g["da_lambda"].reshape(512),
        "w_o_da": g["w_o_da"], "rw_w2": g["rw_w2"], "rw_a2": g["rw_a2"], "rw_g2": g["rw_g2"],
        "w_o_rw": g["w_o_rw"], "w_out": g["w_out"], "w_up": g["w_up"], "w_down": g["w_down"],
        "nfin": g["norm_final"],
    }
    maps = []
    for b in range(8):
        m = dict(shared)
        m["xp"] = g["x_prompt"][b]
        m["xs"] = g["x_sample"][b]
        m["ck"] = np.ascontiguousarray(g["cache_k"][:, b].reshape(2, TP, D))
        m["cv"] = np.ascontiguousarray(g["cache_v"][:, b].reshape(2, TP, D))
        sw = g["state_wkv"][:, b].reshape(2, 8, 2, 64, 64).transpose(0, 1, 2, 4, 3).reshape(2, 8, 128, 64)
        m["swkv"] = np.ascontiguousarray(sw)
        m["sfm"] = np.ascontiguousarray(g["state_shift"][:, b, 0].reshape(2, 26, 128).transpose(0, 2, 1))
        cf = g["state_ffn_conv"][:, b].reshape(2, 2, 44, 128).transpose(0, 3, 2, 1).reshape(2, 128, 88)
        m["cfm"] = np.ascontiguousarray(cf)
        maps.append(m)
    return maps


def _assemble(results):
    def st(name):
        return np.stack([np.asarray(r[name]) for r in results], axis=0)

    y_prompt = st("yp")
    y_sample = st("ys")
    k_prompt = st("kp").transpose(1, 0, 2, 3).reshape(2, 8, TP, 8, 128)
    v_prompt = st("vp").transpose(1, 0, 2, 3).reshape(2, 8, TP, 8, 128)
    k_sample = st("ks").transpose(1, 0, 2, 3).reshape(2, 8, TSV, 8, 128)
    v_sample = st("vs").transpose(1, 0, 2, 3).reshape(2, 8, TSV, 8, 128)

    def wkv(name):
        a = st(name).reshape(8, 2, 8, 2, 64, 64).transpose(1, 0, 2, 3, 5, 4).reshape(2, 8, 16, 64, 64)
        return np.ascontiguousarray(a)

    def shift(name):
        a = st(name).transpose(1, 0, 3, 2).reshape(2, 8, 1, P_RW)
        return np.ascontiguousarray(a)

    def conv(name):
        a = st(name).reshape(8, 2, 128, 44, 2).transpose(1, 0, 4, 3, 2).reshape(2, 8, 2, 2 * DFF)
        return np.ascontiguousarray(a)

    return (np.ascontiguousarray(y_prompt), np.ascontiguousarray(y_sample),
            np.ascontiguousarray(k_prompt), np.ascontiguousarray(v_prompt),
            wkv("wkvp"), shift("shp"), conv("cvp"),
            np.ascontiguousarray(k_sample), np.ascontiguousarray(v_sample),
            wkv("wkvs"), shift("shs"), conv("cvs"))


def kernel(**inputs):
    maps = _prep_inputs(inputs)
    nc = build()
    res = run_bass_kernel_spmd(nc, maps, core_ids=list(range(8)))
    return _assemble(res.results)
```
